# Optimizing a Trainium2 kernel written in Bass

```python
import math
import jax, jax.numpy as jnp
from jax import lax
import numpy as np

D_MODEL = 1024
BATCH = 8
SEQ = 2048
DEPTH = 4
DEC_BATCH = 128
DEC_SEQ = 4
PAST_LEN = 16384
PAGE_SIZE = 128

N_MIXERS = 2
N_CHUNK_LAYERS = (DEPTH + 1) // 2
N_SSM_LAYERS = DEPTH // 2
CHUNK = 128
EXP_A = 2 * D_MODEL
N_GROUPS_A = 8
GROUP_DIM_A = EXP_A // N_GROUPS_A
EXP_B = D_MODEL
SSM_GROUP = 16
N_GROUPS_B = EXP_B // SSM_GROUP
STATE_P = 64
DT_MIN = 1e-3
DT_MAX = 1e-1
EPS = 1e-6

kernel_name = "hybrid_chunk_gmlp_s5_decode_step"


def rmsnorm(x, g):
    xf = x.astype(jnp.float32)
    y = xf * lax.rsqrt(jnp.mean(xf * xf, axis=-1, keepdims=True) + EPS)
    return (y * g.astype(jnp.float32)).astype(x.dtype)


def layernorm(x, g, b):
    xf = x.astype(jnp.float32)
    mu = jnp.mean(xf, axis=-1, keepdims=True)
    xc = xf - mu
    y = xc * lax.rsqrt(jnp.mean(xc * xc, axis=-1, keepdims=True) + EPS)
    return (y * g.astype(jnp.float32) + b.astype(jnp.float32)).astype(x.dtype)


def chunk_gmlp(h, w_in, ln_g, ln_b, w_s, b_s, w_out):
    bn, seq_len, _ = h.shape
    u, v, z = jnp.split(h @ w_in, 3, axis=-1)
    v = layernorm(v, ln_g, ln_b)
    c = min(seq_len, CHUNK)
    n_chunks = seq_len // c
    mask = jnp.tril(jnp.ones((c, c), dtype=bool))
    ws = jnp.where(mask, w_s[:, :c, :c], 0)
    vc = v.reshape(bn, n_chunks, c, N_GROUPS_A, GROUP_DIM_A)
    s = jnp.einsum('hts,bnshd->bnthd', ws, vc) + jnp.transpose(b_s[:, :c])[None, None, :, :, None]
    gated = u * s.reshape(bn, seq_len, EXP_A) * jax.nn.silu(z)
    return gated @ w_out, v


def zoh(a_re, a_im, log_dt, b_re, b_im):
    dt = jnp.exp(log_dt.astype(jnp.float32))[:, None]
    ar = a_re.astype(jnp.float32)
    ai = a_im.astype(jnp.float32)
    mag = jnp.exp(dt * ar)
    ang = dt * ai
    abar_re = mag * jnp.cos(ang)
    abar_im = mag * jnp.sin(ang)
    nr = abar_re - 1.0
    ni = abar_im
    den = ar * ar + ai * ai
    coef_re = (nr * ar + ni * ai) / den
    coef_im = (ni * ar - nr * ai) / den
    br = b_re.astype(jnp.float32)
    bi = b_im.astype(jnp.float32)
    bbar_re = coef_re[..., None] * br - coef_im[..., None] * bi
    bbar_im = coef_re[..., None] * bi + coef_im[..., None] * br
    return abar_re, abar_im, bbar_re, bbar_im


def _combine(e1, e2):
    a1r, a1i, b1r, b1i = e1
    a2r, a2i, b2r, b2i = e2
    return (a2r * a1r - a2i * a1i,
            a2r * a1i + a2i * a1r,
            a2r * b1r - a2i * b1i + b2r,
            a2r * b1i + a2i * b1r + b2i)


def s5_mixer(h, h0_re, h0_im, w_in, a_re, a_im, log_dt, b_re, b_im, c_re, c_im, d_skip,
             w_glu1, b_glu1, w_glu2, b_glu2, w_out):
    bn, seq_len, _ = h.shape
    xb, z = jnp.split(h @ w_in, 2, axis=-1)
    xg = xb.astype(jnp.float32).reshape(bn, seq_len, N_GROUPS_B, SSM_GROUP)
    abr, abi, bbr, bbi = zoh(a_re, a_im, log_dt, b_re, b_im)
    bu_re = jnp.einsum('blgk,gpk->blgp', xg, bbr)
    bu_im = jnp.einsum('blgk,gpk->blgp', xg, bbi)
    ar = jnp.broadcast_to(abr, bu_re.shape)
    ai = jnp.broadcast_to(abi, bu_re.shape)
    acr, aci, hr, hi = lax.associative_scan(_combine, (ar, ai, bu_re, bu_im), axis=1)
    if h0_re is not None:
        h0r = h0_re.astype(jnp.float32)[:, None]
        h0i = h0_im.astype(jnp.float32)[:, None]
        hr = hr + acr * h0r - aci * h0i
        hi = hi + acr * h0i + aci * h0r
    cr = c_re.astype(jnp.float32)
    ci = c_im.astype(jnp.float32)
    y = jnp.einsum('gkp,blgp->blgk', cr, hr) - jnp.einsum('gkp,blgp->blgk', ci, hi)
    y = y.reshape(bn, seq_len, EXP_B) + d_skip.astype(jnp.float32) * xb.astype(jnp.float32)
    y = jax.nn.gelu(y).astype(h.dtype)
    y = (y @ w_glu1 + b_glu1) * jax.nn.sigmoid(y @ w_glu2 + b_glu2)
    out = (y * jax.nn.silu(z)) @ w_out
    return out, hr[:, -1], hi[:, -1]


def _trunk(x, h0_re, h0_im, norm_pre, norm_post,
           w_in_a, ln_v_g, ln_v_b, w_s, b_s, w_out_a,
           w_in_b, a_re, a_im, log_dt, b_re, b_im, c_re, c_im, d_skip,
           w_glu1, b_glu1, w_glu2, b_glu2, w_out_b):
    v_rows, st_re, st_im = [], [], []
    for i in range(DEPTH):
        j = i // N_MIXERS
        hn = rmsnorm(x, norm_pre[i])
        if i % N_MIXERS == 0:
            out, v = chunk_gmlp(hn, w_in_a[j], ln_v_g[j], ln_v_b[j], w_s[j], b_s[j], w_out_a[j])
            v_rows.append(v)
        else:
            h0r = None if h0_re is None else h0_re[j]
            h0i = None if h0_im is None else h0_im[j]
            out, hr, hi = s5_mixer(hn, h0r, h0i, w_in_b[j], a_re[j], a_im[j], log_dt[j],
                                   b_re[j], b_im[j], c_re[j], c_im[j], d_skip[j],
                                   w_glu1[j], b_glu1[j], w_glu2[j], b_glu2[j], w_out_b[j])
            st_re.append(hr)
            st_im.append(hi)
        x = x + rmsnorm(out, norm_post[i])
    return x, jnp.stack(v_rows), jnp.stack(st_re), jnp.stack(st_im)


def setup_inputs(seed: int = 0) -> dict:
    key = jax.random.key(seed)
    ks = jax.random.split(key, 28)
    f32 = jnp.float32
    nrm = lambda k, s, sc: jax.random.normal(k, s, f32) * sc
    na, nb = N_CHUNK_LAYERS, N_SSM_LAYERS
    n_idx = jnp.arange(STATE_P, dtype=f32)
    return {
        "x_prompt": nrm(ks[0], (BATCH, SEQ, D_MODEL), 1.0),
        "x_sample": nrm(ks[1], (DEC_BATCH, DEC_SEQ, D_MODEL), 1.0),
        "state_ssm_re": nrm(ks[2], (nb, DEC_BATCH, N_GROUPS_B, STATE_P), 0.3),
        "state_ssm_im": nrm(ks[3], (nb, DEC_BATCH, N_GROUPS_B, STATE_P), 0.3),
        "norm_pre": 1.0 + nrm(ks[4], (DEPTH, D_MODEL), 0.02),
        "norm_post": 1.0 + nrm(ks[5], (DEPTH, D_MODEL), 0.02),
        "w_in_a": nrm(ks[6], (na, D_MODEL, 3 * EXP_A), D_MODEL ** -0.5),
        "ln_v_g": 1.0 + nrm(ks[7], (na, EXP_A), 0.02),
        "ln_v_b": nrm(ks[8], (na, EXP_A), 0.01),
        "w_s": nrm(ks[9], (na, N_GROUPS_A, CHUNK, CHUNK), CHUNK ** -0.5),
        "b_s": 1.0 + nrm(ks[10], (na, N_GROUPS_A, CHUNK), 0.1),
        "w_out_a": nrm(ks[11], (na, EXP_A, D_MODEL), EXP_A ** -0.5),
        "w_in_b": nrm(ks[12], (nb, D_MODEL, 2 * EXP_B), D_MODEL ** -0.5),
        "a_re": -0.5 + nrm(ks[13], (nb, N_GROUPS_B, STATE_P), 0.01),
        "a_im": jnp.pi * n_idx + nrm(ks[14], (nb, N_GROUPS_B, STATE_P), 0.01),
        "log_dt": jax.random.uniform(ks[15], (nb, N_GROUPS_B), f32, math.log(DT_MIN), math.log(DT_MAX)),
        "b_re": nrm(ks[16], (nb, N_GROUPS_B, STATE_P, SSM_GROUP), (2 * SSM_GROUP) ** -0.5),
        "b_im": nrm(ks[17], (nb, N_GROUPS_B, STATE_P, SSM_GROUP), (2 * SSM_GROUP) ** -0.5),
        "c_re": nrm(ks[18], (nb, N_GROUPS_B, SSM_GROUP, STATE_P), (2 * STATE_P) ** -0.5),
        "c_im": nrm(ks[19], (nb, N_GROUPS_B, SSM_GROUP, STATE_P), (2 * STATE_P) ** -0.5),
        "d_skip": nrm(ks[20], (nb, EXP_B), 1.0),
        "w_glu1": nrm(ks[21], (nb, EXP_B, EXP_B), EXP_B ** -0.5),
        "b_glu1": nrm(ks[22], (nb, EXP_B), 0.01),
        "w_glu2": nrm(ks[23], (nb, EXP_B, EXP_B), EXP_B ** -0.5),
        "b_glu2": nrm(ks[24], (nb, EXP_B), 0.01),
        "w_out_b": nrm(ks[25], (nb, EXP_B, D_MODEL), EXP_B ** -0.5),
    }


def reference(x_prompt, x_sample, state_ssm_re, state_ssm_im, norm_pre, norm_post,
              w_in_a, ln_v_g, ln_v_b, w_s, b_s, w_out_a,
              w_in_b, a_re, a_im, log_dt, b_re, b_im, c_re, c_im, d_skip,
              w_glu1, b_glu1, w_glu2, b_glu2, w_out_b):
    y_prompt, _, ssm_re_prompt, ssm_im_prompt = _trunk(
        x_prompt, None, None, norm_pre, norm_post,
        w_in_a, ln_v_g, ln_v_b, w_s, b_s, w_out_a,
        w_in_b, a_re, a_im, log_dt, b_re, b_im, c_re, c_im, d_skip,
        w_glu1, b_glu1, w_glu2, b_glu2, w_out_b)
    y_sample, chunk_v_sample, ssm_re_sample, ssm_im_sample = _trunk(
        x_sample, state_ssm_re, state_ssm_im, norm_pre, norm_post,
        w_in_a, ln_v_g, ln_v_b, w_s, b_s, w_out_a,
        w_in_b, a_re, a_im, log_dt, b_re, b_im, c_re, c_im, d_skip,
        w_glu1, b_glu1, w_glu2, b_glu2, w_out_b)
    return (y_prompt, y_sample, chunk_v_sample, ssm_re_prompt, ssm_im_prompt, ssm_re_sample, ssm_im_sample)
```

```python
import os
import numpy as np
import concourse.bass as bass
import concourse.mybir as mybir
from concourse.bass_utils import run_bass_kernel_spmd

F32 = mybir.dt.float32
BF16 = mybir.dt.bfloat16
I32 = mybir.dt.int32
AF = mybir.ActivationFunctionType
ALU = mybir.AluOpType

NCORES = 8
D = 1024
NTA = 1056
NTB = 1088
CB = 136
TILES_A = [(0, 512), (512, 512), (1024, 32)]
TILES_B = [(0, 512), (512, 512), (1024, 64)]
EPS = 1e-6
NBLK_PER_J = 26
SCAN_NOSYNC = os.environ.get("SCAN_SYNC", "0") != "1"
BUILD_NOSYNC = SCAN_NOSYNC


class Tl:
    __slots__ = ("name", "w", "r", "excl")

    def __init__(self, name, excl=False):
        self.name = name
        self.w = None
        self.r = {}
        self.excl = excl


class Chan:
    def __init__(self, nc, name):
        self.sem = nc.alloc_semaphore(name)
        self.cnt = 0
        self.name = name
        self.pending = 0
        self.selfw = 0


class Op:
    __slots__ = ("eng", "fn", "chan", "signal", "deps", "semval", "val", "final")


class Prog:
    CE = ("pe", "act", "dve", "pool")

    def __init__(self, nc):
        self.nc = nc
        self.ops = []
        self.sem = {e: nc.alloc_semaphore("s_" + e) for e in self.CE}
        self.chans = []
        self.last = {}

    def chan(self, name):
        c = Chan(self.nc, name)
        self.chans.append(c)
        return c

    def add(self, eng, fn, reads=(), writes=(), chan=None, extra=(), nosync=False):
        op = Op()
        op.eng = eng
        op.fn = fn
        op.chan = chan
        op.signal = False
        op.final = False
        xw = tuple(t for t in reads if t.excl)
        if xw:
            writes = tuple(writes) + xw
        deps = {}
        for t in reads:
            if t.w is not None:
                deps[id(t.w)] = t.w
        for t in writes:
            if t.w is not None:
                deps[id(t.w)] = t.w
            for r in t.r.values():
                deps[id(r)] = r
        for d in extra:
            deps[id(d)] = d
        for t in writes:
            t.w = op
            t.r = {}
        for t in reads:
            key = id(chan) if chan is not None else eng
            t.r[key] = op
        dl = []
        for d in deps.values():
            if d is op:
                continue
            if d.chan is not None:
                dl.append(("dma", d.chan, d.chan.cnt * 16))
                d.chan.pending = max(d.chan.pending, d.chan.cnt * 16)
            else:
                if d.eng == eng and chan is None and (eng == "pe" or nosync):
                    continue
                d.signal = True
                dl.append(("cmp", d))
        if chan is not None and chan.pending > chan.selfw:
            dl.append(("dma", chan, chan.pending))
            chan.selfw = chan.pending
        op.deps = dl
        if chan is not None:
            chan.cnt += 1
            op.val = chan.cnt * 16
        self.ops.append(op)
        self.last[eng if chan is None else ("q", eng)] = op
        return op

    def barrier(self, tiny):
        lasts = [self.last[e] for e in self.CE if e in self.last]
        dmas = [self.last[k] for k in self.last if isinstance(k, tuple)]
        out = []
        for e in ("act", "dve", "pool"):
            out.append(self.add(e, tiny[e], extra=lasts + dmas))
        return out + dmas

    def emit(self):
        cnt = {e: 0 for e in self.CE}
        for op in self.ops:
            if op.chan is None and op.signal:
                cnt[op.eng] += 1
                op.semval = cnt[op.eng]
        nc = self.nc
        by = {e: [o for o in self.ops if o.eng == e] for e in ("pe", "act", "dve", "pool", "sp")}
        sems = self.sem
        chans = self.chans

        def run(e, lst, is_sp=False):
            waited = {}
            for op in lst:
                need = {}
                for d in op.deps:
                    if d[0] == "dma":
                        sem, val = d[1].sem, d[2]
                    else:
                        sem, val = sems[d[1].eng], d[1].semval
                    k = id(sem)
                    if k not in need or need[k][1] < val:
                        need[k] = (sem, val)
                for k, (sem, val) in need.items():
                    if waited.get(k, 0) >= val:
                        continue
                    e.wait_ge(sem, val)
                    waited[k] = val
                ins = op.fn(e)
                if op.chan is not None:
                    ins.then_inc(op.chan.sem, 16)
                elif op.signal:
                    ins.then_inc(sems[op.eng], 1)
            if is_sp:
                for c in chans:
                    n_em = sum(1 for o in self.ops if o.chan is c)
                    if n_em > 0:
                        e.wait_ge(c.sem, n_em * 16)

        with nc.Block() as block:
            @block.tensor
            def _(e):
                run(e, by["pe"])

            @block.scalar
            def _(e):
                run(e, by["act"])

            @block.vector
            def _(e):
                run(e, by["dve"])

            @block.gpsimd
            def _(e):
                run(e, by["pool"])

            @block.sync
            def _(e):
                run(e, by["sp"], True)


def build_program(layers=("A", "B", "A", "B"), npass=2, max_ops=None):
    nc = bass.Bass("TRN2", target_bir_lowering=False)
    P = Prog(nc)

    def din(name, shape, dt=F32):
        return nc.dram_tensor(name, list(shape), dt, kind="ExternalInput").ap()

    def dout(name, shape, dt=F32):
        return nc.dram_tensor(name, list(shape), dt, kind="ExternalOutput").ap()

    xT = din("xT", [2, 128, 8, NTA])
    WALL = din("wall", [2 * NBLK_PER_J, 128, 4096])
    gpre_d = din("gpre", [128, 32])
    gpost_d = din("gpost", [128, 32])
    lncol_d = din("lncol", [128, 2, 2, 16])
    lnbc_d = din("lnbc", [2, 2, 32, 2048])
    wsT_d = din("wsT", [2, 128, 8, 128])
    wsrep_d = din("wsrep", [2, 32, 8, 32])
    bsB_d = din("bsB", [2, 128, 8, 128])
    bglu_d = din("bglu", [128, 2, 2, 8])
    aL2_d = din("aL2", [2, 128, 3, 32])
    bL2_d = din("bL2", [2, 128, 2, 32, 16])
    cL2_d = din("cL2", [2, 128, 2, 32, 16])
    dcol_d = din("dcol", [2, 128, 64])
    h0_d = din("h0L2", [2, 2, 128, 8, 64])
    ident_d = din("ident", [128, 128])
    trim_d = din("trimask", [128, 128])
    bdm_d = din("bdmask", [32, 32])
    tmask_d = din("tmask", [128, 128])
    colm_d = din("colmask", [128, 2, 128])
    rowm_d = din("rowmask", [128, 2])

    yT = dout("yT", [2, 128, 8, NTA])
    cv = dout("cv", [2, 2, 32, 2048])
    stp = dout("stp", [2, 128, 64])
    sts = dout("sts", [2, 2, 128, 8, 64])

    SW = nc.dram_tensor("SW", [2, 32, 128, 512], BF16).ap()
    SMV = nc.dram_tensor("SMV", [2, 64, 128, 384], BF16).ap()
    Xd = nc.dram_tensor("Xd", [1024, NTB], BF16).ap()
    Yd = nc.dram_tensor("Yd", [1024, NTB], BF16).ap()

    def sb(name, shape, dt=F32):
        return nc.alloc_sbuf_tensor("sb_" + name, list(shape), dt)

    A0 = sb("A0", [128, 8, NTA])
    A1 = sb("A1", [128, 8, NTB], BF16)
    A2 = sb("A2", [128, 18432], BF16)
    A3 = sb("A3", [128, 17408], BF16)
    A5 = sb("A5", [128, 8, NTB], BF16)
    NWB = 2
    WB = [sb("wb%d" % i, [128, 4096], BF16) for i in range(NWB)]
    NRB = 3
    RB = [sb("rb%d" % i, [128, 1536], BF16) for i in range(NRB)]
    ident = sb("ident", [128, 128])
    ones_bf = sb("ones_bf", [128, 128], BF16)
    ones_f = sb("ones_f", [128, 128])
    trim = sb("trim", [128, 128])
    bdm = sb("bdm", [128, 32])
    tmask = sb("tmask", [128, 128])
    colm = sb("colm", [128, 2, 128])
    rowm = sb("rowm", [128, 2])
    gpre = sb("gpre", [128, 32])
    gpost = sb("gpost", [128, 32])
    lncol = sb("lncol", [128, 2, 2, 16])
    bglu = sb("bglu", [128, 2, 2, 8])
    epsc = sb("epsc", [128, 1])
    hpic = sb("hpic", [128, 1])
    zero1 = sb("zero1", [128, 1])
    ctab = sb("ctab", [128, 16, 128])
    wsTm = sb("wsTm", [128, 8, 128], BF16)
    bdw = sb("bdw", [128, 8, 32], BF16)
    rst = [sb("rst%d" % i, [128, 512]) for i in range(2)]
    sqr = [sb("sq%d" % i, [128, 512], BF16) for i in range(2)]
    tmpb = [sb("tmpb%d" % i, [128, 512], BF16) for i in range(2)]
    xs = [sb("xs%d" % i, [128, NTB], BF16) for i in range(2)]
    stats = sb("stats", [128, 9, 4, 6])
    mv = sb("mv", [128, 9, 2])
    rstdv = sb("rstdv", [128, 9])
    nmr = sb("nmr", [128, 9])
    AR2 = sb("AR2", [128, 2, 64])
    AIp = sb("AIp", [128, 2, 32])
    AIn = sb("AIn", [128, 2, 32])
    AI2 = sb("AI2", [128, 2, 64])
    M4r = sb("M4r", [128, 2, 64])
    M4p = sb("M4p", [128, 2, 32])
    M4n = sb("M4n", [128, 2, 32])
    Hcar = sb("Hcar", [128, 2, 64])
    Hin0 = sb("Hin0", [128, 64])
    h0t = sb("h0t", [128, 8, 64])
    h0p = sb("h0p", [128, 8, 64])
    hsf = sb("hsf", [128, 8, 64])
    st1 = sb("st1", [128, 8, 64])
    st2 = sb("st2", [128, 8, 64])
    barA = sb("barA", [128, 1])
    barV = sb("barV", [128, 1])
    barG = sb("barG", [128, 1])

    PS = nc.alloc_psum_tensor("PS", [128, 8, 512], F32)
    bank = [Tl("bank%d" % i, True) for i in range(8)]

    Xsb = A0
    Hn = A1
    G = A2[:, :].rearrange("p (n d t) -> p n d t", n=9, d=16)
    Gv = A2[:, :].rearrange("p (n f) -> p n f", n=9)
    Ub = A2[:, 0:17408].bitcast(F32).rearrange("p (c g) -> p c g", g=64)
    O_A = A3[:, 0:16896].bitcast(F32).rearrange("p (k c) -> p k c", k=8)
    O_B = A3[:, :].bitcast(F32).rearrange("p (k c) -> p k c", k=8)
    Xs5 = A3[:, 0:8704].rearrange("p (g c) -> p g c", g=64)
    Hb = A3[:, 8704:17408].rearrange("p (g r c) -> p g r c", g=32, r=2)
    yF = A1
    ltmp = O_A[:, 4, 0:1024].rearrange("p (h t) -> p h t", h=8)
    Gb = A5

    T = {}

    def tl(name):
        if name not in T:
            T[name] = Tl(name)
        return T[name]

    OT_ALL = tuple(tl("o_%d_%d" % (d_, t_)) for d_ in range(8) for t_ in range(3))
    RBT = [tl("rb_%d" % i_) for i_ in range(3)]
    XS5_F = [tuple(tl("xs5_%d_%d" % (s_, f_)) for s_ in range(8)) for f_ in range(8)]
    par = P.chan("par")
    ch_x = [P.chan("chx%d" % i) for i in range(2)]
    ch_out = P.chan("chout")
    ch_w = [P.chan("chw%d" % i) for i in range(NWB)]
    ch_rb = [P.chan("chrb_%d" % i) for i in range(NRB)]
    rbstate = {"n": 0}

    def rbload(dst_view_fn, src_ap, src_tile):
        i = rbstate["n"] % NRB
        rbstate["n"] += 1
        P.add("pool", lambda e, o=dst_view_fn(RB[i]), i_=src_ap: e.dma_start(out=o, in_=i_), (src_tile,),
              (RBT[i],), ch_rb[i])
        return i

    ch_scr = P.chan("chscr")
    ch_scr2 = P.chan("chscr2")
    ch_scr3 = P.chan("chscr3")
    ch_xr = [[P.chan("chxr%d_%d" % (f_, q_)) for q_ in range(2)] for f_ in range(8)]
    ch_prep = P.chan("chprep")
    ch_xs = [P.chan("chxs%d" % i) for i in range(2)]

    def dma(q, out, in_, reads, writes, chan, extra=()):
        return P.add(q, lambda e, o=out, i=in_: e.dma_start(out=o, in_=i), reads, writes, chan, extra)

    def mm(out, lhsT, rhs, start, stop, reads, writes):
        return P.add("pe", lambda e, o=out, l=lhsT, r=rhs, s=start, t=stop:
                     e.matmul(o, l, r, start=s, stop=t), reads, writes)

    def act(out, in_, func, reads, writes, bias=None, scale=None):
        def fn(e, o=out, i=in_, f=func, b=bias, s=scale):
            kw = {}
            if b is not None:
                kw["bias"] = b
            if s is not None:
                kw["scale"] = s
            return e.activation(out=o, in_=i, func=f, **kw)
        return P.add("act", fn, reads, writes)

    def tt(eng, out, in0, in1, op, reads, writes, nosync=False):
        return P.add(eng, lambda e, o=out, a=in0, b=in1, p=op: e.tensor_tensor(out=o, in0=a, in1=b, op=p),
                     reads, writes, nosync=nosync)

    def ts(eng, out, in0, s1, s2, op0, op1, reads, writes, nosync=False):
        def fn(e, o=out, a=in0, x=s1, y=s2, p0=op0, p1=op1):
            if p1 is None:
                return e.tensor_scalar(out=o, in0=a, scalar1=x, scalar2=None, op0=p0)
            return e.tensor_scalar(out=o, in0=a, scalar1=x, scalar2=y, op0=p0, op1=p1)
        return P.add(eng, fn, reads, writes)

    def stt(out, in0, scalar, in1, op0, op1, reads, writes):
        return P.add("dve", lambda e, o=out, a=in0, s=scalar, b=in1, p0=op0, p1=op1:
                     e.scalar_tensor_tensor(out=o, in0=a, scalar=s, in1=b, op0=p0, op1=p1), reads, writes)

    def cp(eng, out, in_, reads, writes):
        if eng == "act":
            return P.add("act", lambda e, o=out, i=in_: e.activation(out=o, in_=i, func=AF.Copy), reads, writes)
        return P.add(eng, lambda e, o=out, i=in_: e.tensor_copy(out=o, in_=i), reads, writes)

    def mset(eng, ap, val, writes):
        return P.add(eng, lambda e, a=ap, v=val: e.memset(a, v), (), writes)

    wstate = {"n": 0}
    wtile = [Tl("wb%d" % i) for i in range(NWB)]

    def wload(blk):
        i = wstate["n"] % NWB
        wstate["n"] += 1
        src = WALL[blk].rearrange("p (a b) -> p a b", b=512)
        dst = WB[i][:, :].rearrange("p (a b) -> p a b", b=512)
        dma("pool", dst, src, (), (wtile[i],), ch_w[i])
        return i

    cst = tl("const")
    for dst, src in ((ident, ident_d), (trim, trim_d), (tmask, tmask_d), (colm, colm_d),
                     (rowm, rowm_d), (gpre, gpre_d), (gpost, gpost_d), (lncol, lncol_d), (bglu, bglu_d)):
        dma("sp", dst[:], src, (), (cst,), par)
    dma("sp", bdm[64:96, :], bdm_d, (), (cst,), par)
    mset("pool", bdw[:], 0.0, (tl("tabA"),))
    mset("dve", ones_f[:], 1.0, (cst,))
    mset("dve", epsc[:], EPS, (cst,))
    mset("dve", hpic[:], float(np.pi / 2), (cst,))
    mset("dve", zero1[:], 0.0, (cst,))
    mset("dve", Hin0[:], 0.0, (cst,))
    mset("dve", stats[:], 0.0, (tl("stats"),))
    mset("dve", mv[:], 1.0, (tl("mv"),))
    mset("dve", barV[:], 0.0, (tl("barV"),))
    mset("pool", barG[:], 0.0, (tl("barG"),))
    cp("dve", ones_bf[:], ones_f[:], (cst,), (cst,))
    act(barA[:], zero1[:], AF.Copy, (cst,), (tl("barA"),))

    def prep_s5(j):
        base = A5[:, :, :].rearrange("p a b -> p (a b)")
        f = base.bitcast(F32)
        off = [0]

        def carve(n, shape=None):
            v = f[:, off[0]:off[0] + n]
            off[0] += n
            return v

        aL = carve(96).rearrange("p (a g) -> p a g", a=3)
        dt_ = carve(32)
        dar = carve(32)
        ang = carve(32)
        mag = carve(32)
        magi = carve(32)
        kf = carve(32)
        ki = carve(32).bitcast(I32)
        rr = carve(32)
        m1 = carve(32)
        sn = carve(32)
        cs = carve(32)
        ab = carve(32)
        t1 = carve(32)
        t2 = carve(32)
        t3 = carve(32)
        t4 = carve(32)
        nr = carve(32)
        den = carve(32)
        cfr = carve(32)
        cfi = carve(32)
        PW = carve(17 * 64).rearrange("p (n r g) -> p n r g", n=17, r=2)
        bL = carve(1024).rearrange("p (r g k) -> p r g k", r=2, g=32)
        cL = carve(1024).rearrange("p (r g k) -> p r g k", r=2, g=32)
        assert off[0] <= 4352
        f1 = A1[:, :, :].rearrange("p a b -> p (a b)").bitcast(F32)
        bbar = f1[:, 0:1024].rearrange("p (r g k) -> p r g k", r=2, g=32)
        big1 = f1[:, 1024:1536].rearrange("p (g k) -> p g k", g=32)
        dcl = f1[:, 2048:2112]
        tp = tl("prep_small")
        dma("sp", aL, aL2_d[j], (), (tp,), par)
        dma("sp", bL, bL2_d[j], (), (tp,), par)
        dma("sp", cL, cL2_d[j], (), (tp,), par)
        dma("sp", dcl, dcol_d[j], (), (tp,), par)
        R = (tp, cst)
        W_ = (tp,)
        ar, ai, ldt = aL[:, 0, :], aL[:, 1, :], aL[:, 2, :]
        act(dt_, ldt, AF.Exp, R, W_)
        tt("dve", dar, dt_, ar, ALU.mult, R, W_)
        tt("dve", ang, dt_, ai, ALU.mult, R, W_)
        act(mag, dar, AF.Exp, R, W_)
        act(magi, dar, AF.Exp, R, W_, scale=-1.0)
        ts("dve", kf, ang, float(1.0 / (2 * np.pi)), None, ALU.mult, None, R, W_)
        cp("dve", ki, kf, R, W_)
        cp("dve", kf, ki, R, W_)
        stt(rr, kf, float(-2 * np.pi), ang, ALU.mult, ALU.add, R, W_)
        ts("dve", m1, rr, float(np.pi), float(-2 * np.pi), ALU.is_gt, ALU.mult, R, W_)
        tt("dve", rr, rr, m1, ALU.add, R, W_)
        ts("dve", m1, rr, float(-np.pi), float(2 * np.pi), ALU.is_lt, ALU.mult, R, W_)
        tt("dve", rr, rr, m1, ALU.add, R, W_)
        act(sn, rr, AF.Sin, R, W_)
        act(ab, rr, AF.Abs, R, W_)
        act(cs, ab, AF.Sin, R, W_, bias=hpic[:], scale=-1.0)
        mset("dve", PW[:, 8, 0, :], 1.0, W_)
        mset("dve", PW[:, 8, 1, :], 0.0, W_)
        tt("dve", PW[:, 9, 0, :], mag, cs, ALU.mult, R, W_)
        tt("dve", PW[:, 9, 1, :], mag, sn, ALU.mult, R, W_)
        tt("dve", PW[:, 7, 0, :], magi, cs, ALU.mult, R, W_)
        stt(PW[:, 7, 1, :], magi, -1.0, sn, ALU.mult, ALU.mult, R, W_)

        wt = [carve(128), carve(128), carve(128), f1[:, 1600:1728]]

        def cmulw(o, a, b, w):
            T = [t_[:, 0:w * 32].rearrange("p (a g) -> p a g", a=w) for t_ in wt]
            a_r, a_i = a[:, :, 0, :], a[:, :, 1, :]
            b_r = b[:, :, 0, :].broadcast_to([128, w, 32])
            b_i = b[:, :, 1, :].broadcast_to([128, w, 32])
            tt("dve", T[0], a_r, b_r, ALU.mult, R, W_)
            tt("dve", T[1], a_i, b_i, ALU.mult, R, W_)
            tt("dve", T[2], a_r, b_i, ALU.mult, R, W_)
            tt("dve", T[3], a_i, b_r, ALU.mult, R, W_)
            tt("dve", o[:, :, 0, :], T[0], T[1], ALU.subtract, R, W_)
            tt("dve", o[:, :, 1, :], T[2], T[3], ALU.add, R, W_)

        cmulw(PW[:, 10:11], PW[:, 9:10], PW[:, 9:10], 1)
        cmulw(PW[:, 6:7], PW[:, 7:8], PW[:, 7:8], 1)
        cmulw(PW[:, 11:13], PW[:, 9:11], PW[:, 10:11], 2)
        cmulw(PW[:, 5:3:-1], PW[:, 7:5:-1], PW[:, 6:7], 2)
        cmulw(PW[:, 13:17], PW[:, 9:13], PW[:, 12:13], 4)
        cmulw(PW[:, 3::-1], PW[:, 7:3:-1], PW[:, 4:5], 4)
        for (src_n, tr, tp_, tn_) in ((16, AR2, AIp, AIn), (4, M4r, M4p, M4n)):
            trv = tr[:, j, :].rearrange("p (g r) -> p g r", r=2)
            cp("dve", trv[:, :, 0], PW[:, src_n, 0, :], R, (cst,))
            cp("dve", trv[:, :, 1], PW[:, src_n, 0, :], R, (cst,))
            cp("dve", tp_[:, j, :], PW[:, src_n, 1, :], R, (cst,))
            ts("dve", tn_[:, j, :], PW[:, src_n, 1, :], -1.0, None, ALU.mult, None, R, (cst,))
        ai2v = AI2[:, j, :].rearrange("p (g r) -> p g r", r=2)
        cp("dve", ai2v[:, :, 0], AIn[:, j, :], (cst,), (cst,))
        cp("dve", ai2v[:, :, 1], AIp[:, j, :], (cst,), (cst,))
        ts("dve", nr, PW[:, 9, 0, :], -1.0, None, ALU.add, None, R, W_)
        ni = PW[:, 9, 1, :]
        tt("dve", t1, ar, ar, ALU.mult, R, W_)
        tt("dve", t2, ai, ai, ALU.mult, R, W_)
        tt("dve", den, t1, t2, ALU.add, R, W_)
        P.add("dve", lambda e, o=den, i=den: e.reciprocal(out=o, in_=i), R, W_)
        tt("dve", t1, nr, ar, ALU.mult, R, W_)
        tt("dve", t2, ni, ai, ALU.mult, R, W_)
        tt("dve", t1, t1, t2, ALU.add, R, W_)
        tt("dve", cfr, t1, den, ALU.mult, R, W_)
        tt("dve", t1, ni, ar, ALU.mult, R, W_)
        tt("dve", t2, nr, ai, ALU.mult, R, W_)
        tt("dve", t1, t1, t2, ALU.subtract, R, W_)
        tt("dve", cfi, t1, den, ALU.mult, R, W_)

        def bc16(v):
            return v.unsqueeze(2).broadcast_to([128, 32, 16])

        tb = tl("prep_bbar")
        RB = (tp, cst, tb)
        WB_ = (tb,)
        tt("dve", bbar[:, 0], bc16(cfr), bL[:, 0], ALU.mult, RB, WB_)
        tt("dve", big1, bc16(cfi), bL[:, 1], ALU.mult, RB, WB_)
        tt("dve", bbar[:, 0], bbar[:, 0], big1, ALU.subtract, RB, WB_)
        tt("dve", bbar[:, 1], bc16(cfr), bL[:, 1], ALU.mult, RB, WB_)
        tt("dve", big1, bc16(cfi), bL[:, 0], ALU.mult, RB, WB_)
        tt("dve", bbar[:, 1], bbar[:, 1], big1, ALU.add, RB, WB_)

        PSM = f1[:, 1024:1568].rearrange("p (n g) -> p n g", n=17)
        tt("dve", PSM, PW[:, :, 0, :], PW[:, :, 1, :], ALU.add, (tp, tl("prep_bbar")), (tp, tl("prep_bbar")))
        for hf in range(2):
            g0 = 16 * hf
            a2f = A2[:, :].bitcast(F32)
            WcL = a2f[:, 0:4096].rearrange("p (g r s k) -> p g r s k", g=16, r=2, s=8)
            WnL = a2f[:, 4096:8192].rearrange("p (g r s k) -> p g r s k", g=16, r=2, s=8)
            a3f = A3[:, :].bitcast(F32)
            VL = a3f[:, 0:4096].rearrange("p (g r s k) -> p g r s k", g=16, r=2, s=8)
            tmps = {"dve": (a3f[:, 4096:4352].rearrange("p (g k) -> p g k", g=16),
                            a3f[:, 4352:4608].rearrange("p (g k) -> p g k", g=16)),
                    "pool": (a3f[:, 6656:6912].rearrange("p (g k) -> p g k", g=16),
                             a3f[:, 6912:7168].rearrange("p (g k) -> p g k", g=16))}
            VLm = a3f[:, 4608:6656].rearrange("p (g x r m) -> p g x r m", g=4, x=2, r=2)
            a0 = A0[:, :, :].rearrange("p a b -> p (a b)")
            Wst = a0[:, 0:4096].bitcast(BF16).rearrange("p (g x r m) -> p g x r m", g=16, x=2, r=2)
            Vst = a0[:, 4096:8192].bitcast(BF16).rearrange("p (g x r m) -> p g x r m", g=16, x=2, r=2)
            M0st = f1[:, 2176:4224].bitcast(BF16).rearrange("p (g m) -> p g m", g=32)
            tw = tl("prep_big")
            tstg = tl("prep_stg")

            def bcg(v):
                return v.unsqueeze(2).broadcast_to([128, 16, 16])

            sums = a3f[:, 7168:8192].rearrange("p (q g k) -> p q g k", q=4, g=16)
            tsm = tl("prep_sums")
            br = bbar[:, 0, g0:g0 + 16, :]
            bi = bbar[:, 1, g0:g0 + 16, :]
            crr = cL[:, 0, g0:g0 + 16, :]
            cii = cL[:, 1, g0:g0 + 16, :]
            tt("dve", sums[:, 0], br, bi, ALU.add, (tp, tb), (tsm,))
            tt("dve", sums[:, 1], bi, br, ALU.subtract, (tp, tb), (tsm,))
            tt("dve", sums[:, 2], crr, cii, ALU.add, (tp, tb), (tsm,))
            tt("dve", sums[:, 3], crr, cii, ALU.subtract, (tp, tb), (tsm,))

            def build(name, dst, n_of_s, src_r, xsum, xdif, kind):
                tiles = []
                for s_ in range(8):
                    n = n_of_s(s_) + 8
                    pr = bcg(PW[:, n, 0, g0:g0 + 16])
                    pi = bcg(PW[:, n, 1, g0:g0 + 16])
                    psm = bcg(PSM[:, n, g0:g0 + 16])
                    eng = "dve"
                    tA, tB = tmps[eng]
                    tmt = tl("prep_tmp_" + eng)
                    mt = tl("prep_%s_%d" % (name, s_))
                    tiles.append(mt)
                    RW = (tp, cst, tb, tsm, tmt, mt)
                    WW = (mt, tmt)
                    d0 = dst[:, :, 0, s_, :]
                    d1 = dst[:, :, 1, s_, :]
                    ns_ = BUILD_NOSYNC
                    tt(eng, d1, psm, src_r, ALU.mult, RW, WW, nosync=ns_)
                    tt(eng, tA, pi, xsum, ALU.mult, RW, WW, nosync=ns_)
                    tt(eng, d0, d1, tA, ALU.subtract, RW, WW, nosync=ns_)
                    tt(eng, tA, pr, xdif, ALU.mult, RW, WW, nosync=ns_)
                    if kind == "w":
                        tt(eng, d1, d1, tA, ALU.add, RW, WW, nosync=ns_)
                    else:
                        tt(eng, d1, tA, d1, ALU.subtract, RW, WW, nosync=ns_)
                return tuple(tiles)

            tWc = build("wc", WcL, lambda s_: 7 - s_, br, sums[:, 0], sums[:, 1], "w")
            tWn = build("wn", WnL, lambda s_: -(s_ + 1), br, sums[:, 0], sums[:, 1], "w")
            tV = build("v", VL, lambda s_: s_ + 1, crr, sums[:, 2], sums[:, 3], "v")
            tvst = tl("prep_vst")
            twst = tl("prep_wst")
            tm0 = tl("prep_m0st")
            twb = tl("prep_wnlb")
            for x in range(2):
                act(Vst[:, :, x].rearrange("p g r m -> p g (r m)"),
                    VL.rearrange("p g r s k -> p g (r s k)"), AF.Copy, tV + (cst,), (tvst,), scale=rowm[:, x:x + 1])
            WnLb = a3f[:, 4608:6656].bitcast(BF16).rearrange("p (g r m) -> p g r m", g=16, r=2)
            act(WnLb.rearrange("p g r m -> p (g r m)"), WnL.rearrange("p g r s k -> p (g r s k)"), AF.Copy,
                tWn, (twb,))
            for gl in range(16):
                for ri in range(2):
                    bk = (gl * 2 + ri) % 8
                    P.add("pe", lambda e, o=PS[:, bk, 0:128], i=WcL[:, gl, ri].rearrange("p s k -> p (s k)"):
                          e.transpose(o, i, ident[:]), tWc + (cst,), (bank[bk],))
                    tt("dve", Wst[:, gl, :, ri, :],
                       PS[:, bk, 0:128].unsqueeze(1).broadcast_to([128, 2, 128]), colm[:], ALU.mult,
                       (bank[bk], cst), (twst,))
            for gl in range(16):
                for x in range(2):
                    bk = (gl * 2 + x) % 8
                    gg = 2 * (g0 + gl) + x
                    for ri in range(2):
                        mm(PS[:, bk, 0:128], WnLb[:, gl, ri, :], Vst[:, gl, x, ri, :], ri == 0, ri == 1,
                           (twb, tvst), (bank[bk],))
                    tt("dve", tmpAB[hf][:], PS[:, bk, 0:128], tmask[:], ALU.mult, (bank[bk], cst, tl("tmpAB")),
                       (tl("tmpAB"),))
                    stt(M0st[:, gl * 2 + x, :], ident[:], dcl[:, gg:gg + 1], tmpAB[hf][:], ALU.mult, ALU.add,
                        (tp, cst, tl("tmpAB")), (tm0,))
            dma("sp", SW[j, g0:g0 + 16].rearrange("g p (x r m) -> p g x r m", x=2, r=2), Wst, (twst,),
                (tl("SW"),), ch_prep)
            dma("sp", SMV[j, 2 * g0:2 * g0 + 32, :, 0:128].rearrange("g p m -> p g m"), M0st, (tm0,),
                (tl("SMV"),), ch_prep)
            dma("sp", SMV[j, 2 * g0:2 * g0 + 32, :, 128:384].rearrange("(g x) p (r m) -> p g x r m", x=2, r=2),
                Vst, (tvst,), (tl("SMV"),), ch_prep)

    tmpAB = [sb("tmpAB%d" % i, [128, 128]) for i in range(2)]

    has_b = "B" in layers
    bar_ops = []
    if has_b:
        for j in range(2):
            prep_s5(j)
        tiny = {
            "act": lambda e: e.activation(out=barA[:], in_=zero1[:], func=AF.Copy),
            "dve": lambda e: e.memset(barV[:], 0.0),
            "pool": lambda e: e.memset(barG[:], 0.0),
        }
        bar_ops = P.barrier(tiny)

    mset("pool", A1[:, :, :], 0.0, tuple(tl("hn_%d_%d" % (kc_, t_)) for kc_ in range(8) for t_ in range(3)))

    ring = {"rst": 0, "sq": 0, "tmpb": 0, "mmb": 0, "xs": 0}
    rst_t = [Tl("rst%d" % i) for i in range(2)]
    sq_t = [Tl("sq%d" % i) for i in range(2)]
    tmpb_t = [Tl("tmpb%d" % i) for i in range(2)]
    xs_t = [Tl("xs%d" % i) for i in range(2)]
    STATB = [3, 4, 5]

    def nxt(name, n):
        i = ring[name] % n
        ring[name] += 1
        return i

    def rstd_from_bank(bk, n):
        i = nxt("rst", 2)
        act(rst[i][:, 0:n], PS[:, bk, 0:n], AF.Ln, (bank[bk], cst), (rst_t[i],), bias=epsc[:], scale=1.0 / D)
        act(rst[i][:, 0:n], rst[i][:, 0:n], AF.Exp, (rst_t[i],), (rst_t[i],), scale=-0.5)
        return i

    def xcols(name, kc, c0, n):
        return tl("%s_%d_%d" % (name, kc, c0))

    def prenorm(layer, kind):
        gcol = gpre[:, layer * 8:(layer + 1) * 8]
        if kind == "B":
            for kc in range(8):
                v = Hn[:, kc, :].rearrange("p (s c) -> p s c", c=CB)[:, 0:4, 128:136]
                mset("pool", v, 0.0, tuple(tl("hn_%d_%d" % (kc, t2)) for t2 in range(3)))
        for ti, (c0, n) in enumerate(TILES_A):
            bk = STATB[ti]
            for kc in range(8):
                i = nxt("sq", 2)
                act(sqr[i][:, 0:n], Xsb[:, kc, c0:c0 + n], AF.Square, (tl("x_%d_%d" % (kc, ti)),), (sq_t[i],))
                mm(PS[:, bk, 0:n], ones_bf[:], sqr[i][:, 0:n], kc == 0, kc == 7, (sq_t[i], cst), (bank[bk],))
            r = rstd_from_bank(bk, n)
            for kc in range(8):
                xin = Xsb[:, kc, c0:c0 + n]
                if kind == "A":
                    stt(Hn[:, kc, c0:c0 + n], xin, gcol[:, kc:kc + 1], rst[r][:, 0:n], ALU.mult, ALU.mult,
                        (tl("x_%d_%d" % (kc, ti)), rst_t[r], cst), (tl("hn_%d_%d" % (kc, ti)),))
                else:
                    hv = Hn[:, kc, :].rearrange("p (s c) -> p s c", c=CB)
                    if ti < 2:
                        ov = hv[:, :, 64 * ti:64 * ti + 64]
                        iv = xin.rearrange("p (c s) -> p s c", s=8)
                        rv = rst[r][:, 0:n].rearrange("p (c s) -> p s c", s=8)
                    else:
                        ov = hv[:, 4:8, 128:136]
                        iv = xin.rearrange("p (q i) -> p i q", i=4)
                        rv = rst[r][:, 0:n].rearrange("p (q i) -> p i q", i=4)
                    stt(ov, iv, gcol[:, kc:kc + 1], rv, ALU.mult, ALU.mult,
                        (tl("x_%d_%d" % (kc, ti)), rst_t[r], cst),
                        tuple(tl("hn_%d_%d" % (kc, t2)) for t2 in range(3)))

    def out_stage_evac(bk, dmc, ti, c0, n, Obuf):
        cp("act", Obuf[:, dmc, c0:c0 + n], PS[:, bk, 0:n], (bank[bk],), (tl("o_%d_%d" % (dmc, ti)),))
        i = nxt("sq", 2)
        act(sqr[i][:, 0:n], PS[:, bk, 0:n], AF.Square, (bank[bk],), (sq_t[i],))
        sb_ = STATB[ti]
        flush_stat()
        pend_stat.append((PS[:, sb_, 0:n], sqr[i][:, 0:n], dmc == 0, dmc == 7, (sq_t[i], cst), (bank[sb_],)))

    pend_stat = []

    def flush_stat():
        while pend_stat:
            o_, r_, st_, sp_, rd_, wr_ = pend_stat.pop(0)
            mm(o_, ones_bf[:], r_, st_, sp_, rd_, wr_)

    def postnorm(layer, kind, Obuf, tiles):
        gcol = gpost[:, layer * 8:(layer + 1) * 8]
        for ti, (c0, n) in enumerate(tiles):
            r = rstd_from_bank(STATB[ti], n)
            for dmc in range(8):
                stt(Obuf[:, dmc, c0:c0 + n], Obuf[:, dmc, c0:c0 + n], gcol[:, dmc:dmc + 1], rst[r][:, 0:n],
                    ALU.mult, ALU.mult, (tl("o_%d_%d" % (dmc, ti)), rst_t[r], cst), (tl("o_%d_%d" % (dmc, ti)),))
            if kind == "A":
                for dmc in range(8):
                    tt("dve", Xsb[:, dmc, c0:c0 + n], Xsb[:, dmc, c0:c0 + n], Obuf[:, dmc, c0:c0 + n], ALU.add,
                       (tl("o_%d_%d" % (dmc, ti)), tl("x_%d_%d" % (dmc, ti))), (tl("x_%d_%d" % (dmc, ti)),))
        if kind == "B":
            for hlf in range(3):
                for dmc in range(8):
                    ov = Obuf[:, dmc, :].rearrange("p (s c) -> p s c", c=CB)
                    allo = tuple(tl("o_%d_%d" % (dmc, t2)) for t2 in range(3))
                    if hlf < 2:
                        xv = Xsb[:, dmc, 512 * hlf:512 * hlf + 512].rearrange("p (c s) -> p s c", s=8)
                        tt("dve", xv, xv, ov[:, :, 64 * hlf:64 * hlf + 64], ALU.add,
                           allo + (tl("x_%d_%d" % (dmc, hlf)),), (tl("x_%d_%d" % (dmc, hlf)),))
                    else:
                        xv = Xsb[:, dmc, 1024:1056].rearrange("p (q i) -> p i q", i=4)
                        tt("dve", xv, xv, ov[:, 4:8, 128:136], ALU.add, allo + (tl("x_%d_2" % dmc),),
                           (tl("x_%d_2" % dmc),))

    MRING = (0, 1, 2, 6, 7)

    def mbank():
        return MRING[nxt("mmb", 5)]

    def layer_a(layer, ps):
        j = layer // 2
        wb0 = j * NBLK_PER_J
        tb = tl("tabA")
        lt2 = tl("ltmp2")
        ltmp2 = O_A[:, 0, 0:1024].rearrange("p (h t) -> p h t", h=8)
        ltmp3 = O_A[:, 1, 0:256].rearrange("p (h t) -> p h t", h=8)[64:96]
        dma("sp", ltmp, wsT_d[j], (), (tl("ltmp"),) + OT_ALL, par)
        dma("sp", ltmp2, bsB_d[j], (), (lt2,) + OT_ALL, par)
        dma("sp", ltmp3, wsrep_d[j], (), (lt2,) + OT_ALL, par)

        def tables_compute():
            tt("dve", wsTm[:], ltmp, trim[:].unsqueeze(1).broadcast_to([128, 8, 128]), ALU.mult,
               (tl("ltmp"), cst), (tb,))
            tt("dve", ltmp, ltmp, trim[:].unsqueeze(1).broadcast_to([128, 8, 128]), ALU.mult,
               (tl("ltmp"), cst), (tl("ltmp"),))
            for hh in range(2):
                mm(PS[:, 6 + hh, :], ones_f[:], ltmp[:, 4 * hh:4 * hh + 4, :].rearrange("p a b -> p (a b)"),
                   True, True, (tl("ltmp"), cst), (bank[6 + hh],))
            for dc in range(16):
                hh = dc // 2
                bk = 6 + hh // 4
                stt(ctab[:, dc, :], PS[:, bk, (hh % 4) * 128:(hh % 4) * 128 + 128], lncol[:, j, 1, dc:dc + 1],
                    ltmp2[:, hh, :], ALU.mult, ALU.add, (bank[bk], lt2, cst), (tb,))
            tt("dve", bdw[64:96], ltmp3, bdm[64:96, :].unsqueeze(1).broadcast_to([32, 8, 32]), ALU.mult,
               (lt2, cst), (tb,))

        prenorm(layer, "A")

        def hn_reads(ti):
            return tuple(tl("hn_%d_%d" % (kc, ti)) for kc in range(8))

        wl = [wload(wb0 + 4)]
        for blk in range(4):
            if blk < 3:
                wl.append(wload(wb0 + 4 + blk + 1))
            wi = wl[blk]
            wv = WB[wi][:, :].rearrange("p (a b) -> p a b", b=512)
            for n in range(9):
                if blk == 1 and n == 0:
                    tables_compute()
                M = 128
                c0 = 128 * n if n < 8 else 960
                bk = mbank()
                for kc in range(8):
                    mm(PS[0:M, bk, :], Hn[:, kc, c0:c0 + M], wv[:, kc, :], kc == 0, kc == 7,
                       ((tl("hn_%d_%d" % (kc, n // 4)),) if n < 8 else (tl("hn_%d_1" % kc), tl("hn_%d_2" % kc)))
                       + (wtile[wi],), (bank[bk],))
                cp("act", Gv[0:M, n, blk * 512:(blk + 1) * 512], PS[0:M, bk, :], (bank[bk],),
                   tuple(tl("g_%d_%d" % (n, dc)) for dc in range(4 * blk, 4 * blk + 4)))
                P.add("dve", lambda e, o=stats[0:M, n, blk, :], i=Gv[0:M, n, blk * 512:(blk + 1) * 512]:
                      e.bn_stats(out=o, in_=i), tuple(tl("g_%d_%d" % (n, dc)) for dc in range(4 * blk, 4 * blk + 4)),
                      (tl("stats"),))
        for n in range(9):
            M = 128
            P.add("dve", lambda e, o=mv[0:M, n, :], i=stats[0:M, n, :, :].rearrange("p a b -> p (a b)"):
                  e.bn_aggr(out=o, in_=i), (tl("stats"),), (tl("mv"),))
        act(rstdv[:], mv[:, :, 1], AF.Sqrt, (tl("mv"), cst), (tl("mv"),), bias=epsc[:], scale=1.0)
        P.add("dve", lambda e: e.reciprocal(out=rstdv[:], in_=rstdv[:]), (tl("mv"),), (tl("mv"),))
        stt(nmr[:], mv[:, :, 0], -1.0, rstdv[:], ALU.mult, ALU.mult, (tl("mv"),), (tl("mv"),))
        for n in range(9):
            M = 128
            Nt = 128 if n < 8 else 32
            gts = tuple(tl("g_%d_%d" % (n, dc)) for dc in range(16))
            act(Gv[0:M, n, :], Gv[0:M, n, :], AF.Identity, gts + (tl("mv"),), gts,
                bias=nmr[0:M, n:n + 1], scale=rstdv[0:M, n:n + 1])
            if n == 8:
                cvt = tl("cvt")
                ofl = A3[:, :].bitcast(F32)
                cvs = ofl[64:96, 2112:4160]
                cvg = ofl[64:96, 4160:6208]
                cvb = ofl[64:96, 6208:8256]
                dma("sp", cvg, lnbc_d[j, 0], (), (cvt, tl("ltmp"), lt2) + OT_ALL, par)
                dma("sp", cvb, lnbc_d[j, 1], (), (cvt, tl("ltmp"), lt2) + OT_ALL, par)
                tt("dve", cvs, Gv[64:96, 8, :], cvg, ALU.mult, gts + (cvt,), (cvt,))
                tt("dve", cvs, cvs, cvb, ALU.add, (cvt,), (cvt,))
                dma("sp", cv[j, ps], cvs, (cvt,) + OT_ALL, (tl("cv_out"),), ch_out)
            for dc in range(16):
                bk = 4 + dc // 4
                rhs = wsTm[:, dc // 2, :] if n < 8 else bdw[:, dc // 2, :]
                mm(PS[:, bk, (dc % 4) * 128:(dc % 4) * 128 + Nt], G[0:M, n, dc, :], rhs, True, True,
                   (tl("g_%d_%d" % (n, dc)), tb), (bank[bk],))
            if n < 8:
                for b4 in range(4):
                    bk = 4 + b4
                    pv = PS[:, bk, :].rearrange("p (d t) -> p d t", d=4)
                    gB = lncol[:, j, 0, 4 * b4:4 * b4 + 4].unsqueeze(2).broadcast_to([128, 4, 128])
                    tt("dve", pv, pv, gB, ALU.mult, (bank[bk], cst), (bank[bk],))
                    tt("dve", G[:, n, 4 * b4:4 * b4 + 4, :], pv, ctab[:, 4 * b4:4 * b4 + 4, :], ALU.add,
                       (bank[bk], tb), tuple(tl("g_%d_%d" % (n, dc_)) for dc_ in range(4 * b4, 4 * b4 + 4)))
            for dc in range(16 if n == 8 else 0):
                bk = 4 + dc // 4
                pin = PS[:, bk, (dc % 4) * 128:(dc % 4) * 128 + Nt]
                if n < 8:
                    pass
                else:
                    stt(G[:, 8, dc, 0:32].rearrange("p (q i) -> p q i", i=4),
                        pin.rearrange("p (q i) -> p q i", i=4), lncol[:, j, 0, dc:dc + 1],
                        ctab[:, dc, 0:4].unsqueeze(1).broadcast_to([128, 8, 4]), ALU.mult, ALU.add,
                        (bank[bk], tb, cst), (tl("g_8_%d" % dc),))

        def gview(ti, dc):
            if ti < 2:
                return G[:, 4 * ti:4 * ti + 4, dc, :]
            return G[:, 8, dc, 0:32]

        def gtiles(ti, dc):
            if ti < 2:
                return tuple(tl("g_%d_%d" % (n, dc)) for n in range(4 * ti, 4 * ti + 4))
            return (tl("g_8_%d" % dc),)

        for stage, b0 in (("u", 0), ("z", 8)):
            wl = [wload(wb0 + b0)]
            for blk in range(4):
                if blk < 3:
                    wl.append(wload(wb0 + b0 + blk + 1))
                wi = wl[blk]
                wv = WB[wi][:, :].rearrange("p (a b) -> p a b", b=512)
                for dcl in range(4):
                    dc = 4 * blk + dcl
                    for ti, (c0, n) in enumerate(TILES_A):
                        bk = mbank()
                        for kc in range(8):
                            mm(PS[:, bk, 0:n], wv[:, kc, dcl * 128:(dcl + 1) * 128], Hn[:, kc, c0:c0 + n],
                               kc == 0, kc == 7, (tl("hn_%d_%d" % (kc, ti)), wtile[wi]), (bank[bk],))
                        pv = PS[:, bk, 0:n]
                        if ti < 2:
                            pv = pv.rearrange("p (a b) -> p a b", b=128)
                        gt = gtiles(ti, dc)
                        if stage == "u":
                            tt("dve", gview(ti, dc), pv, gview(ti, dc), ALU.mult, (bank[bk],) + gt, gt)
                        else:
                            i = nxt("tmpb", 2)
                            act(tmpb[i][:, 0:n], PS[:, bk, 0:n], AF.Silu, (bank[bk],), (tmpb_t[i],))
                            tv = tmpb[i][:, 0:n]
                            if ti < 2:
                                tv = tv.rearrange("p (a b) -> p a b", b=128)
                            tt("dve", gview(ti, dc), tv, gview(ti, dc), ALU.mult, (tmpb_t[i],) + gt, gt)
        wl = [wload(wb0 + 12)]
        for blk in range(4):
            if blk < 3:
                wl.append(wload(wb0 + 12 + blk + 1))
            wi = wl[blk]
            wv = WB[wi][:, :].rearrange("p (a b) -> p a b", b=256)
            for dml in range(2):
                dmc = 2 * blk + dml
                for ti, (c0, n) in enumerate(TILES_A):
                    bk = mbank()
                    for dc in range(16):
                        mm(PS[:, bk, 0:n], wv[:, dc, dml * 128:(dml + 1) * 128], gview(ti, dc), dc == 0, dc == 15,
                           gtiles(ti, dc) + (wtile[wi],), (bank[bk],))
                    out_stage_evac(bk, dmc, ti, c0, n, O_A)
        flush_stat()
        postnorm(layer, "A", O_A, TILES_A)

    def layer_b(layer, ps):
        j = layer // 2
        wb0 = j * NBLK_PER_J + 16
        prenorm(layer, "B")
        for nm_, t_ in list(T.items()):
            if nm_.startswith(("xs5_", "Yd_", "Xd_", "yf_", "y_")) and not nm_.startswith("y_out"):
                if t_.w is not None and t_.w.chan is not None:
                    t_.w = None
                for k_ in [k_ for k_ in t_.r if isinstance(k_, int)]:
                    del t_.r[k_]
        tsx = tl("s5small")
        dma("sp", h0t[:], h0_d[j, ps], (), (tsx,), par)

        def c3(v64):
            return v64.unsqueeze(1).broadcast_to([128, 8, 64])

        def c3h(v32):
            return v32.unsqueeze(1).broadcast_to([128, 8, 32])

        def cstep(dst, src, tr, tp_, tn_, addend, eng="dve"):
            sv = src.rearrange("p q (g r) -> p q g r", r=2)
            t2v = st2[:].rearrange("p q (g r) -> p q g r", r=2)
            tt(eng, st1[:], src, c3(tr), ALU.mult, (tsx, cst), (tsx,))
            tt(eng, t2v[:, :, :, 0], sv[:, :, :, 1], c3h(tn_), ALU.mult, (tsx, cst), (tsx,))
            tt(eng, t2v[:, :, :, 1], sv[:, :, :, 0], c3h(tp_), ALU.mult, (tsx, cst), (tsx,))
            tt(eng, st1[:], st1[:], st2[:], ALU.add, (tsx,), (tsx,))
            if addend is None:
                cp(eng, dst, st1[:], (tsx,), (tsx,))
            else:
                tt(eng, dst, st1[:], addend, ALU.add, (tsx, tl("ub_a"), tl("ub_d")), (tsx,))

        cstep(h0p[:], h0t[:], M4r[:, j, :], M4p[:, j, :], M4n[:, j, :], None)

        Xdv = Xd.rearrange("(g k) (s c) -> s k g c", k=16, c=CB)

        def xreadback(f_):
            for s8 in range(8):
                dma("sp", Xs5[16 * s8:16 * s8 + 16, 8 * f_:8 * f_ + 8, :],
                    Xdv[s8][:, 8 * f_:8 * f_ + 8, :],
                    (tl("Xd_%d" % f_),), (tl("xs5_%d_%d" % (s8, f_)),) + (OT_ALL if (s8 == 0 and f_ == 0) else ()),
                    ch_xr[f_][s8 % 2])

        wl = [wload(wb0 + 0), wload(wb0 + 1)]
        for fc in range(8):
            wi = wl[fc // 4]
            wv = WB[wi][:, :].rearrange("p (a b) -> p a b", b=512)
            dcl = fc % 4
            xi = nxt("xs", 2)
            for ti, (c0, n) in enumerate(TILES_B):
                bk = mbank()
                for kc in range(8):
                    mm(PS[:, bk, 0:n], wv[:, kc, dcl * 128:(dcl + 1) * 128], Hn[:, kc, c0:c0 + n],
                       kc == 0, kc == 7, (tl("hn_%d_%d" % (kc, ti)), wtile[wi]), (bank[bk],))
                cp("act", xs[xi][:, c0:c0 + n], PS[:, bk, 0:n], (bank[bk],), (xs_t[xi],))
            dma("sp", Xd[fc * 128:(fc + 1) * 128, :], xs[xi][:], (xs_t[xi],), (tl("Xd_%d" % fc),), ch_xs[xi])
            if fc >= 1:
                xreadback(fc - 1)
        xreadback(7)
        cp("act", Hb[:, :, :, 128:136].rearrange("p g r c -> p (g r) c"),
           h0p[:].rearrange("p q g -> p g q"), (tsx,), (tl("hb_5"),))
        ub = tl("ub_all")
        UBQ = [tl("ub_q%d" % q_) for q_ in range(4)]
        UBS = tl("ub_s")
        batches1 = [(g0_, min(3, 32 - g0_)) for g0_ in range(0, 32, 3)]

        def load1(b_):
            g0_, n_ = batches1[b_]
            return rbload(lambda t_, n_=n_: t_[:, 0:n_ * 512].rearrange("p (a m) -> p a m", a=n_),
                          SW[j, g0_:g0_ + n_].rearrange("a p m -> p a m"), tl("SW"))

        pend = [load1(0), load1(1)]
        for b_, (g0_, n_) in enumerate(batches1):
            if b_ + 2 < len(batches1):
                pend.append(load1(b_ + 2))
            si = pend[b_]
            for a_ in range(n_):
                gp = g0_ + a_
                wv = RB[si][:, a_ * 512:(a_ + 1) * 512].rearrange("p (x r m) -> p x r m", x=2, r=2)
                bk = mbank()
                pu = PS[:, bk, 0:272].rearrange("p (r c) -> p r c", r=2)
                for ri in range(2):
                    for x in range(2):
                        mm(pu[:, ri, :], wv[:, x, ri, :], Xs5[:, 2 * gp + x, :], x == 0, x == 1,
                           (RBT[si],) + XS5_F[gp // 4], (bank[bk],))
                cp("act" if gp % 2 == 0 else "dve", Ub[:, :, 2 * gp:2 * gp + 2].rearrange("p c r -> p r c"), pu,
                   (bank[bk],), (tl("ub_a" if gp % 2 == 0 else "ub_d"),))
        wl = [wload(wb0 + 2), wload(wb0 + 3)]
        for fc in range(8):
            wi = wl[fc // 4]
            wv = WB[wi][:, :].rearrange("p (a b) -> p a b", b=512)
            dcl = fc % 4
            for ti, (c0, n) in enumerate(TILES_B):
                bk = mbank()
                for kc in range(8):
                    mm(PS[:, bk, 0:n], wv[:, kc, dcl * 128:(dcl + 1) * 128], Hn[:, kc, c0:c0 + n],
                       kc == 0, kc == 7, (tl("hn_%d_%d" % (kc, ti)), wtile[wi]), (bank[bk],))
                act(Gb[:, fc, c0:c0 + n], PS[:, bk, 0:n], AF.Silu, (bank[bk],), (tl("gb_%d_%d" % (fc, ti)),))
        hin = Hin0[:] if ps == 0 else Hcar[:, j, :]
        tr = AR2[:, j, :]
        tpp = AIp[:, j, :]
        tnn = AIn[:, j, :]
        sc1 = st1[:, 0, :]
        sc2 = st2[:, 0, :].rearrange("p (g r) -> p g r", r=2)
        scn = tl("scan")
        hbv = Hb.rearrange("p g r c -> p c g r")
        cp("act", hbv[:, 0, :, :], hin.rearrange("p (g r) -> p g r", r=2), (cst, tsx, tl("hcar")), (tl("hb_0"),))
        for c in range(128):
            prev = hin if c == 0 else Ub[:, c - 1, :]
            pv = prev.rearrange("p (g r) -> p g r", r=2)
            uq = UBQ[c // 32]
            rd = (uq, UBQ[max(c - 1, 0) // 32], scn, cst, tsx, tl("hcar"), tl("ub_a"), tl("ub_d"))
            ns = SCAN_NOSYNC and c > 0
            tt("dve", sc1, prev, tr, ALU.mult, rd, (scn,), nosync=ns)
            tt("dve", sc2, pv[:, :, ::-1], AI2[:, j, :].rearrange("p (g r) -> p g r", r=2), ALU.mult, rd, (scn,), nosync=ns)
            tt("dve", Ub[:, c, :], Ub[:, c, :], sc1, ALU.add, rd, (uq,), nosync=ns)
            tt("dve", Ub[:, c, :], Ub[:, c, :], st2[:, 0, :], ALU.add, rd, (uq,), nosync=ns)
            if c % 32 == 31:
                q4 = c // 32
                ln_ = 32 if q4 < 3 else 31
                cp("act", Hb[:, :, :, 1 + 32 * q4:1 + 32 * q4 + ln_].rearrange("p g r c -> p (g r) c"),
                   Ub[:, 32 * q4:32 * q4 + ln_, :].rearrange("p c g -> p g c"), (uq,), (tl("hb_%d" % (q4 + 1)),))
        cp("dve", Hcar[:, j, :], Ub[:, 127, :], (UBQ[3],), (tl("hcar"),))
        if ps == npass - 1:
            dma("sp", stp[j], Hcar[:, j, :], (tl("hcar"),), (tl("stp_out_%d" % j),), ch_out)
        wb1 = j * NBLK_PER_J + 20
        wl_glu1 = [wload(wb1 + 0), wload(wb1 + 1)]
        def load3(b_):
            return rbload(lambda t_: t_[:, :].rearrange("p (a m) -> p a m", a=4),
                          SMV[j, 4 * b_:4 * b_ + 4].rearrange("a p m -> p a m"), tl("SMV"))

        pend = [load3(0), load3(1)]
        for g in range(64):
            b_ = g // 4
            if g % 4 == 0 and b_ + 2 < 16:
                pend.append(load3(b_ + 2))
            si = pend[b_]
            mv_ = RB[si][:, (g % 4) * 384:(g % 4) * 384 + 384]
            bk = mbank()
            py = PS[:, bk, 0:CB]
            mm(py, mv_[:, 0:128], Xs5[:, g, :], True, False, (RBT[si], tl("y_%d" % g)) + XS5_F[g // 8],
               (bank[bk],))
            for ri in range(2):
                mm(py, mv_[:, 128 + 128 * ri:256 + 128 * ri], Hb[:, g // 2, ri, :], False, ri == 1,
                   (RBT[si],) + tuple(tl("hb_%d" % q) for q in range(6)), (bank[bk],))
            act(Xs5[:, g, :], py, AF.Gelu_apprx_tanh, (bank[bk],), (tl("y_%d" % g),))
            if g % 8 == 7:
                fc = g // 8
                Ydv = Yd.rearrange("(g k) (t c) -> t k g c", k=16, c=CB)
                ys = tuple(tl("y_%d" % g_) for g_ in range(8 * fc, 8 * fc + 8))
                for t8 in range(8):
                    dma("sp", Ydv[t8][:, 8 * fc:8 * fc + 8, :], Xs5[16 * t8:16 * t8 + 16, 8 * fc:8 * fc + 8, :],
                        ys + XS5_F[fc], (tl("Yd_%d_%d" % (fc, t8)),), ch_scr)
                for fr in ([fc - 1] if fc >= 1 else []) + ([7] if fc == 7 else []):
                    dma("sp", yF[:, fr, :], Yd[fr * 128:(fr + 1) * 128, :],
                        tuple(tl("Yd_%d_%d" % (fr, t_)) for t_ in range(8)),
                        (tl("yf_%d" % fr),) + tuple(tl("hn_%d_%d" % (fr, t)) for t in range(3)), ch_scr3)
        cstep(hsf[:], h0p[:], tr, tpp, tnn, Ub[:, 128:136, :])
        dma("sp", sts[j, ps], hsf[:], (tsx,), (tl("sts_out"),), ch_out)
        for which in range(2):
            wl = wl_glu1 if which == 0 else [wload(wb1 + 2), wload(wb1 + 3)]
            for fc in range(8):
                wi = wl[fc // 4]
                wv = WB[wi][:, :].rearrange("p (a b) -> p a b", b=512)
                dcl = fc % 4
                for ti, (c0, n) in enumerate(TILES_B):
                    bk = mbank()
                    for kc in range(8):
                        mm(PS[:, bk, 0:n], wv[:, kc, dcl * 128:(dcl + 1) * 128], yF[:, kc, c0:c0 + n],
                           kc == 0, kc == 7, (tl("yf_%d" % kc), wtile[wi]), (bank[bk],))
                    gt = tl("gb_%d_%d" % (fc, ti))
                    if which == 0:
                        stt(Gb[:, fc, c0:c0 + n], PS[:, bk, 0:n], bglu[:, j, 0, fc:fc + 1], Gb[:, fc, c0:c0 + n],
                            ALU.add, ALU.mult, (bank[bk], gt, cst), (gt,))
                    else:
                        i = nxt("tmpb", 2)
                        act(tmpb[i][:, 0:n], PS[:, bk, 0:n], AF.Sigmoid, (bank[bk], cst), (tmpb_t[i],),
                            bias=bglu[:, j, 1, fc:fc + 1], scale=1.0)
                        tt("dve", Gb[:, fc, c0:c0 + n], tmpb[i][:, 0:n], Gb[:, fc, c0:c0 + n], ALU.mult,
                           (tmpb_t[i], gt), (gt,))
        wb2 = j * NBLK_PER_J + 24
        wl = [wload(wb2), wload(wb2 + 1)]
        for dmc in range(8):
            wi = wl[dmc // 4]
            wv = WB[wi][:, :].rearrange("p (a b) -> p a b", b=512)
            dcl = dmc % 4
            for ti, (c0, n) in enumerate(TILES_B):
                bk = mbank()
                for kc in range(8):
                    mm(PS[:, bk, 0:n], wv[:, kc, dcl * 128:(dcl + 1) * 128], Gb[:, kc, c0:c0 + n],
                       kc == 0, kc == 7, (tl("gb_%d_%d" % (kc, ti)), wtile[wi]), (bank[bk],))
                out_stage_evac(bk, dmc, ti, c0, n, O_B)
        flush_stat()
        postnorm(layer, "B", O_B, TILES_B)

    for ps in range(npass):
        for kc in range(8):
            dma("sp", Xsb[:, kc, :], xT[ps, :, kc, :], (),
                tuple(tl("x_%d_%d" % (kc, ti)) for ti in range(3)), ch_x[ps % 2],
                extra=(bar_ops if ps == 0 else ()))
        for layer, kind in enumerate(layers):
            if kind == "A":
                layer_a(layer, ps)
            elif kind == "B":
                layer_b(layer, ps)
        for kc in range(8):
            dma("sp", yT[ps, :, kc, :], Xsb[:, kc, :], tuple(tl("x_%d_%d" % (kc, ti)) for ti in range(3)),
                (tl("y_out_%d" % kc),), ch_out)
    print('sbuf bytes remaining', nc.sbuf_bytes_remaining)
    if max_ops is not None:
        print('total ops', len(P.ops))
        P.ops = P.ops[:max_ops]
    P.emit()
    return nc


def _blk(w, nb, kc, n):
    return np.ascontiguousarray(w.reshape(kc, 128, nb, n).transpose(2, 1, 0, 3)).reshape(nb, 128, kc * n)


def _consts():
    ident = np.eye(128, dtype=np.float32)
    s = np.arange(128)
    trimask = (s[:, None] <= s[None, :]).astype(np.float32)
    q = np.arange(32)
    bdmask = ((q[:, None] // 4 == q[None, :] // 4) & (q[:, None] % 4 <= q[None, :] % 4)).astype(np.float32)
    tmask = ((s[:, None] // 16) <= (s[None, :] // 16)).astype(np.float32)
    colmask = np.zeros((128, 2, 128), np.float32)
    colmask[:, 0, :64] = 1
    colmask[:, 1, 64:] = 1
    rowmask = np.zeros((128, 2), np.float32)
    rowmask[:64, 0] = 1
    rowmask[64:, 1] = 1
    return dict(ident=ident, trimask=trimask, bdmask=bdmask, tmask=tmask, colmask=colmask, rowmask=rowmask)


def _shared_inputs(inp):
    f = lambda a: np.ascontiguousarray(np.asarray(a, dtype=np.float32))
    blocks = []
    for j in range(2):
        blocks.append(_blk(f(inp["w_in_a"][j]), 12, 8, 512))
        blocks.append(_blk(f(inp["w_out_a"][j]), 4, 16, 256))
        blocks.append(_blk(f(inp["w_in_b"][j]), 4, 8, 512))
        blocks.append(_blk(f(inp["w_glu1"][j]), 2, 8, 512))
        blocks.append(_blk(f(inp["w_glu2"][j]), 2, 8, 512))
        blocks.append(_blk(f(inp["w_out_b"][j]), 2, 8, 512))
    d = dict(wall=np.concatenate(blocks, axis=0))
    col = lambda v, n: np.ascontiguousarray(f(v).reshape(n, 128).T)
    d["gpre"] = np.concatenate([col(inp["norm_pre"][l], 8) for l in range(4)], axis=1)
    d["gpost"] = np.concatenate([col(inp["norm_post"][l], 8) for l in range(4)], axis=1)
    d["lncol"] = np.ascontiguousarray(np.stack(
        [np.stack([col(inp["ln_v_g"][j], 16), col(inp["ln_v_b"][j], 16)], axis=1) for j in range(2)], axis=1))
    d["lnbc"] = np.ascontiguousarray(np.stack(
        [np.stack([np.broadcast_to(f(inp["ln_v_g"][j])[None, :], (32, 2048)),
                   np.broadcast_to(f(inp["ln_v_b"][j])[None, :], (32, 2048))]) for j in range(2)]))
    ws = f(inp["w_s"])
    d["wsT"] = np.ascontiguousarray(ws.transpose(0, 3, 1, 2))
    corner = ws[:, :, :4, :4].transpose(0, 3, 1, 2)
    d["wsrep"] = np.ascontiguousarray(np.tile(corner, (1, 8, 1, 8)))
    d["bsB"] = np.ascontiguousarray(np.broadcast_to(f(inp["b_s"])[:, None, :, :], (2, 128, 8, 128)))
    d["bglu"] = np.ascontiguousarray(np.stack(
        [np.stack([col(inp["b_glu1"][j], 8), col(inp["b_glu2"][j], 8)], axis=1) for j in range(2)], axis=1))

    def l2(a):
        return a.reshape(32, 2, 64).transpose(1, 2, 0).reshape(128, 32)

    aL2 = np.zeros((2, 128, 3, 32), np.float32)
    bL2 = np.zeros((2, 128, 2, 32, 16), np.float32)
    cL2 = np.zeros((2, 128, 2, 32, 16), np.float32)
    dcol = np.zeros((2, 128, 64), np.float32)
    for j in range(2):
        aL2[j, :, 0] = l2(f(inp["a_re"][j]))
        aL2[j, :, 1] = l2(f(inp["a_im"][j]))
        aL2[j, :, 2] = l2(np.broadcast_to(f(inp["log_dt"][j])[:, None], (64, 64)))
        for r, nm in enumerate(("b_re", "b_im")):
            b = f(inp[nm][j])
            bL2[j, :, r] = b.reshape(32, 2, 64, 16).transpose(1, 2, 0, 3).reshape(128, 32, 16)
        for r, nm in enumerate(("c_re", "c_im")):
            c = f(inp[nm][j])
            cL2[j, :, r] = c.reshape(32, 2, 16, 64).transpose(1, 3, 0, 2).reshape(128, 32, 16)
        dk = f(inp["d_skip"][j]).reshape(64, 16)
        dcol[j] = np.tile(dk.T, (8, 1))
    d.update(aL2=aL2, bL2=bL2, cL2=cL2, dcol=dcol)
    d.update(_consts())
    return d


def _core_inputs(inp, cid):
    f = lambda a: np.asarray(a, dtype=np.float32)
    xp = f(inp["x_prompt"][cid])
    xsm = f(inp["x_sample"][16 * cid:16 * cid + 16])
    xT = np.zeros((2, 128, 8, NTA), np.float32)
    for ps in range(2):
        cols = np.concatenate([xp[1024 * ps:1024 * ps + 1024], xsm[8 * ps:8 * ps + 8].reshape(32, 1024)], axis=0)
        xT[ps] = cols.T.reshape(8, 128, NTA).transpose(1, 0, 2)
    h0 = np.zeros((2, 2, 128, 8, 64), np.float32)
    for j in range(2):
        for ps in range(2):
            sl = slice(16 * cid + 8 * ps, 16 * cid + 8 * ps + 8)
            re = f(inp["state_ssm_re"][j, sl])
            im = f(inp["state_ssm_im"][j, sl])
            st = np.stack([re, im], axis=-1)
            st = st.reshape(8, 32, 2, 64, 2).transpose(2, 3, 0, 1, 4)
            h0[j, ps] = st.reshape(128, 8, 64)
    return dict(xT=xT, h0L2=h0)


_NC_CACHE = {}


def kernel(**inputs):
    inp = {k: np.asarray(v) for k, v in inputs.items()}
    shared = _shared_inputs(inp)
    in_maps = []
    for cid in range(NCORES):
        m = dict(shared)
        m.update(_core_inputs(inp, cid))
        in_maps.append(m)
    if "nc" not in _NC_CACHE:
        _NC_CACHE["nc"] = build_program()
    nc = _NC_CACHE["nc"]
    res = run_bass_kernel_spmd(nc, in_maps, core_ids=list(range(NCORES)))
    y_prompt = np.zeros((8, 2048, 1024), np.float32)
    y_sample = np.zeros((128, 4, 1024), np.float32)
    cvs = np.zeros((2, 128, 4, 2048), np.float32)
    srp = np.zeros((2, 8, 64, 64), np.float32)
    sip = np.zeros((2, 8, 64, 64), np.float32)
    srs = np.zeros((2, 128, 64, 64), np.float32)
    sis = np.zeros((2, 128, 64, 64), np.float32)
    for cid in range(NCORES):
        r = res.results[cid]
        yT = np.asarray(r["yT"])
        for ps in range(2):
            cols = yT[ps].transpose(2, 1, 0).reshape(NTA, 1024)
            y_prompt[cid, 1024 * ps:1024 * ps + 1024] = cols[:1024]
            y_sample[16 * cid + 8 * ps:16 * cid + 8 * ps + 8] = cols[1024:].reshape(8, 4, 1024)
        cvv = np.asarray(r["cv"])
        stpv = np.asarray(r["stp"])
        stsv = np.asarray(r["sts"])
        for j in range(2):
            for ps in range(2):
                cvs[j, 16 * cid + 8 * ps:16 * cid + 8 * ps + 8] = cvv[j, ps].reshape(8, 4, 2048)
                s = stsv[j, ps].reshape(2, 64, 8, 32, 2).transpose(2, 3, 0, 1, 4).reshape(8, 64, 64, 2)
                srs[j, 16 * cid + 8 * ps:16 * cid + 8 * ps + 8] = s[..., 0]
                sis[j, 16 * cid + 8 * ps:16 * cid + 8 * ps + 8] = s[..., 1]
            s = stpv[j].reshape(2, 64, 32, 2).transpose(2, 0, 1, 3).reshape(64, 64, 2)
            srp[j, cid] = s[..., 0]
            sip[j, cid] = s[..., 1]
    return (y_prompt, y_sample, cvs, srp, sip, srs, sis)
```

```python
import os
import numpy as np
import concourse.bass as bass
import concourse.mybir as mybir
from concourse.bass_utils import run_bass_kernel_spmd

F32 = mybir.dt.float32
BF16 = mybir.dt.bfloat16
I32 = mybir.dt.int32
AF = mybir.ActivationFunctionType
ALU = mybir.AluOpType

NCORES = 8
D = 1024
NTA = 1056
NTB = 1088
CB = 136
TILES_A = [(0, 512), (512, 512), (1024, 32)]
TILES_B = [(0, 512), (512, 512), (1024, 64)]
EPS = 1e-6
NBLK_PER_J = 26
SCAN_NOSYNC = os.environ.get("SCAN_SYNC", "0") != "1"
BUILD_NOSYNC = SCAN_NOSYNC


class Tl:
    __slots__ = ("name", "w", "r", "excl")

    def __init__(self, name, excl=False):
        self.name = name
        self.w = None
        self.r = {}
        self.excl = excl


class Chan:
    def __init__(self, nc, name):
        self.sem = nc.alloc_semaphore(name)
        self.cnt = 0
        self.name = name
        self.pending = 0
        self.selfw = 0


class Op:
    __slots__ = ("eng", "fn", "chan", "signal", "deps", "semval", "val", "final")


class Prog:
    CE = ("pe", "act", "dve", "pool")

    def __init__(self, nc):
        self.nc = nc
        self.ops = []
        self.sem = {e: nc.alloc_semaphore("s_" + e) for e in self.CE}
        self.chans = []
        self.last = {}

    def chan(self, name):
        c = Chan(self.nc, name)
        self.chans.append(c)
        return c

    def add(self, eng, fn, reads=(), writes=(), chan=None, extra=(), nosync=False):
        op = Op()
        op.eng = eng
        op.fn = fn
        op.chan = chan
        op.signal = False
        op.final = False
        xw = tuple(t for t in reads if t.excl)
        if xw:
            writes = tuple(writes) + xw
        deps = {}
        for t in reads:
            if t.w is not None:
                deps[id(t.w)] = t.w
        for t in writes:
            if t.w is not None:
                deps[id(t.w)] = t.w
            for r in t.r.values():
                deps[id(r)] = r
        for d in extra:
            deps[id(d)] = d
        for t in writes:
            t.w = op
            t.r = {}
        for t in reads:
            key = id(chan) if chan is not None else eng
            t.r[key] = op
        dl = []
        for d in deps.values():
            if d is op:
                continue
            if d.chan is not None:
                dl.append(("dma", d.chan, d.chan.cnt * 16))
                d.chan.pending = max(d.chan.pending, d.chan.cnt * 16)
            else:
                if d.eng == eng and chan is None and (eng == "pe" or nosync):
                    continue
                d.signal = True
                dl.append(("cmp", d))
        if chan is not None and chan.pending > chan.selfw:
            dl.append(("dma", chan, chan.pending))
            chan.selfw = chan.pending
        op.deps = dl
        if chan is not None:
            chan.cnt += 1
            op.val = chan.cnt * 16
        self.ops.append(op)
        self.last[eng if chan is None else ("q", eng)] = op
        return op

    def barrier(self, tiny):
        lasts = [self.last[e] for e in self.CE if e in self.last]
        dmas = [self.last[k] for k in self.last if isinstance(k, tuple)]
        out = []
        for e in ("act", "dve", "pool"):
            out.append(self.add(e, tiny[e], extra=lasts + dmas))
        return out + dmas

    def emit(self):
        cnt = {e: 0 for e in self.CE}
        for op in self.ops:
            if op.chan is None and op.signal:
                cnt[op.eng] += 1
                op.semval = cnt[op.eng]
        nc = self.nc
        by = {e: [o for o in self.ops if o.eng == e] for e in ("pe", "act", "dve", "pool", "sp")}
        sems = self.sem
        chans = self.chans

        def run(e, lst, is_sp=False):
            waited = {}
            for op in lst:
                need = {}
                for d in op.deps:
                    if d[0] == "dma":
                        sem, val = d[1].sem, d[2]
                    else:
                        sem, val = sems[d[1].eng], d[1].semval
                    k = id(sem)
                    if k not in need or need[k][1] < val:
                        need[k] = (sem, val)
                for k, (sem, val) in need.items():
                    if waited.get(k, 0) >= val:
                        continue
                    e.wait_ge(sem, val)
                    waited[k] = val
                ins = op.fn(e)
                if op.chan is not None:
                    ins.then_inc(op.chan.sem, 16)
                elif op.signal:
                    ins.then_inc(sems[op.eng], 1)
            if is_sp:
                for c in chans:
                    n_em = sum(1 for o in self.ops if o.chan is c)
                    if n_em > 0:
                        e.wait_ge(c.sem, n_em * 16)

        with nc.Block() as block:
            @block.tensor
            def _(e):
                run(e, by["pe"])

            @block.scalar
            def _(e):
                run(e, by["act"])

            @block.vector
            def _(e):
                run(e, by["dve"])

            @block.gpsimd
            def _(e):
                run(e, by["pool"])

            @block.sync
            def _(e):
                run(e, by["sp"], True)


def build_program(layers=("A", "B", "A", "B"), npass=2, max_ops=None):
    nc = bass.Bass("TRN2", target_bir_lowering=False)
    P = Prog(nc)

    def din(name, shape, dt=F32):
        return nc.dram_tensor(name, list(shape), dt, kind="ExternalInput").ap()

    def dout(name, shape, dt=F32):
        return nc.dram_tensor(name, list(shape), dt, kind="ExternalOutput").ap()

    xT = din("xT", [2, 128, 8, NTA])
    WALL = din("wall", [2 * NBLK_PER_J, 128, 4096])
    gpre_d = din("gpre", [128, 32])
    gpost_d = din("gpost", [128, 32])
    lncol_d = din("lncol", [128, 2, 2, 16])
    lnbc_d = din("lnbc", [2, 2, 32, 2048])
    wsT_d = din("wsT", [2, 128, 8, 128])
    wsrep_d = din("wsrep", [2, 32, 8, 32])
    bsB_d = din("bsB", [2, 128, 8, 128])
    bglu_d = din("bglu", [128, 2, 2, 8])
    aL2_d = din("aL2", [2, 128, 3, 32])
    bL2_d = din("bL2", [2, 128, 2, 32, 16])
    cL2_d = din("cL2", [2, 128, 2, 32, 16])
    dcol_d = din("dcol", [2, 128, 64])
    h0_d = din("h0L2", [2, 2, 128, 8, 64])
    ident_d = din("ident", [128, 128])
    trim_d = din("trimask", [128, 128])
    bdm_d = din("bdmask", [32, 32])
    tmask_d = din("tmask", [128, 128])
    colm_d = din("colmask", [128, 2, 128])
    rowm_d = din("rowmask", [128, 2])

    yT = dout("yT", [2, 128, 8, NTA])
    cv = dout("cv", [2, 2, 32, 2048])
    stp = dout("stp", [2, 128, 64])
    sts = dout("sts", [2, 2, 128, 8, 64])

    SW = nc.dram_tensor("SW", [2, 32, 128, 512], BF16).ap()
    SMV = nc.dram_tensor("SMV", [2, 64, 128, 384], BF16).ap()
    Xd = nc.dram_tensor("Xd", [1024, NTB], BF16).ap()
    Yd = nc.dram_tensor("Yd", [1024, NTB], BF16).ap()

    def sb(name, shape, dt=F32):
        return nc.alloc_sbuf_tensor("sb_" + name, list(shape), dt)

    A0 = sb("A0", [128, 8, NTA])
    A1 = sb("A1", [128, 8, NTB], BF16)
    A2 = sb("A2", [128, 18432], BF16)
    A3 = sb("A3", [128, 17408], BF16)
    A5 = sb("A5", [128, 8, NTB], BF16)
    NWB = 2
    WB = [sb("wb%d" % i, [128, 4096], BF16) for i in range(NWB)]
    NRB = 3
    RB = [sb("rb%d" % i, [128, 1536], BF16) for i in range(NRB)]
    ident = sb("ident", [128, 128])
    ones_bf = sb("ones_bf", [128, 128], BF16)
    ones_f = sb("ones_f", [128, 128])
    trim = sb("trim", [128, 128])
    bdm = sb("bdm", [128, 32])
    tmask = sb("tmask", [128, 128])
    colm = sb("colm", [128, 2, 128])
    rowm = sb("rowm", [128, 2])
    gpre = sb("gpre", [128, 32])
    gpost = sb("gpost", [128, 32])
    lncol = sb("lncol", [128, 2, 2, 16])
    bglu = sb("bglu", [128, 2, 2, 8])
    epsc = sb("epsc", [128, 1])
    hpic = sb("hpic", [128, 1])
    zero1 = sb("zero1", [128, 1])
    ctab = sb("ctab", [128, 16, 128])
    wsTm = sb("wsTm", [128, 8, 128], BF16)
    bdw = sb("bdw", [128, 8, 32], BF16)
    rst = [sb("rst%d" % i, [128, 512]) for i in range(2)]
    sqr = [sb("sq%d" % i, [128, 512], BF16) for i in range(2)]
    tmpb = [sb("tmpb%d" % i, [128, 512], BF16) for i in range(2)]
    xs = [sb("xs%d" % i, [128, NTB], BF16) for i in range(2)]
    stats = sb("stats", [128, 9, 4, 6])
    mv = sb("mv", [128, 9, 2])
    rstdv = sb("rstdv", [128, 9])
    nmr = sb("nmr", [128, 9])
    AR2 = sb("AR2", [128, 2, 64])
    AIp = sb("AIp", [128, 2, 32])
    AIn = sb("AIn", [128, 2, 32])
    AI2 = sb("AI2", [128, 2, 64])
    M4r = sb("M4r", [128, 2, 64])
    M4p = sb("M4p", [128, 2, 32])
    M4n = sb("M4n", [128, 2, 32])
    Hcar = sb("Hcar", [128, 2, 64])
    Hin0 = sb("Hin0", [128, 64])
    h0t = sb("h0t", [128, 8, 64])
    h0p = sb("h0p", [128, 8, 64])
    hsf = sb("hsf", [128, 8, 64])
    st1 = sb("st1", [128, 8, 64])
    st2 = sb("st2", [128, 8, 64])
    barA = sb("barA", [128, 1])
    barV = sb("barV", [128, 1])
    barG = sb("barG", [128, 1])

    PS = nc.alloc_psum_tensor("PS", [128, 8, 512], F32)
    bank = [Tl("bank%d" % i, True) for i in range(8)]

    Xsb = A0
    Hn = A1
    G = A2[:, :].rearrange("p (n d t) -> p n d t", n=9, d=16)
    Gv = A2[:, :].rearrange("p (n f) -> p n f", n=9)
    Ub = A2[:, 0:17408].bitcast(F32).rearrange("p (c g) -> p c g", g=64)
    O_A = A3[:, 0:16896].bitcast(F32).rearrange("p (k c) -> p k c", k=8)
    O_B = A3[:, :].bitcast(F32).rearrange("p (k c) -> p k c", k=8)
    Xs5 = A3[:, 0:8704].rearrange("p (g c) -> p g c", g=64)
    Hb = A3[:, 8704:17408].rearrange("p (g r c) -> p g r c", g=32, r=2)
    yF = A1
    ltmp = O_A[:, 4, 0:1024].rearrange("p (h t) -> p h t", h=8)
    Gb = A5

    T = {}

    def tl(name):
        if name not in T:
            T[name] = Tl(name)
        return T[name]

    OT_ALL = tuple(tl("o_%d_%d" % (d_, t_)) for d_ in range(8) for t_ in range(3))
    RBT = [tl("rb_%d" % i_) for i_ in range(3)]
    XS5_F = [tuple(tl("xs5_%d_%d" % (s_, f_)) for s_ in range(8)) for f_ in range(8)]
    par = P.chan("par")
    ch_x = [P.chan("chx%d" % i) for i in range(2)]
    ch_out = P.chan("chout")
    ch_w = [P.chan("chw%d" % i) for i in range(NWB)]
    ch_rb = [P.chan("chrb_%d" % i) for i in range(NRB)]
    rbstate = {"n": 0}

    def rbload(dst_view_fn, src_ap, src_tile):
        i = rbstate["n"] % NRB
        rbstate["n"] += 1
        P.add("pool", lambda e, o=dst_view_fn(RB[i]), i_=src_ap: e.dma_start(out=o, in_=i_), (src_tile,),
              (RBT[i],), ch_rb[i])
        return i

    ch_scr = P.chan("chscr")
    ch_scr2 = P.chan("chscr2")
    ch_scr3 = P.chan("chscr3")
    ch_xr = [[P.chan("chxr%d_%d" % (f_, q_)) for q_ in range(2)] for f_ in range(8)]
    ch_prep = P.chan("chprep")
    ch_xs = [P.chan("chxs%d" % i) for i in range(2)]

    def dma(q, out, in_, reads, writes, chan, extra=()):
        return P.add(q, lambda e, o=out, i=in_: e.dma_start(out=o, in_=i), reads, writes, chan, extra)

    def mm(out, lhsT, rhs, start, stop, reads, writes):
        return P.add("pe", lambda e, o=out, l=lhsT, r=rhs, s=start, t=stop:
                     e.matmul(o, l, r, start=s, stop=t), reads, writes)

    def act(out, in_, func, reads, writes, bias=None, scale=None):
        def fn(e, o=out, i=in_, f=func, b=bias, s=scale):
            kw = {}
            if b is not None:
                kw["bias"] = b
            if s is not None:
                kw["scale"] = s
            return e.activation(out=o, in_=i, func=f, **kw)
        return P.add("act", fn, reads, writes)

    def tt(eng, out, in0, in1, op, reads, writes, nosync=False):
        return P.add(eng, lambda e, o=out, a=in0, b=in1, p=op: e.tensor_tensor(out=o, in0=a, in1=b, op=p),
                     reads, writes, nosync=nosync)

    def ts(eng, out, in0, s1, s2, op0, op1, reads, writes, nosync=False):
        def fn(e, o=out, a=in0, x=s1, y=s2, p0=op0, p1=op1):
            if p1 is None:
                return e.tensor_scalar(out=o, in0=a, scalar1=x, scalar2=None, op0=p0)
            return e.tensor_scalar(out=o, in0=a, scalar1=x, scalar2=y, op0=p0, op1=p1)
        return P.add(eng, fn, reads, writes)

    def stt(out, in0, scalar, in1, op0, op1, reads, writes):
        return P.add("dve", lambda e, o=out, a=in0, s=scalar, b=in1, p0=op0, p1=op1:
                     e.scalar_tensor_tensor(out=o, in0=a, scalar=s, in1=b, op0=p0, op1=p1), reads, writes)

    def cp(eng, out, in_, reads, writes):
        if eng == "act":
            return P.add("act", lambda e, o=out, i=in_: e.activation(out=o, in_=i, func=AF.Copy), reads, writes)
        return P.add(eng, lambda e, o=out, i=in_: e.tensor_copy(out=o, in_=i), reads, writes)

    def mset(eng, ap, val, writes):
        return P.add(eng, lambda e, a=ap, v=val: e.memset(a, v), (), writes)

    wstate = {"n": 0}
    wtile = [Tl("wb%d" % i) for i in range(NWB)]

    def wload(blk):
        i = wstate["n"] % NWB
        wstate["n"] += 1
        src = WALL[blk].rearrange("p (a b) -> p a b", b=512)
        dst = WB[i][:, :].rearrange("p (a b) -> p a b", b=512)
        dma("pool", dst, src, (), (wtile[i],), ch_w[i])
        return i

    cst = tl("const")
    for dst, src in ((ident, ident_d), (trim, trim_d), (tmask, tmask_d), (colm, colm_d),
                     (rowm, rowm_d), (gpre, gpre_d), (gpost, gpost_d), (lncol, lncol_d), (bglu, bglu_d)):
        dma("sp", dst[:], src, (), (cst,), par)
    dma("sp", bdm[64:96, :], bdm_d, (), (cst,), par)
    mset("pool", bdw[:], 0.0, (tl("tabA"),))
    mset("dve", ones_f[:], 1.0, (cst,))
    mset("dve", epsc[:], EPS, (cst,))
    mset("dve", hpic[:], float(np.pi / 2), (cst,))
    mset("dve", zero1[:], 0.0, (cst,))
    mset("dve", Hin0[:], 0.0, (cst,))
    mset("dve", stats[:], 0.0, (tl("stats"),))
    mset("dve", mv[:], 1.0, (tl("mv"),))
    mset("dve", barV[:], 0.0, (tl("barV"),))
    mset("pool", barG[:], 0.0, (tl("barG"),))
    cp("dve", ones_bf[:], ones_f[:], (cst,), (cst,))
    act(barA[:], zero1[:], AF.Copy, (cst,), (tl("barA"),))

    def prep_s5(j):
        base = A5[:, :, :].rearrange("p a b -> p (a b)")
        f = base.bitcast(F32)
        off = [0]

        def carve(n, shape=None):
            v = f[:, off[0]:off[0] + n]
            off[0] += n
            return v

        aL = carve(96).rearrange("p (a g) -> p a g", a=3)
        dt_ = carve(32)
        dar = carve(32)
        ang = carve(32)
        mag = carve(32)
        magi = carve(32)
        kf = carve(32)
        ki = carve(32).bitcast(I32)
        rr = carve(32)
        m1 = carve(32)
        sn = carve(32)
        cs = carve(32)
        ab = carve(32)
        t1 = carve(32)
        t2 = carve(32)
        t3 = carve(32)
        t4 = carve(32)
        nr = carve(32)
        den = carve(32)
        cfr = carve(32)
        cfi = carve(32)
        PW = carve(17 * 64).rearrange("p (n r g) -> p n r g", n=17, r=2)
        bL = carve(1024).rearrange("p (r g k) -> p r g k", r=2, g=32)
        cL = carve(1024).rearrange("p (r g k) -> p r g k", r=2, g=32)
        assert off[0] <= 4352
        f1 = A1[:, :, :].rearrange("p a b -> p (a b)").bitcast(F32)
        bbar = f1[:, 0:1024].rearrange("p (r g k) -> p r g k", r=2, g=32)
        big1 = f1[:, 1024:1536].rearrange("p (g k) -> p g k", g=32)
        dcl = f1[:, 2048:2112]
        tp = tl("prep_small")
        dma("sp", aL, aL2_d[j], (), (tp,), par)
        dma("sp", bL, bL2_d[j], (), (tp,), par)
        dma("sp", cL, cL2_d[j], (), (tp,), par)
        dma("sp", dcl, dcol_d[j], (), (tp,), par)
        R = (tp, cst)
        W_ = (tp,)
        ar, ai, ldt = aL[:, 0, :], aL[:, 1, :], aL[:, 2, :]
        act(dt_, ldt, AF.Exp, R, W_)
        tt("dve", dar, dt_, ar, ALU.mult, R, W_)
        tt("dve", ang, dt_, ai, ALU.mult, R, W_)
        act(mag, dar, AF.Exp, R, W_)
        act(magi, dar, AF.Exp, R, W_, scale=-1.0)
        ts("dve", kf, ang, float(1.0 / (2 * np.pi)), None, ALU.mult, None, R, W_)
        cp("dve", ki, kf, R, W_)
        cp("dve", kf, ki, R, W_)
        stt(rr, kf, float(-2 * np.pi), ang, ALU.mult, ALU.add, R, W_)
        ts("dve", m1, rr, float(np.pi), float(-2 * np.pi), ALU.is_gt, ALU.mult, R, W_)
        tt("dve", rr, rr, m1, ALU.add, R, W_)
        ts("dve", m1, rr, float(-np.pi), float(2 * np.pi), ALU.is_lt, ALU.mult, R, W_)
        tt("dve", rr, rr, m1, ALU.add, R, W_)
        act(sn, rr, AF.Sin, R, W_)
        act(ab, rr, AF.Abs, R, W_)
        act(cs, ab, AF.Sin, R, W_, bias=hpic[:], scale=-1.0)
        mset("dve", PW[:, 8, 0, :], 1.0, W_)
        mset("dve", PW[:, 8, 1, :], 0.0, W_)
        tt("dve", PW[:, 9, 0, :], mag, cs, ALU.mult, R, W_)
        tt("dve", PW[:, 9, 1, :], mag, sn, ALU.mult, R, W_)
        tt("dve", PW[:, 7, 0, :], magi, cs, ALU.mult, R, W_)
        stt(PW[:, 7, 1, :], magi, -1.0, sn, ALU.mult, ALU.mult, R, W_)

        wt = [carve(128), carve(128), carve(128), f1[:, 1600:1728]]

        def cmulw(o, a, b, w):
            T = [t_[:, 0:w * 32].rearrange("p (a g) -> p a g", a=w) for t_ in wt]
            a_r, a_i = a[:, :, 0, :], a[:, :, 1, :]
            b_r = b[:, :, 0, :].broadcast_to([128, w, 32])
            b_i = b[:, :, 1, :].broadcast_to([128, w, 32])
            tt("dve", T[0], a_r, b_r, ALU.mult, R, W_)
            tt("dve", T[1], a_i, b_i, ALU.mult, R, W_)
            tt("dve", T[2], a_r, b_i, ALU.mult, R, W_)
            tt("dve", T[3], a_i, b_r, ALU.mult, R, W_)
            tt("dve", o[:, :, 0, :], T[0], T[1], ALU.subtract, R, W_)
            tt("dve", o[:, :, 1, :], T[2], T[3], ALU.add, R, W_)

        cmulw(PW[:, 10:11], PW[:, 9:10], PW[:, 9:10], 1)
        cmulw(PW[:, 6:7], PW[:, 7:8], PW[:, 7:8], 1)
        cmulw(PW[:, 11:13], PW[:, 9:11], PW[:, 10:11], 2)
        cmulw(PW[:, 5:3:-1], PW[:, 7:5:-1], PW[:, 6:7], 2)
        cmulw(PW[:, 13:17], PW[:, 9:13], PW[:, 12:13], 4)
        cmulw(PW[:, 3::-1], PW[:, 7:3:-1], PW[:, 4:5], 4)
        for (src_n, tr, tp_, tn_) in ((16, AR2, AIp, AIn), (4, M4r, M4p, M4n)):
            trv = tr[:, j, :].rearrange("p (g r) -> p g r", r=2)
            cp("dve", trv[:, :, 0], PW[:, src_n, 0, :], R, (cst,))
            cp("dve", trv[:, :, 1], PW[:, src_n, 0, :], R, (cst,))
            cp("dve", tp_[:, j, :], PW[:, src_n, 1, :], R, (cst,))
            ts("dve", tn_[:, j, :], PW[:, src_n, 1, :], -1.0, None, ALU.mult, None, R, (cst,))
        ai2v = AI2[:, j, :].rearrange("p (g r) -> p g r", r=2)
        cp("dve", ai2v[:, :, 0], AIn[:, j, :], (cst,), (cst,))
        cp("dve", ai2v[:, :, 1], AIp[:, j, :], (cst,), (cst,))
        ts("dve", nr, PW[:, 9, 0, :], -1.0, None, ALU.add, None, R, W_)
        ni = PW[:, 9, 1, :]
        tt("dve", t1, ar, ar, ALU.mult, R, W_)
        tt("dve", t2, ai, ai, ALU.mult, R, W_)
        tt("dve", den, t1, t2, ALU.add, R, W_)
        P.add("dve", lambda e, o=den, i=den: e.reciprocal(out=o, in_=i), R, W_)
        tt("dve", t1, nr, ar, ALU.mult, R, W_)
        tt("dve", t2, ni, ai, ALU.mult, R, W_)
        tt("dve", t1, t1, t2, ALU.add, R, W_)
        tt("dve", cfr, t1, den, ALU.mult, R, W_)
        tt("dve", t1, ni, ar, ALU.mult, R, W_)
        tt("dve", t2, nr, ai, ALU.mult, R, W_)
        tt("dve", t1, t1, t2, ALU.subtract, R, W_)
        tt("dve", cfi, t1, den, ALU.mult, R, W_)

        def bc16(v):
            return v.unsqueeze(2).broadcast_to([128, 32, 16])

        tb = tl("prep_bbar")
        RB = (tp, cst, tb)
        WB_ = (tb,)
        tt("dve", bbar[:, 0], bc16(cfr), bL[:, 0], ALU.mult, RB, WB_)
        tt("dve", big1, bc16(cfi), bL[:, 1], ALU.mult, RB, WB_)
        tt("dve", bbar[:, 0], bbar[:, 0], big1, ALU.subtract, RB, WB_)
        tt("dve", bbar[:, 1], bc16(cfr), bL[:, 1], ALU.mult, RB, WB_)
        tt("dve", big1, bc16(cfi), bL[:, 0], ALU.mult, RB, WB_)
        tt("dve", bbar[:, 1], bbar[:, 1], big1, ALU.add, RB, WB_)

        PSM = f1[:, 1024:1568].rearrange("p (n g) -> p n g", n=17)
        tt("dve", PSM, PW[:, :, 0, :], PW[:, :, 1, :], ALU.add, (tp, tl("prep_bbar")), (tp, tl("prep_bbar")))
        for hf in range(2):
            g0 = 16 * hf
            a2f = A2[:, :].bitcast(F32)
            WcL = a2f[:, 0:4096].rearrange("p (g r s k) -> p g r s k", g=16, r=2, s=8)
            WnL = a2f[:, 4096:8192].rearrange("p (g r s k) -> p g r s k", g=16, r=2, s=8)
            a3f = A3[:, :].bitcast(F32)
            VL = a3f[:, 0:4096].rearrange("p (g r s k) -> p g r s k", g=16, r=2, s=8)
            tmps = {"dve": (a3f[:, 4096:4352].rearrange("p (g k) -> p g k", g=16),
                            a3f[:, 4352:4608].rearrange("p (g k) -> p g k", g=16)),
                    "pool": (a3f[:, 6656:6912].rearrange("p (g k) -> p g k", g=16),
                             a3f[:, 6912:7168].rearrange("p (g k) -> p g k", g=16))}
            VLm = a3f[:, 4608:6656].rearrange("p (g x r m) -> p g x r m", g=4, x=2, r=2)
            a0 = A0[:, :, :].rearrange("p a b -> p (a b)")
            Wst = a0[:, 0:4096].bitcast(BF16).rearrange("p (g x r m) -> p g x r m", g=16, x=2, r=2)
            Vst = a0[:, 4096:8192].bitcast(BF16).rearrange("p (g x r m) -> p g x r m", g=16, x=2, r=2)
            M0st = f1[:, 2176:4224].bitcast(BF16).rearrange("p (g m) -> p g m", g=32)
            tw = tl("prep_big")
            tstg = tl("prep_stg")

            def bcg(v):
                return v.unsqueeze(2).broadcast_to([128, 16, 16])

            sums = a3f[:, 7168:8192].rearrange("p (q g k) -> p q g k", q=4, g=16)
            tsm = tl("prep_sums")
            br = bbar[:, 0, g0:g0 + 16, :]
            bi = bbar[:, 1, g0:g0 + 16, :]
            crr = cL[:, 0, g0:g0 + 16, :]
            cii = cL[:, 1, g0:g0 + 16, :]
            tt("dve", sums[:, 0], br, bi, ALU.add, (tp, tb), (tsm,))
            tt("dve", sums[:, 1], bi, br, ALU.subtract, (tp, tb), (tsm,))
            tt("dve", sums[:, 2], crr, cii, ALU.add, (tp, tb), (tsm,))
            tt("dve", sums[:, 3], crr, cii, ALU.subtract, (tp, tb), (tsm,))

            def build(name, dst, n_of_s, src_r, xsum, xdif, kind):
                tiles = []
                for s_ in range(8):
                    n = n_of_s(s_) + 8
                    pr = bcg(PW[:, n, 0, g0:g0 + 16])
                    pi = bcg(PW[:, n, 1, g0:g0 + 16])
                    psm = bcg(PSM[:, n, g0:g0 + 16])
                    eng = "dve"
                    tA, tB = tmps[eng]
                    tmt = tl("prep_tmp_" + eng)
                    mt = tl("prep_%s_%d" % (name, s_))
                    tiles.append(mt)
                    RW = (tp, cst, tb, tsm, tmt, mt)
                    WW = (mt, tmt)
                    d0 = dst[:, :, 0, s_, :]
                    d1 = dst[:, :, 1, s_, :]
                    ns_ = BUILD_NOSYNC
                    tt(eng, d1, psm, src_r, ALU.mult, RW, WW, nosync=ns_)
                    tt(eng, tA, pi, xsum, ALU.mult, RW, WW, nosync=ns_)
                    tt(eng, d0, d1, tA, ALU.subtract, RW, WW, nosync=ns_)
                    tt(eng, tA, pr, xdif, ALU.mult, RW, WW, nosync=ns_)
                    if kind == "w":
                        tt(eng, d1, d1, tA, ALU.add, RW, WW, nosync=ns_)
                    else:
                        tt(eng, d1, tA, d1, ALU.subtract, RW, WW, nosync=ns_)
                return tuple(tiles)

            tWc = build("wc", WcL, lambda s_: 7 - s_, br, sums[:, 0], sums[:, 1], "w")
            tWn = build("wn", WnL, lambda s_: -(s_ + 1), br, sums[:, 0], sums[:, 1], "w")
            tV = build("v", VL, lambda s_: s_ + 1, crr, sums[:, 2], sums[:, 3], "v")
            tvst = tl("prep_vst")
            twst = tl("prep_wst")
            tm0 = tl("prep_m0st")
            twb = tl("prep_wnlb")
            for x in range(2):
                act(Vst[:, :, x].rearrange("p g r m -> p g (r m)"),
                    VL.rearrange("p g r s k -> p g (r s k)"), AF.Copy, tV + (cst,), (tvst,), scale=rowm[:, x:x + 1])
            WnLb = a3f[:, 4608:6656].bitcast(BF16).rearrange("p (g r m) -> p g r m", g=16, r=2)
            act(WnLb.rearrange("p g r m -> p (g r m)"), WnL.rearrange("p g r s k -> p (g r s k)"), AF.Copy,
                tWn, (twb,))
            for gl in range(16):
                for ri in range(2):
                    bk = (gl * 2 + ri) % 8
                    P.add("pe", lambda e, o=PS[:, bk, 0:128], i=WcL[:, gl, ri].rearrange("p s k -> p (s k)"):
                          e.transpose(o, i, ident[:]), tWc + (cst,), (bank[bk],))
                    tt("dve", Wst[:, gl, :, ri, :],
                       PS[:, bk, 0:128].unsqueeze(1).broadcast_to([128, 2, 128]), colm[:], ALU.mult,
                       (bank[bk], cst), (twst,))
            for gl in range(16):
                for x in range(2):
                    bk = (gl * 2 + x) % 8
                    gg = 2 * (g0 + gl) + x
                    for ri in range(2):
                        mm(PS[:, bk, 0:128], WnLb[:, gl, ri, :], Vst[:, gl, x, ri, :], ri == 0, ri == 1,
                           (twb, tvst), (bank[bk],))
                    tt("dve", tmpAB[hf][:], PS[:, bk, 0:128], tmask[:], ALU.mult, (bank[bk], cst, tl("tmpAB")),
                       (tl("tmpAB"),))
                    stt(M0st[:, gl * 2 + x, :], ident[:], dcl[:, gg:gg + 1], tmpAB[hf][:], ALU.mult, ALU.add,
                        (tp, cst, tl("tmpAB")), (tm0,))
            dma("sp", SW[j, g0:g0 + 16].rearrange("g p (x r m) -> p g x r m", x=2, r=2), Wst, (twst,),
                (tl("SW"),), ch_prep)
            dma("sp", SMV[j, 2 * g0:2 * g0 + 32, :, 0:128].rearrange("g p m -> p g m"), M0st, (tm0,),
                (tl("SMV"),), ch_prep)
            dma("sp", SMV[j, 2 * g0:2 * g0 + 32, :, 128:384].rearrange("(g x) p (r m) -> p g x r m", x=2, r=2),
                Vst, (tvst,), (tl("SMV"),), ch_prep)

    tmpAB = [sb("tmpAB%d" % i, [128, 128]) for i in range(2)]

    has_b = "B" in layers
    bar_ops = []
    if has_b:
        for j in range(2):
            prep_s5(j)
        tiny = {
            "act": lambda e: e.activation(out=barA[:], in_=zero1[:], func=AF.Copy),
            "dve": lambda e: e.memset(barV[:], 0.0),
            "pool": lambda e: e.memset(barG[:], 0.0),
        }
        bar_ops = P.barrier(tiny)

    mset("pool", A1[:, :, :], 0.0, tuple(tl("hn_%d_%d" % (kc_, t_)) for kc_ in range(8) for t_ in range(3)))

    ring = {"rst": 0, "sq": 0, "tmpb": 0, "mmb": 0, "xs": 0}
    rst_t = [Tl("rst%d" % i) for i in range(2)]
    sq_t = [Tl("sq%d" % i) for i in range(2)]
    tmpb_t = [Tl("tmpb%d" % i) for i in range(2)]
    xs_t = [Tl("xs%d" % i) for i in range(2)]
    STATB = [3, 4, 5]

    def nxt(name, n):
        i = ring[name] % n
        ring[name] += 1
        return i

    def rstd_from_bank(bk, n):
        i = nxt("rst", 2)
        act(rst[i][:, 0:n], PS[:, bk, 0:n], AF.Ln, (bank[bk], cst), (rst_t[i],), bias=epsc[:], scale=1.0 / D)
        act(rst[i][:, 0:n], rst[i][:, 0:n], AF.Exp, (rst_t[i],), (rst_t[i],), scale=-0.5)
        return i

    def xcols(name, kc, c0, n):
        return tl("%s_%d_%d" % (name, kc, c0))

    def prenorm(layer, kind):
        gcol = gpre[:, layer * 8:(layer + 1) * 8]
        if kind == "B":
            for kc in range(8):
                v = Hn[:, kc, :].rearrange("p (s c) -> p s c", c=CB)[:, 0:4, 128:136]
                mset("pool", v, 0.0, tuple(tl("hn_%d_%d" % (kc, t2)) for t2 in range(3)))
        for ti, (c0, n) in enumerate(TILES_A):
            bk = STATB[ti]
            for kc in range(8):
                i = nxt("sq", 2)
                act(sqr[i][:, 0:n], Xsb[:, kc, c0:c0 + n], AF.Square, (tl("x_%d_%d" % (kc, ti)),), (sq_t[i],))
                mm(PS[:, bk, 0:n], ones_bf[:], sqr[i][:, 0:n], kc == 0, kc == 7, (sq_t[i], cst), (bank[bk],))
            r = rstd_from_bank(bk, n)
            for kc in range(8):
                xin = Xsb[:, kc, c0:c0 + n]
                if kind == "A":
                    stt(Hn[:, kc, c0:c0 + n], xin, gcol[:, kc:kc + 1], rst[r][:, 0:n], ALU.mult, ALU.mult,
                        (tl("x_%d_%d" % (kc, ti)), rst_t[r], cst), (tl("hn_%d_%d" % (kc, ti)),))
                else:
                    hv = Hn[:, kc, :].rearrange("p (s c) -> p s c", c=CB)
                    if ti < 2:
                        ov = hv[:, :, 64 * ti:64 * ti + 64]
                        iv = xin.rearrange("p (c s) -> p s c", s=8)
                        rv = rst[r][:, 0:n].rearrange("p (c s) -> p s c", s=8)
                    else:
                        ov = hv[:, 4:8, 128:136]
                        iv = xin.rearrange("p (q i) -> p i q", i=4)
                        rv = rst[r][:, 0:n].rearrange("p (q i) -> p i q", i=4)
                    stt(ov, iv, gcol[:, kc:kc + 1], rv, ALU.mult, ALU.mult,
                        (tl("x_%d_%d" % (kc, ti)), rst_t[r], cst),
                        tuple(tl("hn_%d_%d" % (kc, t2)) for t2 in range(3)))

    def out_stage_evac(bk, dmc, ti, c0, n, Obuf):
        cp("act", Obuf[:, dmc, c0:c0 + n], PS[:, bk, 0:n], (bank[bk],), (tl("o_%d_%d" % (dmc, ti)),))
        i = nxt("sq", 2)
        act(sqr[i][:, 0:n], PS[:, bk, 0:n], AF.Square, (bank[bk],), (sq_t[i],))
        sb_ = STATB[ti]
        flush_stat()
        pend_stat.append((PS[:, sb_, 0:n], sqr[i][:, 0:n], dmc == 0, dmc == 7, (sq_t[i], cst), (bank[sb_],)))

    pend_stat = []

    def flush_stat():
        while pend_stat:
            o_, r_, st_, sp_, rd_, wr_ = pend_stat.pop(0)
            mm(o_, ones_bf[:], r_, st_, sp_, rd_, wr_)

    def postnorm(layer, kind, Obuf, tiles):
        gcol = gpost[:, layer * 8:(layer + 1) * 8]
        for ti, (c0, n) in enumerate(tiles):
            r = rstd_from_bank(STATB[ti], n)
            for dmc in range(8):
                stt(Obuf[:, dmc, c0:c0 + n], Obuf[:, dmc, c0:c0 + n], gcol[:, dmc:dmc + 1], rst[r][:, 0:n],
                    ALU.mult, ALU.mult, (tl("o_%d_%d" % (dmc, ti)), rst_t[r], cst), (tl("o_%d_%d" % (dmc, ti)),))
            if kind == "A":
                for dmc in range(8):
                    tt("dve", Xsb[:, dmc, c0:c0 + n], Xsb[:, dmc, c0:c0 + n], Obuf[:, dmc, c0:c0 + n], ALU.add,
                       (tl("o_%d_%d" % (dmc, ti)), tl("x_%d_%d" % (dmc, ti))), (tl("x_%d_%d" % (dmc, ti)),))
        if kind == "B":
            for hlf in range(3):
                for dmc in range(8):
                    ov = Obuf[:, dmc, :].rearrange("p (s c) -> p s c", c=CB)
                    allo = tuple(tl("o_%d_%d" % (dmc, t2)) for t2 in range(3))
                    if hlf < 2:
                        xv = Xsb[:, dmc, 512 * hlf:512 * hlf + 512].rearrange("p (c s) -> p s c", s=8)
                        tt("dve", xv, xv, ov[:, :, 64 * hlf:64 * hlf + 64], ALU.add,
                           allo + (tl("x_%d_%d" % (dmc, hlf)),), (tl("x_%d_%d" % (dmc, hlf)),))
                    else:
                        xv = Xsb[:, dmc, 1024:1056].rearrange("p (q i) -> p i q", i=4)
                        tt("dve", xv, xv, ov[:, 4:8, 128:136], ALU.add, allo + (tl("x_%d_2" % dmc),),
                           (tl("x_%d_2" % dmc),))

    MRING = (0, 1, 2, 6, 7)

    def mbank():
        return MRING[nxt("mmb", 5)]

    def layer_a(layer, ps):
        j = layer // 2
        wb0 = j * NBLK_PER_J
        tb = tl("tabA")
        lt2 = tl("ltmp2")
        ltmp2 = O_A[:, 0, 0:1024].rearrange("p (h t) -> p h t", h=8)
        ltmp3 = O_A[:, 1, 0:256].rearrange("p (h t) -> p h t", h=8)[64:96]
        dma("sp", ltmp, wsT_d[j], (), (tl("ltmp"),) + OT_ALL, par)
        dma("sp", ltmp2, bsB_d[j], (), (lt2,) + OT_ALL, par)
        dma("sp", ltmp3, wsrep_d[j], (), (lt2,) + OT_ALL, par)

        def tables_dve1():
            tt("dve", wsTm[:], ltmp, trim[:].unsqueeze(1).broadcast_to([128, 8, 128]), ALU.mult,
               (tl("ltmp"), cst), (tb,))
            tt("dve", ltmp, ltmp, trim[:].unsqueeze(1).broadcast_to([128, 8, 128]), ALU.mult,
               (tl("ltmp"), cst), (tl("ltmp"),))

        def tables_compute():
            for hh in range(2):
                mm(PS[:, 3 + hh, :], ones_f[:], ltmp[:, 4 * hh:4 * hh + 4, :].rearrange("p a b -> p (a b)"),
                   True, True, (tl("ltmp"), cst), (bank[3 + hh],))
            for dc in range(16):
                hh = dc // 2
                bk = 3 + hh // 4
                stt(ctab[:, dc, :], PS[:, bk, (hh % 4) * 128:(hh % 4) * 128 + 128], lncol[:, j, 1, dc:dc + 1],
                    ltmp2[:, hh, :], ALU.mult, ALU.add, (bank[bk], lt2, cst), (tb,))
            tt("dve", bdw[64:96], ltmp3, bdm[64:96, :].unsqueeze(1).broadcast_to([32, 8, 32]), ALU.mult,
               (lt2, cst), (tb,))

        prenorm(layer, "A")

        def hn_reads(ti):
            return tuple(tl("hn_%d_%d" % (kc, ti)) for kc in range(8))

        wl = [wload(wb0 + 4)]
        for blk in range(4):
            if blk < 3:
                wl.append(wload(wb0 + 4 + blk + 1))
            wi = wl[blk]
            wv = WB[wi][:, :].rearrange("p (a b) -> p a b", b=512)
            for n in range(9):
                if blk == 0 and n == 4:
                    tables_dve1()
                if blk == 1 and n == 0:
                    tables_compute()
                M = 128
                c0 = 128 * n if n < 8 else 960
                bk = mbank()
                for kc in range(8):
                    mm(PS[0:M, bk, :], Hn[:, kc, c0:c0 + M], wv[:, kc, :], kc == 0, kc == 7,
                       ((tl("hn_%d_%d" % (kc, n // 4)),) if n < 8 else (tl("hn_%d_1" % kc), tl("hn_%d_2" % kc)))
                       + (wtile[wi],), (bank[bk],))
                cp("act", Gv[0:M, n, blk * 512:(blk + 1) * 512], PS[0:M, bk, :], (bank[bk],),
                   tuple(tl("g_%d_%d" % (n, dc)) for dc in range(4 * blk, 4 * blk + 4)))
                P.add("dve", lambda e, o=stats[0:M, n, blk, :], i=Gv[0:M, n, blk * 512:(blk + 1) * 512]:
                      e.bn_stats(out=o, in_=i), tuple(tl("g_%d_%d" % (n, dc)) for dc in range(4 * blk, 4 * blk + 4)),
                      (tl("stats"),))
        for n in range(9):
            M = 128
            P.add("dve", lambda e, o=mv[0:M, n, :], i=stats[0:M, n, :, :].rearrange("p a b -> p (a b)"):
                  e.bn_aggr(out=o, in_=i), (tl("stats"),), (tl("mv"),))
        act(rstdv[:], mv[:, :, 1], AF.Sqrt, (tl("mv"), cst), (tl("mv"),), bias=epsc[:], scale=1.0)
        P.add("dve", lambda e: e.reciprocal(out=rstdv[:], in_=rstdv[:]), (tl("mv"),), (tl("mv"),))
        stt(nmr[:], mv[:, :, 0], -1.0, rstdv[:], ALU.mult, ALU.mult, (tl("mv"),), (tl("mv"),))
        for n in range(9):
            M = 128
            Nt = 128 if n < 8 else 32
            gts = tuple(tl("g_%d_%d" % (n, dc)) for dc in range(16))
            act(Gv[0:M, n, :], Gv[0:M, n, :], AF.Identity, gts + (tl("mv"),), gts,
                bias=nmr[0:M, n:n + 1], scale=rstdv[0:M, n:n + 1])
            if n == 8:
                cvt = tl("cvt")
                ofl = A3[:, :].bitcast(F32)
                cvs = ofl[64:96, 2112:4160]
                cvg = ofl[64:96, 4160:6208]
                cvb = ofl[64:96, 6208:8256]
                dma("sp", cvg, lnbc_d[j, 0], (), (cvt, tl("ltmp"), lt2) + OT_ALL, par)
                dma("sp", cvb, lnbc_d[j, 1], (), (cvt, tl("ltmp"), lt2) + OT_ALL, par)
                tt("dve", cvs, Gv[64:96, 8, :], cvg, ALU.mult, gts + (cvt,), (cvt,))
                tt("dve", cvs, cvs, cvb, ALU.add, (cvt,), (cvt,))
                dma("sp", cv[j, ps], cvs, (cvt,) + OT_ALL, (tl("cv_out"),), ch_out)
            for dc in range(16):
                bk = 4 + dc // 4
                rhs = wsTm[:, dc // 2, :] if n < 8 else bdw[:, dc // 2, :]
                mm(PS[:, bk, (dc % 4) * 128:(dc % 4) * 128 + Nt], G[0:M, n, dc, :], rhs, True, True,
                   (tl("g_%d_%d" % (n, dc)), tb), (bank[bk],))
            if n < 8:
                for b4 in range(4):
                    bk = 4 + b4
                    pv = PS[:, bk, :].rearrange("p (d t) -> p d t", d=4)
                    gB = lncol[:, j, 0, 4 * b4:4 * b4 + 4].unsqueeze(2).broadcast_to([128, 4, 128])
                    tt("dve", pv, pv, gB, ALU.mult, (bank[bk], cst), (bank[bk],))
                    tt("dve", G[:, n, 4 * b4:4 * b4 + 4, :], pv, ctab[:, 4 * b4:4 * b4 + 4, :], ALU.add,
                       (bank[bk], tb), tuple(tl("g_%d_%d" % (n, dc_)) for dc_ in range(4 * b4, 4 * b4 + 4)))
            for dc in range(16 if n == 8 else 0):
                bk = 4 + dc // 4
                pin = PS[:, bk, (dc % 4) * 128:(dc % 4) * 128 + Nt]
                if n < 8:
                    pass
                else:
                    stt(G[:, 8, dc, 0:32].rearrange("p (q i) -> p q i", i=4),
                        pin.rearrange("p (q i) -> p q i", i=4), lncol[:, j, 0, dc:dc + 1],
                        ctab[:, dc, 0:4].unsqueeze(1).broadcast_to([128, 8, 4]), ALU.mult, ALU.add,
                        (bank[bk], tb, cst), (tl("g_8_%d" % dc),))

        def gview(ti, dc):
            if ti < 2:
                return G[:, 4 * ti:4 * ti + 4, dc, :]
            return G[:, 8, dc, 0:32]

        def gtiles(ti, dc):
            if ti < 2:
                return tuple(tl("g_%d_%d" % (n, dc)) for n in range(4 * ti, 4 * ti + 4))
            return (tl("g_8_%d" % dc),)

        for stage, b0 in (("u", 0), ("z", 8)):
            wl = [wload(wb0 + b0)]
            for blk in range(4):
                if blk < 3:
                    wl.append(wload(wb0 + b0 + blk + 1))
                wi = wl[blk]
                wv = WB[wi][:, :].rearrange("p (a b) -> p a b", b=512)
                for dcl in range(4):
                    dc = 4 * blk + dcl
                    for ti, (c0, n) in enumerate(TILES_A):
                        bk = mbank()
                        for kc in range(8):
                            mm(PS[:, bk, 0:n], wv[:, kc, dcl * 128:(dcl + 1) * 128], Hn[:, kc, c0:c0 + n],
                               kc == 0, kc == 7, (tl("hn_%d_%d" % (kc, ti)), wtile[wi]), (bank[bk],))
                        pv = PS[:, bk, 0:n]
                        if ti < 2:
                            pv = pv.rearrange("p (a b) -> p a b", b=128)
                        gt = gtiles(ti, dc)
                        if stage == "u":
                            tt("dve", gview(ti, dc), pv, gview(ti, dc), ALU.mult, (bank[bk],) + gt, gt)
                        else:
                            i = nxt("tmpb", 2)
                            act(tmpb[i][:, 0:n], PS[:, bk, 0:n], AF.Silu, (bank[bk],), (tmpb_t[i],))
                            tv = tmpb[i][:, 0:n]
                            if ti < 2:
                                tv = tv.rearrange("p (a b) -> p a b", b=128)
                            tt("dve", gview(ti, dc), tv, gview(ti, dc), ALU.mult, (tmpb_t[i],) + gt, gt)
        wl = [wload(wb0 + 12)]
        for blk in range(4):
            if blk < 3:
                wl.append(wload(wb0 + 12 + blk + 1))
            wi = wl[blk]
            wv = WB[wi][:, :].rearrange("p (a b) -> p a b", b=256)
            for dml in range(2):
                dmc = 2 * blk + dml
                for ti, (c0, n) in enumerate(TILES_A):
                    bk = mbank()
                    for dc in range(16):
                        mm(PS[:, bk, 0:n], wv[:, dc, dml * 128:(dml + 1) * 128], gview(ti, dc), dc == 0, dc == 15,
                           gtiles(ti, dc) + (wtile[wi],), (bank[bk],))
                    out_stage_evac(bk, dmc, ti, c0, n, O_A)
        flush_stat()
        postnorm(layer, "A", O_A, TILES_A)

    def layer_b(layer, ps):
        j = layer // 2
        wb0 = j * NBLK_PER_J + 16
        prenorm(layer, "B")
        for nm_, t_ in list(T.items()):
            if nm_.startswith(("xs5_", "Yd_", "Xd_", "yf_", "y_")) and not nm_.startswith("y_out"):
                if t_.w is not None and t_.w.chan is not None:
                    t_.w = None
                for k_ in [k_ for k_ in t_.r if isinstance(k_, int)]:
                    del t_.r[k_]
        tsx = tl("s5small")
        dma("sp", h0t[:], h0_d[j, ps], (), (tsx,), par)

        def c3(v64):
            return v64.unsqueeze(1).broadcast_to([128, 8, 64])

        def c3h(v32):
            return v32.unsqueeze(1).broadcast_to([128, 8, 32])

        def cstep(dst, src, tr, tp_, tn_, addend, eng="dve"):
            sv = src.rearrange("p q (g r) -> p q g r", r=2)
            t2v = st2[:].rearrange("p q (g r) -> p q g r", r=2)
            tt(eng, st1[:], src, c3(tr), ALU.mult, (tsx, cst), (tsx,))
            tt(eng, t2v[:, :, :, 0], sv[:, :, :, 1], c3h(tn_), ALU.mult, (tsx, cst), (tsx,))
            tt(eng, t2v[:, :, :, 1], sv[:, :, :, 0], c3h(tp_), ALU.mult, (tsx, cst), (tsx,))
            tt(eng, st1[:], st1[:], st2[:], ALU.add, (tsx,), (tsx,))
            if addend is None:
                cp(eng, dst, st1[:], (tsx,), (tsx,))
            else:
                tt(eng, dst, st1[:], addend, ALU.add, (tsx, tl("ub_a"), tl("ub_d")), (tsx,))

        cstep(h0p[:], h0t[:], M4r[:, j, :], M4p[:, j, :], M4n[:, j, :], None)

        Xdv = Xd.rearrange("(g k) (s c) -> s k g c", k=16, c=CB)

        def xreadback(f_):
            for s8 in range(8):
                dma("sp", Xs5[16 * s8:16 * s8 + 16, 8 * f_:8 * f_ + 8, :],
                    Xdv[s8][:, 8 * f_:8 * f_ + 8, :],
                    (tl("Xd_%d" % f_),), (tl("xs5_%d_%d" % (s8, f_)),) + (OT_ALL if (s8 == 0 and f_ == 0) else ()),
                    ch_xr[f_][s8 % 2])

        wl = [wload(wb0 + 0), wload(wb0 + 1)]
        for fc in range(8):
            wi = wl[fc // 4]
            wv = WB[wi][:, :].rearrange("p (a b) -> p a b", b=512)
            dcl = fc % 4
            xi = nxt("xs", 2)
            for ti, (c0, n) in enumerate(TILES_B):
                bk = mbank()
                for kc in range(8):
                    mm(PS[:, bk, 0:n], wv[:, kc, dcl * 128:(dcl + 1) * 128], Hn[:, kc, c0:c0 + n],
                       kc == 0, kc == 7, (tl("hn_%d_%d" % (kc, ti)), wtile[wi]), (bank[bk],))
                cp("act", xs[xi][:, c0:c0 + n], PS[:, bk, 0:n], (bank[bk],), (xs_t[xi],))
            dma("sp", Xd[fc * 128:(fc + 1) * 128, :], xs[xi][:], (xs_t[xi],), (tl("Xd_%d" % fc),), ch_xs[xi])
            if fc >= 1:
                xreadback(fc - 1)
        xreadback(7)
        cp("act", Hb[:, :, :, 128:136].rearrange("p g r c -> p (g r) c"),
           h0p[:].rearrange("p q g -> p g q"), (tsx,), (tl("hb_5"),))
        ub = tl("ub_all")
        UBQ = [tl("ub_q%d" % q_) for q_ in range(4)]
        UBS = tl("ub_s")
        batches1 = [(g0_, min(3, 32 - g0_)) for g0_ in range(0, 32, 3)]

        def load1(b_):
            g0_, n_ = batches1[b_]
            return rbload(lambda t_, n_=n_: t_[:, 0:n_ * 512].rearrange("p (a m) -> p a m", a=n_),
                          SW[j, g0_:g0_ + n_].rearrange("a p m -> p a m"), tl("SW"))

        pend = [load1(0), load1(1)]
        for b_, (g0_, n_) in enumerate(batches1):
            if b_ + 2 < len(batches1):
                pend.append(load1(b_ + 2))
            si = pend[b_]
            for a_ in range(n_):
                gp = g0_ + a_
                wv = RB[si][:, a_ * 512:(a_ + 1) * 512].rearrange("p (x r m) -> p x r m", x=2, r=2)
                bk = mbank()
                pu = PS[:, bk, 0:272].rearrange("p (r c) -> p r c", r=2)
                for ri in range(2):
                    for x in range(2):
                        mm(pu[:, ri, :], wv[:, x, ri, :], Xs5[:, 2 * gp + x, :], x == 0, x == 1,
                           (RBT[si],) + XS5_F[gp // 4], (bank[bk],))
                cp("act" if gp % 2 == 0 else "dve", Ub[:, :, 2 * gp:2 * gp + 2].rearrange("p c r -> p r c"), pu,
                   (bank[bk],), (tl("ub_a" if gp % 2 == 0 else "ub_d"),))
        wl = [wload(wb0 + 2), wload(wb0 + 3)]
        for fc in range(8):
            wi = wl[fc // 4]
            wv = WB[wi][:, :].rearrange("p (a b) -> p a b", b=512)
            dcl = fc % 4
            for ti, (c0, n) in enumerate(TILES_B):
                bk = mbank()
                for kc in range(8):
                    mm(PS[:, bk, 0:n], wv[:, kc, dcl * 128:(dcl + 1) * 128], Hn[:, kc, c0:c0 + n],
                       kc == 0, kc == 7, (tl("hn_%d_%d" % (kc, ti)), wtile[wi]), (bank[bk],))
                act(Gb[:, fc, c0:c0 + n], PS[:, bk, 0:n], AF.Silu, (bank[bk],), (tl("gb_%d_%d" % (fc, ti)),))
        hin = Hin0[:] if ps == 0 else Hcar[:, j, :]
        tr = AR2[:, j, :]
        tpp = AIp[:, j, :]
        tnn = AIn[:, j, :]
        sc1 = st1[:, 0, :]
        sc2 = st2[:, 0, :].rearrange("p (g r) -> p g r", r=2)
        scn = tl("scan")
        hbv = Hb.rearrange("p g r c -> p c g r")
        cp("act", hbv[:, 0, :, :], hin.rearrange("p (g r) -> p g r", r=2), (cst, tsx, tl("hcar")), (tl("hb_0"),))
        for c in range(128):
            prev = hin if c == 0 else Ub[:, c - 1, :]
            pv = prev.rearrange("p (g r) -> p g r", r=2)
            uq = UBQ[c // 32]
            rd = (uq, UBQ[max(c - 1, 0) // 32], scn, cst, tsx, tl("hcar"), tl("ub_a"), tl("ub_d"))
            ns = SCAN_NOSYNC and c > 0
            tt("dve", sc1, prev, tr, ALU.mult, rd, (scn,), nosync=ns)
            tt("dve", sc2, pv[:, :, ::-1], AI2[:, j, :].rearrange("p (g r) -> p g r", r=2), ALU.mult, rd, (scn,), nosync=ns)
            tt("dve", Ub[:, c, :], Ub[:, c, :], sc1, ALU.add, rd, (uq,), nosync=ns)
            tt("dve", Ub[:, c, :], Ub[:, c, :], st2[:, 0, :], ALU.add, rd, (uq,), nosync=ns)
            if c % 32 == 31:
                q4 = c // 32
                ln_ = 32 if q4 < 3 else 31
                cp("act", Hb[:, :, :, 1 + 32 * q4:1 + 32 * q4 + ln_].rearrange("p g r c -> p (g r) c"),
                   Ub[:, 32 * q4:32 * q4 + ln_, :].rearrange("p c g -> p g c"), (uq,), (tl("hb_%d" % (q4 + 1)),))
        cp("dve", Hcar[:, j, :], Ub[:, 127, :], (UBQ[3],), (tl("hcar"),))
        if ps == npass - 1:
            dma("sp", stp[j], Hcar[:, j, :], (tl("hcar"),), (tl("stp_out_%d" % j),), ch_out)
        wb1 = j * NBLK_PER_J + 20
        wl_glu1 = [wload(wb1 + 0), wload(wb1 + 1)]
        def load3(b_):
            return rbload(lambda t_: t_[:, :].rearrange("p (a m) -> p a m", a=4),
                          SMV[j, 4 * b_:4 * b_ + 4].rearrange("a p m -> p a m"), tl("SMV"))

        pend = [load3(0), load3(1)]
        for g in range(64):
            b_ = g // 4
            if g % 4 == 0 and b_ + 2 < 16:
                pend.append(load3(b_ + 2))
            si = pend[b_]
            mv_ = RB[si][:, (g % 4) * 384:(g % 4) * 384 + 384]
            bk = mbank()
            py = PS[:, bk, 0:CB]
            mm(py, mv_[:, 0:128], Xs5[:, g, :], True, False, (RBT[si], tl("y_%d" % g)) + XS5_F[g // 8],
               (bank[bk],))
            for ri in range(2):
                mm(py, mv_[:, 128 + 128 * ri:256 + 128 * ri], Hb[:, g // 2, ri, :], False, ri == 1,
                   (RBT[si],) + tuple(tl("hb_%d" % q) for q in range(6)), (bank[bk],))
            act(Xs5[:, g, :], py, AF.Gelu_apprx_tanh, (bank[bk],), (tl("y_%d" % g),))
            if g % 8 == 7:
                fc = g // 8
                Ydv = Yd.rearrange("(g k) (t c) -> t k g c", k=16, c=CB)
                ys = tuple(tl("y_%d" % g_) for g_ in range(8 * fc, 8 * fc + 8))
                for t8 in range(8):
                    dma("sp", Ydv[t8][:, 8 * fc:8 * fc + 8, :], Xs5[16 * t8:16 * t8 + 16, 8 * fc:8 * fc + 8, :],
                        ys + XS5_F[fc], (tl("Yd_%d_%d" % (fc, t8)),), ch_scr)
                for fr in ([fc - 1] if fc >= 1 else []) + ([7] if fc == 7 else []):
                    dma("sp", yF[:, fr, :], Yd[fr * 128:(fr + 1) * 128, :],
                        tuple(tl("Yd_%d_%d" % (fr, t_)) for t_ in range(8)),
                        (tl("yf_%d" % fr),) + tuple(tl("hn_%d_%d" % (fr, t)) for t in range(3)), ch_scr3)
        cstep(hsf[:], h0p[:], tr, tpp, tnn, Ub[:, 128:136, :])
        dma("sp", sts[j, ps], hsf[:], (tsx,), (tl("sts_out"),), ch_out)
        for which in range(2):
            wl = wl_glu1 if which == 0 else [wload(wb1 + 2), wload(wb1 + 3)]
            for fc in range(8):
                wi = wl[fc // 4]
                wv = WB[wi][:, :].rearrange("p (a b) -> p a b", b=512)
                dcl = fc % 4
                for ti, (c0, n) in enumerate(TILES_B):
                    bk = mbank()
                    for kc in range(8):
                        mm(PS[:, bk, 0:n], wv[:, kc, dcl * 128:(dcl + 1) * 128], yF[:, kc, c0:c0 + n],
                           kc == 0, kc == 7, (tl("yf_%d" % kc), wtile[wi]), (bank[bk],))
                    gt = tl("gb_%d_%d" % (fc, ti))
                    if which == 0:
                        stt(Gb[:, fc, c0:c0 + n], PS[:, bk, 0:n], bglu[:, j, 0, fc:fc + 1], Gb[:, fc, c0:c0 + n],
                            ALU.add, ALU.mult, (bank[bk], gt, cst), (gt,))
                    else:
                        i = nxt("tmpb", 2)
                        act(tmpb[i][:, 0:n], PS[:, bk, 0:n], AF.Sigmoid, (bank[bk], cst), (tmpb_t[i],),
                            bias=bglu[:, j, 1, fc:fc + 1], scale=1.0)
                        tt("dve", Gb[:, fc, c0:c0 + n], tmpb[i][:, 0:n], Gb[:, fc, c0:c0 + n], ALU.mult,
                           (tmpb_t[i], gt), (gt,))
        wb2 = j * NBLK_PER_J + 24
        wl = [wload(wb2), wload(wb2 + 1)]
        for dmc in range(8):
            wi = wl[dmc // 4]
            wv = WB[wi][:, :].rearrange("p (a b) -> p a b", b=512)
            dcl = dmc % 4
            for ti, (c0, n) in enumerate(TILES_B):
                bk = mbank()
                for kc in range(8):
                    mm(PS[:, bk, 0:n], wv[:, kc, dcl * 128:(dcl + 1) * 128], Gb[:, kc, c0:c0 + n],
                       kc == 0, kc == 7, (tl("gb_%d_%d" % (kc, ti)), wtile[wi]), (bank[bk],))
                out_stage_evac(bk, dmc, ti, c0, n, O_B)
        flush_stat()
        postnorm(layer, "B", O_B, TILES_B)

    for ps in range(npass):
        for kc in range(8):
            dma("sp", Xsb[:, kc, :], xT[ps, :, kc, :], (),
                tuple(tl("x_%d_%d" % (kc, ti)) for ti in range(3)), ch_x[ps % 2],
                extra=(bar_ops if ps == 0 else ()))
        for layer, kind in enumerate(layers):
            if kind == "A":
                layer_a(layer, ps)
            elif kind == "B":
                layer_b(layer, ps)
        for kc in range(8):
            dma("sp", yT[ps, :, kc, :], Xsb[:, kc, :], tuple(tl("x_%d_%d" % (kc, ti)) for ti in range(3)),
                (tl("y_out_%d" % kc),), ch_out)
    print('sbuf bytes remaining', nc.sbuf_bytes_remaining)
    if max_ops is not None:
        print('total ops', len(P.ops))
        P.ops = P.ops[:max_ops]
    P.emit()
    return nc


def _blk(w, nb, kc, n):
    return np.ascontiguousarray(w.reshape(kc, 128, nb, n).transpose(2, 1, 0, 3)).reshape(nb, 128, kc * n)


def _consts():
    ident = np.eye(128, dtype=np.float32)
    s = np.arange(128)
    trimask = (s[:, None] <= s[None, :]).astype(np.float32)
    q = np.arange(32)
    bdmask = ((q[:, None] // 4 == q[None, :] // 4) & (q[:, None] % 4 <= q[None, :] % 4)).astype(np.float32)
    tmask = ((s[:, None] // 16) <= (s[None, :] // 16)).astype(np.float32)
    colmask = np.zeros((128, 2, 128), np.float32)
    colmask[:, 0, :64] = 1
    colmask[:, 1, 64:] = 1
    rowmask = np.zeros((128, 2), np.float32)
    rowmask[:64, 0] = 1
    rowmask[64:, 1] = 1
    return dict(ident=ident, trimask=trimask, bdmask=bdmask, tmask=tmask, colmask=colmask, rowmask=rowmask)


def _shared_inputs(inp):
    f = lambda a: np.ascontiguousarray(np.asarray(a, dtype=np.float32))
    blocks = []
    for j in range(2):
        blocks.append(_blk(f(inp["w_in_a"][j]), 12, 8, 512))
        blocks.append(_blk(f(inp["w_out_a"][j]), 4, 16, 256))
        blocks.append(_blk(f(inp["w_in_b"][j]), 4, 8, 512))
        blocks.append(_blk(f(inp["w_glu1"][j]), 2, 8, 512))
        blocks.append(_blk(f(inp["w_glu2"][j]), 2, 8, 512))
        blocks.append(_blk(f(inp["w_out_b"][j]), 2, 8, 512))
    d = dict(wall=np.concatenate(blocks, axis=0))
    col = lambda v, n: np.ascontiguousarray(f(v).reshape(n, 128).T)
    d["gpre"] = np.concatenate([col(inp["norm_pre"][l], 8) for l in range(4)], axis=1)
    d["gpost"] = np.concatenate([col(inp["norm_post"][l], 8) for l in range(4)], axis=1)
    d["lncol"] = np.ascontiguousarray(np.stack(
        [np.stack([col(inp["ln_v_g"][j], 16), col(inp["ln_v_b"][j], 16)], axis=1) for j in range(2)], axis=1))
    d["lnbc"] = np.ascontiguousarray(np.stack(
        [np.stack([np.broadcast_to(f(inp["ln_v_g"][j])[None, :], (32, 2048)),
                   np.broadcast_to(f(inp["ln_v_b"][j])[None, :], (32, 2048))]) for j in range(2)]))
    ws = f(inp["w_s"])
    d["wsT"] = np.ascontiguousarray(ws.transpose(0, 3, 1, 2))
    corner = ws[:, :, :4, :4].transpose(0, 3, 1, 2)
    d["wsrep"] = np.ascontiguousarray(np.tile(corner, (1, 8, 1, 8)))
    d["bsB"] = np.ascontiguousarray(np.broadcast_to(f(inp["b_s"])[:, None, :, :], (2, 128, 8, 128)))
    d["bglu"] = np.ascontiguousarray(np.stack(
        [np.stack([col(inp["b_glu1"][j], 8), col(inp["b_glu2"][j], 8)], axis=1) for j in range(2)], axis=1))

    def l2(a):
        return a.reshape(32, 2, 64).transpose(1, 2, 0).reshape(128, 32)

    aL2 = np.zeros((2, 128, 3, 32), np.float32)
    bL2 = np.zeros((2, 128, 2, 32, 16), np.float32)
    cL2 = np.zeros((2, 128, 2, 32, 16), np.float32)
    dcol = np.zeros((2, 128, 64), np.float32)
    for j in range(2):
        aL2[j, :, 0] = l2(f(inp["a_re"][j]))
        aL2[j, :, 1] = l2(f(inp["a_im"][j]))
        aL2[j, :, 2] = l2(np.broadcast_to(f(inp["log_dt"][j])[:, None], (64, 64)))
        for r, nm in enumerate(("b_re", "b_im")):
            b = f(inp[nm][j])
            bL2[j, :, r] = b.reshape(32, 2, 64, 16).transpose(1, 2, 0, 3).reshape(128, 32, 16)
        for r, nm in enumerate(("c_re", "c_im")):
            c = f(inp[nm][j])
            cL2[j, :, r] = c.reshape(32, 2, 16, 64).transpose(1, 3, 0, 2).reshape(128, 32, 16)
        dk = f(inp["d_skip"][j]).reshape(64, 16)
        dcol[j] = np.tile(dk.T, (8, 1))
    d.update(aL2=aL2, bL2=bL2, cL2=cL2, dcol=dcol)
    d.update(_consts())
    return d


def _core_inputs(inp, cid):
    f = lambda a: np.asarray(a, dtype=np.float32)
    xp = f(inp["x_prompt"][cid])
    xsm = f(inp["x_sample"][16 * cid:16 * cid + 16])
    xT = np.zeros((2, 128, 8, NTA), np.float32)
    for ps in range(2):
        cols = np.concatenate([xp[1024 * ps:1024 * ps + 1024], xsm[8 * ps:8 * ps + 8].reshape(32, 1024)], axis=0)
        xT[ps] = cols.T.reshape(8, 128, NTA).transpose(1, 0, 2)
    h0 = np.zeros((2, 2, 128, 8, 64), np.float32)
    for j in range(2):
        for ps in range(2):
            sl = slice(16 * cid + 8 * ps, 16 * cid + 8 * ps + 8)
            re = f(inp["state_ssm_re"][j, sl])
            im = f(inp["state_ssm_im"][j, sl])
            st = np.stack([re, im], axis=-1)
            st = st.reshape(8, 32, 2, 64, 2).transpose(2, 3, 0, 1, 4)
            h0[j, ps] = st.reshape(128, 8, 64)
    return dict(xT=xT, h0L2=h0)


_NC_CACHE = {}


def kernel(**inputs):
    inp = {k: np.asarray(v) for k, v in inputs.items()}
    shared = _shared_inputs(inp)
    in_maps = []
    for cid in range(NCORES):
        m = dict(shared)
        m.update(_core_inputs(inp, cid))
        in_maps.append(m)
    if "nc" not in _NC_CACHE:
        _NC_CACHE["nc"] = build_program()
    nc = _NC_CACHE["nc"]
    res = run_bass_kernel_spmd(nc, in_maps, core_ids=list(range(NCORES)))
    y_prompt = np.zeros((8, 2048, 1024), np.float32)
    y_sample = np.zeros((128, 4, 1024), np.float32)
    cvs = np.zeros((2, 128, 4, 2048), np.float32)
    srp = np.zeros((2, 8, 64, 64), np.float32)
    sip = np.zeros((2, 8, 64, 64), np.float32)
    srs = np.zeros((2, 128, 64, 64), np.float32)
    sis = np.zeros((2, 128, 64, 64), np.float32)
    for cid in range(NCORES):
        r = res.results[cid]
        yT = np.asarray(r["yT"])
        for ps in range(2):
            cols = yT[ps].transpose(2, 1, 0).reshape(NTA, 1024)
            y_prompt[cid, 1024 * ps:1024 * ps + 1024] = cols[:1024]
            y_sample[16 * cid + 8 * ps:16 * cid + 8 * ps + 8] = cols[1024:].reshape(8, 4, 1024)
        cvv = np.asarray(r["cv"])
        stpv = np.asarray(r["stp"])
        stsv = np.asarray(r["sts"])
        for j in range(2):
            for ps in range(2):
                cvs[j, 16 * cid + 8 * ps:16 * cid + 8 * ps + 8] = cvv[j, ps].reshape(8, 4, 2048)
                s = stsv[j, ps].reshape(2, 64, 8, 32, 2).transpose(2, 3, 0, 1, 4).reshape(8, 64, 64, 2)
                srs[j, 16 * cid + 8 * ps:16 * cid + 8 * ps + 8] = s[..., 0]
                sis[j, 16 * cid + 8 * ps:16 * cid + 8 * ps + 8] = s[..., 1]
            s = stpv[j].reshape(2, 64, 32, 2).transpose(2, 0, 1, 3).reshape(64, 64, 2)
            srp[j, cid] = s[..., 0]
            sip[j, cid] = s[..., 1]
    return (y_prompt, y_sample, cvs, srp, sip, srs, sis)
```

```python
import os
import numpy as np
import concourse.bass as bass
import concourse.mybir as mybir
from concourse.bass_utils import run_bass_kernel_spmd

F32 = mybir.dt.float32
BF16 = mybir.dt.bfloat16
I32 = mybir.dt.int32
AF = mybir.ActivationFunctionType
ALU = mybir.AluOpType

NCORES = 8
D = 1024
NTA = 1056
NTB = 1088
CB = 136
TILES_A = [(0, 512), (512, 512), (1024, 32)]
TILES_B = [(0, 512), (512, 512), (1024, 64)]
EPS = 1e-6
NBLK_PER_J = 26
SCAN_NOSYNC = os.environ.get("SCAN_SYNC", "0") != "1"
BUILD_NOSYNC = SCAN_NOSYNC


class Tl:
    __slots__ = ("name", "w", "r", "excl")

    def __init__(self, name, excl=False):
        self.name = name
        self.w = None
        self.r = {}
        self.excl = excl


class Chan:
    def __init__(self, nc, name):
        self.sem = nc.alloc_semaphore(name)
        self.cnt = 0
        self.name = name
        self.pending = 0
        self.selfw = 0


class Op:
    __slots__ = ("eng", "fn", "chan", "signal", "deps", "semval", "val", "final")


class Prog:
    CE = ("pe", "act", "dve", "pool")

    def __init__(self, nc):
        self.nc = nc
        self.ops = []
        self.sem = {e: nc.alloc_semaphore("s_" + e) for e in self.CE}
        self.chans = []
        self.last = {}

    def chan(self, name):
        c = Chan(self.nc, name)
        self.chans.append(c)
        return c

    def add(self, eng, fn, reads=(), writes=(), chan=None, extra=(), nosync=False):
        op = Op()
        op.eng = eng
        op.fn = fn
        op.chan = chan
        op.signal = False
        op.final = False
        xw = tuple(t for t in reads if t.excl)
        if xw:
            writes = tuple(writes) + xw
        deps = {}
        for t in reads:
            if t.w is not None:
                deps[id(t.w)] = t.w
        for t in writes:
            if t.w is not None:
                deps[id(t.w)] = t.w
            for r in t.r.values():
                deps[id(r)] = r
        for d in extra:
            deps[id(d)] = d
        for t in writes:
            t.w = op
            t.r = {}
        for t in reads:
            key = id(chan) if chan is not None else eng
            t.r[key] = op
        dl = []
        for d in deps.values():
            if d is op:
                continue
            if d.chan is not None:
                dl.append(("dma", d.chan, d.chan.cnt * 16))
                d.chan.pending = max(d.chan.pending, d.chan.cnt * 16)
            else:
                if d.eng == eng and chan is None and (eng == "pe" or nosync):
                    continue
                d.signal = True
                dl.append(("cmp", d))
        if chan is not None and chan.pending > chan.selfw:
            dl.append(("dma", chan, chan.pending))
            chan.selfw = chan.pending
        op.deps = dl
        if chan is not None:
            chan.cnt += 1
            op.val = chan.cnt * 16
        self.ops.append(op)
        self.last[eng if chan is None else ("q", eng)] = op
        return op

    def barrier(self, tiny):
        lasts = [self.last[e] for e in self.CE if e in self.last]
        dmas = [self.last[k] for k in self.last if isinstance(k, tuple)]
        out = []
        for e in ("act", "dve", "pool"):
            out.append(self.add(e, tiny[e], extra=lasts + dmas))
        return out + dmas

    def emit(self):
        cnt = {e: 0 for e in self.CE}
        for op in self.ops:
            if op.chan is None and op.signal:
                cnt[op.eng] += 1
                op.semval = cnt[op.eng]
        nc = self.nc
        by = {e: [o for o in self.ops if o.eng == e] for e in ("pe", "act", "dve", "pool", "sp")}
        sems = self.sem
        chans = self.chans

        def run(e, lst, is_sp=False):
            waited = {}
            for op in lst:
                need = {}
                for d in op.deps:
                    if d[0] == "dma":
                        sem, val = d[1].sem, d[2]
                    else:
                        sem, val = sems[d[1].eng], d[1].semval
                    k = id(sem)
                    if k not in need or need[k][1] < val:
                        need[k] = (sem, val)
                for k, (sem, val) in need.items():
                    if waited.get(k, 0) >= val:
                        continue
                    e.wait_ge(sem, val)
                    waited[k] = val
                ins = op.fn(e)
                if op.chan is not None:
                    ins.then_inc(op.chan.sem, 16)
                elif op.signal:
                    ins.then_inc(sems[op.eng], 1)
            if is_sp:
                for c in chans:
                    n_em = sum(1 for o in self.ops if o.chan is c)
                    if n_em > 0:
                        e.wait_ge(c.sem, n_em * 16)

        with nc.Block() as block:
            @block.tensor
            def _(e):
                run(e, by["pe"])

            @block.scalar
            def _(e):
                run(e, by["act"])

            @block.vector
            def _(e):
                run(e, by["dve"])

            @block.gpsimd
            def _(e):
                run(e, by["pool"])

            @block.sync
            def _(e):
                run(e, by["sp"], True)


def build_program(layers=("A", "B", "A", "B"), npass=2, max_ops=None):
    nc = bass.Bass("TRN2", target_bir_lowering=False)
    P = Prog(nc)

    def din(name, shape, dt=F32):
        return nc.dram_tensor(name, list(shape), dt, kind="ExternalInput").ap()

    def dout(name, shape, dt=F32):
        return nc.dram_tensor(name, list(shape), dt, kind="ExternalOutput").ap()

    xT = din("xT", [2, 128, 8, NTA])
    WALL = din("wall", [2 * NBLK_PER_J, 128, 4096])
    gpre_d = din("gpre", [128, 32])
    gpost_d = din("gpost", [128, 32])
    lncol_d = din("lncol", [128, 2, 2, 16])
    lnbc_d = din("lnbc", [2, 2, 32, 2048])
    wsT_d = din("wsT", [2, 128, 8, 128])
    wsrep_d = din("wsrep", [2, 32, 8, 32])
    bsB_d = din("bsB", [2, 128, 8, 128])
    bglu_d = din("bglu", [128, 2, 2, 8])
    aL2_d = din("aL2", [2, 128, 3, 32])
    bL2_d = din("bL2", [2, 128, 2, 32, 16])
    cL2_d = din("cL2", [2, 128, 2, 32, 16])
    dcol_d = din("dcol", [2, 128, 64])
    h0_d = din("h0L2", [2, 2, 128, 8, 64])
    ident_d = din("ident", [128, 128])
    trim_d = din("trimask", [128, 128])
    bdm_d = din("bdmask", [32, 32])
    tmask_d = din("tmask", [128, 128])
    colm_d = din("colmask", [128, 2, 128])
    rowm_d = din("rowmask", [128, 2])

    yT = dout("yT", [2, 128, 8, NTA])
    cv = dout("cv", [2, 2, 32, 2048])
    stp = dout("stp", [2, 128, 64])
    sts = dout("sts", [2, 2, 128, 8, 64])

    SW = nc.dram_tensor("SW", [2, 32, 128, 512], BF16).ap()
    SMV = nc.dram_tensor("SMV", [2, 64, 128, 384], BF16).ap()
    Xd = nc.dram_tensor("Xd", [1024, NTB], BF16).ap()
    Yd = nc.dram_tensor("Yd", [1024, NTB], BF16).ap()

    def sb(name, shape, dt=F32):
        return nc.alloc_sbuf_tensor("sb_" + name, list(shape), dt)

    A0 = sb("A0", [128, 8, NTA])
    A1 = sb("A1", [128, 8, NTB], BF16)
    A2 = sb("A2", [128, 18432], BF16)
    A3 = sb("A3", [128, 17408], BF16)
    A5 = sb("A5", [128, 8, NTB], BF16)
    NWB = 2
    WB = [sb("wb%d" % i, [128, 4096], BF16) for i in range(NWB)]
    NRB = 3
    RB = [sb("rb%d" % i, [128, 1536], BF16) for i in range(NRB)]
    ident = sb("ident", [128, 128])
    ones_bf = sb("ones_bf", [128, 128], BF16)
    ones_f = sb("ones_f", [128, 128])
    trim = sb("trim", [128, 128])
    bdm = sb("bdm", [128, 32])
    tmask = sb("tmask", [128, 128])
    colm = sb("colm", [128, 2, 128])
    rowm = sb("rowm", [128, 2])
    gpre = sb("gpre", [128, 32])
    gpost = sb("gpost", [128, 32])
    lncol = sb("lncol", [128, 2, 2, 16])
    bglu = sb("bglu", [128, 2, 2, 8])
    epsc = sb("epsc", [128, 1])
    hpic = sb("hpic", [128, 1])
    zero1 = sb("zero1", [128, 1])
    ctab = sb("ctab", [128, 16, 128])
    wsTm = sb("wsTm", [128, 8, 128], BF16)
    bdw = sb("bdw", [128, 8, 32], BF16)
    rst = [sb("rst%d" % i, [128, 512]) for i in range(2)]
    sqr = [sb("sq%d" % i, [128, 512], BF16) for i in range(2)]
    tmpb = [sb("tmpb%d" % i, [128, 512], BF16) for i in range(2)]
    xs = [sb("xs%d" % i, [128, NTB], BF16) for i in range(2)]
    stats = sb("stats", [128, 9, 4, 6])
    mv = sb("mv", [128, 9, 2])
    rstdv = sb("rstdv", [128, 9])
    nmr = sb("nmr", [128, 9])
    AR2 = sb("AR2", [128, 2, 64])
    AIp = sb("AIp", [128, 2, 32])
    AIn = sb("AIn", [128, 2, 32])
    AI2 = sb("AI2", [128, 2, 64])
    M4r = sb("M4r", [128, 2, 64])
    M4p = sb("M4p", [128, 2, 32])
    M4n = sb("M4n", [128, 2, 32])
    Hcar = sb("Hcar", [128, 2, 64])
    Hin0 = sb("Hin0", [128, 64])
    h0t = sb("h0t", [128, 8, 64])
    h0p = sb("h0p", [128, 8, 64])
    hsf = sb("hsf", [128, 8, 64])
    st1 = sb("st1", [128, 8, 64])
    st2 = sb("st2", [128, 8, 64])
    barA = sb("barA", [128, 1])
    barV = sb("barV", [128, 1])
    barG = sb("barG", [128, 1])

    PS = nc.alloc_psum_tensor("PS", [128, 8, 512], F32)
    bank = [Tl("bank%d" % i, True) for i in range(8)]

    Xsb = A0
    Hn = A1
    G = A2[:, :].rearrange("p (n d t) -> p n d t", n=9, d=16)
    Gv = A2[:, :].rearrange("p (n f) -> p n f", n=9)
    Ub = A2[:, 0:17408].bitcast(F32).rearrange("p (c g) -> p c g", g=64)
    O_A = A3[:, 0:16896].bitcast(F32).rearrange("p (k c) -> p k c", k=8)
    O_B = A3[:, :].bitcast(F32).rearrange("p (k c) -> p k c", k=8)
    Xs5 = A3[:, 0:8704].rearrange("p (g c) -> p g c", g=64)
    Hb = A3[:, 8704:17408].rearrange("p (g r c) -> p g r c", g=32, r=2)
    yF = A1
    ltmp = O_A[:, 4, 0:1024].rearrange("p (h t) -> p h t", h=8)
    Gb = A5

    T = {}

    def tl(name):
        if name not in T:
            T[name] = Tl(name)
        return T[name]

    OT_ALL = tuple(tl("o_%d_%d" % (d_, t_)) for d_ in range(8) for t_ in range(3))
    RBT = [tl("rb_%d" % i_) for i_ in range(3)]
    XS5_F = [tuple(tl("xs5_%d_%d" % (s_, f_)) for s_ in range(8)) for f_ in range(8)]
    par = P.chan("par")
    ch_x = [P.chan("chx%d" % i) for i in range(2)]
    ch_out = P.chan("chout")
    ch_w = [P.chan("chw%d" % i) for i in range(NWB)]
    ch_rb = [P.chan("chrb_%d" % i) for i in range(NRB)]
    rbstate = {"n": 0}

    def rbload(dst_view_fn, src_ap, src_tile):
        i = rbstate["n"] % NRB
        rbstate["n"] += 1
        P.add("pool", lambda e, o=dst_view_fn(RB[i]), i_=src_ap: e.dma_start(out=o, in_=i_), (src_tile,),
              (RBT[i],), ch_rb[i])
        return i

    ch_scr = P.chan("chscr")
    ch_yw = [P.chan("chyw%d" % i_) for i_ in range(2)]
    ch_scr2 = P.chan("chscr2")
    ch_scr3 = P.chan("chscr3")
    ch_xr = [[P.chan("chxr%d_%d" % (f_, q_)) for q_ in range(2)] for f_ in range(8)]
    ch_prep = P.chan("chprep")
    ch_xs = [P.chan("chxs%d" % i) for i in range(2)]

    def dma(q, out, in_, reads, writes, chan, extra=()):
        return P.add(q, lambda e, o=out, i=in_: e.dma_start(out=o, in_=i), reads, writes, chan, extra)

    def mm(out, lhsT, rhs, start, stop, reads, writes):
        return P.add("pe", lambda e, o=out, l=lhsT, r=rhs, s=start, t=stop:
                     e.matmul(o, l, r, start=s, stop=t), reads, writes)

    def act(out, in_, func, reads, writes, bias=None, scale=None):
        def fn(e, o=out, i=in_, f=func, b=bias, s=scale):
            kw = {}
            if b is not None:
                kw["bias"] = b
            if s is not None:
                kw["scale"] = s
            return e.activation(out=o, in_=i, func=f, **kw)
        return P.add("act", fn, reads, writes)

    def tt(eng, out, in0, in1, op, reads, writes, nosync=False):
        return P.add(eng, lambda e, o=out, a=in0, b=in1, p=op: e.tensor_tensor(out=o, in0=a, in1=b, op=p),
                     reads, writes, nosync=nosync)

    def ts(eng, out, in0, s1, s2, op0, op1, reads, writes, nosync=False):
        def fn(e, o=out, a=in0, x=s1, y=s2, p0=op0, p1=op1):
            if p1 is None:
                return e.tensor_scalar(out=o, in0=a, scalar1=x, scalar2=None, op0=p0)
            return e.tensor_scalar(out=o, in0=a, scalar1=x, scalar2=y, op0=p0, op1=p1)
        return P.add(eng, fn, reads, writes)

    def stt(out, in0, scalar, in1, op0, op1, reads, writes):
        return P.add("dve", lambda e, o=out, a=in0, s=scalar, b=in1, p0=op0, p1=op1:
                     e.scalar_tensor_tensor(out=o, in0=a, scalar=s, in1=b, op0=p0, op1=p1), reads, writes)

    def cp(eng, out, in_, reads, writes):
        if eng == "act":
            return P.add("act", lambda e, o=out, i=in_: e.activation(out=o, in_=i, func=AF.Copy), reads, writes)
        return P.add(eng, lambda e, o=out, i=in_: e.tensor_copy(out=o, in_=i), reads, writes)

    def mset(eng, ap, val, writes):
        return P.add(eng, lambda e, a=ap, v=val: e.memset(a, v), (), writes)

    wstate = {"n": 0}
    wtile = [Tl("wb%d" % i) for i in range(NWB)]

    def wload(blk):
        i = wstate["n"] % NWB
        wstate["n"] += 1
        src = WALL[blk].rearrange("p (a b) -> p a b", b=512)
        dst = WB[i][:, :].rearrange("p (a b) -> p a b", b=512)
        dma("pool", dst, src, (), (wtile[i],), ch_w[i])
        return i

    cst = tl("const")
    for dst, src in ((ident, ident_d), (trim, trim_d), (tmask, tmask_d), (colm, colm_d),
                     (rowm, rowm_d), (gpre, gpre_d), (gpost, gpost_d), (lncol, lncol_d), (bglu, bglu_d)):
        dma("sp", dst[:], src, (), (cst,), par)
    dma("sp", bdm[64:96, :], bdm_d, (), (cst,), par)
    mset("pool", bdw[:], 0.0, (tl("tabA"),))
    mset("dve", ones_f[:], 1.0, (cst,))
    mset("dve", epsc[:], EPS, (cst,))
    mset("dve", hpic[:], float(np.pi / 2), (cst,))
    mset("dve", zero1[:], 0.0, (cst,))
    mset("dve", Hin0[:], 0.0, (cst,))
    mset("dve", stats[:], 0.0, (tl("stats"),))
    mset("dve", mv[:], 1.0, (tl("mv"),))
    mset("dve", barV[:], 0.0, (tl("barV"),))
    mset("pool", barG[:], 0.0, (tl("barG"),))
    cp("dve", ones_bf[:], ones_f[:], (cst,), (cst,))
    act(barA[:], zero1[:], AF.Copy, (cst,), (tl("barA"),))

    def prep_s5(j):
        base = A5[:, :, :].rearrange("p a b -> p (a b)")
        f = base.bitcast(F32)
        off = [0]

        def carve(n, shape=None):
            v = f[:, off[0]:off[0] + n]
            off[0] += n
            return v

        aL = carve(96).rearrange("p (a g) -> p a g", a=3)
        dt_ = carve(32)
        dar = carve(32)
        ang = carve(32)
        mag = carve(32)
        magi = carve(32)
        kf = carve(32)
        ki = carve(32).bitcast(I32)
        rr = carve(32)
        m1 = carve(32)
        sn = carve(32)
        cs = carve(32)
        ab = carve(32)
        t1 = carve(32)
        t2 = carve(32)
        t3 = carve(32)
        t4 = carve(32)
        nr = carve(32)
        den = carve(32)
        cfr = carve(32)
        cfi = carve(32)
        PW = carve(17 * 64).rearrange("p (n r g) -> p n r g", n=17, r=2)
        bL = carve(1024).rearrange("p (r g k) -> p r g k", r=2, g=32)
        cL = carve(1024).rearrange("p (r g k) -> p r g k", r=2, g=32)
        assert off[0] <= 4352
        f1 = A1[:, :, :].rearrange("p a b -> p (a b)").bitcast(F32)
        bbar = f1[:, 0:1024].rearrange("p (r g k) -> p r g k", r=2, g=32)
        big1 = f1[:, 1024:1536].rearrange("p (g k) -> p g k", g=32)
        dcl = f1[:, 2048:2112]
        tp = tl("prep_small")
        dma("sp", aL, aL2_d[j], (), (tp,), par)
        dma("sp", bL, bL2_d[j], (), (tp,), par)
        dma("sp", cL, cL2_d[j], (), (tp,), par)
        dma("sp", dcl, dcol_d[j], (), (tp,), par)
        R = (tp, cst)
        W_ = (tp,)
        ar, ai, ldt = aL[:, 0, :], aL[:, 1, :], aL[:, 2, :]
        act(dt_, ldt, AF.Exp, R, W_)
        tt("dve", dar, dt_, ar, ALU.mult, R, W_)
        tt("dve", ang, dt_, ai, ALU.mult, R, W_)
        act(mag, dar, AF.Exp, R, W_)
        act(magi, dar, AF.Exp, R, W_, scale=-1.0)
        ts("dve", kf, ang, float(1.0 / (2 * np.pi)), None, ALU.mult, None, R, W_)
        cp("dve", ki, kf, R, W_)
        cp("dve", kf, ki, R, W_)
        stt(rr, kf, float(-2 * np.pi), ang, ALU.mult, ALU.add, R, W_)
        ts("dve", m1, rr, float(np.pi), float(-2 * np.pi), ALU.is_gt, ALU.mult, R, W_)
        tt("dve", rr, rr, m1, ALU.add, R, W_)
        ts("dve", m1, rr, float(-np.pi), float(2 * np.pi), ALU.is_lt, ALU.mult, R, W_)
        tt("dve", rr, rr, m1, ALU.add, R, W_)
        act(sn, rr, AF.Sin, R, W_)
        act(ab, rr, AF.Abs, R, W_)
        act(cs, ab, AF.Sin, R, W_, bias=hpic[:], scale=-1.0)
        mset("dve", PW[:, 8, 0, :], 1.0, W_)
        mset("dve", PW[:, 8, 1, :], 0.0, W_)
        tt("dve", PW[:, 9, 0, :], mag, cs, ALU.mult, R, W_)
        tt("dve", PW[:, 9, 1, :], mag, sn, ALU.mult, R, W_)
        tt("dve", PW[:, 7, 0, :], magi, cs, ALU.mult, R, W_)
        stt(PW[:, 7, 1, :], magi, -1.0, sn, ALU.mult, ALU.mult, R, W_)

        wt = [carve(128), carve(128), carve(128), f1[:, 1600:1728]]

        def cmulw(o, a, b, w):
            T = [t_[:, 0:w * 32].rearrange("p (a g) -> p a g", a=w) for t_ in wt]
            a_r, a_i = a[:, :, 0, :], a[:, :, 1, :]
            b_r = b[:, :, 0, :].broadcast_to([128, w, 32])
            b_i = b[:, :, 1, :].broadcast_to([128, w, 32])
            tt("dve", T[0], a_r, b_r, ALU.mult, R, W_)
            tt("dve", T[1], a_i, b_i, ALU.mult, R, W_)
            tt("dve", T[2], a_r, b_i, ALU.mult, R, W_)
            tt("dve", T[3], a_i, b_r, ALU.mult, R, W_)
            tt("dve", o[:, :, 0, :], T[0], T[1], ALU.subtract, R, W_)
            tt("dve", o[:, :, 1, :], T[2], T[3], ALU.add, R, W_)

        cmulw(PW[:, 10:11], PW[:, 9:10], PW[:, 9:10], 1)
        cmulw(PW[:, 6:7], PW[:, 7:8], PW[:, 7:8], 1)
        cmulw(PW[:, 11:13], PW[:, 9:11], PW[:, 10:11], 2)
        cmulw(PW[:, 5:3:-1], PW[:, 7:5:-1], PW[:, 6:7], 2)
        cmulw(PW[:, 13:17], PW[:, 9:13], PW[:, 12:13], 4)
        cmulw(PW[:, 3::-1], PW[:, 7:3:-1], PW[:, 4:5], 4)
        for (src_n, tr, tp_, tn_) in ((16, AR2, AIp, AIn), (4, M4r, M4p, M4n)):
            trv = tr[:, j, :].rearrange("p (g r) -> p g r", r=2)
            cp("dve", trv[:, :, 0], PW[:, src_n, 0, :], R, (cst,))
            cp("dve", trv[:, :, 1], PW[:, src_n, 0, :], R, (cst,))
            cp("dve", tp_[:, j, :], PW[:, src_n, 1, :], R, (cst,))
            ts("dve", tn_[:, j, :], PW[:, src_n, 1, :], -1.0, None, ALU.mult, None, R, (cst,))
        ai2v = AI2[:, j, :].rearrange("p (g r) -> p g r", r=2)
        cp("dve", ai2v[:, :, 0], AIn[:, j, :], (cst,), (cst,))
        cp("dve", ai2v[:, :, 1], AIp[:, j, :], (cst,), (cst,))
        ts("dve", nr, PW[:, 9, 0, :], -1.0, None, ALU.add, None, R, W_)
        ni = PW[:, 9, 1, :]
        tt("dve", t1, ar, ar, ALU.mult, R, W_)
        tt("dve", t2, ai, ai, ALU.mult, R, W_)
        tt("dve", den, t1, t2, ALU.add, R, W_)
        P.add("dve", lambda e, o=den, i=den: e.reciprocal(out=o, in_=i), R, W_)
        tt("dve", t1, nr, ar, ALU.mult, R, W_)
        tt("dve", t2, ni, ai, ALU.mult, R, W_)
        tt("dve", t1, t1, t2, ALU.add, R, W_)
        tt("dve", cfr, t1, den, ALU.mult, R, W_)
        tt("dve", t1, ni, ar, ALU.mult, R, W_)
        tt("dve", t2, nr, ai, ALU.mult, R, W_)
        tt("dve", t1, t1, t2, ALU.subtract, R, W_)
        tt("dve", cfi, t1, den, ALU.mult, R, W_)

        def bc16(v):
            return v.unsqueeze(2).broadcast_to([128, 32, 16])

        tb = tl("prep_bbar")
        RB = (tp, cst, tb)
        WB_ = (tb,)
        tt("dve", bbar[:, 0], bc16(cfr), bL[:, 0], ALU.mult, RB, WB_)
        tt("dve", big1, bc16(cfi), bL[:, 1], ALU.mult, RB, WB_)
        tt("dve", bbar[:, 0], bbar[:, 0], big1, ALU.subtract, RB, WB_)
        tt("dve", bbar[:, 1], bc16(cfr), bL[:, 1], ALU.mult, RB, WB_)
        tt("dve", big1, bc16(cfi), bL[:, 0], ALU.mult, RB, WB_)
        tt("dve", bbar[:, 1], bbar[:, 1], big1, ALU.add, RB, WB_)

        PSM = f1[:, 1024:1568].rearrange("p (n g) -> p n g", n=17)
        tt("dve", PSM, PW[:, :, 0, :], PW[:, :, 1, :], ALU.add, (tp, tl("prep_bbar")), (tp, tl("prep_bbar")))
        for hf in range(2):
            g0 = 16 * hf
            a2f = A2[:, :].bitcast(F32)
            WcL = a2f[:, 0:4096].rearrange("p (g r s k) -> p g r s k", g=16, r=2, s=8)
            WnL = a2f[:, 4096:8192].rearrange("p (g r s k) -> p g r s k", g=16, r=2, s=8)
            a3f = A3[:, :].bitcast(F32)
            VL = a3f[:, 0:4096].rearrange("p (g r s k) -> p g r s k", g=16, r=2, s=8)
            tmps = {"dve": (a3f[:, 4096:4352].rearrange("p (g k) -> p g k", g=16),
                            a3f[:, 4352:4608].rearrange("p (g k) -> p g k", g=16)),
                    "pool": (a3f[:, 6656:6912].rearrange("p (g k) -> p g k", g=16),
                             a3f[:, 6912:7168].rearrange("p (g k) -> p g k", g=16))}
            VLm = a3f[:, 4608:6656].rearrange("p (g x r m) -> p g x r m", g=4, x=2, r=2)
            a0 = A0[:, :, :].rearrange("p a b -> p (a b)")
            Wst = a0[:, 0:4096].bitcast(BF16).rearrange("p (g x r m) -> p g x r m", g=16, x=2, r=2)
            Vst = a0[:, 4096:8192].bitcast(BF16).rearrange("p (g x r m) -> p g x r m", g=16, x=2, r=2)
            M0st = f1[:, 2176:4224].bitcast(BF16).rearrange("p (g m) -> p g m", g=32)
            tw = tl("prep_big")
            tstg = tl("prep_stg")

            def bcg(v):
                return v.unsqueeze(2).broadcast_to([128, 16, 16])

            sums = a3f[:, 7168:8192].rearrange("p (q g k) -> p q g k", q=4, g=16)
            tsm = tl("prep_sums")
            br = bbar[:, 0, g0:g0 + 16, :]
            bi = bbar[:, 1, g0:g0 + 16, :]
            crr = cL[:, 0, g0:g0 + 16, :]
            cii = cL[:, 1, g0:g0 + 16, :]
            tt("dve", sums[:, 0], br, bi, ALU.add, (tp, tb), (tsm,))
            tt("dve", sums[:, 1], bi, br, ALU.subtract, (tp, tb), (tsm,))
            tt("dve", sums[:, 2], crr, cii, ALU.add, (tp, tb), (tsm,))
            tt("dve", sums[:, 3], crr, cii, ALU.subtract, (tp, tb), (tsm,))

            def build(name, dst, n_of_s, src_r, xsum, xdif, kind):
                tiles = []
                for s_ in range(8):
                    n = n_of_s(s_) + 8
                    pr = bcg(PW[:, n, 0, g0:g0 + 16])
                    pi = bcg(PW[:, n, 1, g0:g0 + 16])
                    psm = bcg(PSM[:, n, g0:g0 + 16])
                    eng = "dve"
                    tA, tB = tmps[eng]
                    tmt = tl("prep_tmp_" + eng)
                    mt = tl("prep_%s_%d" % (name, s_))
                    tiles.append(mt)
                    RW = (tp, cst, tb, tsm, tmt, mt)
                    WW = (mt, tmt)
                    d0 = dst[:, :, 0, s_, :]
                    d1 = dst[:, :, 1, s_, :]
                    ns_ = BUILD_NOSYNC
                    tt(eng, d1, psm, src_r, ALU.mult, RW, WW, nosync=ns_)
                    tt(eng, tA, pi, xsum, ALU.mult, RW, WW, nosync=ns_)
                    tt(eng, d0, d1, tA, ALU.subtract, RW, WW, nosync=ns_)
                    tt(eng, tA, pr, xdif, ALU.mult, RW, WW, nosync=ns_)
                    if kind == "w":
                        tt(eng, d1, d1, tA, ALU.add, RW, WW, nosync=ns_)
                    else:
                        tt(eng, d1, tA, d1, ALU.subtract, RW, WW, nosync=ns_)
                return tuple(tiles)

            tWc = build("wc", WcL, lambda s_: 7 - s_, br, sums[:, 0], sums[:, 1], "w")
            tWn = build("wn", WnL, lambda s_: -(s_ + 1), br, sums[:, 0], sums[:, 1], "w")
            tV = build("v", VL, lambda s_: s_ + 1, crr, sums[:, 2], sums[:, 3], "v")
            tvst = tl("prep_vst")
            twst = tl("prep_wst")
            tm0 = tl("prep_m0st")
            twb = tl("prep_wnlb")
            for x in range(2):
                act(Vst[:, :, x].rearrange("p g r m -> p g (r m)"),
                    VL.rearrange("p g r s k -> p g (r s k)"), AF.Copy, tV + (cst,), (tvst,), scale=rowm[:, x:x + 1])
            WnLb = a3f[:, 4608:6656].bitcast(BF16).rearrange("p (g r m) -> p g r m", g=16, r=2)
            act(WnLb.rearrange("p g r m -> p (g r m)"), WnL.rearrange("p g r s k -> p (g r s k)"), AF.Copy,
                tWn, (twb,))
            for gl in range(16):
                for ri in range(2):
                    bk = (gl * 2 + ri) % 8
                    P.add("pe", lambda e, o=PS[:, bk, 0:128], i=WcL[:, gl, ri].rearrange("p s k -> p (s k)"):
                          e.transpose(o, i, ident[:]), tWc + (cst,), (bank[bk],))
                    tt("dve", Wst[:, gl, :, ri, :],
                       PS[:, bk, 0:128].unsqueeze(1).broadcast_to([128, 2, 128]), colm[:], ALU.mult,
                       (bank[bk], cst), (twst,))
            for gl in range(16):
                for x in range(2):
                    bk = (gl * 2 + x) % 8
                    gg = 2 * (g0 + gl) + x
                    for ri in range(2):
                        mm(PS[:, bk, 0:128], WnLb[:, gl, ri, :], Vst[:, gl, x, ri, :], ri == 0, ri == 1,
                           (twb, tvst), (bank[bk],))
                    tt("dve", tmpAB[hf][:], PS[:, bk, 0:128], tmask[:], ALU.mult, (bank[bk], cst, tl("tmpAB")),
                       (tl("tmpAB"),))
                    stt(M0st[:, gl * 2 + x, :], ident[:], dcl[:, gg:gg + 1], tmpAB[hf][:], ALU.mult, ALU.add,
                        (tp, cst, tl("tmpAB")), (tm0,))
            dma("sp", SW[j, g0:g0 + 16].rearrange("g p (x r m) -> p g x r m", x=2, r=2), Wst, (twst,),
                (tl("SW"),), ch_prep)
            dma("sp", SMV[j, 2 * g0:2 * g0 + 32, :, 0:128].rearrange("g p m -> p g m"), M0st, (tm0,),
                (tl("SMV"),), ch_prep)
            dma("sp", SMV[j, 2 * g0:2 * g0 + 32, :, 128:384].rearrange("(g x) p (r m) -> p g x r m", x=2, r=2),
                Vst, (tvst,), (tl("SMV"),), ch_prep)

    tmpAB = [sb("tmpAB%d" % i, [128, 128]) for i in range(2)]

    has_b = "B" in layers
    bar_ops = []
    if has_b:
        for j in range(2):
            prep_s5(j)
        tiny = {
            "act": lambda e: e.activation(out=barA[:], in_=zero1[:], func=AF.Copy),
            "dve": lambda e: e.memset(barV[:], 0.0),
            "pool": lambda e: e.memset(barG[:], 0.0),
        }
        bar_ops = P.barrier(tiny)

    mset("pool", A1[:, :, :], 0.0, tuple(tl("hn_%d_%d" % (kc_, t_)) for kc_ in range(8) for t_ in range(3)))

    ring = {"rst": 0, "sq": 0, "tmpb": 0, "mmb": 0, "xs": 0}
    rst_t = [Tl("rst%d" % i) for i in range(2)]
    sq_t = [Tl("sq%d" % i) for i in range(2)]
    tmpb_t = [Tl("tmpb%d" % i) for i in range(2)]
    xs_t = [Tl("xs%d" % i) for i in range(2)]
    STATB = [3, 4, 5]

    def nxt(name, n):
        i = ring[name] % n
        ring[name] += 1
        return i

    def rstd_from_bank(bk, n):
        i = nxt("rst", 2)
        act(rst[i][:, 0:n], PS[:, bk, 0:n], AF.Ln, (bank[bk], cst), (rst_t[i],), bias=epsc[:], scale=1.0 / D)
        act(rst[i][:, 0:n], rst[i][:, 0:n], AF.Exp, (rst_t[i],), (rst_t[i],), scale=-0.5)
        return i

    def xcols(name, kc, c0, n):
        return tl("%s_%d_%d" % (name, kc, c0))

    def prenorm(layer, kind):
        gcol = gpre[:, layer * 8:(layer + 1) * 8]
        if kind == "B":
            for kc in range(8):
                v = Hn[:, kc, :].rearrange("p (s c) -> p s c", c=CB)[:, 0:4, 128:136]
                mset("pool", v, 0.0, tuple(tl("hn_%d_%d" % (kc, t2)) for t2 in range(3)))
        for ti, (c0, n) in enumerate(TILES_A):
            bk = STATB[ti]
            for kc in range(8):
                i = nxt("sq", 2)
                act(sqr[i][:, 0:n], Xsb[:, kc, c0:c0 + n], AF.Square, (tl("x_%d_%d" % (kc, ti)),), (sq_t[i],))
                mm(PS[:, bk, 0:n], ones_bf[:], sqr[i][:, 0:n], kc == 0, kc == 7, (sq_t[i], cst), (bank[bk],))
            r = rstd_from_bank(bk, n)
            for kc in range(8):
                xin = Xsb[:, kc, c0:c0 + n]
                if kind == "A":
                    stt(Hn[:, kc, c0:c0 + n], xin, gcol[:, kc:kc + 1], rst[r][:, 0:n], ALU.mult, ALU.mult,
                        (tl("x_%d_%d" % (kc, ti)), rst_t[r], cst), (tl("hn_%d_%d" % (kc, ti)),))
                else:
                    hv = Hn[:, kc, :].rearrange("p (s c) -> p s c", c=CB)
                    if ti < 2:
                        ov = hv[:, :, 64 * ti:64 * ti + 64]
                        iv = xin.rearrange("p (c s) -> p s c", s=8)
                        rv = rst[r][:, 0:n].rearrange("p (c s) -> p s c", s=8)
                    else:
                        ov = hv[:, 4:8, 128:136]
                        iv = xin.rearrange("p (q i) -> p i q", i=4)
                        rv = rst[r][:, 0:n].rearrange("p (q i) -> p i q", i=4)
                    stt(ov, iv, gcol[:, kc:kc + 1], rv, ALU.mult, ALU.mult,
                        (tl("x_%d_%d" % (kc, ti)), rst_t[r], cst),
                        tuple(tl("hn_%d_%d" % (kc, t2)) for t2 in range(3)))

    def out_stage_evac(bk, dmc, ti, c0, n, Obuf):
        cp("act", Obuf[:, dmc, c0:c0 + n], PS[:, bk, 0:n], (bank[bk],), (tl("o_%d_%d" % (dmc, ti)),))
        i = nxt("sq", 2)
        act(sqr[i][:, 0:n], PS[:, bk, 0:n], AF.Square, (bank[bk],), (sq_t[i],))
        sb_ = STATB[ti]
        flush_stat()
        pend_stat.append((PS[:, sb_, 0:n], sqr[i][:, 0:n], dmc == 0, dmc == 7, (sq_t[i], cst), (bank[sb_],)))

    pend_stat = []

    def flush_stat():
        while pend_stat:
            o_, r_, st_, sp_, rd_, wr_ = pend_stat.pop(0)
            mm(o_, ones_bf[:], r_, st_, sp_, rd_, wr_)

    def postnorm(layer, kind, Obuf, tiles):
        gcol = gpost[:, layer * 8:(layer + 1) * 8]
        for ti, (c0, n) in enumerate(tiles):
            r = rstd_from_bank(STATB[ti], n)
            for dmc in range(8):
                stt(Obuf[:, dmc, c0:c0 + n], Obuf[:, dmc, c0:c0 + n], gcol[:, dmc:dmc + 1], rst[r][:, 0:n],
                    ALU.mult, ALU.mult, (tl("o_%d_%d" % (dmc, ti)), rst_t[r], cst), (tl("o_%d_%d" % (dmc, ti)),))
            if kind == "A":
                for dmc in range(8):
                    tt("dve", Xsb[:, dmc, c0:c0 + n], Xsb[:, dmc, c0:c0 + n], Obuf[:, dmc, c0:c0 + n], ALU.add,
                       (tl("o_%d_%d" % (dmc, ti)), tl("x_%d_%d" % (dmc, ti))), (tl("x_%d_%d" % (dmc, ti)),))
        if kind == "B":
            for hlf in range(3):
                for dmc in range(8):
                    ov = Obuf[:, dmc, :].rearrange("p (s c) -> p s c", c=CB)
                    allo = tuple(tl("o_%d_%d" % (dmc, t2)) for t2 in range(3))
                    if hlf < 2:
                        xv = Xsb[:, dmc, 512 * hlf:512 * hlf + 512].rearrange("p (c s) -> p s c", s=8)
                        tt("dve", xv, xv, ov[:, :, 64 * hlf:64 * hlf + 64], ALU.add,
                           allo + (tl("x_%d_%d" % (dmc, hlf)),), (tl("x_%d_%d" % (dmc, hlf)),))
                    else:
                        xv = Xsb[:, dmc, 1024:1056].rearrange("p (q i) -> p i q", i=4)
                        tt("dve", xv, xv, ov[:, 4:8, 128:136], ALU.add, allo + (tl("x_%d_2" % dmc),),
                           (tl("x_%d_2" % dmc),))

    MRING = (0, 1, 2, 6, 7)

    def mbank():
        return MRING[nxt("mmb", 5)]

    def layer_a(layer, ps):
        j = layer // 2
        wb0 = j * NBLK_PER_J
        tb = tl("tabA")
        lt2 = tl("ltmp2")
        ltmp2 = O_A[:, 0, 0:1024].rearrange("p (h t) -> p h t", h=8)
        ltmp3 = O_A[:, 1, 0:256].rearrange("p (h t) -> p h t", h=8)[64:96]
        dma("sp", ltmp, wsT_d[j], (), (tl("ltmp"),) + OT_ALL, par)
        dma("sp", ltmp2, bsB_d[j], (), (lt2,) + OT_ALL, par)
        dma("sp", ltmp3, wsrep_d[j], (), (lt2,) + OT_ALL, par)

        def tables_dve1():
            tt("dve", wsTm[:], ltmp, trim[:].unsqueeze(1).broadcast_to([128, 8, 128]), ALU.mult,
               (tl("ltmp"), cst), (tb,))
            tt("dve", ltmp, ltmp, trim[:].unsqueeze(1).broadcast_to([128, 8, 128]), ALU.mult,
               (tl("ltmp"), cst), (tl("ltmp"),))

        def tables_compute():
            for hh in range(2):
                mm(PS[:, 3 + hh, :], ones_f[:], ltmp[:, 4 * hh:4 * hh + 4, :].rearrange("p a b -> p (a b)"),
                   True, True, (tl("ltmp"), cst), (bank[3 + hh],))
            for dc in range(16):
                hh = dc // 2
                bk = 3 + hh // 4
                stt(ctab[:, dc, :], PS[:, bk, (hh % 4) * 128:(hh % 4) * 128 + 128], lncol[:, j, 1, dc:dc + 1],
                    ltmp2[:, hh, :], ALU.mult, ALU.add, (bank[bk], lt2, cst), (tb,))
            tt("dve", bdw[64:96], ltmp3, bdm[64:96, :].unsqueeze(1).broadcast_to([32, 8, 32]), ALU.mult,
               (lt2, cst), (tb,))

        prenorm(layer, "A")

        def hn_reads(ti):
            return tuple(tl("hn_%d_%d" % (kc, ti)) for kc in range(8))

        wl = [wload(wb0 + 4)]
        for blk in range(4):
            if blk < 3:
                wl.append(wload(wb0 + 4 + blk + 1))
            wi = wl[blk]
            wv = WB[wi][:, :].rearrange("p (a b) -> p a b", b=512)
            for n in range(9):
                if blk == 0 and n == 4:
                    tables_dve1()
                if blk == 1 and n == 0:
                    tables_compute()
                M = 128
                c0 = 128 * n if n < 8 else 960
                bk = mbank()
                for kc in range(8):
                    mm(PS[0:M, bk, :], Hn[:, kc, c0:c0 + M], wv[:, kc, :], kc == 0, kc == 7,
                       ((tl("hn_%d_%d" % (kc, n // 4)),) if n < 8 else (tl("hn_%d_1" % kc), tl("hn_%d_2" % kc)))
                       + (wtile[wi],), (bank[bk],))
                cp("act", Gv[0:M, n, blk * 512:(blk + 1) * 512], PS[0:M, bk, :], (bank[bk],),
                   tuple(tl("g_%d_%d" % (n, dc)) for dc in range(4 * blk, 4 * blk + 4)))
                P.add("dve", lambda e, o=stats[0:M, n, blk, :], i=Gv[0:M, n, blk * 512:(blk + 1) * 512]:
                      e.bn_stats(out=o, in_=i), tuple(tl("g_%d_%d" % (n, dc)) for dc in range(4 * blk, 4 * blk + 4)),
                      (tl("stats"),))
        for n in range(9):
            M = 128
            P.add("dve", lambda e, o=mv[0:M, n, :], i=stats[0:M, n, :, :].rearrange("p a b -> p (a b)"):
                  e.bn_aggr(out=o, in_=i), (tl("stats"),), (tl("mv"),))
        act(rstdv[:], mv[:, :, 1], AF.Sqrt, (tl("mv"), cst), (tl("mv"),), bias=epsc[:], scale=1.0)
        P.add("dve", lambda e: e.reciprocal(out=rstdv[:], in_=rstdv[:]), (tl("mv"),), (tl("mv"),))
        stt(nmr[:], mv[:, :, 0], -1.0, rstdv[:], ALU.mult, ALU.mult, (tl("mv"),), (tl("mv"),))
        for n in range(9):
            M = 128
            Nt = 128 if n < 8 else 32
            gts = tuple(tl("g_%d_%d" % (n, dc)) for dc in range(16))
            act(Gv[0:M, n, :], Gv[0:M, n, :], AF.Identity, gts + (tl("mv"),), gts,
                bias=nmr[0:M, n:n + 1], scale=rstdv[0:M, n:n + 1])
            if n == 8:
                cvt = tl("cvt")
                ofl = A3[:, :].bitcast(F32)
                cvs = ofl[64:96, 2112:4160]
                cvg = ofl[64:96, 4160:6208]
                cvb = ofl[64:96, 6208:8256]
                dma("sp", cvg, lnbc_d[j, 0], (), (cvt, tl("ltmp"), lt2) + OT_ALL, par)
                dma("sp", cvb, lnbc_d[j, 1], (), (cvt, tl("ltmp"), lt2) + OT_ALL, par)
                tt("dve", cvs, Gv[64:96, 8, :], cvg, ALU.mult, gts + (cvt,), (cvt,))
                tt("dve", cvs, cvs, cvb, ALU.add, (cvt,), (cvt,))
                dma("sp", cv[j, ps], cvs, (cvt,) + OT_ALL, (tl("cv_out"),), ch_out)
            for dc in range(16):
                bk = 4 + dc // 4
                rhs = wsTm[:, dc // 2, :] if n < 8 else bdw[:, dc // 2, :]
                mm(PS[:, bk, (dc % 4) * 128:(dc % 4) * 128 + Nt], G[0:M, n, dc, :], rhs, True, True,
                   (tl("g_%d_%d" % (n, dc)), tb), (bank[bk],))
            if n < 8:
                for b4 in range(4):
                    bk = 4 + b4
                    pv = PS[:, bk, :].rearrange("p (d t) -> p d t", d=4)
                    gB = lncol[:, j, 0, 4 * b4:4 * b4 + 4].unsqueeze(2).broadcast_to([128, 4, 128])
                    tt("dve", pv, pv, gB, ALU.mult, (bank[bk], cst), (bank[bk],))
                    tt("dve", G[:, n, 4 * b4:4 * b4 + 4, :], pv, ctab[:, 4 * b4:4 * b4 + 4, :], ALU.add,
                       (bank[bk], tb), tuple(tl("g_%d_%d" % (n, dc_)) for dc_ in range(4 * b4, 4 * b4 + 4)))
            for dc in range(16 if n == 8 else 0):
                bk = 4 + dc // 4
                pin = PS[:, bk, (dc % 4) * 128:(dc % 4) * 128 + Nt]
                if n < 8:
                    pass
                else:
                    stt(G[:, 8, dc, 0:32].rearrange("p (q i) -> p q i", i=4),
                        pin.rearrange("p (q i) -> p q i", i=4), lncol[:, j, 0, dc:dc + 1],
                        ctab[:, dc, 0:4].unsqueeze(1).broadcast_to([128, 8, 4]), ALU.mult, ALU.add,
                        (bank[bk], tb, cst), (tl("g_8_%d" % dc),))

        def gview(ti, dc):
            if ti < 2:
                return G[:, 4 * ti:4 * ti + 4, dc, :]
            return G[:, 8, dc, 0:32]

        def gtiles(ti, dc):
            if ti < 2:
                return tuple(tl("g_%d_%d" % (n, dc)) for n in range(4 * ti, 4 * ti + 4))
            return (tl("g_8_%d" % dc),)

        for stage, b0 in (("u", 0), ("z", 8)):
            wl = [wload(wb0 + b0)]
            for blk in range(4):
                if blk < 3:
                    wl.append(wload(wb0 + b0 + blk + 1))
                wi = wl[blk]
                wv = WB[wi][:, :].rearrange("p (a b) -> p a b", b=512)
                for dcl in range(4):
                    dc = 4 * blk + dcl
                    for ti, (c0, n) in enumerate(TILES_A):
                        bk = mbank()
                        for kc in range(8):
                            mm(PS[:, bk, 0:n], wv[:, kc, dcl * 128:(dcl + 1) * 128], Hn[:, kc, c0:c0 + n],
                               kc == 0, kc == 7, (tl("hn_%d_%d" % (kc, ti)), wtile[wi]), (bank[bk],))
                        pv = PS[:, bk, 0:n]
                        if ti < 2:
                            pv = pv.rearrange("p (a b) -> p a b", b=128)
                        gt = gtiles(ti, dc)
                        if stage == "u":
                            tt("dve", gview(ti, dc), pv, gview(ti, dc), ALU.mult, (bank[bk],) + gt, gt)
                        else:
                            i = nxt("tmpb", 2)
                            act(tmpb[i][:, 0:n], PS[:, bk, 0:n], AF.Silu, (bank[bk],), (tmpb_t[i],))
                            tv = tmpb[i][:, 0:n]
                            if ti < 2:
                                tv = tv.rearrange("p (a b) -> p a b", b=128)
                            tt("dve", gview(ti, dc), tv, gview(ti, dc), ALU.mult, (tmpb_t[i],) + gt, gt)
        wl = [wload(wb0 + 12)]
        for blk in range(4):
            if blk < 3:
                wl.append(wload(wb0 + 12 + blk + 1))
            wi = wl[blk]
            wv = WB[wi][:, :].rearrange("p (a b) -> p a b", b=256)
            for dml in range(2):
                dmc = 2 * blk + dml
                for ti, (c0, n) in enumerate(TILES_A):
                    bk = mbank()
                    for dc in range(16):
                        mm(PS[:, bk, 0:n], wv[:, dc, dml * 128:(dml + 1) * 128], gview(ti, dc), dc == 0, dc == 15,
                           gtiles(ti, dc) + (wtile[wi],), (bank[bk],))
                    out_stage_evac(bk, dmc, ti, c0, n, O_A)
        flush_stat()
        postnorm(layer, "A", O_A, TILES_A)

    def layer_b(layer, ps):
        j = layer // 2
        wb0 = j * NBLK_PER_J + 16
        prenorm(layer, "B")
        for nm_, t_ in list(T.items()):
            if nm_.startswith(("xs5_", "Yd_", "Xd_", "yf_", "y_")) and not nm_.startswith("y_out"):
                if t_.w is not None and t_.w.chan is not None:
                    t_.w = None
                for k_ in [k_ for k_ in t_.r if isinstance(k_, int)]:
                    del t_.r[k_]
        tsx = tl("s5small")
        dma("sp", h0t[:], h0_d[j, ps], (), (tsx,), par)

        def c3(v64):
            return v64.unsqueeze(1).broadcast_to([128, 8, 64])

        def c3h(v32):
            return v32.unsqueeze(1).broadcast_to([128, 8, 32])

        def cstep(dst, src, tr, tp_, tn_, addend, eng="dve"):
            sv = src.rearrange("p q (g r) -> p q g r", r=2)
            t2v = st2[:].rearrange("p q (g r) -> p q g r", r=2)
            tt(eng, st1[:], src, c3(tr), ALU.mult, (tsx, cst), (tsx,))
            tt(eng, t2v[:, :, :, 0], sv[:, :, :, 1], c3h(tn_), ALU.mult, (tsx, cst), (tsx,))
            tt(eng, t2v[:, :, :, 1], sv[:, :, :, 0], c3h(tp_), ALU.mult, (tsx, cst), (tsx,))
            tt(eng, st1[:], st1[:], st2[:], ALU.add, (tsx,), (tsx,))
            if addend is None:
                cp(eng, dst, st1[:], (tsx,), (tsx,))
            else:
                tt(eng, dst, st1[:], addend, ALU.add, (tsx, tl("ub_a"), tl("ub_d")), (tsx,))

        cstep(h0p[:], h0t[:], M4r[:, j, :], M4p[:, j, :], M4n[:, j, :], None)

        Xdv = Xd.rearrange("(g k) (s c) -> s k g c", k=16, c=CB)

        def xreadback(f_):
            for s8 in range(8):
                dma("sp", Xs5[16 * s8:16 * s8 + 16, 8 * f_:8 * f_ + 8, :],
                    Xdv[s8][:, 8 * f_:8 * f_ + 8, :],
                    (tl("Xd_%d" % f_),), (tl("xs5_%d_%d" % (s8, f_)),) + (OT_ALL if (s8 == 0 and f_ == 0) else ()),
                    ch_xr[f_][s8 % 2])

        wl = [wload(wb0 + 0), wload(wb0 + 1)]
        for fc in range(8):
            wi = wl[fc // 4]
            wv = WB[wi][:, :].rearrange("p (a b) -> p a b", b=512)
            dcl = fc % 4
            xi = nxt("xs", 2)
            for ti, (c0, n) in enumerate(TILES_B):
                bk = mbank()
                for kc in range(8):
                    mm(PS[:, bk, 0:n], wv[:, kc, dcl * 128:(dcl + 1) * 128], Hn[:, kc, c0:c0 + n],
                       kc == 0, kc == 7, (tl("hn_%d_%d" % (kc, ti)), wtile[wi]), (bank[bk],))
                cp("act", xs[xi][:, c0:c0 + n], PS[:, bk, 0:n], (bank[bk],), (xs_t[xi],))
            dma("sp", Xd[fc * 128:(fc + 1) * 128, :], xs[xi][:], (xs_t[xi],), (tl("Xd_%d" % fc),), ch_xs[xi])
            if fc >= 1:
                xreadback(fc - 1)
        xreadback(7)
        cp("act", Hb[:, :, :, 128:136].rearrange("p g r c -> p (g r) c"),
           h0p[:].rearrange("p q g -> p g q"), (tsx,), (tl("hb_5"),))
        ub = tl("ub_all")
        UBQ = [tl("ub_q%d" % q_) for q_ in range(4)]
        UBS = tl("ub_s")
        batches1 = [(g0_, min(3, 32 - g0_)) for g0_ in range(0, 32, 3)]

        def load1(b_):
            g0_, n_ = batches1[b_]
            return rbload(lambda t_, n_=n_: t_[:, 0:n_ * 512].rearrange("p (a m) -> p a m", a=n_),
                          SW[j, g0_:g0_ + n_].rearrange("a p m -> p a m"), tl("SW"))

        pend = [load1(0), load1(1)]
        for b_, (g0_, n_) in enumerate(batches1):
            if b_ + 2 < len(batches1):
                pend.append(load1(b_ + 2))
            si = pend[b_]
            for a_ in range(n_):
                gp = g0_ + a_
                wv = RB[si][:, a_ * 512:(a_ + 1) * 512].rearrange("p (x r m) -> p x r m", x=2, r=2)
                bk = mbank()
                pu = PS[:, bk, 0:272].rearrange("p (r c) -> p r c", r=2)
                for ri in range(2):
                    for x in range(2):
                        mm(pu[:, ri, :], wv[:, x, ri, :], Xs5[:, 2 * gp + x, :], x == 0, x == 1,
                           (RBT[si],) + XS5_F[gp // 4], (bank[bk],))
                cp("act" if gp % 2 == 0 else "dve", Ub[:, :, 2 * gp:2 * gp + 2].rearrange("p c r -> p r c"), pu,
                   (bank[bk],), (tl("ub_a" if gp % 2 == 0 else "ub_d"),))
        wl = [wload(wb0 + 2), wload(wb0 + 3)]
        for fc in range(8):
            wi = wl[fc // 4]
            wv = WB[wi][:, :].rearrange("p (a b) -> p a b", b=512)
            dcl = fc % 4
            for ti, (c0, n) in enumerate(TILES_B):
                bk = mbank()
                for kc in range(8):
                    mm(PS[:, bk, 0:n], wv[:, kc, dcl * 128:(dcl + 1) * 128], Hn[:, kc, c0:c0 + n],
                       kc == 0, kc == 7, (tl("hn_%d_%d" % (kc, ti)), wtile[wi]), (bank[bk],))
                act(Gb[:, fc, c0:c0 + n], PS[:, bk, 0:n], AF.Silu, (bank[bk],), (tl("gb_%d_%d" % (fc, ti)),))
        hin = Hin0[:] if ps == 0 else Hcar[:, j, :]
        tr = AR2[:, j, :]
        tpp = AIp[:, j, :]
        tnn = AIn[:, j, :]
        sc1 = st1[:, 0, :]
        sc2 = st2[:, 0, :].rearrange("p (g r) -> p g r", r=2)
        scn = tl("scan")
        hbv = Hb.rearrange("p g r c -> p c g r")
        cp("act", hbv[:, 0, :, :], hin.rearrange("p (g r) -> p g r", r=2), (cst, tsx, tl("hcar")), (tl("hb_0"),))
        for c in range(128):
            prev = hin if c == 0 else Ub[:, c - 1, :]
            pv = prev.rearrange("p (g r) -> p g r", r=2)
            uq = UBQ[c // 32]
            rd = (uq, UBQ[max(c - 1, 0) // 32], scn, cst, tsx, tl("hcar"), tl("ub_a"), tl("ub_d"))
            ns = SCAN_NOSYNC and c > 0
            tt("dve", sc1, prev, tr, ALU.mult, rd, (scn,), nosync=ns)
            tt("dve", sc2, pv[:, :, ::-1], AI2[:, j, :].rearrange("p (g r) -> p g r", r=2), ALU.mult, rd, (scn,), nosync=ns)
            tt("dve", Ub[:, c, :], Ub[:, c, :], sc1, ALU.add, rd, (uq,), nosync=ns)
            tt("dve", Ub[:, c, :], Ub[:, c, :], st2[:, 0, :], ALU.add, rd, (uq,), nosync=ns)
            if c % 32 == 31:
                q4 = c // 32
                ln_ = 32 if q4 < 3 else 31
                cp("act", Hb[:, :, :, 1 + 32 * q4:1 + 32 * q4 + ln_].rearrange("p g r c -> p (g r) c"),
                   Ub[:, 32 * q4:32 * q4 + ln_, :].rearrange("p c g -> p g c"), (uq,), (tl("hb_%d" % (q4 + 1)),))
        cp("dve", Hcar[:, j, :], Ub[:, 127, :], (UBQ[3],), (tl("hcar"),))
        if ps == npass - 1:
            dma("sp", stp[j], Hcar[:, j, :], (tl("hcar"),), (tl("stp_out_%d" % j),), ch_out)
        wb1 = j * NBLK_PER_J + 20
        wl_glu1 = [wload(wb1 + 0), wload(wb1 + 1)]
        def load3(b_):
            return rbload(lambda t_: t_[:, :].rearrange("p (a m) -> p a m", a=4),
                          SMV[j, 4 * b_:4 * b_ + 4].rearrange("a p m -> p a m"), tl("SMV"))

        pend = [load3(0), load3(1)]
        for g in range(64):
            b_ = g // 4
            if g % 4 == 0 and b_ + 2 < 16:
                pend.append(load3(b_ + 2))
            si = pend[b_]
            mv_ = RB[si][:, (g % 4) * 384:(g % 4) * 384 + 384]
            bk = mbank()
            py = PS[:, bk, 0:CB]
            mm(py, mv_[:, 0:128], Xs5[:, g, :], True, False, (RBT[si], tl("y_%d" % g)) + XS5_F[g // 8],
               (bank[bk],))
            for ri in range(2):
                mm(py, mv_[:, 128 + 128 * ri:256 + 128 * ri], Hb[:, g // 2, ri, :], False, ri == 1,
                   (RBT[si],) + tuple(tl("hb_%d" % q) for q in range(6)), (bank[bk],))
            act(Xs5[:, g, :], py, AF.Gelu_apprx_tanh, (bank[bk],), (tl("y_%d" % g),))
            if g % 8 == 7:
                fc = g // 8
                Ydv = Yd.rearrange("(g k) (t c) -> t k g c", k=16, c=CB)
                ys = tuple(tl("y_%d" % g_) for g_ in range(8 * fc, 8 * fc + 8))
                for t8 in range(8):
                    dma("sp", Ydv[t8][:, 8 * fc:8 * fc + 8, :], Xs5[16 * t8:16 * t8 + 16, 8 * fc:8 * fc + 8, :],
                        ys + XS5_F[fc], (tl("Yd_%d_%d" % (fc, t8)),), ch_yw[fc % 2])
                for fr in ([fc - 1] if fc >= 1 else []) + ([7] if fc == 7 else []):
                    dma("sp", yF[:, fr, :], Yd[fr * 128:(fr + 1) * 128, :],
                        tuple(tl("Yd_%d_%d" % (fr, t_)) for t_ in range(8)),
                        (tl("yf_%d" % fr),) + tuple(tl("hn_%d_%d" % (fr, t)) for t in range(3)), ch_scr3)
        cstep(hsf[:], h0p[:], tr, tpp, tnn, Ub[:, 128:136, :])
        dma("sp", sts[j, ps], hsf[:], (tsx,), (tl("sts_out"),), ch_out)
        for which in range(2):
            wl = wl_glu1 if which == 0 else [wload(wb1 + 2), wload(wb1 + 3)]
            for fc in range(8):
                wi = wl[fc // 4]
                wv = WB[wi][:, :].rearrange("p (a b) -> p a b", b=512)
                dcl = fc % 4
                for ti, (c0, n) in enumerate(TILES_B):
                    bk = mbank()
                    for kc in range(8):
                        mm(PS[:, bk, 0:n], wv[:, kc, dcl * 128:(dcl + 1) * 128], yF[:, kc, c0:c0 + n],
                           kc == 0, kc == 7, (tl("yf_%d" % kc), wtile[wi]), (bank[bk],))
                    gt = tl("gb_%d_%d" % (fc, ti))
                    if which == 0:
                        stt(Gb[:, fc, c0:c0 + n], PS[:, bk, 0:n], bglu[:, j, 0, fc:fc + 1], Gb[:, fc, c0:c0 + n],
                            ALU.add, ALU.mult, (bank[bk], gt, cst), (gt,))
                    else:
                        i = nxt("tmpb", 2)
                        act(tmpb[i][:, 0:n], PS[:, bk, 0:n], AF.Sigmoid, (bank[bk], cst), (tmpb_t[i],),
                            bias=bglu[:, j, 1, fc:fc + 1], scale=1.0)
                        tt("dve", Gb[:, fc, c0:c0 + n], tmpb[i][:, 0:n], Gb[:, fc, c0:c0 + n], ALU.mult,
                           (tmpb_t[i], gt), (gt,))
        wb2 = j * NBLK_PER_J + 24
        wl = [wload(wb2), wload(wb2 + 1)]
        for dmc in range(8):
            wi = wl[dmc // 4]
            wv = WB[wi][:, :].rearrange("p (a b) -> p a b", b=512)
            dcl = dmc % 4
            for ti, (c0, n) in enumerate(TILES_B):
                bk = mbank()
                for kc in range(8):
                    mm(PS[:, bk, 0:n], wv[:, kc, dcl * 128:(dcl + 1) * 128], Gb[:, kc, c0:c0 + n],
                       kc == 0, kc == 7, (tl("gb_%d_%d" % (kc, ti)), wtile[wi]), (bank[bk],))
                out_stage_evac(bk, dmc, ti, c0, n, O_B)
        flush_stat()
        postnorm(layer, "B", O_B, TILES_B)

    for ps in range(npass):
        for kc in range(8):
            dma("sp", Xsb[:, kc, :], xT[ps, :, kc, :], (),
                tuple(tl("x_%d_%d" % (kc, ti)) for ti in range(3)), ch_x[ps % 2],
                extra=(bar_ops if ps == 0 else ()))
        for layer, kind in enumerate(layers):
            if kind == "A":
                layer_a(layer, ps)
            elif kind == "B":
                layer_b(layer, ps)
        for kc in range(8):
            dma("sp", yT[ps, :, kc, :], Xsb[:, kc, :], tuple(tl("x_%d_%d" % (kc, ti)) for ti in range(3)),
                (tl("y_out_%d" % kc),), ch_out)
    print('sbuf bytes remaining', nc.sbuf_bytes_remaining)
    if max_ops is not None:
        print('total ops', len(P.ops))
        P.ops = P.ops[:max_ops]
    P.emit()
    return nc


def _blk(w, nb, kc, n):
    return np.ascontiguousarray(w.reshape(kc, 128, nb, n).transpose(2, 1, 0, 3)).reshape(nb, 128, kc * n)


def _consts():
    ident = np.eye(128, dtype=np.float32)
    s = np.arange(128)
    trimask = (s[:, None] <= s[None, :]).astype(np.float32)
    q = np.arange(32)
    bdmask = ((q[:, None] // 4 == q[None, :] // 4) & (q[:, None] % 4 <= q[None, :] % 4)).astype(np.float32)
    tmask = ((s[:, None] // 16) <= (s[None, :] // 16)).astype(np.float32)
    colmask = np.zeros((128, 2, 128), np.float32)
    colmask[:, 0, :64] = 1
    colmask[:, 1, 64:] = 1
    rowmask = np.zeros((128, 2), np.float32)
    rowmask[:64, 0] = 1
    rowmask[64:, 1] = 1
    return dict(ident=ident, trimask=trimask, bdmask=bdmask, tmask=tmask, colmask=colmask, rowmask=rowmask)


def _shared_inputs(inp):
    f = lambda a: np.ascontiguousarray(np.asarray(a, dtype=np.float32))
    blocks = []
    for j in range(2):
        blocks.append(_blk(f(inp["w_in_a"][j]), 12, 8, 512))
        blocks.append(_blk(f(inp["w_out_a"][j]), 4, 16, 256))
        blocks.append(_blk(f(inp["w_in_b"][j]), 4, 8, 512))
        blocks.append(_blk(f(inp["w_glu1"][j]), 2, 8, 512))
        blocks.append(_blk(f(inp["w_glu2"][j]), 2, 8, 512))
        blocks.append(_blk(f(inp["w_out_b"][j]), 2, 8, 512))
    d = dict(wall=np.concatenate(blocks, axis=0))
    col = lambda v, n: np.ascontiguousarray(f(v).reshape(n, 128).T)
    d["gpre"] = np.concatenate([col(inp["norm_pre"][l], 8) for l in range(4)], axis=1)
    d["gpost"] = np.concatenate([col(inp["norm_post"][l], 8) for l in range(4)], axis=1)
    d["lncol"] = np.ascontiguousarray(np.stack(
        [np.stack([col(inp["ln_v_g"][j], 16), col(inp["ln_v_b"][j], 16)], axis=1) for j in range(2)], axis=1))
    d["lnbc"] = np.ascontiguousarray(np.stack(
        [np.stack([np.broadcast_to(f(inp["ln_v_g"][j])[None, :], (32, 2048)),
                   np.broadcast_to(f(inp["ln_v_b"][j])[None, :], (32, 2048))]) for j in range(2)]))
    ws = f(inp["w_s"])
    d["wsT"] = np.ascontiguousarray(ws.transpose(0, 3, 1, 2))
    corner = ws[:, :, :4, :4].transpose(0, 3, 1, 2)
    d["wsrep"] = np.ascontiguousarray(np.tile(corner, (1, 8, 1, 8)))
    d["bsB"] = np.ascontiguousarray(np.broadcast_to(f(inp["b_s"])[:, None, :, :], (2, 128, 8, 128)))
    d["bglu"] = np.ascontiguousarray(np.stack(
        [np.stack([col(inp["b_glu1"][j], 8), col(inp["b_glu2"][j], 8)], axis=1) for j in range(2)], axis=1))

    def l2(a):
        return a.reshape(32, 2, 64).transpose(1, 2, 0).reshape(128, 32)

    aL2 = np.zeros((2, 128, 3, 32), np.float32)
    bL2 = np.zeros((2, 128, 2, 32, 16), np.float32)
    cL2 = np.zeros((2, 128, 2, 32, 16), np.float32)
    dcol = np.zeros((2, 128, 64), np.float32)
    for j in range(2):
        aL2[j, :, 0] = l2(f(inp["a_re"][j]))
        aL2[j, :, 1] = l2(f(inp["a_im"][j]))
        aL2[j, :, 2] = l2(np.broadcast_to(f(inp["log_dt"][j])[:, None], (64, 64)))
        for r, nm in enumerate(("b_re", "b_im")):
            b = f(inp[nm][j])
            bL2[j, :, r] = b.reshape(32, 2, 64, 16).transpose(1, 2, 0, 3).reshape(128, 32, 16)
        for r, nm in enumerate(("c_re", "c_im")):
            c = f(inp[nm][j])
            cL2[j, :, r] = c.reshape(32, 2, 16, 64).transpose(1, 3, 0, 2).reshape(128, 32, 16)
        dk = f(inp["d_skip"][j]).reshape(64, 16)
        dcol[j] = np.tile(dk.T, (8, 1))
    d.update(aL2=aL2, bL2=bL2, cL2=cL2, dcol=dcol)
    d.update(_consts())
    return d


def _core_inputs(inp, cid):
    f = lambda a: np.asarray(a, dtype=np.float32)
    xp = f(inp["x_prompt"][cid])
    xsm = f(inp["x_sample"][16 * cid:16 * cid + 16])
    xT = np.zeros((2, 128, 8, NTA), np.float32)
    for ps in range(2):
        cols = np.concatenate([xp[1024 * ps:1024 * ps + 1024], xsm[8 * ps:8 * ps + 8].reshape(32, 1024)], axis=0)
        xT[ps] = cols.T.reshape(8, 128, NTA).transpose(1, 0, 2)
    h0 = np.zeros((2, 2, 128, 8, 64), np.float32)
    for j in range(2):
        for ps in range(2):
            sl = slice(16 * cid + 8 * ps, 16 * cid + 8 * ps + 8)
            re = f(inp["state_ssm_re"][j, sl])
            im = f(inp["state_ssm_im"][j, sl])
            st = np.stack([re, im], axis=-1)
            st = st.reshape(8, 32, 2, 64, 2).transpose(2, 3, 0, 1, 4)
            h0[j, ps] = st.reshape(128, 8, 64)
    return dict(xT=xT, h0L2=h0)


_NC_CACHE = {}


def kernel(**inputs):
    inp = {k: np.asarray(v) for k, v in inputs.items()}
    shared = _shared_inputs(inp)
    in_maps = []
    for cid in range(NCORES):
        m = dict(shared)
        m.update(_core_inputs(inp, cid))
        in_maps.append(m)
    if "nc" not in _NC_CACHE:
        _NC_CACHE["nc"] = build_program()
    nc = _NC_CACHE["nc"]
    res = run_bass_kernel_spmd(nc, in_maps, core_ids=list(range(NCORES)))
    y_prompt = np.zeros((8, 2048, 1024), np.float32)
    y_sample = np.zeros((128, 4, 1024), np.float32)
    cvs = np.zeros((2, 128, 4, 2048), np.float32)
    srp = np.zeros((2, 8, 64, 64), np.float32)
    sip = np.zeros((2, 8, 64, 64), np.float32)
    srs = np.zeros((2, 128, 64, 64), np.float32)
    sis = np.zeros((2, 128, 64, 64), np.float32)
    for cid in range(NCORES):
        r = res.results[cid]
        yT = np.asarray(r["yT"])
        for ps in range(2):
            cols = yT[ps].transpose(2, 1, 0).reshape(NTA, 1024)
            y_prompt[cid, 1024 * ps:1024 * ps + 1024] = cols[:1024]
            y_sample[16 * cid + 8 * ps:16 * cid + 8 * ps + 8] = cols[1024:].reshape(8, 4, 1024)
        cvv = np.asarray(r["cv"])
        stpv = np.asarray(r["stp"])
        stsv = np.asarray(r["sts"])
        for j in range(2):
            for ps in range(2):
                cvs[j, 16 * cid + 8 * ps:16 * cid + 8 * ps + 8] = cvv[j, ps].reshape(8, 4, 2048)
                s = stsv[j, ps].reshape(2, 64, 8, 32, 2).transpose(2, 3, 0, 1, 4).reshape(8, 64, 64, 2)
                srs[j, 16 * cid + 8 * ps:16 * cid + 8 * ps + 8] = s[..., 0]
                sis[j, 16 * cid + 8 * ps:16 * cid + 8 * ps + 8] = s[..., 1]
            s = stpv[j].reshape(2, 64, 32, 2).transpose(2, 0, 1, 3).reshape(64, 64, 2)
            srp[j, cid] = s[..., 0]
            sip[j, cid] = s[..., 1]
    return (y_prompt, y_sample, cvs, srp, sip, srs, sis)
```

```python
import os
import numpy as np
import concourse.bass as bass
import concourse.mybir as mybir
from concourse.bass_utils import run_bass_kernel_spmd

F32 = mybir.dt.float32
BF16 = mybir.dt.bfloat16
I32 = mybir.dt.int32
AF = mybir.ActivationFunctionType
ALU = mybir.AluOpType

NCORES = 8
D = 1024
NTA = 1056
NTB = 1088
CB = 136
TILES_A = [(0, 512), (512, 512), (1024, 32)]
TILES_B = [(0, 512), (512, 512), (1024, 64)]
EPS = 1e-6
NBLK_PER_J = 26
SCAN_NOSYNC = os.environ.get("SCAN_SYNC", "0") != "1"
BUILD_NOSYNC = SCAN_NOSYNC


class Tl:
    __slots__ = ("name", "w", "r", "excl")

    def __init__(self, name, excl=False):
        self.name = name
        self.w = None
        self.r = {}
        self.excl = excl


class Chan:
    def __init__(self, nc, name):
        self.sem = nc.alloc_semaphore(name)
        self.cnt = 0
        self.name = name
        self.pending = 0
        self.selfw = 0


class Op:
    __slots__ = ("eng", "fn", "chan", "signal", "deps", "semval", "val", "final")


class Prog:
    CE = ("pe", "act", "dve", "pool")

    def __init__(self, nc):
        self.nc = nc
        self.ops = []
        self.sem = {e: nc.alloc_semaphore("s_" + e) for e in self.CE}
        self.chans = []
        self.last = {}

    def chan(self, name):
        c = Chan(self.nc, name)
        self.chans.append(c)
        return c

    def add(self, eng, fn, reads=(), writes=(), chan=None, extra=(), nosync=False):
        op = Op()
        op.eng = eng
        op.fn = fn
        op.chan = chan
        op.signal = False
        op.final = False
        xw = tuple(t for t in reads if t.excl)
        if xw:
            writes = tuple(writes) + xw
        deps = {}
        for t in reads:
            if t.w is not None:
                deps[id(t.w)] = t.w
        for t in writes:
            if t.w is not None:
                deps[id(t.w)] = t.w
            for r in t.r.values():
                deps[id(r)] = r
        for d in extra:
            deps[id(d)] = d
        for t in writes:
            t.w = op
            t.r = {}
        for t in reads:
            key = id(chan) if chan is not None else eng
            t.r[key] = op
        dl = []
        for d in deps.values():
            if d is op:
                continue
            if d.chan is not None:
                dl.append(("dma", d.chan, d.chan.cnt * 16))
                d.chan.pending = max(d.chan.pending, d.chan.cnt * 16)
            else:
                if d.eng == eng and chan is None and (eng == "pe" or nosync):
                    continue
                d.signal = True
                dl.append(("cmp", d))
        if chan is not None and chan.pending > chan.selfw:
            dl.append(("dma", chan, chan.pending))
            chan.selfw = chan.pending
        op.deps = dl
        if chan is not None:
            chan.cnt += 1
            op.val = chan.cnt * 16
        self.ops.append(op)
        self.last[eng if chan is None else ("q", eng)] = op
        return op

    def barrier(self, tiny):
        lasts = [self.last[e] for e in self.CE if e in self.last]
        dmas = [self.last[k] for k in self.last if isinstance(k, tuple)]
        out = []
        for e in ("act", "dve", "pool"):
            out.append(self.add(e, tiny[e], extra=lasts + dmas))
        return out + dmas

    def emit(self):
        cnt = {e: 0 for e in self.CE}
        for op in self.ops:
            if op.chan is None and op.signal:
                cnt[op.eng] += 1
                op.semval = cnt[op.eng]
        nc = self.nc
        by = {e: [o for o in self.ops if o.eng == e] for e in ("pe", "act", "dve", "pool", "sp")}
        sems = self.sem
        chans = self.chans

        def run(e, lst, is_sp=False):
            waited = {}
            for op in lst:
                need = {}
                for d in op.deps:
                    if d[0] == "dma":
                        sem, val = d[1].sem, d[2]
                    else:
                        sem, val = sems[d[1].eng], d[1].semval
                    k = id(sem)
                    if k not in need or need[k][1] < val:
                        need[k] = (sem, val)
                for k, (sem, val) in need.items():
                    if waited.get(k, 0) >= val:
                        continue
                    e.wait_ge(sem, val)
                    waited[k] = val
                ins = op.fn(e)
                if op.chan is not None:
                    ins.then_inc(op.chan.sem, 16)
                elif op.signal:
                    ins.then_inc(sems[op.eng], 1)
            if is_sp:
                for c in chans:
                    n_em = sum(1 for o in self.ops if o.chan is c)
                    if n_em > 0:
                        e.wait_ge(c.sem, n_em * 16)

        with nc.Block() as block:
            @block.tensor
            def _(e):
                run(e, by["pe"])

            @block.scalar
            def _(e):
                run(e, by["act"])

            @block.vector
            def _(e):
                run(e, by["dve"])

            @block.gpsimd
            def _(e):
                run(e, by["pool"])

            @block.sync
            def _(e):
                run(e, by["sp"], True)


def build_program(layers=("A", "B", "A", "B"), npass=2, max_ops=None):
    nc = bass.Bass("TRN2", target_bir_lowering=False)
    P = Prog(nc)

    def din(name, shape, dt=F32):
        return nc.dram_tensor(name, list(shape), dt, kind="ExternalInput").ap()

    def dout(name, shape, dt=F32):
        return nc.dram_tensor(name, list(shape), dt, kind="ExternalOutput").ap()

    xT = din("xT", [2, 128, 8, NTA])
    WALL = din("wall", [2 * NBLK_PER_J, 128, 4096])
    gpre_d = din("gpre", [128, 32])
    gpost_d = din("gpost", [128, 32])
    lncol_d = din("lncol", [128, 2, 2, 16])
    lnbc_d = din("lnbc", [2, 2, 32, 2048])
    wsT_d = din("wsT", [2, 128, 8, 128])
    wsrep_d = din("wsrep", [2, 32, 8, 32])
    bsB_d = din("bsB", [2, 128, 8, 128])
    bglu_d = din("bglu", [128, 2, 2, 8])
    aL2_d = din("aL2", [2, 128, 3, 32])
    bL2_d = din("bL2", [2, 128, 2, 32, 16])
    cL2_d = din("cL2", [2, 128, 2, 32, 16])
    dcol_d = din("dcol", [2, 128, 64])
    h0_d = din("h0L2", [2, 2, 128, 8, 64])
    ident_d = din("ident", [128, 128])
    trim_d = din("trimask", [128, 128])
    bdm_d = din("bdmask", [32, 32])
    tmask_d = din("tmask", [128, 128])
    colm_d = din("colmask", [128, 2, 128])
    rowm_d = din("rowmask", [128, 2])

    yT = dout("yT", [2, 128, 8, NTA])
    cv = dout("cv", [2, 2, 32, 2048])
    stp = dout("stp", [2, 128, 64])
    sts = dout("sts", [2, 2, 128, 8, 64])

    SW = nc.dram_tensor("SW", [2, 32, 128, 512], BF16).ap()
    SMV = nc.dram_tensor("SMV", [2, 64, 128, 384], BF16).ap()
    Xd = nc.dram_tensor("Xd", [1024, NTB], BF16).ap()
    Yd = nc.dram_tensor("Yd", [1024, NTB], BF16).ap()

    def sb(name, shape, dt=F32):
        return nc.alloc_sbuf_tensor("sb_" + name, list(shape), dt)

    A0 = sb("A0", [128, 8, NTA])
    A1 = sb("A1", [128, 8, NTB], BF16)
    A2 = sb("A2", [128, 18432], BF16)
    A3 = sb("A3", [128, 17408], BF16)
    A5 = sb("A5", [128, 8, NTB], BF16)
    NWB = 2
    WB = [sb("wb%d" % i, [128, 4096], BF16) for i in range(NWB)]
    NRB = 3
    RB = [sb("rb%d" % i, [128, 1536], BF16) for i in range(NRB)]
    ident = sb("ident", [128, 128])
    ones_bf = sb("ones_bf", [128, 128], BF16)
    ones_f = sb("ones_f", [128, 128])
    trim = sb("trim", [128, 128])
    bdm = sb("bdm", [128, 32])
    tmask = sb("tmask", [128, 128])
    colm = sb("colm", [128, 2, 128])
    rowm = sb("rowm", [128, 2])
    gpre = sb("gpre", [128, 32])
    gpost = sb("gpost", [128, 32])
    lncol = sb("lncol", [128, 2, 2, 16])
    bglu = sb("bglu", [128, 2, 2, 8])
    epsc = sb("epsc", [128, 1])
    hpic = sb("hpic", [128, 1])
    zero1 = sb("zero1", [128, 1])
    ctab = sb("ctab", [128, 16, 128])
    wsTm = sb("wsTm", [128, 8, 128], BF16)
    bdw = sb("bdw", [128, 8, 32], BF16)
    rst = [sb("rst%d" % i, [128, 512]) for i in range(2)]
    sqr = [sb("sq%d" % i, [128, 512], BF16) for i in range(2)]
    tmpb = [sb("tmpb%d" % i, [128, 512], BF16) for i in range(2)]
    xs = [sb("xs%d" % i, [128, NTB], BF16) for i in range(2)]
    stats = sb("stats", [128, 9, 4, 6])
    mv = sb("mv", [128, 9, 2])
    rstdv = sb("rstdv", [128, 9])
    nmr = sb("nmr", [128, 9])
    AR2 = sb("AR2", [128, 2, 64])
    AIp = sb("AIp", [128, 2, 32])
    AIn = sb("AIn", [128, 2, 32])
    AI2 = sb("AI2", [128, 2, 64])
    M4r = sb("M4r", [128, 2, 64])
    M4p = sb("M4p", [128, 2, 32])
    M4n = sb("M4n", [128, 2, 32])
    Hcar = sb("Hcar", [128, 2, 64])
    Hin0 = sb("Hin0", [128, 64])
    h0t = sb("h0t", [128, 8, 64])
    h0p = sb("h0p", [128, 8, 64])
    hsf = sb("hsf", [128, 8, 64])
    st1 = sb("st1", [128, 8, 64])
    st2 = sb("st2", [128, 8, 64])
    barA = sb("barA", [128, 1])
    barV = sb("barV", [128, 1])
    barG = sb("barG", [128, 1])

    PS = nc.alloc_psum_tensor("PS", [128, 8, 512], F32)
    bank = [Tl("bank%d" % i, True) for i in range(8)]

    Xsb = A0
    Hn = A1
    G = A2[:, :].rearrange("p (n d t) -> p n d t", n=9, d=16)
    Gv = A2[:, :].rearrange("p (n f) -> p n f", n=9)
    Ub = A2[:, 0:17408].bitcast(F32).rearrange("p (c g) -> p c g", g=64)
    O_A = A3[:, 0:16896].bitcast(F32).rearrange("p (k c) -> p k c", k=8)
    O_B = A3[:, :].bitcast(F32).rearrange("p (k c) -> p k c", k=8)
    Xs5 = A3[:, 0:8704].rearrange("p (g c) -> p g c", g=64)
    Hb = A3[:, 8704:17408].rearrange("p (g r c) -> p g r c", g=32, r=2)
    yF = A1
    ltmp = O_A[:, 4, 0:1024].rearrange("p (h t) -> p h t", h=8)
    Gb = A5

    T = {}

    def tl(name):
        if name not in T:
            T[name] = Tl(name)
        return T[name]

    OT_ALL = tuple(tl("o_%d_%d" % (d_, t_)) for d_ in range(8) for t_ in range(3))
    RBT = [tl("rb_%d" % i_) for i_ in range(3)]
    XS5_F = [tuple(tl("xs5_%d_%d" % (s_, f_)) for s_ in range(8)) for f_ in range(8)]
    par = P.chan("par")
    ch_x = [P.chan("chx%d" % i) for i in range(2)]
    ch_out = P.chan("chout")
    ch_w = [P.chan("chw%d" % i) for i in range(NWB)]
    ch_rb = [P.chan("chrb_%d" % i) for i in range(NRB)]
    rbstate = {"n": 0}

    def rbload(dst_view_fn, src_ap, src_tile):
        i = rbstate["n"] % NRB
        rbstate["n"] += 1
        P.add("pool", lambda e, o=dst_view_fn(RB[i]), i_=src_ap: e.dma_start(out=o, in_=i_), (src_tile,),
              (RBT[i],), ch_rb[i])
        return i

    ch_scr = P.chan("chscr")
    ch_yw = [P.chan("chyw%d" % i_) for i_ in range(2)]
    ch_pp = [P.chan("chpp%d" % i_) for i_ in range(4)]
    ch_scr2 = P.chan("chscr2")
    ch_scr3 = P.chan("chscr3")
    ch_xr = [[P.chan("chxr%d_%d" % (f_, q_)) for q_ in range(2)] for f_ in range(8)]
    ch_prep = P.chan("chprep")
    ch_xs = [P.chan("chxs%d" % i) for i in range(2)]

    def dma(q, out, in_, reads, writes, chan, extra=()):
        return P.add(q, lambda e, o=out, i=in_: e.dma_start(out=o, in_=i), reads, writes, chan, extra)

    def mm(out, lhsT, rhs, start, stop, reads, writes):
        return P.add("pe", lambda e, o=out, l=lhsT, r=rhs, s=start, t=stop:
                     e.matmul(o, l, r, start=s, stop=t), reads, writes)

    def act(out, in_, func, reads, writes, bias=None, scale=None):
        def fn(e, o=out, i=in_, f=func, b=bias, s=scale):
            kw = {}
            if b is not None:
                kw["bias"] = b
            if s is not None:
                kw["scale"] = s
            return e.activation(out=o, in_=i, func=f, **kw)
        return P.add("act", fn, reads, writes)

    def tt(eng, out, in0, in1, op, reads, writes, nosync=False):
        return P.add(eng, lambda e, o=out, a=in0, b=in1, p=op: e.tensor_tensor(out=o, in0=a, in1=b, op=p),
                     reads, writes, nosync=nosync)

    def ts(eng, out, in0, s1, s2, op0, op1, reads, writes, nosync=False):
        def fn(e, o=out, a=in0, x=s1, y=s2, p0=op0, p1=op1):
            if p1 is None:
                return e.tensor_scalar(out=o, in0=a, scalar1=x, scalar2=None, op0=p0)
            return e.tensor_scalar(out=o, in0=a, scalar1=x, scalar2=y, op0=p0, op1=p1)
        return P.add(eng, fn, reads, writes)

    def stt(out, in0, scalar, in1, op0, op1, reads, writes):
        return P.add("dve", lambda e, o=out, a=in0, s=scalar, b=in1, p0=op0, p1=op1:
                     e.scalar_tensor_tensor(out=o, in0=a, scalar=s, in1=b, op0=p0, op1=p1), reads, writes)

    def cp(eng, out, in_, reads, writes):
        if eng == "act":
            return P.add("act", lambda e, o=out, i=in_: e.activation(out=o, in_=i, func=AF.Copy), reads, writes)
        return P.add(eng, lambda e, o=out, i=in_: e.tensor_copy(out=o, in_=i), reads, writes)

    def mset(eng, ap, val, writes):
        return P.add(eng, lambda e, a=ap, v=val: e.memset(a, v), (), writes)

    wstate = {"n": 0}
    wtile = [Tl("wb%d" % i) for i in range(NWB)]

    def wload(blk):
        i = wstate["n"] % NWB
        wstate["n"] += 1
        src = WALL[blk].rearrange("p (a b) -> p a b", b=512)
        dst = WB[i][:, :].rearrange("p (a b) -> p a b", b=512)
        dma("pool", dst, src, (), (wtile[i],), ch_w[i])
        return i

    cst = tl("const")
    for dst, src in ((ident, ident_d), (trim, trim_d), (tmask, tmask_d), (colm, colm_d),
                     (rowm, rowm_d), (gpre, gpre_d), (gpost, gpost_d), (lncol, lncol_d), (bglu, bglu_d)):
        dma("sp", dst[:], src, (), (cst,), par)
    dma("sp", bdm[64:96, :], bdm_d, (), (cst,), par)
    mset("pool", bdw[:], 0.0, (tl("tabA"),))
    mset("dve", ones_f[:], 1.0, (cst,))
    mset("dve", epsc[:], EPS, (cst,))
    mset("dve", hpic[:], float(np.pi / 2), (cst,))
    mset("dve", zero1[:], 0.0, (cst,))
    mset("dve", Hin0[:], 0.0, (cst,))
    mset("dve", stats[:], 0.0, (tl("stats"),))
    mset("dve", mv[:], 1.0, (tl("mv"),))
    mset("dve", barV[:], 0.0, (tl("barV"),))
    mset("pool", barG[:], 0.0, (tl("barG"),))
    cp("dve", ones_bf[:], ones_f[:], (cst,), (cst,))
    act(barA[:], zero1[:], AF.Copy, (cst,), (tl("barA"),))

    def prep_s5(j):
        base = A5[:, :, :].rearrange("p a b -> p (a b)")
        f = base.bitcast(F32)
        off = [0]

        def carve(n, shape=None):
            v = f[:, off[0]:off[0] + n]
            off[0] += n
            return v

        aL = carve(96).rearrange("p (a g) -> p a g", a=3)
        dt_ = carve(32)
        dar = carve(32)
        ang = carve(32)
        mag = carve(32)
        magi = carve(32)
        kf = carve(32)
        ki = carve(32).bitcast(I32)
        rr = carve(32)
        m1 = carve(32)
        sn = carve(32)
        cs = carve(32)
        ab = carve(32)
        t1 = carve(32)
        t2 = carve(32)
        t3 = carve(32)
        t4 = carve(32)
        nr = carve(32)
        den = carve(32)
        cfr = carve(32)
        cfi = carve(32)
        PW = carve(17 * 64).rearrange("p (n r g) -> p n r g", n=17, r=2)
        bL = carve(1024).rearrange("p (r g k) -> p r g k", r=2, g=32)
        cL = carve(1024).rearrange("p (r g k) -> p r g k", r=2, g=32)
        assert off[0] <= 4352
        f1 = A1[:, :, :].rearrange("p a b -> p (a b)").bitcast(F32)
        bbar = f1[:, 0:1024].rearrange("p (r g k) -> p r g k", r=2, g=32)
        big1 = f1[:, 1024:1536].rearrange("p (g k) -> p g k", g=32)
        dcl = f1[:, 2048:2112]
        tp = tl("prep_small")
        tpa, tpb, tpc, tpd = tl("prep_aL"), tl("prep_bL"), tl("prep_cL"), tl("prep_dcl")
        dma("sp", aL, aL2_d[j], (tp,), (tpa,), ch_pp[0])
        dma("sp", bL, bL2_d[j], (tp,), (tpb,), ch_pp[1])
        dma("sp", cL, cL2_d[j], (tp,), (tpc,), ch_pp[2])
        dma("sp", dcl, dcol_d[j], (tp,), (tpd,), ch_pp[3])
        R = (tp, cst, tpa, tpb, tpc, tpd)
        W_ = (tp,)
        ar, ai, ldt = aL[:, 0, :], aL[:, 1, :], aL[:, 2, :]
        act(dt_, ldt, AF.Exp, R, W_)
        tt("dve", dar, dt_, ar, ALU.mult, R, W_)
        tt("dve", ang, dt_, ai, ALU.mult, R, W_)
        act(mag, dar, AF.Exp, R, W_)
        act(magi, dar, AF.Exp, R, W_, scale=-1.0)
        ts("dve", kf, ang, float(1.0 / (2 * np.pi)), None, ALU.mult, None, R, W_)
        cp("dve", ki, kf, R, W_)
        cp("dve", kf, ki, R, W_)
        stt(rr, kf, float(-2 * np.pi), ang, ALU.mult, ALU.add, R, W_)
        ts("dve", m1, rr, float(np.pi), float(-2 * np.pi), ALU.is_gt, ALU.mult, R, W_)
        tt("dve", rr, rr, m1, ALU.add, R, W_)
        ts("dve", m1, rr, float(-np.pi), float(2 * np.pi), ALU.is_lt, ALU.mult, R, W_)
        tt("dve", rr, rr, m1, ALU.add, R, W_)
        act(sn, rr, AF.Sin, R, W_)
        act(ab, rr, AF.Abs, R, W_)
        act(cs, ab, AF.Sin, R, W_, bias=hpic[:], scale=-1.0)
        mset("dve", PW[:, 8, 0, :], 1.0, W_)
        mset("dve", PW[:, 8, 1, :], 0.0, W_)
        tt("dve", PW[:, 9, 0, :], mag, cs, ALU.mult, R, W_)
        tt("dve", PW[:, 9, 1, :], mag, sn, ALU.mult, R, W_)
        tt("dve", PW[:, 7, 0, :], magi, cs, ALU.mult, R, W_)
        stt(PW[:, 7, 1, :], magi, -1.0, sn, ALU.mult, ALU.mult, R, W_)

        wt = [carve(128), carve(128), carve(128), f1[:, 1600:1728]]

        def cmulw(o, a, b, w):
            T = [t_[:, 0:w * 32].rearrange("p (a g) -> p a g", a=w) for t_ in wt]
            a_r, a_i = a[:, :, 0, :], a[:, :, 1, :]
            b_r = b[:, :, 0, :].broadcast_to([128, w, 32])
            b_i = b[:, :, 1, :].broadcast_to([128, w, 32])
            tt("dve", T[0], a_r, b_r, ALU.mult, R, W_)
            tt("dve", T[1], a_i, b_i, ALU.mult, R, W_)
            tt("dve", T[2], a_r, b_i, ALU.mult, R, W_)
            tt("dve", T[3], a_i, b_r, ALU.mult, R, W_)
            tt("dve", o[:, :, 0, :], T[0], T[1], ALU.subtract, R, W_)
            tt("dve", o[:, :, 1, :], T[2], T[3], ALU.add, R, W_)

        cmulw(PW[:, 10:11], PW[:, 9:10], PW[:, 9:10], 1)
        cmulw(PW[:, 6:7], PW[:, 7:8], PW[:, 7:8], 1)
        cmulw(PW[:, 11:13], PW[:, 9:11], PW[:, 10:11], 2)
        cmulw(PW[:, 5:3:-1], PW[:, 7:5:-1], PW[:, 6:7], 2)
        cmulw(PW[:, 13:17], PW[:, 9:13], PW[:, 12:13], 4)
        cmulw(PW[:, 3::-1], PW[:, 7:3:-1], PW[:, 4:5], 4)
        for (src_n, tr, tp_, tn_) in ((16, AR2, AIp, AIn), (4, M4r, M4p, M4n)):
            trv = tr[:, j, :].rearrange("p (g r) -> p g r", r=2)
            cp("dve", trv[:, :, 0], PW[:, src_n, 0, :], R, (cst,))
            cp("dve", trv[:, :, 1], PW[:, src_n, 0, :], R, (cst,))
            cp("dve", tp_[:, j, :], PW[:, src_n, 1, :], R, (cst,))
            ts("dve", tn_[:, j, :], PW[:, src_n, 1, :], -1.0, None, ALU.mult, None, R, (cst,))
        ai2v = AI2[:, j, :].rearrange("p (g r) -> p g r", r=2)
        cp("dve", ai2v[:, :, 0], AIn[:, j, :], (cst,), (cst,))
        cp("dve", ai2v[:, :, 1], AIp[:, j, :], (cst,), (cst,))
        ts("dve", nr, PW[:, 9, 0, :], -1.0, None, ALU.add, None, R, W_)
        ni = PW[:, 9, 1, :]
        tt("dve", t1, ar, ar, ALU.mult, R, W_)
        tt("dve", t2, ai, ai, ALU.mult, R, W_)
        tt("dve", den, t1, t2, ALU.add, R, W_)
        P.add("dve", lambda e, o=den, i=den: e.reciprocal(out=o, in_=i), R, W_)
        tt("dve", t1, nr, ar, ALU.mult, R, W_)
        tt("dve", t2, ni, ai, ALU.mult, R, W_)
        tt("dve", t1, t1, t2, ALU.add, R, W_)
        tt("dve", cfr, t1, den, ALU.mult, R, W_)
        tt("dve", t1, ni, ar, ALU.mult, R, W_)
        tt("dve", t2, nr, ai, ALU.mult, R, W_)
        tt("dve", t1, t1, t2, ALU.subtract, R, W_)
        tt("dve", cfi, t1, den, ALU.mult, R, W_)

        def bc16(v):
            return v.unsqueeze(2).broadcast_to([128, 32, 16])

        tb = tl("prep_bbar")
        RB = (tp, cst, tb, tpa, tpb, tpc, tpd)
        WB_ = (tb,)
        tt("dve", bbar[:, 0], bc16(cfr), bL[:, 0], ALU.mult, RB, WB_)
        tt("dve", big1, bc16(cfi), bL[:, 1], ALU.mult, RB, WB_)
        tt("dve", bbar[:, 0], bbar[:, 0], big1, ALU.subtract, RB, WB_)
        tt("dve", bbar[:, 1], bc16(cfr), bL[:, 1], ALU.mult, RB, WB_)
        tt("dve", big1, bc16(cfi), bL[:, 0], ALU.mult, RB, WB_)
        tt("dve", bbar[:, 1], bbar[:, 1], big1, ALU.add, RB, WB_)

        PSM = f1[:, 1024:1568].rearrange("p (n g) -> p n g", n=17)
        tt("dve", PSM, PW[:, :, 0, :], PW[:, :, 1, :], ALU.add, (tp, tl("prep_bbar")), (tp, tl("prep_bbar")))
        for hf in range(2):
            g0 = 16 * hf
            a2f = A2[:, :].bitcast(F32)
            WcL = a2f[:, 0:4096].rearrange("p (g r s k) -> p g r s k", g=16, r=2, s=8)
            WnL = a2f[:, 4096:8192].rearrange("p (g r s k) -> p g r s k", g=16, r=2, s=8)
            a3f = A3[:, :].bitcast(F32)
            VL = a3f[:, 0:4096].rearrange("p (g r s k) -> p g r s k", g=16, r=2, s=8)
            tmps = {"dve": (a3f[:, 4096:4352].rearrange("p (g k) -> p g k", g=16),
                            a3f[:, 4352:4608].rearrange("p (g k) -> p g k", g=16)),
                    "pool": (a3f[:, 6656:6912].rearrange("p (g k) -> p g k", g=16),
                             a3f[:, 6912:7168].rearrange("p (g k) -> p g k", g=16))}
            VLm = a3f[:, 4608:6656].rearrange("p (g x r m) -> p g x r m", g=4, x=2, r=2)
            a0 = A0[:, :, :].rearrange("p a b -> p (a b)")
            Wst = a0[:, 0:4096].bitcast(BF16).rearrange("p (g x r m) -> p g x r m", g=16, x=2, r=2)
            Vst = a0[:, 4096:8192].bitcast(BF16).rearrange("p (g x r m) -> p g x r m", g=16, x=2, r=2)
            M0st = f1[:, 2176:4224].bitcast(BF16).rearrange("p (g m) -> p g m", g=32)
            tw = tl("prep_big")
            tstg = tl("prep_stg")

            def bcg(v):
                return v.unsqueeze(2).broadcast_to([128, 16, 16])

            sums = a3f[:, 7168:8192].rearrange("p (q g k) -> p q g k", q=4, g=16)
            tsm = tl("prep_sums")
            br = bbar[:, 0, g0:g0 + 16, :]
            bi = bbar[:, 1, g0:g0 + 16, :]
            crr = cL[:, 0, g0:g0 + 16, :]
            cii = cL[:, 1, g0:g0 + 16, :]
            tt("dve", sums[:, 0], br, bi, ALU.add, (tp, tb, tpb, tpc), (tsm,))
            tt("dve", sums[:, 1], bi, br, ALU.subtract, (tp, tb, tpb, tpc), (tsm,))
            tt("dve", sums[:, 2], crr, cii, ALU.add, (tp, tb, tpb, tpc), (tsm,))
            tt("dve", sums[:, 3], crr, cii, ALU.subtract, (tp, tb, tpb, tpc), (tsm,))

            def build(name, dst, n_of_s, src_r, xsum, xdif, kind):
                tiles = []
                for s_ in range(8):
                    n = n_of_s(s_) + 8
                    pr = bcg(PW[:, n, 0, g0:g0 + 16])
                    pi = bcg(PW[:, n, 1, g0:g0 + 16])
                    psm = bcg(PSM[:, n, g0:g0 + 16])
                    eng = "dve"
                    tA, tB = tmps[eng]
                    tmt = tl("prep_tmp_" + eng)
                    mt = tl("prep_%s_%d" % (name, s_))
                    tiles.append(mt)
                    RW = (tp, cst, tb, tsm, tmt, mt, tpb, tpc)
                    WW = (mt, tmt)
                    d0 = dst[:, :, 0, s_, :]
                    d1 = dst[:, :, 1, s_, :]
                    ns_ = BUILD_NOSYNC
                    tt(eng, d1, psm, src_r, ALU.mult, RW, WW, nosync=ns_)
                    tt(eng, tA, pi, xsum, ALU.mult, RW, WW, nosync=ns_)
                    tt(eng, d0, d1, tA, ALU.subtract, RW, WW, nosync=ns_)
                    tt(eng, tA, pr, xdif, ALU.mult, RW, WW, nosync=ns_)
                    if kind == "w":
                        tt(eng, d1, d1, tA, ALU.add, RW, WW, nosync=ns_)
                    else:
                        tt(eng, d1, tA, d1, ALU.subtract, RW, WW, nosync=ns_)
                return tuple(tiles)

            tWc = build("wc", WcL, lambda s_: 7 - s_, br, sums[:, 0], sums[:, 1], "w")
            tWn = build("wn", WnL, lambda s_: -(s_ + 1), br, sums[:, 0], sums[:, 1], "w")
            tV = build("v", VL, lambda s_: s_ + 1, crr, sums[:, 2], sums[:, 3], "v")
            tvst = tl("prep_vst")
            twst = tl("prep_wst")
            tm0 = tl("prep_m0st")
            twb = tl("prep_wnlb")
            for x in range(2):
                act(Vst[:, :, x].rearrange("p g r m -> p g (r m)"),
                    VL.rearrange("p g r s k -> p g (r s k)"), AF.Copy, tV + (cst,), (tvst,), scale=rowm[:, x:x + 1])
            WnLb = a3f[:, 4608:6656].bitcast(BF16).rearrange("p (g r m) -> p g r m", g=16, r=2)
            act(WnLb.rearrange("p g r m -> p (g r m)"), WnL.rearrange("p g r s k -> p (g r s k)"), AF.Copy,
                tWn, (twb,))
            for gl in range(16):
                for ri in range(2):
                    bk = (gl * 2 + ri) % 8
                    P.add("pe", lambda e, o=PS[:, bk, 0:128], i=WcL[:, gl, ri].rearrange("p s k -> p (s k)"):
                          e.transpose(o, i, ident[:]), tWc + (cst,), (bank[bk],))
                    tt("dve", Wst[:, gl, :, ri, :],
                       PS[:, bk, 0:128].unsqueeze(1).broadcast_to([128, 2, 128]), colm[:], ALU.mult,
                       (bank[bk], cst), (twst,))
            for gl in range(16):
                for x in range(2):
                    bk = (gl * 2 + x) % 8
                    gg = 2 * (g0 + gl) + x
                    for ri in range(2):
                        mm(PS[:, bk, 0:128], WnLb[:, gl, ri, :], Vst[:, gl, x, ri, :], ri == 0, ri == 1,
                           (twb, tvst), (bank[bk],))
                    tt("dve", tmpAB[hf][:], PS[:, bk, 0:128], tmask[:], ALU.mult, (bank[bk], cst, tl("tmpAB")),
                       (tl("tmpAB"),))
                    stt(M0st[:, gl * 2 + x, :], ident[:], dcl[:, gg:gg + 1], tmpAB[hf][:], ALU.mult, ALU.add,
                        (tp, tpd, cst, tl("tmpAB")), (tm0,))
            dma("sp", SW[j, g0:g0 + 16].rearrange("g p (x r m) -> p g x r m", x=2, r=2), Wst, (twst,),
                (tl("SW"),), ch_prep)
            dma("sp", SMV[j, 2 * g0:2 * g0 + 32, :, 0:128].rearrange("g p m -> p g m"), M0st, (tm0,),
                (tl("SMV"),), ch_prep)
            dma("sp", SMV[j, 2 * g0:2 * g0 + 32, :, 128:384].rearrange("(g x) p (r m) -> p g x r m", x=2, r=2),
                Vst, (tvst,), (tl("SMV"),), ch_prep)

    tmpAB = [sb("tmpAB%d" % i, [128, 128]) for i in range(2)]

    has_b = "B" in layers
    bar_ops = []
    if has_b:
        for j in range(2):
            prep_s5(j)
        tiny = {
            "act": lambda e: e.activation(out=barA[:], in_=zero1[:], func=AF.Copy),
            "dve": lambda e: e.memset(barV[:], 0.0),
            "pool": lambda e: e.memset(barG[:], 0.0),
        }
        bar_ops = P.barrier(tiny)

    mset("pool", A1[:, :, :], 0.0, tuple(tl("hn_%d_%d" % (kc_, t_)) for kc_ in range(8) for t_ in range(3)))

    ring = {"rst": 0, "sq": 0, "tmpb": 0, "mmb": 0, "xs": 0}
    rst_t = [Tl("rst%d" % i) for i in range(2)]
    sq_t = [Tl("sq%d" % i) for i in range(2)]
    tmpb_t = [Tl("tmpb%d" % i) for i in range(2)]
    xs_t = [Tl("xs%d" % i) for i in range(2)]
    STATB = [3, 4, 5]

    def nxt(name, n):
        i = ring[name] % n
        ring[name] += 1
        return i

    def rstd_from_bank(bk, n):
        i = nxt("rst", 2)
        act(rst[i][:, 0:n], PS[:, bk, 0:n], AF.Ln, (bank[bk], cst), (rst_t[i],), bias=epsc[:], scale=1.0 / D)
        act(rst[i][:, 0:n], rst[i][:, 0:n], AF.Exp, (rst_t[i],), (rst_t[i],), scale=-0.5)
        return i

    def xcols(name, kc, c0, n):
        return tl("%s_%d_%d" % (name, kc, c0))

    def prenorm(layer, kind):
        gcol = gpre[:, layer * 8:(layer + 1) * 8]
        if kind == "B":
            for kc in range(8):
                v = Hn[:, kc, :].rearrange("p (s c) -> p s c", c=CB)[:, 0:4, 128:136]
                mset("pool", v, 0.0, tuple(tl("hn_%d_%d" % (kc, t2)) for t2 in range(3)))
        for ti, (c0, n) in enumerate(TILES_A):
            bk = STATB[ti]
            for kc in range(8):
                i = nxt("sq", 2)
                act(sqr[i][:, 0:n], Xsb[:, kc, c0:c0 + n], AF.Square, (tl("x_%d_%d" % (kc, ti)),), (sq_t[i],))
                mm(PS[:, bk, 0:n], ones_bf[:], sqr[i][:, 0:n], kc == 0, kc == 7, (sq_t[i], cst), (bank[bk],))
            r = rstd_from_bank(bk, n)
            for kc in range(8):
                xin = Xsb[:, kc, c0:c0 + n]
                if kind == "A":
                    stt(Hn[:, kc, c0:c0 + n], xin, gcol[:, kc:kc + 1], rst[r][:, 0:n], ALU.mult, ALU.mult,
                        (tl("x_%d_%d" % (kc, ti)), rst_t[r], cst), (tl("hn_%d_%d" % (kc, ti)),))
                else:
                    hv = Hn[:, kc, :].rearrange("p (s c) -> p s c", c=CB)
                    if ti < 2:
                        ov = hv[:, :, 64 * ti:64 * ti + 64]
                        iv = xin.rearrange("p (c s) -> p s c", s=8)
                        rv = rst[r][:, 0:n].rearrange("p (c s) -> p s c", s=8)
                    else:
                        ov = hv[:, 4:8, 128:136]
                        iv = xin.rearrange("p (q i) -> p i q", i=4)
                        rv = rst[r][:, 0:n].rearrange("p (q i) -> p i q", i=4)
                    stt(ov, iv, gcol[:, kc:kc + 1], rv, ALU.mult, ALU.mult,
                        (tl("x_%d_%d" % (kc, ti)), rst_t[r], cst),
                        tuple(tl("hn_%d_%d" % (kc, t2)) for t2 in range(3)))

    def out_stage_evac(bk, dmc, ti, c0, n, Obuf):
        cp("act", Obuf[:, dmc, c0:c0 + n], PS[:, bk, 0:n], (bank[bk],), (tl("o_%d_%d" % (dmc, ti)),))
        i = nxt("sq", 2)
        act(sqr[i][:, 0:n], PS[:, bk, 0:n], AF.Square, (bank[bk],), (sq_t[i],))
        sb_ = STATB[ti]
        flush_stat()
        pend_stat.append((PS[:, sb_, 0:n], sqr[i][:, 0:n], dmc == 0, dmc == 7, (sq_t[i], cst), (bank[sb_],)))

    pend_stat = []

    def flush_stat():
        while pend_stat:
            o_, r_, st_, sp_, rd_, wr_ = pend_stat.pop(0)
            mm(o_, ones_bf[:], r_, st_, sp_, rd_, wr_)

    def postnorm(layer, kind, Obuf, tiles):
        gcol = gpost[:, layer * 8:(layer + 1) * 8]
        for ti, (c0, n) in enumerate(tiles):
            r = rstd_from_bank(STATB[ti], n)
            for dmc in range(8):
                stt(Obuf[:, dmc, c0:c0 + n], Obuf[:, dmc, c0:c0 + n], gcol[:, dmc:dmc + 1], rst[r][:, 0:n],
                    ALU.mult, ALU.mult, (tl("o_%d_%d" % (dmc, ti)), rst_t[r], cst), (tl("o_%d_%d" % (dmc, ti)),))
            if kind == "A":
                for dmc in range(8):
                    tt("dve", Xsb[:, dmc, c0:c0 + n], Xsb[:, dmc, c0:c0 + n], Obuf[:, dmc, c0:c0 + n], ALU.add,
                       (tl("o_%d_%d" % (dmc, ti)), tl("x_%d_%d" % (dmc, ti))), (tl("x_%d_%d" % (dmc, ti)),))
        if kind == "B":
            for hlf in range(3):
                for dmc in range(8):
                    ov = Obuf[:, dmc, :].rearrange("p (s c) -> p s c", c=CB)
                    allo = tuple(tl("o_%d_%d" % (dmc, t2)) for t2 in range(3))
                    if hlf < 2:
                        xv = Xsb[:, dmc, 512 * hlf:512 * hlf + 512].rearrange("p (c s) -> p s c", s=8)
                        tt("dve", xv, xv, ov[:, :, 64 * hlf:64 * hlf + 64], ALU.add,
                           allo + (tl("x_%d_%d" % (dmc, hlf)),), (tl("x_%d_%d" % (dmc, hlf)),))
                    else:
                        xv = Xsb[:, dmc, 1024:1056].rearrange("p (q i) -> p i q", i=4)
                        tt("dve", xv, xv, ov[:, 4:8, 128:136], ALU.add, allo + (tl("x_%d_2" % dmc),),
                           (tl("x_%d_2" % dmc),))

    MRING = (0, 1, 2, 6, 7)

    def mbank():
        return MRING[nxt("mmb", 5)]

    def layer_a(layer, ps):
        j = layer // 2
        wb0 = j * NBLK_PER_J
        tb = tl("tabA")
        lt2 = tl("ltmp2")
        ltmp2 = O_A[:, 0, 0:1024].rearrange("p (h t) -> p h t", h=8)
        ltmp3 = O_A[:, 1, 0:256].rearrange("p (h t) -> p h t", h=8)[64:96]
        dma("sp", ltmp, wsT_d[j], (), (tl("ltmp"),) + OT_ALL, par)
        dma("sp", ltmp2, bsB_d[j], (), (lt2,) + OT_ALL, par)
        dma("sp", ltmp3, wsrep_d[j], (), (lt2,) + OT_ALL, par)

        def tables_dve1():
            tt("dve", wsTm[:], ltmp, trim[:].unsqueeze(1).broadcast_to([128, 8, 128]), ALU.mult,
               (tl("ltmp"), cst), (tb,))
            tt("dve", ltmp, ltmp, trim[:].unsqueeze(1).broadcast_to([128, 8, 128]), ALU.mult,
               (tl("ltmp"), cst), (tl("ltmp"),))

        def tables_compute():
            for hh in range(2):
                mm(PS[:, 3 + hh, :], ones_f[:], ltmp[:, 4 * hh:4 * hh + 4, :].rearrange("p a b -> p (a b)"),
                   True, True, (tl("ltmp"), cst), (bank[3 + hh],))
            for dc in range(16):
                hh = dc // 2
                bk = 3 + hh // 4
                stt(ctab[:, dc, :], PS[:, bk, (hh % 4) * 128:(hh % 4) * 128 + 128], lncol[:, j, 1, dc:dc + 1],
                    ltmp2[:, hh, :], ALU.mult, ALU.add, (bank[bk], lt2, cst), (tb,))
            tt("dve", bdw[64:96], ltmp3, bdm[64:96, :].unsqueeze(1).broadcast_to([32, 8, 32]), ALU.mult,
               (lt2, cst), (tb,))

        prenorm(layer, "A")

        def hn_reads(ti):
            return tuple(tl("hn_%d_%d" % (kc, ti)) for kc in range(8))

        wl = [wload(wb0 + 4)]
        for blk in range(4):
            if blk < 3:
                wl.append(wload(wb0 + 4 + blk + 1))
            wi = wl[blk]
            wv = WB[wi][:, :].rearrange("p (a b) -> p a b", b=512)
            for n in range(9):
                if blk == 0 and n == 4:
                    tables_dve1()
                if blk == 1 and n == 0:
                    tables_compute()
                M = 128
                c0 = 128 * n if n < 8 else 960
                bk = mbank()
                for kc in range(8):
                    mm(PS[0:M, bk, :], Hn[:, kc, c0:c0 + M], wv[:, kc, :], kc == 0, kc == 7,
                       ((tl("hn_%d_%d" % (kc, n // 4)),) if n < 8 else (tl("hn_%d_1" % kc), tl("hn_%d_2" % kc)))
                       + (wtile[wi],), (bank[bk],))
                cp("act", Gv[0:M, n, blk * 512:(blk + 1) * 512], PS[0:M, bk, :], (bank[bk],),
                   tuple(tl("g_%d_%d" % (n, dc)) for dc in range(4 * blk, 4 * blk + 4)))
                P.add("dve", lambda e, o=stats[0:M, n, blk, :], i=Gv[0:M, n, blk * 512:(blk + 1) * 512]:
                      e.bn_stats(out=o, in_=i), tuple(tl("g_%d_%d" % (n, dc)) for dc in range(4 * blk, 4 * blk + 4)),
                      (tl("stats"),))
        for n in range(9):
            M = 128
            P.add("dve", lambda e, o=mv[0:M, n, :], i=stats[0:M, n, :, :].rearrange("p a b -> p (a b)"):
                  e.bn_aggr(out=o, in_=i), (tl("stats"),), (tl("mv"),))
        act(rstdv[:], mv[:, :, 1], AF.Sqrt, (tl("mv"), cst), (tl("mv"),), bias=epsc[:], scale=1.0)
        P.add("dve", lambda e: e.reciprocal(out=rstdv[:], in_=rstdv[:]), (tl("mv"),), (tl("mv"),))
        stt(nmr[:], mv[:, :, 0], -1.0, rstdv[:], ALU.mult, ALU.mult, (tl("mv"),), (tl("mv"),))
        for n in range(9):
            M = 128
            Nt = 128 if n < 8 else 32
            gts = tuple(tl("g_%d_%d" % (n, dc)) for dc in range(16))
            act(Gv[0:M, n, :], Gv[0:M, n, :], AF.Identity, gts + (tl("mv"),), gts,
                bias=nmr[0:M, n:n + 1], scale=rstdv[0:M, n:n + 1])
            if n == 8:
                cvt = tl("cvt")
                ofl = A3[:, :].bitcast(F32)
                cvs = ofl[64:96, 2112:4160]
                cvg = ofl[64:96, 4160:6208]
                cvb = ofl[64:96, 6208:8256]
                dma("sp", cvg, lnbc_d[j, 0], (), (cvt, tl("ltmp"), lt2) + OT_ALL, par)
                dma("sp", cvb, lnbc_d[j, 1], (), (cvt, tl("ltmp"), lt2) + OT_ALL, par)
                tt("dve", cvs, Gv[64:96, 8, :], cvg, ALU.mult, gts + (cvt,), (cvt,))
                tt("dve", cvs, cvs, cvb, ALU.add, (cvt,), (cvt,))
                dma("sp", cv[j, ps], cvs, (cvt,) + OT_ALL, (tl("cv_out"),), ch_out)
            for dc in range(16):
                bk = 4 + dc // 4
                rhs = wsTm[:, dc // 2, :] if n < 8 else bdw[:, dc // 2, :]
                mm(PS[:, bk, (dc % 4) * 128:(dc % 4) * 128 + Nt], G[0:M, n, dc, :], rhs, True, True,
                   (tl("g_%d_%d" % (n, dc)), tb), (bank[bk],))
            if n < 8:
                for b4 in range(4):
                    bk = 4 + b4
                    pv = PS[:, bk, :].rearrange("p (d t) -> p d t", d=4)
                    gB = lncol[:, j, 0, 4 * b4:4 * b4 + 4].unsqueeze(2).broadcast_to([128, 4, 128])
                    tt("dve", pv, pv, gB, ALU.mult, (bank[bk], cst), (bank[bk],))
                    tt("dve", G[:, n, 4 * b4:4 * b4 + 4, :], pv, ctab[:, 4 * b4:4 * b4 + 4, :], ALU.add,
                       (bank[bk], tb), tuple(tl("g_%d_%d" % (n, dc_)) for dc_ in range(4 * b4, 4 * b4 + 4)))
            for dc in range(16 if n == 8 else 0):
                bk = 4 + dc // 4
                pin = PS[:, bk, (dc % 4) * 128:(dc % 4) * 128 + Nt]
                if n < 8:
                    pass
                else:
                    stt(G[:, 8, dc, 0:32].rearrange("p (q i) -> p q i", i=4),
                        pin.rearrange("p (q i) -> p q i", i=4), lncol[:, j, 0, dc:dc + 1],
                        ctab[:, dc, 0:4].unsqueeze(1).broadcast_to([128, 8, 4]), ALU.mult, ALU.add,
                        (bank[bk], tb, cst), (tl("g_8_%d" % dc),))

        def gview(ti, dc):
            if ti < 2:
                return G[:, 4 * ti:4 * ti + 4, dc, :]
            return G[:, 8, dc, 0:32]

        def gtiles(ti, dc):
            if ti < 2:
                return tuple(tl("g_%d_%d" % (n, dc)) for n in range(4 * ti, 4 * ti + 4))
            return (tl("g_8_%d" % dc),)

        for stage, b0 in (("u", 0), ("z", 8)):
            wl = [wload(wb0 + b0)]
            for blk in range(4):
                if blk < 3:
                    wl.append(wload(wb0 + b0 + blk + 1))
                wi = wl[blk]
                wv = WB[wi][:, :].rearrange("p (a b) -> p a b", b=512)
                for dcl in range(4):
                    dc = 4 * blk + dcl
                    for ti, (c0, n) in enumerate(TILES_A):
                        bk = mbank()
                        for kc in range(8):
                            mm(PS[:, bk, 0:n], wv[:, kc, dcl * 128:(dcl + 1) * 128], Hn[:, kc, c0:c0 + n],
                               kc == 0, kc == 7, (tl("hn_%d_%d" % (kc, ti)), wtile[wi]), (bank[bk],))
                        pv = PS[:, bk, 0:n]
                        if ti < 2:
                            pv = pv.rearrange("p (a b) -> p a b", b=128)
                        gt = gtiles(ti, dc)
                        if stage == "u":
                            tt("dve", gview(ti, dc), pv, gview(ti, dc), ALU.mult, (bank[bk],) + gt, gt)
                        else:
                            i = nxt("tmpb", 2)
                            act(tmpb[i][:, 0:n], PS[:, bk, 0:n], AF.Silu, (bank[bk],), (tmpb_t[i],))
                            tv = tmpb[i][:, 0:n]
                            if ti < 2:
                                tv = tv.rearrange("p (a b) -> p a b", b=128)
                            tt("dve", gview(ti, dc), tv, gview(ti, dc), ALU.mult, (tmpb_t[i],) + gt, gt)
        wl = [wload(wb0 + 12)]
        for blk in range(4):
            if blk < 3:
                wl.append(wload(wb0 + 12 + blk + 1))
            wi = wl[blk]
            wv = WB[wi][:, :].rearrange("p (a b) -> p a b", b=256)
            for dml in range(2):
                dmc = 2 * blk + dml
                for ti, (c0, n) in enumerate(TILES_A):
                    bk = mbank()
                    for dc in range(16):
                        mm(PS[:, bk, 0:n], wv[:, dc, dml * 128:(dml + 1) * 128], gview(ti, dc), dc == 0, dc == 15,
                           gtiles(ti, dc) + (wtile[wi],), (bank[bk],))
                    out_stage_evac(bk, dmc, ti, c0, n, O_A)
        flush_stat()
        postnorm(layer, "A", O_A, TILES_A)

    def layer_b(layer, ps):
        j = layer // 2
        wb0 = j * NBLK_PER_J + 16
        prenorm(layer, "B")
        for nm_, t_ in list(T.items()):
            if nm_.startswith(("xs5_", "Yd_", "Xd_", "yf_", "y_")) and not nm_.startswith("y_out"):
                if t_.w is not None and t_.w.chan is not None:
                    t_.w = None
                for k_ in [k_ for k_ in t_.r if isinstance(k_, int)]:
                    del t_.r[k_]
        tsx = tl("s5small")
        dma("sp", h0t[:], h0_d[j, ps], (), (tsx,), par)

        def c3(v64):
            return v64.unsqueeze(1).broadcast_to([128, 8, 64])

        def c3h(v32):
            return v32.unsqueeze(1).broadcast_to([128, 8, 32])

        def cstep(dst, src, tr, tp_, tn_, addend, eng="dve"):
            sv = src.rearrange("p q (g r) -> p q g r", r=2)
            t2v = st2[:].rearrange("p q (g r) -> p q g r", r=2)
            tt(eng, st1[:], src, c3(tr), ALU.mult, (tsx, cst), (tsx,))
            tt(eng, t2v[:, :, :, 0], sv[:, :, :, 1], c3h(tn_), ALU.mult, (tsx, cst), (tsx,))
            tt(eng, t2v[:, :, :, 1], sv[:, :, :, 0], c3h(tp_), ALU.mult, (tsx, cst), (tsx,))
            tt(eng, st1[:], st1[:], st2[:], ALU.add, (tsx,), (tsx,))
            if addend is None:
                cp(eng, dst, st1[:], (tsx,), (tsx,))
            else:
                tt(eng, dst, st1[:], addend, ALU.add, (tsx, tl("ub_a"), tl("ub_d")), (tsx,))

        cstep(h0p[:], h0t[:], M4r[:, j, :], M4p[:, j, :], M4n[:, j, :], None)

        Xdv = Xd.rearrange("(g k) (s c) -> s k g c", k=16, c=CB)

        def xreadback(f_):
            for s8 in range(8):
                dma("sp", Xs5[16 * s8:16 * s8 + 16, 8 * f_:8 * f_ + 8, :],
                    Xdv[s8][:, 8 * f_:8 * f_ + 8, :],
                    (tl("Xd_%d" % f_),), (tl("xs5_%d_%d" % (s8, f_)),) + (OT_ALL if (s8 == 0 and f_ == 0) else ()),
                    ch_xr[f_][s8 % 2])

        wl = [wload(wb0 + 0), wload(wb0 + 1)]
        for fc in range(8):
            wi = wl[fc // 4]
            wv = WB[wi][:, :].rearrange("p (a b) -> p a b", b=512)
            dcl = fc % 4
            xi = nxt("xs", 2)
            for ti, (c0, n) in enumerate(TILES_B):
                bk = mbank()
                for kc in range(8):
                    mm(PS[:, bk, 0:n], wv[:, kc, dcl * 128:(dcl + 1) * 128], Hn[:, kc, c0:c0 + n],
                       kc == 0, kc == 7, (tl("hn_%d_%d" % (kc, ti)), wtile[wi]), (bank[bk],))
                cp("act", xs[xi][:, c0:c0 + n], PS[:, bk, 0:n], (bank[bk],), (xs_t[xi],))
            dma("sp", Xd[fc * 128:(fc + 1) * 128, :], xs[xi][:], (xs_t[xi],), (tl("Xd_%d" % fc),), ch_xs[xi])
            if fc >= 1:
                xreadback(fc - 1)
        xreadback(7)
        cp("act", Hb[:, :, :, 128:136].rearrange("p g r c -> p (g r) c"),
           h0p[:].rearrange("p q g -> p g q"), (tsx,), (tl("hb_5"),))
        ub = tl("ub_all")
        UBQ = [tl("ub_q%d" % q_) for q_ in range(4)]
        UBS = tl("ub_s")
        batches1 = [(g0_, min(3, 32 - g0_)) for g0_ in range(0, 32, 3)]

        def load1(b_):
            g0_, n_ = batches1[b_]
            return rbload(lambda t_, n_=n_: t_[:, 0:n_ * 512].rearrange("p (a m) -> p a m", a=n_),
                          SW[j, g0_:g0_ + n_].rearrange("a p m -> p a m"), tl("SW"))

        pend = [load1(0), load1(1)]
        for b_, (g0_, n_) in enumerate(batches1):
            if b_ + 2 < len(batches1):
                pend.append(load1(b_ + 2))
            si = pend[b_]
            for a_ in range(n_):
                gp = g0_ + a_
                wv = RB[si][:, a_ * 512:(a_ + 1) * 512].rearrange("p (x r m) -> p x r m", x=2, r=2)
                bk = mbank()
                pu = PS[:, bk, 0:272].rearrange("p (r c) -> p r c", r=2)
                for ri in range(2):
                    for x in range(2):
                        mm(pu[:, ri, :], wv[:, x, ri, :], Xs5[:, 2 * gp + x, :], x == 0, x == 1,
                           (RBT[si],) + XS5_F[gp // 4], (bank[bk],))
                cp("act" if gp % 2 == 0 else "dve", Ub[:, :, 2 * gp:2 * gp + 2].rearrange("p c r -> p r c"), pu,
                   (bank[bk],), (tl("ub_a" if gp % 2 == 0 else "ub_d"),))
        wl = [wload(wb0 + 2), wload(wb0 + 3)]
        for fc in range(8):
            wi = wl[fc // 4]
            wv = WB[wi][:, :].rearrange("p (a b) -> p a b", b=512)
            dcl = fc % 4
            for ti, (c0, n) in enumerate(TILES_B):
                bk = mbank()
                for kc in range(8):
                    mm(PS[:, bk, 0:n], wv[:, kc, dcl * 128:(dcl + 1) * 128], Hn[:, kc, c0:c0 + n],
                       kc == 0, kc == 7, (tl("hn_%d_%d" % (kc, ti)), wtile[wi]), (bank[bk],))
                act(Gb[:, fc, c0:c0 + n], PS[:, bk, 0:n], AF.Silu, (bank[bk],), (tl("gb_%d_%d" % (fc, ti)),))
        hin = Hin0[:] if ps == 0 else Hcar[:, j, :]
        tr = AR2[:, j, :]
        tpp = AIp[:, j, :]
        tnn = AIn[:, j, :]
        sc1 = st1[:, 0, :]
        sc2 = st2[:, 0, :].rearrange("p (g r) -> p g r", r=2)
        scn = tl("scan")
        hbv = Hb.rearrange("p g r c -> p c g r")
        cp("act", hbv[:, 0, :, :], hin.rearrange("p (g r) -> p g r", r=2), (cst, tsx, tl("hcar")), (tl("hb_0"),))
        for c in range(128):
            prev = hin if c == 0 else Ub[:, c - 1, :]
            pv = prev.rearrange("p (g r) -> p g r", r=2)
            uq = UBQ[c // 32]
            rd = (uq, UBQ[max(c - 1, 0) // 32], scn, cst, tsx, tl("hcar"), tl("ub_a"), tl("ub_d"))
            ns = SCAN_NOSYNC and c > 0
            tt("dve", sc1, prev, tr, ALU.mult, rd, (scn,), nosync=ns)
            tt("dve", sc2, pv[:, :, ::-1], AI2[:, j, :].rearrange("p (g r) -> p g r", r=2), ALU.mult, rd, (scn,), nosync=ns)
            tt("dve", Ub[:, c, :], Ub[:, c, :], sc1, ALU.add, rd, (uq,), nosync=ns)
            tt("dve", Ub[:, c, :], Ub[:, c, :], st2[:, 0, :], ALU.add, rd, (uq,), nosync=ns)
            if c % 32 == 31:
                q4 = c // 32
                ln_ = 32 if q4 < 3 else 31
                cp("act", Hb[:, :, :, 1 + 32 * q4:1 + 32 * q4 + ln_].rearrange("p g r c -> p (g r) c"),
                   Ub[:, 32 * q4:32 * q4 + ln_, :].rearrange("p c g -> p g c"), (uq,), (tl("hb_%d" % (q4 + 1)),))
        cp("dve", Hcar[:, j, :], Ub[:, 127, :], (UBQ[3],), (tl("hcar"),))
        if ps == npass - 1:
            dma("sp", stp[j], Hcar[:, j, :], (tl("hcar"),), (tl("stp_out_%d" % j),), ch_out)
        wb1 = j * NBLK_PER_J + 20
        wl_glu1 = [wload(wb1 + 0), wload(wb1 + 1)]
        def load3(b_):
            return rbload(lambda t_: t_[:, :].rearrange("p (a m) -> p a m", a=4),
                          SMV[j, 4 * b_:4 * b_ + 4].rearrange("a p m -> p a m"), tl("SMV"))

        pend = [load3(0), load3(1)]
        for g in range(64):
            b_ = g // 4
            if g % 4 == 0 and b_ + 2 < 16:
                pend.append(load3(b_ + 2))
            si = pend[b_]
            mv_ = RB[si][:, (g % 4) * 384:(g % 4) * 384 + 384]
            bk = mbank()
            py = PS[:, bk, 0:CB]
            mm(py, mv_[:, 0:128], Xs5[:, g, :], True, False, (RBT[si], tl("y_%d" % g)) + XS5_F[g // 8],
               (bank[bk],))
            for ri in range(2):
                mm(py, mv_[:, 128 + 128 * ri:256 + 128 * ri], Hb[:, g // 2, ri, :], False, ri == 1,
                   (RBT[si],) + tuple(tl("hb_%d" % q) for q in range(6)), (bank[bk],))
            act(Xs5[:, g, :], py, AF.Gelu_apprx_tanh, (bank[bk],), (tl("y_%d" % g),))
            if g % 8 == 7:
                fc = g // 8
                Ydv = Yd.rearrange("(g k) (t c) -> t k g c", k=16, c=CB)
                ys = tuple(tl("y_%d" % g_) for g_ in range(8 * fc, 8 * fc + 8))
                for t8 in range(8):
                    dma("sp", Ydv[t8][:, 8 * fc:8 * fc + 8, :], Xs5[16 * t8:16 * t8 + 16, 8 * fc:8 * fc + 8, :],
                        ys + XS5_F[fc], (tl("Yd_%d_%d" % (fc, t8)),), ch_yw[fc % 2])
                for fr in ([fc - 1] if fc >= 1 else []) + ([7] if fc == 7 else []):
                    dma("sp", yF[:, fr, :], Yd[fr * 128:(fr + 1) * 128, :],
                        tuple(tl("Yd_%d_%d" % (fr, t_)) for t_ in range(8)),
                        (tl("yf_%d" % fr),) + tuple(tl("hn_%d_%d" % (fr, t)) for t in range(3)), ch_scr3)
        cstep(hsf[:], h0p[:], tr, tpp, tnn, Ub[:, 128:136, :])
        dma("sp", sts[j, ps], hsf[:], (tsx,), (tl("sts_out"),), ch_out)
        for which in range(2):
            wl = wl_glu1 if which == 0 else [wload(wb1 + 2), wload(wb1 + 3)]
            for fc in range(8):
                wi = wl[fc // 4]
                wv = WB[wi][:, :].rearrange("p (a b) -> p a b", b=512)
                dcl = fc % 4
                for ti, (c0, n) in enumerate(TILES_B):
                    bk = mbank()
                    for kc in range(8):
                        mm(PS[:, bk, 0:n], wv[:, kc, dcl * 128:(dcl + 1) * 128], yF[:, kc, c0:c0 + n],
                           kc == 0, kc == 7, (tl("yf_%d" % kc), wtile[wi]), (bank[bk],))
                    gt = tl("gb_%d_%d" % (fc, ti))
                    if which == 0:
                        stt(Gb[:, fc, c0:c0 + n], PS[:, bk, 0:n], bglu[:, j, 0, fc:fc + 1], Gb[:, fc, c0:c0 + n],
                            ALU.add, ALU.mult, (bank[bk], gt, cst), (gt,))
                    else:
                        i = nxt("tmpb", 2)
                        act(tmpb[i][:, 0:n], PS[:, bk, 0:n], AF.Sigmoid, (bank[bk], cst), (tmpb_t[i],),
                            bias=bglu[:, j, 1, fc:fc + 1], scale=1.0)
                        tt("dve", Gb[:, fc, c0:c0 + n], tmpb[i][:, 0:n], Gb[:, fc, c0:c0 + n], ALU.mult,
                           (tmpb_t[i], gt), (gt,))
        wb2 = j * NBLK_PER_J + 24
        wl = [wload(wb2), wload(wb2 + 1)]
        for dmc in range(8):
            wi = wl[dmc // 4]
            wv = WB[wi][:, :].rearrange("p (a b) -> p a b", b=512)
            dcl = dmc % 4
            for ti, (c0, n) in enumerate(TILES_B):
                bk = mbank()
                for kc in range(8):
                    mm(PS[:, bk, 0:n], wv[:, kc, dcl * 128:(dcl + 1) * 128], Gb[:, kc, c0:c0 + n],
                       kc == 0, kc == 7, (tl("gb_%d_%d" % (kc, ti)), wtile[wi]), (bank[bk],))
                out_stage_evac(bk, dmc, ti, c0, n, O_B)
        flush_stat()
        postnorm(layer, "B", O_B, TILES_B)

    for ps in range(npass):
        for kc in range(8):
            dma("sp", Xsb[:, kc, :], xT[ps, :, kc, :], (),
                tuple(tl("x_%d_%d" % (kc, ti)) for ti in range(3)), ch_x[ps % 2],
                extra=(bar_ops if ps == 0 else ()))
        for layer, kind in enumerate(layers):
            if kind == "A":
                layer_a(layer, ps)
            elif kind == "B":
                layer_b(layer, ps)
        for kc in range(8):
            dma("sp", yT[ps, :, kc, :], Xsb[:, kc, :], tuple(tl("x_%d_%d" % (kc, ti)) for ti in range(3)),
                (tl("y_out_%d" % kc),), ch_out)
    print('sbuf bytes remaining', nc.sbuf_bytes_remaining)
    if max_ops is not None:
        print('total ops', len(P.ops))
        P.ops = P.ops[:max_ops]
    P.emit()
    return nc


def _blk(w, nb, kc, n):
    return np.ascontiguousarray(w.reshape(kc, 128, nb, n).transpose(2, 1, 0, 3)).reshape(nb, 128, kc * n)


def _consts():
    ident = np.eye(128, dtype=np.float32)
    s = np.arange(128)
    trimask = (s[:, None] <= s[None, :]).astype(np.float32)
    q = np.arange(32)
    bdmask = ((q[:, None] // 4 == q[None, :] // 4) & (q[:, None] % 4 <= q[None, :] % 4)).astype(np.float32)
    tmask = ((s[:, None] // 16) <= (s[None, :] // 16)).astype(np.float32)
    colmask = np.zeros((128, 2, 128), np.float32)
    colmask[:, 0, :64] = 1
    colmask[:, 1, 64:] = 1
    rowmask = np.zeros((128, 2), np.float32)
    rowmask[:64, 0] = 1
    rowmask[64:, 1] = 1
    return dict(ident=ident, trimask=trimask, bdmask=bdmask, tmask=tmask, colmask=colmask, rowmask=rowmask)


def _shared_inputs(inp):
    f = lambda a: np.ascontiguousarray(np.asarray(a, dtype=np.float32))
    blocks = []
    for j in range(2):
        blocks.append(_blk(f(inp["w_in_a"][j]), 12, 8, 512))
        blocks.append(_blk(f(inp["w_out_a"][j]), 4, 16, 256))
        blocks.append(_blk(f(inp["w_in_b"][j]), 4, 8, 512))
        blocks.append(_blk(f(inp["w_glu1"][j]), 2, 8, 512))
        blocks.append(_blk(f(inp["w_glu2"][j]), 2, 8, 512))
        blocks.append(_blk(f(inp["w_out_b"][j]), 2, 8, 512))
    d = dict(wall=np.concatenate(blocks, axis=0))
    col = lambda v, n: np.ascontiguousarray(f(v).reshape(n, 128).T)
    d["gpre"] = np.concatenate([col(inp["norm_pre"][l], 8) for l in range(4)], axis=1)
    d["gpost"] = np.concatenate([col(inp["norm_post"][l], 8) for l in range(4)], axis=1)
    d["lncol"] = np.ascontiguousarray(np.stack(
        [np.stack([col(inp["ln_v_g"][j], 16), col(inp["ln_v_b"][j], 16)], axis=1) for j in range(2)], axis=1))
    d["lnbc"] = np.ascontiguousarray(np.stack(
        [np.stack([np.broadcast_to(f(inp["ln_v_g"][j])[None, :], (32, 2048)),
                   np.broadcast_to(f(inp["ln_v_b"][j])[None, :], (32, 2048))]) for j in range(2)]))
    ws = f(inp["w_s"])
    d["wsT"] = np.ascontiguousarray(ws.transpose(0, 3, 1, 2))
    corner = ws[:, :, :4, :4].transpose(0, 3, 1, 2)
    d["wsrep"] = np.ascontiguousarray(np.tile(corner, (1, 8, 1, 8)))
    d["bsB"] = np.ascontiguousarray(np.broadcast_to(f(inp["b_s"])[:, None, :, :], (2, 128, 8, 128)))
    d["bglu"] = np.ascontiguousarray(np.stack(
        [np.stack([col(inp["b_glu1"][j], 8), col(inp["b_glu2"][j], 8)], axis=1) for j in range(2)], axis=1))

    def l2(a):
        return a.reshape(32, 2, 64).transpose(1, 2, 0).reshape(128, 32)

    aL2 = np.zeros((2, 128, 3, 32), np.float32)
    bL2 = np.zeros((2, 128, 2, 32, 16), np.float32)
    cL2 = np.zeros((2, 128, 2, 32, 16), np.float32)
    dcol = np.zeros((2, 128, 64), np.float32)
    for j in range(2):
        aL2[j, :, 0] = l2(f(inp["a_re"][j]))
        aL2[j, :, 1] = l2(f(inp["a_im"][j]))
        aL2[j, :, 2] = l2(np.broadcast_to(f(inp["log_dt"][j])[:, None], (64, 64)))
        for r, nm in enumerate(("b_re", "b_im")):
            b = f(inp[nm][j])
            bL2[j, :, r] = b.reshape(32, 2, 64, 16).transpose(1, 2, 0, 3).reshape(128, 32, 16)
        for r, nm in enumerate(("c_re", "c_im")):
            c = f(inp[nm][j])
            cL2[j, :, r] = c.reshape(32, 2, 16, 64).transpose(1, 3, 0, 2).reshape(128, 32, 16)
        dk = f(inp["d_skip"][j]).reshape(64, 16)
        dcol[j] = np.tile(dk.T, (8, 1))
    d.update(aL2=aL2, bL2=bL2, cL2=cL2, dcol=dcol)
    d.update(_consts())
    return d


def _core_inputs(inp, cid):
    f = lambda a: np.asarray(a, dtype=np.float32)
    xp = f(inp["x_prompt"][cid])
    xsm = f(inp["x_sample"][16 * cid:16 * cid + 16])
    xT = np.zeros((2, 128, 8, NTA), np.float32)
    for ps in range(2):
        cols = np.concatenate([xp[1024 * ps:1024 * ps + 1024], xsm[8 * ps:8 * ps + 8].reshape(32, 1024)], axis=0)
        xT[ps] = cols.T.reshape(8, 128, NTA).transpose(1, 0, 2)
    h0 = np.zeros((2, 2, 128, 8, 64), np.float32)
    for j in range(2):
        for ps in range(2):
            sl = slice(16 * cid + 8 * ps, 16 * cid + 8 * ps + 8)
            re = f(inp["state_ssm_re"][j, sl])
            im = f(inp["state_ssm_im"][j, sl])
            st = np.stack([re, im], axis=-1)
            st = st.reshape(8, 32, 2, 64, 2).transpose(2, 3, 0, 1, 4)
            h0[j, ps] = st.reshape(128, 8, 64)
    return dict(xT=xT, h0L2=h0)


_NC_CACHE = {}


def kernel(**inputs):
    inp = {k: np.asarray(v) for k, v in inputs.items()}
    shared = _shared_inputs(inp)
    in_maps = []
    for cid in range(NCORES):
        m = dict(shared)
        m.update(_core_inputs(inp, cid))
        in_maps.append(m)
    if "nc" not in _NC_CACHE:
        _NC_CACHE["nc"] = build_program()
    nc = _NC_CACHE["nc"]
    res = run_bass_kernel_spmd(nc, in_maps, core_ids=list(range(NCORES)))
    y_prompt = np.zeros((8, 2048, 1024), np.float32)
    y_sample = np.zeros((128, 4, 1024), np.float32)
    cvs = np.zeros((2, 128, 4, 2048), np.float32)
    srp = np.zeros((2, 8, 64, 64), np.float32)
    sip = np.zeros((2, 8, 64, 64), np.float32)
    srs = np.zeros((2, 128, 64, 64), np.float32)
    sis = np.zeros((2, 128, 64, 64), np.float32)
    for cid in range(NCORES):
        r = res.results[cid]
        yT = np.asarray(r["yT"])
        for ps in range(2):
            cols = yT[ps].transpose(2, 1, 0).reshape(NTA, 1024)
            y_prompt[cid, 1024 * ps:1024 * ps + 1024] = cols[:1024]
            y_sample[16 * cid + 8 * ps:16 * cid + 8 * ps + 8] = cols[1024:].reshape(8, 4, 1024)
        cvv = np.asarray(r["cv"])
        stpv = np.asarray(r["stp"])
        stsv = np.asarray(r["sts"])
        for j in range(2):
            for ps in range(2):
                cvs[j, 16 * cid + 8 * ps:16 * cid + 8 * ps + 8] = cvv[j, ps].reshape(8, 4, 2048)
                s = stsv[j, ps].reshape(2, 64, 8, 32, 2).transpose(2, 3, 0, 1, 4).reshape(8, 64, 64, 2)
                srs[j, 16 * cid + 8 * ps:16 * cid + 8 * ps + 8] = s[..., 0]
                sis[j, 16 * cid + 8 * ps:16 * cid + 8 * ps + 8] = s[..., 1]
            s = stpv[j].reshape(2, 64, 32, 2).transpose(2, 0, 1, 3).reshape(64, 64, 2)
            srp[j, cid] = s[..., 0]
            sip[j, cid] = s[..., 1]
    return (y_prompt, y_sample, cvs, srp, sip, srs, sis)
```

```python
import os
import numpy as np
import concourse.bass as bass
import concourse.mybir as mybir
from concourse.bass_utils import run_bass_kernel_spmd

F32 = mybir.dt.float32
BF16 = mybir.dt.bfloat16
I32 = mybir.dt.int32
AF = mybir.ActivationFunctionType
ALU = mybir.AluOpType

NCORES = 8
D = 1024
NTA = 1056
NTB = 1088
CB = 136
TILES_A = [(0, 512), (512, 512), (1024, 32)]
TILES_B = [(0, 512), (512, 512), (1024, 64)]
EPS = 1e-6
NBLK_PER_J = 26
SCAN_NOSYNC = os.environ.get("SCAN_SYNC", "0") != "1"
BUILD_NOSYNC = SCAN_NOSYNC


class Tl:
    __slots__ = ("name", "w", "r", "excl")

    def __init__(self, name, excl=False):
        self.name = name
        self.w = None
        self.r = {}
        self.excl = excl


class Chan:
    def __init__(self, nc, name):
        self.sem = nc.alloc_semaphore(name)
        self.cnt = 0
        self.name = name
        self.pending = 0
        self.selfw = 0


class Op:
    __slots__ = ("eng", "fn", "chan", "signal", "deps", "semval", "val", "final")


class Prog:
    CE = ("pe", "act", "dve", "pool")

    def __init__(self, nc):
        self.nc = nc
        self.ops = []
        self.sem = {e: nc.alloc_semaphore("s_" + e) for e in self.CE}
        self.chans = []
        self.last = {}

    def chan(self, name):
        c = Chan(self.nc, name)
        self.chans.append(c)
        return c

    def add(self, eng, fn, reads=(), writes=(), chan=None, extra=(), nosync=False):
        op = Op()
        op.eng = eng
        op.fn = fn
        op.chan = chan
        op.signal = False
        op.final = False
        xw = tuple(t for t in reads if t.excl)
        if xw:
            writes = tuple(writes) + xw
        deps = {}
        for t in reads:
            if t.w is not None:
                deps[id(t.w)] = t.w
        for t in writes:
            if t.w is not None:
                deps[id(t.w)] = t.w
            for r in t.r.values():
                deps[id(r)] = r
        for d in extra:
            deps[id(d)] = d
        for t in writes:
            t.w = op
            t.r = {}
        for t in reads:
            key = id(chan) if chan is not None else eng
            t.r[key] = op
        dl = []
        for d in deps.values():
            if d is op:
                continue
            if d.chan is not None:
                dl.append(("dma", d.chan, d.chan.cnt * 16))
                d.chan.pending = max(d.chan.pending, d.chan.cnt * 16)
            else:
                if d.eng == eng and chan is None and (eng == "pe" or nosync):
                    continue
                d.signal = True
                dl.append(("cmp", d))
        if chan is not None and chan.pending > chan.selfw:
            dl.append(("dma", chan, chan.pending))
            chan.selfw = chan.pending
        op.deps = dl
        if chan is not None:
            chan.cnt += 1
            op.val = chan.cnt * 16
        self.ops.append(op)
        self.last[eng if chan is None else ("q", eng)] = op
        return op

    def barrier(self, tiny):
        lasts = [self.last[e] for e in self.CE if e in self.last]
        dmas = [self.last[k] for k in self.last if isinstance(k, tuple)]
        out = []
        for e in ("act", "dve", "pool"):
            out.append(self.add(e, tiny[e], extra=lasts + dmas))
        return out + dmas

    def emit(self):
        cnt = {e: 0 for e in self.CE}
        for op in self.ops:
            if op.chan is None and op.signal:
                cnt[op.eng] += 1
                op.semval = cnt[op.eng]
        nc = self.nc
        by = {e: [o for o in self.ops if o.eng == e] for e in ("pe", "act", "dve", "pool", "sp")}
        sems = self.sem
        chans = self.chans

        def run(e, lst, is_sp=False):
            waited = {}
            for op in lst:
                need = {}
                for d in op.deps:
                    if d[0] == "dma":
                        sem, val = d[1].sem, d[2]
                    else:
                        sem, val = sems[d[1].eng], d[1].semval
                    k = id(sem)
                    if k not in need or need[k][1] < val:
                        need[k] = (sem, val)
                for k, (sem, val) in need.items():
                    if waited.get(k, 0) >= val:
                        continue
                    e.wait_ge(sem, val)
                    waited[k] = val
                ins = op.fn(e)
                if op.chan is not None:
                    ins.then_inc(op.chan.sem, 16)
                elif op.signal:
                    ins.then_inc(sems[op.eng], 1)
            if is_sp:
                for c in chans:
                    n_em = sum(1 for o in self.ops if o.chan is c)
                    if n_em > 0:
                        e.wait_ge(c.sem, n_em * 16)

        with nc.Block() as block:
            @block.tensor
            def _(e):
                run(e, by["pe"])

            @block.scalar
            def _(e):
                run(e, by["act"])

            @block.vector
            def _(e):
                run(e, by["dve"])

            @block.gpsimd
            def _(e):
                run(e, by["pool"])

            @block.sync
            def _(e):
                run(e, by["sp"], True)


def build_program(layers=("A", "B", "A", "B"), npass=2, max_ops=None):
    nc = bass.Bass("TRN2", target_bir_lowering=False)
    P = Prog(nc)

    def din(name, shape, dt=F32):
        return nc.dram_tensor(name, list(shape), dt, kind="ExternalInput").ap()

    def dout(name, shape, dt=F32):
        return nc.dram_tensor(name, list(shape), dt, kind="ExternalOutput").ap()

    xT = din("xT", [2, 128, 8, NTA])
    WALL = din("wall", [2 * NBLK_PER_J, 128, 4096])
    gpre_d = din("gpre", [128, 32])
    gpost_d = din("gpost", [128, 32])
    lncol_d = din("lncol", [128, 2, 2, 16])
    lnbc_d = din("lnbc", [2, 2, 32, 2048])
    wsT_d = din("wsT", [2, 128, 8, 128])
    wsrep_d = din("wsrep", [2, 32, 8, 32])
    bsB_d = din("bsB", [2, 128, 8, 128])
    bglu_d = din("bglu", [128, 2, 2, 8])
    aL2_d = din("aL2", [2, 128, 3, 32])
    bL2_d = din("bL2", [2, 128, 2, 32, 16])
    cL2_d = din("cL2", [2, 128, 2, 32, 16])
    dcol_d = din("dcol", [2, 128, 64])
    h0_d = din("h0L2", [2, 2, 128, 8, 64])
    ident_d = din("ident", [128, 128])
    trim_d = din("trimask", [128, 128])
    bdm_d = din("bdmask", [32, 32])
    tmask_d = din("tmask", [128, 128])
    colm_d = din("colmask", [128, 2, 128])
    rowm_d = din("rowmask", [128, 2])

    yT = dout("yT", [2, 128, 8, NTA])
    cv = dout("cv", [2, 2, 32, 2048])
    stp = dout("stp", [2, 128, 64])
    sts = dout("sts", [2, 2, 128, 8, 64])

    SW = nc.dram_tensor("SW", [2, 32, 128, 512], BF16).ap()
    SMV = nc.dram_tensor("SMV", [2, 64, 128, 384], BF16).ap()
    Xd = nc.dram_tensor("Xd", [1024, NTB], BF16).ap()
    Yd = nc.dram_tensor("Yd", [1024, NTB], BF16).ap()

    def sb(name, shape, dt=F32):
        return nc.alloc_sbuf_tensor("sb_" + name, list(shape), dt)

    A0 = sb("A0", [128, 8, NTA])
    A1 = sb("A1", [128, 8, NTB], BF16)
    A2 = sb("A2", [128, 18432], BF16)
    A3 = sb("A3", [128, 17408], BF16)
    A5 = sb("A5", [128, 8, NTB], BF16)
    NWB = 2
    WB = [sb("wb%d" % i, [128, 4096], BF16) for i in range(NWB)]
    NRB = 3
    RB = [sb("rb%d" % i, [128, 1536], BF16) for i in range(NRB)]
    ident = sb("ident", [128, 128])
    ones_bf = sb("ones_bf", [128, 128], BF16)
    ones_f = sb("ones_f", [128, 128])
    trim = sb("trim", [128, 128])
    bdm = sb("bdm", [128, 32])
    tmask = sb("tmask", [128, 128])
    colm = sb("colm", [128, 2, 128])
    rowm = sb("rowm", [128, 2])
    gpre = sb("gpre", [128, 32])
    gpost = sb("gpost", [128, 32])
    lncol = sb("lncol", [128, 2, 2, 16])
    bglu = sb("bglu", [128, 2, 2, 8])
    epsc = sb("epsc", [128, 1])
    hpic = sb("hpic", [128, 1])
    zero1 = sb("zero1", [128, 1])
    ctab = sb("ctab", [128, 16, 128])
    wsTm = sb("wsTm", [128, 8, 128], BF16)
    bdw = sb("bdw", [128, 8, 32], BF16)
    rst = [sb("rst%d" % i, [128, 512]) for i in range(2)]
    sqr = [sb("sq%d" % i, [128, 512], BF16) for i in range(2)]
    tmpb = [sb("tmpb%d" % i, [128, 512], BF16) for i in range(2)]
    xs = [sb("xs%d" % i, [128, NTB], BF16) for i in range(2)]
    stats = sb("stats", [128, 9, 4, 6])
    mv = sb("mv", [128, 9, 2])
    rstdv = sb("rstdv", [128, 9])
    nmr = sb("nmr", [128, 9])
    AR2 = sb("AR2", [128, 2, 64])
    AIp = sb("AIp", [128, 2, 32])
    AIn = sb("AIn", [128, 2, 32])
    AI2 = sb("AI2", [128, 2, 64])
    M4r = sb("M4r", [128, 2, 64])
    M4p = sb("M4p", [128, 2, 32])
    M4n = sb("M4n", [128, 2, 32])
    Hcar = sb("Hcar", [128, 2, 64])
    Hin0 = sb("Hin0", [128, 64])
    h0t = sb("h0t", [128, 8, 64])
    h0p = sb("h0p", [128, 8, 64])
    hsf = sb("hsf", [128, 8, 64])
    st1 = sb("st1", [128, 8, 64])
    st2 = sb("st2", [128, 8, 64])
    barA = sb("barA", [128, 1])
    barV = sb("barV", [128, 1])
    barG = sb("barG", [128, 1])

    PS = nc.alloc_psum_tensor("PS", [128, 8, 512], F32)
    bank = [Tl("bank%d" % i, True) for i in range(8)]

    Xsb = A0
    Hn = A1
    G = A2[:, :].rearrange("p (n d t) -> p n d t", n=9, d=16)
    Gv = A2[:, :].rearrange("p (n f) -> p n f", n=9)
    Ub = A2[:, 0:17408].bitcast(F32).rearrange("p (c g) -> p c g", g=64)
    O_A = A3[:, 0:16896].bitcast(F32).rearrange("p (k c) -> p k c", k=8)
    O_B = A3[:, :].bitcast(F32).rearrange("p (k c) -> p k c", k=8)
    Xs5 = A3[:, 0:8704].rearrange("p (g c) -> p g c", g=64)
    Hb = A3[:, 8704:17408].rearrange("p (g r c) -> p g r c", g=32, r=2)
    yF = A1
    ltmp = O_A[:, 4, 0:1024].rearrange("p (h t) -> p h t", h=8)
    Gb = A5

    T = {}

    def tl(name):
        if name not in T:
            T[name] = Tl(name)
        return T[name]

    OT_ALL = tuple(tl("o_%d_%d" % (d_, t_)) for d_ in range(8) for t_ in range(3))
    RBT = [tl("rb_%d" % i_) for i_ in range(3)]
    XS5_F = [tuple(tl("xs5_%d_%d" % (s_, f_)) for s_ in range(8)) for f_ in range(8)]
    par = P.chan("par")
    ch_x = [P.chan("chx%d" % i) for i in range(2)]
    ch_out = P.chan("chout")
    ch_w = [P.chan("chw%d" % i) for i in range(NWB)]
    ch_rb = [P.chan("chrb_%d" % i) for i in range(NRB)]
    rbstate = {"n": 0}

    def rbload(dst_view_fn, src_ap, src_tile):
        i = rbstate["n"] % NRB
        rbstate["n"] += 1
        P.add("pool", lambda e, o=dst_view_fn(RB[i]), i_=src_ap: e.dma_start(out=o, in_=i_), (src_tile,),
              (RBT[i],), ch_rb[i])
        return i

    ch_scr = P.chan("chscr")
    ch_yw = [P.chan("chyw%d" % i_) for i_ in range(2)]
    ch_pp = [P.chan("chpp%d" % i_) for i_ in range(4)]
    ch_scr2 = P.chan("chscr2")
    ch_scr3 = P.chan("chscr3")
    ch_xr = [[P.chan("chxr%d_%d" % (f_, q_)) for q_ in range(2)] for f_ in range(8)]
    ch_prep = P.chan("chprep")
    ch_xs = [P.chan("chxs%d" % i) for i in range(2)]

    def dma(q, out, in_, reads, writes, chan, extra=()):
        return P.add(q, lambda e, o=out, i=in_: e.dma_start(out=o, in_=i), reads, writes, chan, extra)

    def mm(out, lhsT, rhs, start, stop, reads, writes):
        return P.add("pe", lambda e, o=out, l=lhsT, r=rhs, s=start, t=stop:
                     e.matmul(o, l, r, start=s, stop=t), reads, writes)

    def act(out, in_, func, reads, writes, bias=None, scale=None):
        def fn(e, o=out, i=in_, f=func, b=bias, s=scale):
            kw = {}
            if b is not None:
                kw["bias"] = b
            if s is not None:
                kw["scale"] = s
            return e.activation(out=o, in_=i, func=f, **kw)
        return P.add("act", fn, reads, writes)

    def tt(eng, out, in0, in1, op, reads, writes, nosync=False):
        return P.add(eng, lambda e, o=out, a=in0, b=in1, p=op: e.tensor_tensor(out=o, in0=a, in1=b, op=p),
                     reads, writes, nosync=nosync)

    def ts(eng, out, in0, s1, s2, op0, op1, reads, writes, nosync=False):
        def fn(e, o=out, a=in0, x=s1, y=s2, p0=op0, p1=op1):
            if p1 is None:
                return e.tensor_scalar(out=o, in0=a, scalar1=x, scalar2=None, op0=p0)
            return e.tensor_scalar(out=o, in0=a, scalar1=x, scalar2=y, op0=p0, op1=p1)
        return P.add(eng, fn, reads, writes)

    def stt(out, in0, scalar, in1, op0, op1, reads, writes):
        return P.add("dve", lambda e, o=out, a=in0, s=scalar, b=in1, p0=op0, p1=op1:
                     e.scalar_tensor_tensor(out=o, in0=a, scalar=s, in1=b, op0=p0, op1=p1), reads, writes)

    def cp(eng, out, in_, reads, writes):
        if eng == "act":
            return P.add("act", lambda e, o=out, i=in_: e.activation(out=o, in_=i, func=AF.Copy), reads, writes)
        return P.add(eng, lambda e, o=out, i=in_: e.tensor_copy(out=o, in_=i), reads, writes)

    def mset(eng, ap, val, writes):
        return P.add(eng, lambda e, a=ap, v=val: e.memset(a, v), (), writes)

    wstate = {"n": 0}
    wtile = [Tl("wb%d" % i) for i in range(NWB)]

    def wload(blk):
        i = wstate["n"] % NWB
        wstate["n"] += 1
        src = WALL[blk].rearrange("p (a b) -> p a b", b=512)
        dst = WB[i][:, :].rearrange("p (a b) -> p a b", b=512)
        dma("pool", dst, src, (), (wtile[i],), ch_w[i])
        return i

    cst = tl("const")
    for dst, src in ((ident, ident_d), (trim, trim_d), (tmask, tmask_d), (colm, colm_d),
                     (rowm, rowm_d), (gpre, gpre_d), (gpost, gpost_d), (lncol, lncol_d), (bglu, bglu_d)):
        dma("sp", dst[:], src, (), (cst,), par)
    dma("sp", bdm[64:96, :], bdm_d, (), (cst,), par)
    mset("pool", bdw[:], 0.0, (tl("tabA"),))
    mset("dve", ones_f[:], 1.0, (cst,))
    mset("dve", epsc[:], EPS, (cst,))
    mset("dve", hpic[:], float(np.pi / 2), (cst,))
    mset("dve", zero1[:], 0.0, (cst,))
    mset("dve", Hin0[:], 0.0, (cst,))
    mset("dve", stats[:], 0.0, (tl("stats"),))
    mset("dve", mv[:], 1.0, (tl("mv"),))
    mset("dve", barV[:], 0.0, (tl("barV"),))
    mset("pool", barG[:], 0.0, (tl("barG"),))
    cp("dve", ones_bf[:], ones_f[:], (cst,), (cst,))
    act(barA[:], zero1[:], AF.Copy, (cst,), (tl("barA"),))

    def prep_s5(j):
        base = A5[:, :, :].rearrange("p a b -> p (a b)")
        f = base.bitcast(F32)
        off = [0]

        def carve(n, shape=None):
            v = f[:, off[0]:off[0] + n]
            off[0] += n
            return v

        aL = carve(96).rearrange("p (a g) -> p a g", a=3)
        dt_ = carve(32)
        dar = carve(32)
        ang = carve(32)
        mag = carve(32)
        magi = carve(32)
        kf = carve(32)
        ki = carve(32).bitcast(I32)
        rr = carve(32)
        m1 = carve(32)
        sn = carve(32)
        cs = carve(32)
        ab = carve(32)
        t1 = carve(32)
        t2 = carve(32)
        t3 = carve(32)
        t4 = carve(32)
        nr = carve(32)
        den = carve(32)
        cfr = carve(32)
        cfi = carve(32)
        PW = carve(17 * 64).rearrange("p (n r g) -> p n r g", n=17, r=2)
        bL = carve(1024).rearrange("p (r g k) -> p r g k", r=2, g=32)
        cL = carve(1024).rearrange("p (r g k) -> p r g k", r=2, g=32)
        assert off[0] <= 4352
        f1 = A1[:, :, :].rearrange("p a b -> p (a b)").bitcast(F32)
        bbar = f1[:, 0:1024].rearrange("p (r g k) -> p r g k", r=2, g=32)
        big1 = f1[:, 1024:1536].rearrange("p (g k) -> p g k", g=32)
        dcl = f1[:, 2048:2112]
        tp = tl("prep_small")
        tpa, tpb, tpc, tpd = tl("prep_aL"), tl("prep_bL"), tl("prep_cL"), tl("prep_dcl")
        dma("sp", aL, aL2_d[j], (tp,), (tpa,), ch_pp[0])
        dma("sp", bL, bL2_d[j], (tp,), (tpb,), ch_pp[1])
        dma("sp", cL, cL2_d[j], (tp,), (tpc,), ch_pp[2])
        dma("sp", dcl, dcol_d[j], (tp,), (tpd,), ch_pp[3])
        R = (tp, cst, tpa, tpb, tpc, tpd)
        W_ = (tp,)
        ar, ai, ldt = aL[:, 0, :], aL[:, 1, :], aL[:, 2, :]
        act(dt_, ldt, AF.Exp, R, W_)
        tt("dve", dar, dt_, ar, ALU.mult, R, W_)
        tt("dve", ang, dt_, ai, ALU.mult, R, W_)
        act(mag, dar, AF.Exp, R, W_)
        act(magi, dar, AF.Exp, R, W_, scale=-1.0)
        ts("dve", kf, ang, float(1.0 / (2 * np.pi)), None, ALU.mult, None, R, W_)
        cp("dve", ki, kf, R, W_)
        cp("dve", kf, ki, R, W_)
        stt(rr, kf, float(-2 * np.pi), ang, ALU.mult, ALU.add, R, W_)
        ts("dve", m1, rr, float(np.pi), float(-2 * np.pi), ALU.is_gt, ALU.mult, R, W_)
        tt("dve", rr, rr, m1, ALU.add, R, W_)
        ts("dve", m1, rr, float(-np.pi), float(2 * np.pi), ALU.is_lt, ALU.mult, R, W_)
        tt("dve", rr, rr, m1, ALU.add, R, W_)
        act(sn, rr, AF.Sin, R, W_)
        act(ab, rr, AF.Abs, R, W_)
        act(cs, ab, AF.Sin, R, W_, bias=hpic[:], scale=-1.0)
        mset("dve", PW[:, 8, 0, :], 1.0, W_)
        mset("dve", PW[:, 8, 1, :], 0.0, W_)
        tt("dve", PW[:, 9, 0, :], mag, cs, ALU.mult, R, W_)
        tt("dve", PW[:, 9, 1, :], mag, sn, ALU.mult, R, W_)
        tt("dve", PW[:, 7, 0, :], magi, cs, ALU.mult, R, W_)
        stt(PW[:, 7, 1, :], magi, -1.0, sn, ALU.mult, ALU.mult, R, W_)

        wt = [carve(128), carve(128), carve(128), f1[:, 1600:1728]]

        def cmulw(o, a, b, w):
            T = [t_[:, 0:w * 32].rearrange("p (a g) -> p a g", a=w) for t_ in wt]
            a_r, a_i = a[:, :, 0, :], a[:, :, 1, :]
            b_r = b[:, :, 0, :].broadcast_to([128, w, 32])
            b_i = b[:, :, 1, :].broadcast_to([128, w, 32])
            tt("dve", T[0], a_r, b_r, ALU.mult, R, W_)
            tt("dve", T[1], a_i, b_i, ALU.mult, R, W_)
            tt("dve", T[2], a_r, b_i, ALU.mult, R, W_)
            tt("dve", T[3], a_i, b_r, ALU.mult, R, W_)
            tt("dve", o[:, :, 0, :], T[0], T[1], ALU.subtract, R, W_)
            tt("dve", o[:, :, 1, :], T[2], T[3], ALU.add, R, W_)

        cmulw(PW[:, 10:11], PW[:, 9:10], PW[:, 9:10], 1)
        cmulw(PW[:, 6:7], PW[:, 7:8], PW[:, 7:8], 1)
        cmulw(PW[:, 11:13], PW[:, 9:11], PW[:, 10:11], 2)
        cmulw(PW[:, 5:3:-1], PW[:, 7:5:-1], PW[:, 6:7], 2)
        cmulw(PW[:, 13:17], PW[:, 9:13], PW[:, 12:13], 4)
        cmulw(PW[:, 3::-1], PW[:, 7:3:-1], PW[:, 4:5], 4)
        for (src_n, tr, tp_, tn_) in ((16, AR2, AIp, AIn), (4, M4r, M4p, M4n)):
            trv = tr[:, j, :].rearrange("p (g r) -> p g r", r=2)
            cp("dve", trv[:, :, 0], PW[:, src_n, 0, :], R, (cst,))
            cp("dve", trv[:, :, 1], PW[:, src_n, 0, :], R, (cst,))
            cp("dve", tp_[:, j, :], PW[:, src_n, 1, :], R, (cst,))
            ts("dve", tn_[:, j, :], PW[:, src_n, 1, :], -1.0, None, ALU.mult, None, R, (cst,))
        ai2v = AI2[:, j, :].rearrange("p (g r) -> p g r", r=2)
        cp("dve", ai2v[:, :, 0], AIn[:, j, :], (cst,), (cst,))
        cp("dve", ai2v[:, :, 1], AIp[:, j, :], (cst,), (cst,))
        ts("dve", nr, PW[:, 9, 0, :], -1.0, None, ALU.add, None, R, W_)
        ni = PW[:, 9, 1, :]
        tt("dve", t1, ar, ar, ALU.mult, R, W_)
        tt("dve", t2, ai, ai, ALU.mult, R, W_)
        tt("dve", den, t1, t2, ALU.add, R, W_)
        P.add("dve", lambda e, o=den, i=den: e.reciprocal(out=o, in_=i), R, W_)
        tt("dve", t1, nr, ar, ALU.mult, R, W_)
        tt("dve", t2, ni, ai, ALU.mult, R, W_)
        tt("dve", t1, t1, t2, ALU.add, R, W_)
        tt("dve", cfr, t1, den, ALU.mult, R, W_)
        tt("dve", t1, ni, ar, ALU.mult, R, W_)
        tt("dve", t2, nr, ai, ALU.mult, R, W_)
        tt("dve", t1, t1, t2, ALU.subtract, R, W_)
        tt("dve", cfi, t1, den, ALU.mult, R, W_)

        def bc16(v):
            return v.unsqueeze(2).broadcast_to([128, 32, 16])

        tb = tl("prep_bbar")
        RB = (tp, cst, tb, tpa, tpb, tpc, tpd)
        WB_ = (tb,)
        tt("dve", bbar[:, 0], bc16(cfr), bL[:, 0], ALU.mult, RB, WB_)
        tt("dve", big1, bc16(cfi), bL[:, 1], ALU.mult, RB, WB_)
        tt("dve", bbar[:, 0], bbar[:, 0], big1, ALU.subtract, RB, WB_)
        tt("dve", bbar[:, 1], bc16(cfr), bL[:, 1], ALU.mult, RB, WB_)
        tt("dve", big1, bc16(cfi), bL[:, 0], ALU.mult, RB, WB_)
        tt("dve", bbar[:, 1], bbar[:, 1], big1, ALU.add, RB, WB_)

        PSM = f1[:, 1024:1568].rearrange("p (n g) -> p n g", n=17)
        tt("dve", PSM, PW[:, :, 0, :], PW[:, :, 1, :], ALU.add, (tp, tl("prep_bbar")), (tp, tl("prep_bbar")))
        for hf in range(2):
            g0 = 16 * hf
            a2f = A2[:, :].bitcast(F32)
            WcL = a2f[:, 0:4096].rearrange("p (g r s k) -> p g r s k", g=16, r=2, s=8)
            WnL = a2f[:, 4096:8192].rearrange("p (g r s k) -> p g r s k", g=16, r=2, s=8)
            a3f = A3[:, :].bitcast(F32)
            VL = a3f[:, 0:4096].rearrange("p (g r s k) -> p g r s k", g=16, r=2, s=8)
            tmps = {"dve": (a3f[:, 4096:4352].rearrange("p (g k) -> p g k", g=16),
                            a3f[:, 4352:4608].rearrange("p (g k) -> p g k", g=16)),
                    "pool": (a3f[:, 6656:6912].rearrange("p (g k) -> p g k", g=16),
                             a3f[:, 6912:7168].rearrange("p (g k) -> p g k", g=16))}
            VLm = a3f[:, 4608:6656].rearrange("p (g x r m) -> p g x r m", g=4, x=2, r=2)
            a0 = A0[:, :, :].rearrange("p a b -> p (a b)")
            Wst = a0[:, 0:4096].bitcast(BF16).rearrange("p (g x r m) -> p g x r m", g=16, x=2, r=2)
            Vst = a0[:, 4096:8192].bitcast(BF16).rearrange("p (g x r m) -> p g x r m", g=16, x=2, r=2)
            M0st = f1[:, 2176:4224].bitcast(BF16).rearrange("p (g m) -> p g m", g=32)
            tw = tl("prep_big")
            tstg = tl("prep_stg")

            def bcg(v):
                return v.unsqueeze(2).broadcast_to([128, 16, 16])

            sums = a3f[:, 7168:8192].rearrange("p (q g k) -> p q g k", q=4, g=16)
            tsm = tl("prep_sums")
            br = bbar[:, 0, g0:g0 + 16, :]
            bi = bbar[:, 1, g0:g0 + 16, :]
            crr = cL[:, 0, g0:g0 + 16, :]
            cii = cL[:, 1, g0:g0 + 16, :]
            tt("dve", sums[:, 0], br, bi, ALU.add, (tp, tb, tpb, tpc), (tsm,))
            tt("dve", sums[:, 1], bi, br, ALU.subtract, (tp, tb, tpb, tpc), (tsm,))
            tt("dve", sums[:, 2], crr, cii, ALU.add, (tp, tb, tpb, tpc), (tsm,))
            tt("dve", sums[:, 3], crr, cii, ALU.subtract, (tp, tb, tpb, tpc), (tsm,))

            def build(name, dst, n_of_s, src_r, xsum, xdif, kind):
                tiles = []
                for s_ in range(8):
                    n = n_of_s(s_) + 8
                    pr = bcg(PW[:, n, 0, g0:g0 + 16])
                    pi = bcg(PW[:, n, 1, g0:g0 + 16])
                    psm = bcg(PSM[:, n, g0:g0 + 16])
                    eng = "dve"
                    tA, tB = tmps[eng]
                    tmt = tl("prep_tmp_" + eng)
                    mt = tl("prep_%s_%d" % (name, s_))
                    tiles.append(mt)
                    RW = (tp, cst, tb, tsm, tmt, mt, tpb, tpc)
                    WW = (mt, tmt)
                    d0 = dst[:, :, 0, s_, :]
                    d1 = dst[:, :, 1, s_, :]
                    ns_ = BUILD_NOSYNC
                    tt(eng, d1, psm, src_r, ALU.mult, RW, WW, nosync=ns_)
                    tt(eng, tA, pi, xsum, ALU.mult, RW, WW, nosync=ns_)
                    tt(eng, d0, d1, tA, ALU.subtract, RW, WW, nosync=ns_)
                    tt(eng, tA, pr, xdif, ALU.mult, RW, WW, nosync=ns_)
                    if kind == "w":
                        tt(eng, d1, d1, tA, ALU.add, RW, WW, nosync=ns_)
                    else:
                        tt(eng, d1, tA, d1, ALU.subtract, RW, WW, nosync=ns_)
                return tuple(tiles)

            tWc = build("wc", WcL, lambda s_: 7 - s_, br, sums[:, 0], sums[:, 1], "w")
            tWn = build("wn", WnL, lambda s_: -(s_ + 1), br, sums[:, 0], sums[:, 1], "w")
            tV = build("v", VL, lambda s_: s_ + 1, crr, sums[:, 2], sums[:, 3], "v")
            tvst = tl("prep_vst")
            twst = tl("prep_wst")
            tm0 = tl("prep_m0st")
            twb = tl("prep_wnlb")
            for x in range(2):
                act(Vst[:, :, x].rearrange("p g r m -> p g (r m)"),
                    VL.rearrange("p g r s k -> p g (r s k)"), AF.Copy, tV + (cst,), (tvst,), scale=rowm[:, x:x + 1])
            WnLb = a3f[:, 4608:6656].bitcast(BF16).rearrange("p (g r m) -> p g r m", g=16, r=2)
            act(WnLb.rearrange("p g r m -> p (g r m)"), WnL.rearrange("p g r s k -> p (g r s k)"), AF.Copy,
                tWn, (twb,))
            for gl in range(16):
                for ri in range(2):
                    bk = (gl * 2 + ri) % 8
                    P.add("pe", lambda e, o=PS[:, bk, 0:128], i=WcL[:, gl, ri].rearrange("p s k -> p (s k)"):
                          e.transpose(o, i, ident[:]), tWc + (cst,), (bank[bk],))
                    tt("dve", Wst[:, gl, :, ri, :],
                       PS[:, bk, 0:128].unsqueeze(1).broadcast_to([128, 2, 128]), colm[:], ALU.mult,
                       (bank[bk], cst), (twst,))
            for gl in range(16):
                for x in range(2):
                    bk = (gl * 2 + x) % 8
                    gg = 2 * (g0 + gl) + x
                    for ri in range(2):
                        mm(PS[:, bk, 0:128], WnLb[:, gl, ri, :], Vst[:, gl, x, ri, :], ri == 0, ri == 1,
                           (twb, tvst), (bank[bk],))
                    tt("dve", tmpAB[hf][:], PS[:, bk, 0:128], tmask[:], ALU.mult, (bank[bk], cst, tl("tmpAB")),
                       (tl("tmpAB"),))
                    stt(M0st[:, gl * 2 + x, :], ident[:], dcl[:, gg:gg + 1], tmpAB[hf][:], ALU.mult, ALU.add,
                        (tp, tpd, cst, tl("tmpAB")), (tm0,))
            dma("sp", SW[j, g0:g0 + 16].rearrange("g p (x r m) -> p g x r m", x=2, r=2), Wst, (twst,),
                (tl("SW"),), ch_prep)
            dma("sp", SMV[j, 2 * g0:2 * g0 + 32, :, 0:128].rearrange("g p m -> p g m"), M0st, (tm0,),
                (tl("SMV"),), ch_prep)
            dma("sp", SMV[j, 2 * g0:2 * g0 + 32, :, 128:384].rearrange("(g x) p (r m) -> p g x r m", x=2, r=2),
                Vst, (tvst,), (tl("SMV"),), ch_prep)

    tmpAB = [sb("tmpAB%d" % i, [128, 128]) for i in range(2)]

    has_b = "B" in layers
    bar_ops = []
    if has_b:
        for j in range(2):
            prep_s5(j)
        tiny = {
            "act": lambda e: e.activation(out=barA[:], in_=zero1[:], func=AF.Copy),
            "dve": lambda e: e.memset(barV[:], 0.0),
            "pool": lambda e: e.memset(barG[:], 0.0),
        }
        bar_ops = P.barrier(tiny)

    mset("pool", A1[:, :, :], 0.0, tuple(tl("hn_%d_%d" % (kc_, t_)) for kc_ in range(8) for t_ in range(3)))

    ring = {"rst": 0, "sq": 0, "tmpb": 0, "mmb": 0, "xs": 0}
    rst_t = [Tl("rst%d" % i) for i in range(2)]
    sq_t = [Tl("sq%d" % i) for i in range(2)]
    tmpb_t = [Tl("tmpb%d" % i) for i in range(2)]
    xs_t = [Tl("xs%d" % i) for i in range(2)]
    STATB = [3, 4, 5]

    def nxt(name, n):
        i = ring[name] % n
        ring[name] += 1
        return i

    def rstd_from_bank(bk, n):
        i = nxt("rst", 2)
        act(rst[i][:, 0:n], PS[:, bk, 0:n], AF.Ln, (bank[bk], cst), (rst_t[i],), bias=epsc[:], scale=1.0 / D)
        act(rst[i][:, 0:n], rst[i][:, 0:n], AF.Exp, (rst_t[i],), (rst_t[i],), scale=-0.5)
        return i

    def xcols(name, kc, c0, n):
        return tl("%s_%d_%d" % (name, kc, c0))

    def prenorm(layer, kind):
        gcol = gpre[:, layer * 8:(layer + 1) * 8]
        if kind == "B":
            for kc in range(8):
                v = Hn[:, kc, :].rearrange("p (s c) -> p s c", c=CB)[:, 0:4, 128:136]
                mset("pool", v, 0.0, tuple(tl("hn_%d_%d" % (kc, t2)) for t2 in range(3)))
        for ti, (c0, n) in enumerate(TILES_A):
            bk = STATB[ti]
            for kc in range(8):
                i = nxt("sq", 2)
                xsq = Xsb[:, kc, c0:c0 + n]
                sqo = sqr[i][:, 0:n]
                if kind == "B":
                    if ti < 2:
                        xsq = xsq.rearrange("p (c s) -> p s c", s=8)
                        sqo = sqo.rearrange("p (s c) -> p s c", s=8)
                    else:
                        xsq = xsq.rearrange("p (q i) -> p i q", i=4)
                        sqo = sqo.rearrange("p (i q) -> p i q", i=4)
                act(sqo, xsq, AF.Square, (tl("x_%d_%d" % (kc, ti)),), (sq_t[i],))
                mm(PS[:, bk, 0:n], ones_bf[:], sqr[i][:, 0:n], kc == 0, kc == 7, (sq_t[i], cst), (bank[bk],))
            r = rstd_from_bank(bk, n)
            for kc in range(8):
                xin = Xsb[:, kc, c0:c0 + n]
                if kind == "A":
                    stt(Hn[:, kc, c0:c0 + n], xin, gcol[:, kc:kc + 1], rst[r][:, 0:n], ALU.mult, ALU.mult,
                        (tl("x_%d_%d" % (kc, ti)), rst_t[r], cst), (tl("hn_%d_%d" % (kc, ti)),))
                else:
                    hv = Hn[:, kc, :].rearrange("p (s c) -> p s c", c=CB)
                    if ti < 2:
                        ov = hv[:, :, 64 * ti:64 * ti + 64]
                        iv = xin.rearrange("p (c s) -> p s c", s=8)
                        rv = rst[r][:, 0:n].rearrange("p (s c) -> p s c", s=8)
                    else:
                        ov = hv[:, 4:8, 128:136]
                        iv = xin.rearrange("p (q i) -> p i q", i=4)
                        rv = rst[r][:, 0:n].rearrange("p (i q) -> p i q", i=4)
                    stt(ov, iv, gcol[:, kc:kc + 1], rv, ALU.mult, ALU.mult,
                        (tl("x_%d_%d" % (kc, ti)), rst_t[r], cst),
                        tuple(tl("hn_%d_%d" % (kc, t2)) for t2 in range(3)))

    def out_stage_evac(bk, dmc, ti, c0, n, Obuf):
        cp("act", Obuf[:, dmc, c0:c0 + n], PS[:, bk, 0:n], (bank[bk],), (tl("o_%d_%d" % (dmc, ti)),))
        i = nxt("sq", 2)
        act(sqr[i][:, 0:n], PS[:, bk, 0:n], AF.Square, (bank[bk],), (sq_t[i],))
        sb_ = STATB[ti]
        flush_stat()
        pend_stat.append((PS[:, sb_, 0:n], sqr[i][:, 0:n], dmc == 0, dmc == 7, (sq_t[i], cst), (bank[sb_],)))

    pend_stat = []

    def flush_stat():
        while pend_stat:
            o_, r_, st_, sp_, rd_, wr_ = pend_stat.pop(0)
            mm(o_, ones_bf[:], r_, st_, sp_, rd_, wr_)

    def postnorm(layer, kind, Obuf, tiles):
        gcol = gpost[:, layer * 8:(layer + 1) * 8]
        for ti, (c0, n) in enumerate(tiles):
            r = rstd_from_bank(STATB[ti], n)
            for dmc in range(8):
                stt(Obuf[:, dmc, c0:c0 + n], Obuf[:, dmc, c0:c0 + n], gcol[:, dmc:dmc + 1], rst[r][:, 0:n],
                    ALU.mult, ALU.mult, (tl("o_%d_%d" % (dmc, ti)), rst_t[r], cst), (tl("o_%d_%d" % (dmc, ti)),))
            if kind == "A":
                for dmc in range(8):
                    tt("dve", Xsb[:, dmc, c0:c0 + n], Xsb[:, dmc, c0:c0 + n], Obuf[:, dmc, c0:c0 + n], ALU.add,
                       (tl("o_%d_%d" % (dmc, ti)), tl("x_%d_%d" % (dmc, ti))), (tl("x_%d_%d" % (dmc, ti)),))
        if kind == "B":
            for hlf in range(3):
                for dmc in range(8):
                    ov = Obuf[:, dmc, :].rearrange("p (s c) -> p s c", c=CB)
                    allo = tuple(tl("o_%d_%d" % (dmc, t2)) for t2 in range(3))
                    if hlf < 2:
                        xv = Xsb[:, dmc, 512 * hlf:512 * hlf + 512].rearrange("p (c s) -> p s c", s=8)
                        tt("dve", xv, xv, ov[:, :, 64 * hlf:64 * hlf + 64], ALU.add,
                           allo + (tl("x_%d_%d" % (dmc, hlf)),), (tl("x_%d_%d" % (dmc, hlf)),))
                    else:
                        xv = Xsb[:, dmc, 1024:1056].rearrange("p (q i) -> p i q", i=4)
                        tt("dve", xv, xv, ov[:, 4:8, 128:136], ALU.add, allo + (tl("x_%d_2" % dmc),),
                           (tl("x_%d_2" % dmc),))

    MRING = (0, 1, 2, 6, 7)

    def mbank():
        return MRING[nxt("mmb", 5)]

    def layer_a(layer, ps):
        j = layer // 2
        wb0 = j * NBLK_PER_J
        tb = tl("tabA")
        lt2 = tl("ltmp2")
        ltmp2 = O_A[:, 0, 0:1024].rearrange("p (h t) -> p h t", h=8)
        ltmp3 = O_A[:, 1, 0:256].rearrange("p (h t) -> p h t", h=8)[64:96]
        dma("sp", ltmp, wsT_d[j], (), (tl("ltmp"),) + OT_ALL, par)
        dma("sp", ltmp2, bsB_d[j], (), (lt2,) + OT_ALL, par)
        dma("sp", ltmp3, wsrep_d[j], (), (lt2,) + OT_ALL, par)

        def tables_dve1():
            tt("dve", wsTm[:], ltmp, trim[:].unsqueeze(1).broadcast_to([128, 8, 128]), ALU.mult,
               (tl("ltmp"), cst), (tb,))
            tt("dve", ltmp, ltmp, trim[:].unsqueeze(1).broadcast_to([128, 8, 128]), ALU.mult,
               (tl("ltmp"), cst), (tl("ltmp"),))

        def tables_compute():
            for hh in range(2):
                mm(PS[:, 3 + hh, :], ones_f[:], ltmp[:, 4 * hh:4 * hh + 4, :].rearrange("p a b -> p (a b)"),
                   True, True, (tl("ltmp"), cst), (bank[3 + hh],))
            for dc in range(16):
                hh = dc // 2
                bk = 3 + hh // 4
                stt(ctab[:, dc, :], PS[:, bk, (hh % 4) * 128:(hh % 4) * 128 + 128], lncol[:, j, 1, dc:dc + 1],
                    ltmp2[:, hh, :], ALU.mult, ALU.add, (bank[bk], lt2, cst), (tb,))
            tt("dve", bdw[64:96], ltmp3, bdm[64:96, :].unsqueeze(1).broadcast_to([32, 8, 32]), ALU.mult,
               (lt2, cst), (tb,))

        prenorm(layer, "A")

        def hn_reads(ti):
            return tuple(tl("hn_%d_%d" % (kc, ti)) for kc in range(8))

        wl = [wload(wb0 + 4)]
        for blk in range(4):
            if blk < 3:
                wl.append(wload(wb0 + 4 + blk + 1))
            wi = wl[blk]
            wv = WB[wi][:, :].rearrange("p (a b) -> p a b", b=512)
            for n in range(9):
                if blk == 0 and n == 4:
                    tables_dve1()
                if blk == 1 and n == 0:
                    tables_compute()
                M = 128
                c0 = 128 * n if n < 8 else 960
                bk = mbank()
                for kc in range(8):
                    mm(PS[0:M, bk, :], Hn[:, kc, c0:c0 + M], wv[:, kc, :], kc == 0, kc == 7,
                       ((tl("hn_%d_%d" % (kc, n // 4)),) if n < 8 else (tl("hn_%d_1" % kc), tl("hn_%d_2" % kc)))
                       + (wtile[wi],), (bank[bk],))
                cp("act", Gv[0:M, n, blk * 512:(blk + 1) * 512], PS[0:M, bk, :], (bank[bk],),
                   tuple(tl("g_%d_%d" % (n, dc)) for dc in range(4 * blk, 4 * blk + 4)))
                P.add("dve", lambda e, o=stats[0:M, n, blk, :], i=Gv[0:M, n, blk * 512:(blk + 1) * 512]:
                      e.bn_stats(out=o, in_=i), tuple(tl("g_%d_%d" % (n, dc)) for dc in range(4 * blk, 4 * blk + 4)),
                      (tl("stats"),))
        for n in range(9):
            M = 128
            P.add("dve", lambda e, o=mv[0:M, n, :], i=stats[0:M, n, :, :].rearrange("p a b -> p (a b)"):
                  e.bn_aggr(out=o, in_=i), (tl("stats"),), (tl("mv"),))
        act(rstdv[:], mv[:, :, 1], AF.Sqrt, (tl("mv"), cst), (tl("mv"),), bias=epsc[:], scale=1.0)
        P.add("dve", lambda e: e.reciprocal(out=rstdv[:], in_=rstdv[:]), (tl("mv"),), (tl("mv"),))
        stt(nmr[:], mv[:, :, 0], -1.0, rstdv[:], ALU.mult, ALU.mult, (tl("mv"),), (tl("mv"),))
        for n in range(9):
            M = 128
            Nt = 128 if n < 8 else 32
            gts = tuple(tl("g_%d_%d" % (n, dc)) for dc in range(16))
            act(Gv[0:M, n, :], Gv[0:M, n, :], AF.Identity, gts + (tl("mv"),), gts,
                bias=nmr[0:M, n:n + 1], scale=rstdv[0:M, n:n + 1])
            if n == 8:
                cvt = tl("cvt")
                ofl = A3[:, :].bitcast(F32)
                cvs = ofl[64:96, 2112:4160]
                cvg = ofl[64:96, 4160:6208]
                cvb = ofl[64:96, 6208:8256]
                dma("sp", cvg, lnbc_d[j, 0], (), (cvt, tl("ltmp"), lt2) + OT_ALL, par)
                dma("sp", cvb, lnbc_d[j, 1], (), (cvt, tl("ltmp"), lt2) + OT_ALL, par)
                tt("dve", cvs, Gv[64:96, 8, :], cvg, ALU.mult, gts + (cvt,), (cvt,))
                tt("dve", cvs, cvs, cvb, ALU.add, (cvt,), (cvt,))
                dma("sp", cv[j, ps], cvs, (cvt,) + OT_ALL, (tl("cv_out"),), ch_out)
            for dc in range(16):
                bk = 4 + dc // 4
                rhs = wsTm[:, dc // 2, :] if n < 8 else bdw[:, dc // 2, :]
                mm(PS[:, bk, (dc % 4) * 128:(dc % 4) * 128 + Nt], G[0:M, n, dc, :], rhs, True, True,
                   (tl("g_%d_%d" % (n, dc)), tb), (bank[bk],))
            if n < 8:
                for b4 in range(4):
                    bk = 4 + b4
                    pv = PS[:, bk, :].rearrange("p (d t) -> p d t", d=4)
                    gB = lncol[:, j, 0, 4 * b4:4 * b4 + 4].unsqueeze(2).broadcast_to([128, 4, 128])
                    tt("dve", pv, pv, gB, ALU.mult, (bank[bk], cst), (bank[bk],))
                    tt("dve", G[:, n, 4 * b4:4 * b4 + 4, :], pv, ctab[:, 4 * b4:4 * b4 + 4, :], ALU.add,
                       (bank[bk], tb), tuple(tl("g_%d_%d" % (n, dc_)) for dc_ in range(4 * b4, 4 * b4 + 4)))
            for dc in range(16 if n == 8 else 0):
                bk = 4 + dc // 4
                pin = PS[:, bk, (dc % 4) * 128:(dc % 4) * 128 + Nt]
                if n < 8:
                    pass
                else:
                    stt(G[:, 8, dc, 0:32].rearrange("p (q i) -> p q i", i=4),
                        pin.rearrange("p (q i) -> p q i", i=4), lncol[:, j, 0, dc:dc + 1],
                        ctab[:, dc, 0:4].unsqueeze(1).broadcast_to([128, 8, 4]), ALU.mult, ALU.add,
                        (bank[bk], tb, cst), (tl("g_8_%d" % dc),))

        def gview(ti, dc):
            if ti < 2:
                return G[:, 4 * ti:4 * ti + 4, dc, :]
            return G[:, 8, dc, 0:32]

        def gtiles(ti, dc):
            if ti < 2:
                return tuple(tl("g_%d_%d" % (n, dc)) for n in range(4 * ti, 4 * ti + 4))
            return (tl("g_8_%d" % dc),)

        for stage, b0 in (("u", 0), ("z", 8)):
            wl = [wload(wb0 + b0)]
            for blk in range(4):
                if blk < 3:
                    wl.append(wload(wb0 + b0 + blk + 1))
                wi = wl[blk]
                wv = WB[wi][:, :].rearrange("p (a b) -> p a b", b=512)
                for dcl in range(4):
                    dc = 4 * blk + dcl
                    for ti, (c0, n) in enumerate(TILES_A):
                        bk = mbank()
                        for kc in range(8):
                            mm(PS[:, bk, 0:n], wv[:, kc, dcl * 128:(dcl + 1) * 128], Hn[:, kc, c0:c0 + n],
                               kc == 0, kc == 7, (tl("hn_%d_%d" % (kc, ti)), wtile[wi]), (bank[bk],))
                        pv = PS[:, bk, 0:n]
                        if ti < 2:
                            pv = pv.rearrange("p (a b) -> p a b", b=128)
                        gt = gtiles(ti, dc)
                        if stage == "u":
                            tt("dve", gview(ti, dc), pv, gview(ti, dc), ALU.mult, (bank[bk],) + gt, gt)
                        else:
                            i = nxt("tmpb", 2)
                            act(tmpb[i][:, 0:n], PS[:, bk, 0:n], AF.Silu, (bank[bk],), (tmpb_t[i],))
                            tv = tmpb[i][:, 0:n]
                            if ti < 2:
                                tv = tv.rearrange("p (a b) -> p a b", b=128)
                            tt("dve", gview(ti, dc), tv, gview(ti, dc), ALU.mult, (tmpb_t[i],) + gt, gt)
        wl = [wload(wb0 + 12)]
        for blk in range(4):
            if blk < 3:
                wl.append(wload(wb0 + 12 + blk + 1))
            wi = wl[blk]
            wv = WB[wi][:, :].rearrange("p (a b) -> p a b", b=256)
            for dml in range(2):
                dmc = 2 * blk + dml
                for ti, (c0, n) in enumerate(TILES_A):
                    bk = mbank()
                    for dc in range(16):
                        mm(PS[:, bk, 0:n], wv[:, dc, dml * 128:(dml + 1) * 128], gview(ti, dc), dc == 0, dc == 15,
                           gtiles(ti, dc) + (wtile[wi],), (bank[bk],))
                    out_stage_evac(bk, dmc, ti, c0, n, O_A)
        flush_stat()
        postnorm(layer, "A", O_A, TILES_A)

    def layer_b(layer, ps):
        j = layer // 2
        wb0 = j * NBLK_PER_J + 16
        prenorm(layer, "B")
        for nm_, t_ in list(T.items()):
            if nm_.startswith(("xs5_", "Yd_", "Xd_", "yf_", "y_")) and not nm_.startswith("y_out"):
                if t_.w is not None and t_.w.chan is not None:
                    t_.w = None
                for k_ in [k_ for k_ in t_.r if isinstance(k_, int)]:
                    del t_.r[k_]
        tsx = tl("s5small")
        dma("sp", h0t[:], h0_d[j, ps], (), (tsx,), par)

        def c3(v64):
            return v64.unsqueeze(1).broadcast_to([128, 8, 64])

        def c3h(v32):
            return v32.unsqueeze(1).broadcast_to([128, 8, 32])

        def cstep(dst, src, tr, tp_, tn_, addend, eng="dve"):
            sv = src.rearrange("p q (g r) -> p q g r", r=2)
            t2v = st2[:].rearrange("p q (g r) -> p q g r", r=2)
            tt(eng, st1[:], src, c3(tr), ALU.mult, (tsx, cst), (tsx,))
            tt(eng, t2v[:, :, :, 0], sv[:, :, :, 1], c3h(tn_), ALU.mult, (tsx, cst), (tsx,))
            tt(eng, t2v[:, :, :, 1], sv[:, :, :, 0], c3h(tp_), ALU.mult, (tsx, cst), (tsx,))
            tt(eng, st1[:], st1[:], st2[:], ALU.add, (tsx,), (tsx,))
            if addend is None:
                cp(eng, dst, st1[:], (tsx,), (tsx,))
            else:
                tt(eng, dst, st1[:], addend, ALU.add, (tsx, tl("ub_a"), tl("ub_d")), (tsx,))

        cstep(h0p[:], h0t[:], M4r[:, j, :], M4p[:, j, :], M4n[:, j, :], None)

        Xdv = Xd.rearrange("(g k) (s c) -> s k g c", k=16, c=CB)

        def xreadback(f_):
            for s8 in range(8):
                dma("sp", Xs5[16 * s8:16 * s8 + 16, 8 * f_:8 * f_ + 8, :],
                    Xdv[s8][:, 8 * f_:8 * f_ + 8, :],
                    (tl("Xd_%d" % f_),), (tl("xs5_%d_%d" % (s8, f_)),) + (OT_ALL if (s8 == 0 and f_ == 0) else ()),
                    ch_xr[f_][s8 % 2])

        wl = [wload(wb0 + 0), wload(wb0 + 1)]
        for fc in range(8):
            wi = wl[fc // 4]
            wv = WB[wi][:, :].rearrange("p (a b) -> p a b", b=512)
            dcl = fc % 4
            xi = nxt("xs", 2)
            for ti, (c0, n) in enumerate(TILES_B):
                bk = mbank()
                for kc in range(8):
                    mm(PS[:, bk, 0:n], wv[:, kc, dcl * 128:(dcl + 1) * 128], Hn[:, kc, c0:c0 + n],
                       kc == 0, kc == 7, (tl("hn_%d_%d" % (kc, ti)), wtile[wi]), (bank[bk],))
                cp("act", xs[xi][:, c0:c0 + n], PS[:, bk, 0:n], (bank[bk],), (xs_t[xi],))
            dma("sp", Xd[fc * 128:(fc + 1) * 128, :], xs[xi][:], (xs_t[xi],), (tl("Xd_%d" % fc),), ch_xs[xi])
            if fc >= 1:
                xreadback(fc - 1)
        xreadback(7)
        cp("act", Hb[:, :, :, 128:136].rearrange("p g r c -> p (g r) c"),
           h0p[:].rearrange("p q g -> p g q"), (tsx,), (tl("hb_5"),))
        ub = tl("ub_all")
        UBQ = [tl("ub_q%d" % q_) for q_ in range(4)]
        UBS = tl("ub_s")
        batches1 = [(g0_, min(3, 32 - g0_)) for g0_ in range(0, 32, 3)]

        def load1(b_):
            g0_, n_ = batches1[b_]
            return rbload(lambda t_, n_=n_: t_[:, 0:n_ * 512].rearrange("p (a m) -> p a m", a=n_),
                          SW[j, g0_:g0_ + n_].rearrange("a p m -> p a m"), tl("SW"))

        pend = [load1(0), load1(1)]
        for b_, (g0_, n_) in enumerate(batches1):
            if b_ + 2 < len(batches1):
                pend.append(load1(b_ + 2))
            si = pend[b_]
            for a_ in range(n_):
                gp = g0_ + a_
                wv = RB[si][:, a_ * 512:(a_ + 1) * 512].rearrange("p (x r m) -> p x r m", x=2, r=2)
                bk = mbank()
                pu = PS[:, bk, 0:272].rearrange("p (r c) -> p r c", r=2)
                for ri in range(2):
                    for x in range(2):
                        mm(pu[:, ri, :], wv[:, x, ri, :], Xs5[:, 2 * gp + x, :], x == 0, x == 1,
                           (RBT[si],) + XS5_F[gp // 4], (bank[bk],))
                cp("act" if gp % 2 == 0 else "dve", Ub[:, :, 2 * gp:2 * gp + 2].rearrange("p c r -> p r c"), pu,
                   (bank[bk],), (tl("ub_a" if gp % 2 == 0 else "ub_d"),))
        wl = [wload(wb0 + 2), wload(wb0 + 3)]
        for fc in range(8):
            wi = wl[fc // 4]
            wv = WB[wi][:, :].rearrange("p (a b) -> p a b", b=512)
            dcl = fc % 4
            for ti, (c0, n) in enumerate(TILES_B):
                bk = mbank()
                for kc in range(8):
                    mm(PS[:, bk, 0:n], wv[:, kc, dcl * 128:(dcl + 1) * 128], Hn[:, kc, c0:c0 + n],
                       kc == 0, kc == 7, (tl("hn_%d_%d" % (kc, ti)), wtile[wi]), (bank[bk],))
                act(Gb[:, fc, c0:c0 + n], PS[:, bk, 0:n], AF.Silu, (bank[bk],), (tl("gb_%d_%d" % (fc, ti)),))
        hin = Hin0[:] if ps == 0 else Hcar[:, j, :]
        tr = AR2[:, j, :]
        tpp = AIp[:, j, :]
        tnn = AIn[:, j, :]
        sc1 = st1[:, 0, :]
        sc2 = st2[:, 0, :].rearrange("p (g r) -> p g r", r=2)
        scn = tl("scan")
        hbv = Hb.rearrange("p g r c -> p c g r")
        cp("act", hbv[:, 0, :, :], hin.rearrange("p (g r) -> p g r", r=2), (cst, tsx, tl("hcar")), (tl("hb_0"),))
        for c in range(128):
            prev = hin if c == 0 else Ub[:, c - 1, :]
            pv = prev.rearrange("p (g r) -> p g r", r=2)
            uq = UBQ[c // 32]
            rd = (uq, UBQ[max(c - 1, 0) // 32], scn, cst, tsx, tl("hcar"), tl("ub_a"), tl("ub_d"))
            ns = SCAN_NOSYNC and c > 0
            tt("dve", sc1, prev, tr, ALU.mult, rd, (scn,), nosync=ns)
            tt("dve", sc2, pv[:, :, ::-1], AI2[:, j, :].rearrange("p (g r) -> p g r", r=2), ALU.mult, rd, (scn,), nosync=ns)
            tt("dve", Ub[:, c, :], Ub[:, c, :], sc1, ALU.add, rd, (uq,), nosync=ns)
            tt("dve", Ub[:, c, :], Ub[:, c, :], st2[:, 0, :], ALU.add, rd, (uq,), nosync=ns)
            if c % 32 == 31:
                q4 = c // 32
                ln_ = 32 if q4 < 3 else 31
                cp("act", Hb[:, :, :, 1 + 32 * q4:1 + 32 * q4 + ln_].rearrange("p g r c -> p (g r) c"),
                   Ub[:, 32 * q4:32 * q4 + ln_, :].rearrange("p c g -> p g c"), (uq,), (tl("hb_%d" % (q4 + 1)),))
        cp("dve", Hcar[:, j, :], Ub[:, 127, :], (UBQ[3],), (tl("hcar"),))
        if ps == npass - 1:
            dma("sp", stp[j], Hcar[:, j, :], (tl("hcar"),), (tl("stp_out_%d" % j),), ch_out)
        wb1 = j * NBLK_PER_J + 20
        wl_glu1 = [wload(wb1 + 0), wload(wb1 + 1)]
        def load3(b_):
            return rbload(lambda t_: t_[:, :].rearrange("p (a m) -> p a m", a=4),
                          SMV[j, 4 * b_:4 * b_ + 4].rearrange("a p m -> p a m"), tl("SMV"))

        pend = [load3(0), load3(1)]
        for g in range(64):
            b_ = g // 4
            if g % 4 == 0 and b_ + 2 < 16:
                pend.append(load3(b_ + 2))
            si = pend[b_]
            mv_ = RB[si][:, (g % 4) * 384:(g % 4) * 384 + 384]
            bk = mbank()
            py = PS[:, bk, 0:CB]
            mm(py, mv_[:, 0:128], Xs5[:, g, :], True, False, (RBT[si], tl("y_%d" % g)) + XS5_F[g // 8],
               (bank[bk],))
            for ri in range(2):
                mm(py, mv_[:, 128 + 128 * ri:256 + 128 * ri], Hb[:, g // 2, ri, :], False, ri == 1,
                   (RBT[si],) + tuple(tl("hb_%d" % q) for q in range(6)), (bank[bk],))
            act(Xs5[:, g, :], py, AF.Gelu_apprx_tanh, (bank[bk],), (tl("y_%d" % g),))
            if g % 8 == 7:
                fc = g // 8
                Ydv = Yd.rearrange("(g k) (t c) -> t k g c", k=16, c=CB)
                ys = tuple(tl("y_%d" % g_) for g_ in range(8 * fc, 8 * fc + 8))
                for t8 in range(8):
                    dma("sp", Ydv[t8][:, 8 * fc:8 * fc + 8, :], Xs5[16 * t8:16 * t8 + 16, 8 * fc:8 * fc + 8, :],
                        ys + XS5_F[fc], (tl("Yd_%d_%d" % (fc, t8)),), ch_yw[fc % 2])
                for fr in ([fc - 1] if fc >= 1 else []) + ([7] if fc == 7 else []):
                    dma("sp", yF[:, fr, :], Yd[fr * 128:(fr + 1) * 128, :],
                        tuple(tl("Yd_%d_%d" % (fr, t_)) for t_ in range(8)),
                        (tl("yf_%d" % fr),) + tuple(tl("hn_%d_%d" % (fr, t)) for t in range(3)), ch_scr3)
        cstep(hsf[:], h0p[:], tr, tpp, tnn, Ub[:, 128:136, :])
        dma("sp", sts[j, ps], hsf[:], (tsx,), (tl("sts_out"),), ch_out)
        for which in range(2):
            wl = wl_glu1 if which == 0 else [wload(wb1 + 2), wload(wb1 + 3)]
            for fc in range(8):
                wi = wl[fc // 4]
                wv = WB[wi][:, :].rearrange("p (a b) -> p a b", b=512)
                dcl = fc % 4
                for ti, (c0, n) in enumerate(TILES_B):
                    bk = mbank()
                    for kc in range(8):
                        mm(PS[:, bk, 0:n], wv[:, kc, dcl * 128:(dcl + 1) * 128], yF[:, kc, c0:c0 + n],
                           kc == 0, kc == 7, (tl("yf_%d" % kc), wtile[wi]), (bank[bk],))
                    gt = tl("gb_%d_%d" % (fc, ti))
                    if which == 0:
                        stt(Gb[:, fc, c0:c0 + n], PS[:, bk, 0:n], bglu[:, j, 0, fc:fc + 1], Gb[:, fc, c0:c0 + n],
                            ALU.add, ALU.mult, (bank[bk], gt, cst), (gt,))
                    else:
                        i = nxt("tmpb", 2)
                        act(tmpb[i][:, 0:n], PS[:, bk, 0:n], AF.Sigmoid, (bank[bk], cst), (tmpb_t[i],),
                            bias=bglu[:, j, 1, fc:fc + 1], scale=1.0)
                        tt("dve", Gb[:, fc, c0:c0 + n], tmpb[i][:, 0:n], Gb[:, fc, c0:c0 + n], ALU.mult,
                           (tmpb_t[i], gt), (gt,))
        wb2 = j * NBLK_PER_J + 24
        wl = [wload(wb2), wload(wb2 + 1)]
        for dmc in range(8):
            wi = wl[dmc // 4]
            wv = WB[wi][:, :].rearrange("p (a b) -> p a b", b=512)
            dcl = dmc % 4
            for ti, (c0, n) in enumerate(TILES_B):
                bk = mbank()
                for kc in range(8):
                    mm(PS[:, bk, 0:n], wv[:, kc, dcl * 128:(dcl + 1) * 128], Gb[:, kc, c0:c0 + n],
                       kc == 0, kc == 7, (tl("gb_%d_%d" % (kc, ti)), wtile[wi]), (bank[bk],))
                out_stage_evac(bk, dmc, ti, c0, n, O_B)
        flush_stat()
        postnorm(layer, "B", O_B, TILES_B)

    for ps in range(npass):
        for kc in range(8):
            dma("sp", Xsb[:, kc, :], xT[ps, :, kc, :], (),
                tuple(tl("x_%d_%d" % (kc, ti)) for ti in range(3)), ch_x[ps % 2],
                extra=(bar_ops if ps == 0 else ()))
        for layer, kind in enumerate(layers):
            if kind == "A":
                layer_a(layer, ps)
            elif kind == "B":
                layer_b(layer, ps)
        for kc in range(8):
            dma("sp", yT[ps, :, kc, :], Xsb[:, kc, :], tuple(tl("x_%d_%d" % (kc, ti)) for ti in range(3)),
                (tl("y_out_%d" % kc),), ch_out)
    print('sbuf bytes remaining', nc.sbuf_bytes_remaining)
    if max_ops is not None:
        print('total ops', len(P.ops))
        P.ops = P.ops[:max_ops]
    P.emit()
    return nc


def _blk(w, nb, kc, n):
    return np.ascontiguousarray(w.reshape(kc, 128, nb, n).transpose(2, 1, 0, 3)).reshape(nb, 128, kc * n)


def _consts():
    ident = np.eye(128, dtype=np.float32)
    s = np.arange(128)
    trimask = (s[:, None] <= s[None, :]).astype(np.float32)
    q = np.arange(32)
    bdmask = ((q[:, None] // 4 == q[None, :] // 4) & (q[:, None] % 4 <= q[None, :] % 4)).astype(np.float32)
    tmask = ((s[:, None] // 16) <= (s[None, :] // 16)).astype(np.float32)
    colmask = np.zeros((128, 2, 128), np.float32)
    colmask[:, 0, :64] = 1
    colmask[:, 1, 64:] = 1
    rowmask = np.zeros((128, 2), np.float32)
    rowmask[:64, 0] = 1
    rowmask[64:, 1] = 1
    return dict(ident=ident, trimask=trimask, bdmask=bdmask, tmask=tmask, colmask=colmask, rowmask=rowmask)


def _shared_inputs(inp):
    f = lambda a: np.ascontiguousarray(np.asarray(a, dtype=np.float32))
    blocks = []
    for j in range(2):
        blocks.append(_blk(f(inp["w_in_a"][j]), 12, 8, 512))
        blocks.append(_blk(f(inp["w_out_a"][j]), 4, 16, 256))
        blocks.append(_blk(f(inp["w_in_b"][j]), 4, 8, 512))
        blocks.append(_blk(f(inp["w_glu1"][j]), 2, 8, 512))
        blocks.append(_blk(f(inp["w_glu2"][j]), 2, 8, 512))
        blocks.append(_blk(f(inp["w_out_b"][j]), 2, 8, 512))
    d = dict(wall=np.concatenate(blocks, axis=0))
    col = lambda v, n: np.ascontiguousarray(f(v).reshape(n, 128).T)
    d["gpre"] = np.concatenate([col(inp["norm_pre"][l], 8) for l in range(4)], axis=1)
    d["gpost"] = np.concatenate([col(inp["norm_post"][l], 8) for l in range(4)], axis=1)
    d["lncol"] = np.ascontiguousarray(np.stack(
        [np.stack([col(inp["ln_v_g"][j], 16), col(inp["ln_v_b"][j], 16)], axis=1) for j in range(2)], axis=1))
    d["lnbc"] = np.ascontiguousarray(np.stack(
        [np.stack([np.broadcast_to(f(inp["ln_v_g"][j])[None, :], (32, 2048)),
                   np.broadcast_to(f(inp["ln_v_b"][j])[None, :], (32, 2048))]) for j in range(2)]))
    ws = f(inp["w_s"])
    d["wsT"] = np.ascontiguousarray(ws.transpose(0, 3, 1, 2))
    corner = ws[:, :, :4, :4].transpose(0, 3, 1, 2)
    d["wsrep"] = np.ascontiguousarray(np.tile(corner, (1, 8, 1, 8)))
    d["bsB"] = np.ascontiguousarray(np.broadcast_to(f(inp["b_s"])[:, None, :, :], (2, 128, 8, 128)))
    d["bglu"] = np.ascontiguousarray(np.stack(
        [np.stack([col(inp["b_glu1"][j], 8), col(inp["b_glu2"][j], 8)], axis=1) for j in range(2)], axis=1))

    def l2(a):
        return a.reshape(32, 2, 64).transpose(1, 2, 0).reshape(128, 32)

    aL2 = np.zeros((2, 128, 3, 32), np.float32)
    bL2 = np.zeros((2, 128, 2, 32, 16), np.float32)
    cL2 = np.zeros((2, 128, 2, 32, 16), np.float32)
    dcol = np.zeros((2, 128, 64), np.float32)
    for j in range(2):
        aL2[j, :, 0] = l2(f(inp["a_re"][j]))
        aL2[j, :, 1] = l2(f(inp["a_im"][j]))
        aL2[j, :, 2] = l2(np.broadcast_to(f(inp["log_dt"][j])[:, None], (64, 64)))
        for r, nm in enumerate(("b_re", "b_im")):
            b = f(inp[nm][j])
            bL2[j, :, r] = b.reshape(32, 2, 64, 16).transpose(1, 2, 0, 3).reshape(128, 32, 16)
        for r, nm in enumerate(("c_re", "c_im")):
            c = f(inp[nm][j])
            cL2[j, :, r] = c.reshape(32, 2, 16, 64).transpose(1, 3, 0, 2).reshape(128, 32, 16)
        dk = f(inp["d_skip"][j]).reshape(64, 16)
        dcol[j] = np.tile(dk.T, (8, 1))
    d.update(aL2=aL2, bL2=bL2, cL2=cL2, dcol=dcol)
    d.update(_consts())
    return d


def _core_inputs(inp, cid):
    f = lambda a: np.asarray(a, dtype=np.float32)
    xp = f(inp["x_prompt"][cid])
    xsm = f(inp["x_sample"][16 * cid:16 * cid + 16])
    xT = np.zeros((2, 128, 8, NTA), np.float32)
    for ps in range(2):
        cols = np.concatenate([xp[1024 * ps:1024 * ps + 1024], xsm[8 * ps:8 * ps + 8].reshape(32, 1024)], axis=0)
        xT[ps] = cols.T.reshape(8, 128, NTA).transpose(1, 0, 2)
    h0 = np.zeros((2, 2, 128, 8, 64), np.float32)
    for j in range(2):
        for ps in range(2):
            sl = slice(16 * cid + 8 * ps, 16 * cid + 8 * ps + 8)
            re = f(inp["state_ssm_re"][j, sl])
            im = f(inp["state_ssm_im"][j, sl])
            st = np.stack([re, im], axis=-1)
            st = st.reshape(8, 32, 2, 64, 2).transpose(2, 3, 0, 1, 4)
            h0[j, ps] = st.reshape(128, 8, 64)
    return dict(xT=xT, h0L2=h0)


_NC_CACHE = {}


def kernel(**inputs):
    inp = {k: np.asarray(v) for k, v in inputs.items()}
    shared = _shared_inputs(inp)
    in_maps = []
    for cid in range(NCORES):
        m = dict(shared)
        m.update(_core_inputs(inp, cid))
        in_maps.append(m)
    if "nc" not in _NC_CACHE:
        _NC_CACHE["nc"] = build_program()
    nc = _NC_CACHE["nc"]
    res = run_bass_kernel_spmd(nc, in_maps, core_ids=list(range(NCORES)))
    y_prompt = np.zeros((8, 2048, 1024), np.float32)
    y_sample = np.zeros((128, 4, 1024), np.float32)
    cvs = np.zeros((2, 128, 4, 2048), np.float32)
    srp = np.zeros((2, 8, 64, 64), np.float32)
    sip = np.zeros((2, 8, 64, 64), np.float32)
    srs = np.zeros((2, 128, 64, 64), np.float32)
    sis = np.zeros((2, 128, 64, 64), np.float32)
    for cid in range(NCORES):
        r = res.results[cid]
        yT = np.asarray(r["yT"])
        for ps in range(2):
            cols = yT[ps].transpose(2, 1, 0).reshape(NTA, 1024)
            y_prompt[cid, 1024 * ps:1024 * ps + 1024] = cols[:1024]
            y_sample[16 * cid + 8 * ps:16 * cid + 8 * ps + 8] = cols[1024:].reshape(8, 4, 1024)
        cvv = np.asarray(r["cv"])
        stpv = np.asarray(r["stp"])
        stsv = np.asarray(r["sts"])
        for j in range(2):
            for ps in range(2):
                cvs[j, 16 * cid + 8 * ps:16 * cid + 8 * ps + 8] = cvv[j, ps].reshape(8, 4, 2048)
                s = stsv[j, ps].reshape(2, 64, 8, 32, 2).transpose(2, 3, 0, 1, 4).reshape(8, 64, 64, 2)
                srs[j, 16 * cid + 8 * ps:16 * cid + 8 * ps + 8] = s[..., 0]
                sis[j, 16 * cid + 8 * ps:16 * cid + 8 * ps + 8] = s[..., 1]
            s = stpv[j].reshape(2, 64, 32, 2).transpose(2, 0, 1, 3).reshape(64, 64, 2)
            srp[j, cid] = s[..., 0]
            sip[j, cid] = s[..., 1]
    return (y_prompt, y_sample, cvs, srp, sip, srs, sis)
```

```python
import os
import numpy as np
import concourse.bass as bass
import concourse.mybir as mybir
from concourse.bass_utils import run_bass_kernel_spmd

F32 = mybir.dt.float32
BF16 = mybir.dt.bfloat16
I32 = mybir.dt.int32
AF = mybir.ActivationFunctionType
ALU = mybir.AluOpType

NCORES = 8
D = 1024
NTA = 1056
NTB = 1088
CB = 136
TILES_A = [(0, 512), (512, 512), (1024, 32)]
TILES_B = [(0, 512), (512, 512), (1024, 64)]
EPS = 1e-6
NBLK_PER_J = 26
SCAN_NOSYNC = os.environ.get("SCAN_SYNC", "0") != "1"
BUILD_NOSYNC = SCAN_NOSYNC


class Tl:
    __slots__ = ("name", "w", "r", "excl")

    def __init__(self, name, excl=False):
        self.name = name
        self.w = None
        self.r = {}
        self.excl = excl


class Chan:
    def __init__(self, nc, name):
        self.sem = nc.alloc_semaphore(name)
        self.cnt = 0
        self.name = name
        self.pending = 0
        self.selfw = 0


class Op:
    __slots__ = ("eng", "fn", "chan", "signal", "deps", "semval", "val", "final")


class Prog:
    CE = ("pe", "act", "dve", "pool")

    def __init__(self, nc):
        self.nc = nc
        self.ops = []
        self.sem = {e: nc.alloc_semaphore("s_" + e) for e in self.CE}
        self.chans = []
        self.last = {}

    def chan(self, name):
        c = Chan(self.nc, name)
        self.chans.append(c)
        return c

    def add(self, eng, fn, reads=(), writes=(), chan=None, extra=(), nosync=False):
        op = Op()
        op.eng = eng
        op.fn = fn
        op.chan = chan
        op.signal = False
        op.final = False
        xw = tuple(t for t in reads if t.excl)
        if xw:
            writes = tuple(writes) + xw
        deps = {}
        for t in reads:
            if t.w is not None:
                deps[id(t.w)] = t.w
        for t in writes:
            if t.w is not None:
                deps[id(t.w)] = t.w
            for r in t.r.values():
                deps[id(r)] = r
        for d in extra:
            deps[id(d)] = d
        for t in writes:
            t.w = op
            t.r = {}
        for t in reads:
            key = id(chan) if chan is not None else eng
            t.r[key] = op
        dl = []
        for d in deps.values():
            if d is op:
                continue
            if d.chan is not None:
                dl.append(("dma", d.chan, d.chan.cnt * 16))
                d.chan.pending = max(d.chan.pending, d.chan.cnt * 16)
            else:
                if d.eng == eng and chan is None and (eng == "pe" or nosync):
                    continue
                d.signal = True
                dl.append(("cmp", d))
        if chan is not None and chan.pending > chan.selfw:
            dl.append(("dma", chan, chan.pending))
            chan.selfw = chan.pending
        op.deps = dl
        if chan is not None:
            chan.cnt += 1
            op.val = chan.cnt * 16
        self.ops.append(op)
        self.last[eng if chan is None else ("q", eng)] = op
        return op

    def barrier(self, tiny):
        lasts = [self.last[e] for e in self.CE if e in self.last]
        dmas = [self.last[k] for k in self.last if isinstance(k, tuple)]
        out = []
        for e in ("act", "dve", "pool"):
            out.append(self.add(e, tiny[e], extra=lasts + dmas))
        return out + dmas

    def emit(self):
        cnt = {e: 0 for e in self.CE}
        for op in self.ops:
            if op.chan is None and op.signal:
                cnt[op.eng] += 1
                op.semval = cnt[op.eng]
        nc = self.nc
        by = {e: [o for o in self.ops if o.eng == e] for e in ("pe", "act", "dve", "pool", "sp")}
        sems = self.sem
        chans = self.chans

        def run(e, lst, is_sp=False):
            waited = {}
            for op in lst:
                need = {}
                for d in op.deps:
                    if d[0] == "dma":
                        sem, val = d[1].sem, d[2]
                    else:
                        sem, val = sems[d[1].eng], d[1].semval
                    k = id(sem)
                    if k not in need or need[k][1] < val:
                        need[k] = (sem, val)
                for k, (sem, val) in need.items():
                    if waited.get(k, 0) >= val:
                        continue
                    e.wait_ge(sem, val)
                    waited[k] = val
                ins = op.fn(e)
                if op.chan is not None:
                    ins.then_inc(op.chan.sem, 16)
                elif op.signal:
                    ins.then_inc(sems[op.eng], 1)
            if is_sp:
                for c in chans:
                    n_em = sum(1 for o in self.ops if o.chan is c)
                    if n_em > 0:
                        e.wait_ge(c.sem, n_em * 16)

        with nc.Block() as block:
            @block.tensor
            def _(e):
                run(e, by["pe"])

            @block.scalar
            def _(e):
                run(e, by["act"])

            @block.vector
            def _(e):
                run(e, by["dve"])

            @block.gpsimd
            def _(e):
                run(e, by["pool"])

            @block.sync
            def _(e):
                run(e, by["sp"], True)


def build_program(layers=("A", "B", "A", "B"), npass=2, max_ops=None):
    nc = bass.Bass("TRN2", target_bir_lowering=False)
    P = Prog(nc)

    def din(name, shape, dt=F32):
        return nc.dram_tensor(name, list(shape), dt, kind="ExternalInput").ap()

    def dout(name, shape, dt=F32):
        return nc.dram_tensor(name, list(shape), dt, kind="ExternalOutput").ap()

    xT = din("xT", [2, 128, 8, NTA])
    WALL = din("wall", [2 * NBLK_PER_J, 128, 4096])
    gpre_d = din("gpre", [128, 32])
    gpost_d = din("gpost", [128, 32])
    lncol_d = din("lncol", [128, 2, 2, 16])
    lnbc_d = din("lnbc", [2, 2, 32, 2048])
    wsT_d = din("wsT", [2, 128, 8, 128])
    wsrep_d = din("wsrep", [2, 32, 8, 32])
    bsB_d = din("bsB", [2, 128, 8, 128])
    bglu_d = din("bglu", [128, 2, 2, 8])
    aL2_d = din("aL2", [2, 128, 3, 32])
    bL2_d = din("bL2", [2, 128, 2, 32, 16])
    cL2_d = din("cL2", [2, 128, 2, 32, 16])
    dcol_d = din("dcol", [2, 128, 64])
    h0_d = din("h0L2", [2, 2, 128, 8, 64])
    ident_d = din("ident", [128, 128])
    trim_d = din("trimask", [128, 128])
    bdm_d = din("bdmask", [32, 32])
    tmask_d = din("tmask", [128, 128])
    colm_d = din("colmask", [128, 2, 128])
    rowm_d = din("rowmask", [128, 2])

    yT = dout("yT", [2, 128, 8, NTA])
    cv = dout("cv", [2, 2, 32, 2048])
    stp = dout("stp", [2, 128, 64])
    sts = dout("sts", [2, 2, 128, 8, 64])

    SW = nc.dram_tensor("SW", [2, 32, 128, 512], BF16).ap()
    SMV = nc.dram_tensor("SMV", [2, 64, 128, 384], BF16).ap()
    Xd = nc.dram_tensor("Xd", [1024, NTB], BF16).ap()
    Yd = nc.dram_tensor("Yd", [1024, NTB], BF16).ap()

    def sb(name, shape, dt=F32):
        return nc.alloc_sbuf_tensor("sb_" + name, list(shape), dt)

    A0 = sb("A0", [128, 8, NTA])
    A1 = sb("A1", [128, 8, NTB], BF16)
    A2 = sb("A2", [128, 18432], BF16)
    A3 = sb("A3", [128, 17408], BF16)
    A5 = sb("A5", [128, 8, NTB], BF16)
    NWB = 2
    WB = [sb("wb%d" % i, [128, 4096], BF16) for i in range(NWB)]
    NRB = 3
    RB = [sb("rb%d" % i, [128, 1536], BF16) for i in range(NRB)]
    ident = sb("ident", [128, 128])
    ones_bf = sb("ones_bf", [128, 128], BF16)
    ones_f = sb("ones_f", [128, 128])
    trim = sb("trim", [128, 128])
    bdm = sb("bdm", [128, 32])
    tmask = sb("tmask", [128, 128])
    colm = sb("colm", [128, 2, 128])
    rowm = sb("rowm", [128, 2])
    gpre = sb("gpre", [128, 32])
    gpost = sb("gpost", [128, 32])
    lncol = sb("lncol", [128, 2, 2, 16])
    bglu = sb("bglu", [128, 2, 2, 8])
    epsc = sb("epsc", [128, 1])
    hpic = sb("hpic", [128, 1])
    zero1 = sb("zero1", [128, 1])
    ctab = sb("ctab", [128, 16, 128])
    wsTm = sb("wsTm", [128, 8, 128], BF16)
    bdw = sb("bdw", [128, 8, 32], BF16)
    rst = [sb("rst%d" % i, [128, 512]) for i in range(2)]
    sqr = [sb("sq%d" % i, [128, 512], BF16) for i in range(2)]
    tmpb = [sb("tmpb%d" % i, [128, 512], BF16) for i in range(2)]
    xs = [sb("xs%d" % i, [128, NTB], BF16) for i in range(2)]
    stats = sb("stats", [128, 9, 4, 6])
    mv = sb("mv", [128, 9, 2])
    rstdv = sb("rstdv", [128, 9])
    nmr = sb("nmr", [128, 9])
    AR2 = sb("AR2", [128, 2, 64])
    AIp = sb("AIp", [128, 2, 32])
    AIn = sb("AIn", [128, 2, 32])
    AI2 = sb("AI2", [128, 2, 64])
    M4r = sb("M4r", [128, 2, 64])
    M4p = sb("M4p", [128, 2, 32])
    M4n = sb("M4n", [128, 2, 32])
    Hcar = sb("Hcar", [128, 2, 64])
    Hin0 = sb("Hin0", [128, 64])
    h0t = sb("h0t", [128, 8, 64])
    h0p = sb("h0p", [128, 8, 64])
    hsf = sb("hsf", [128, 8, 64])
    st1 = sb("st1", [128, 8, 64])
    st2 = sb("st2", [128, 8, 64])
    barA = sb("barA", [128, 1])
    barV = sb("barV", [128, 1])
    barG = sb("barG", [128, 1])

    PS = nc.alloc_psum_tensor("PS", [128, 8, 512], F32)
    bank = [Tl("bank%d" % i, True) for i in range(8)]

    Xsb = A0
    Hn = A1
    G = A2[:, :].rearrange("p (n d t) -> p n d t", n=9, d=16)
    Gv = A2[:, :].rearrange("p (n f) -> p n f", n=9)
    Ub = A2[:, 0:17408].bitcast(F32).rearrange("p (c g) -> p c g", g=64)
    O_A = A3[:, 0:16896].bitcast(F32).rearrange("p (k c) -> p k c", k=8)
    O_B = A3[:, :].bitcast(F32).rearrange("p (k c) -> p k c", k=8)
    Xs5 = A3[:, 0:8704].rearrange("p (g c) -> p g c", g=64)
    Hb = A3[:, 8704:17408].rearrange("p (g r c) -> p g r c", g=32, r=2)
    yF = A1
    ltmp = O_A[:, 4, 0:1024].rearrange("p (h t) -> p h t", h=8)
    Gb = A5

    T = {}

    def tl(name):
        if name not in T:
            T[name] = Tl(name)
        return T[name]

    OT_ALL = tuple(tl("o_%d_%d" % (d_, t_)) for d_ in range(8) for t_ in range(3))
    RBT = [tl("rb_%d" % i_) for i_ in range(3)]
    XS5_F = [tuple(tl("xs5_%d_%d" % (s_, f_)) for s_ in range(8)) for f_ in range(8)]
    par = P.chan("par")
    ch_x = [P.chan("chx%d" % i) for i in range(2)]
    ch_out = P.chan("chout")
    ch_w = [P.chan("chw%d" % i) for i in range(NWB)]
    ch_rb = [P.chan("chrb_%d" % i) for i in range(NRB)]
    rbstate = {"n": 0}

    def rbload(dst_view_fn, src_ap, src_tile):
        i = rbstate["n"] % NRB
        rbstate["n"] += 1
        P.add("pool", lambda e, o=dst_view_fn(RB[i]), i_=src_ap: e.dma_start(out=o, in_=i_), (src_tile,),
              (RBT[i],), ch_rb[i])
        return i

    ch_scr = P.chan("chscr")
    ch_yw = [P.chan("chyw%d" % i_) for i_ in range(2)]
    ch_pp = [P.chan("chpp%d" % i_) for i_ in range(4)]
    ch_scr2 = P.chan("chscr2")
    ch_scr3 = P.chan("chscr3")
    ch_xr = [[P.chan("chxr%d_%d" % (f_, q_)) for q_ in range(2)] for f_ in range(8)]
    ch_prep = P.chan("chprep")
    ch_xs = [P.chan("chxs%d" % i) for i in range(2)]

    def dma(q, out, in_, reads, writes, chan, extra=()):
        return P.add(q, lambda e, o=out, i=in_: e.dma_start(out=o, in_=i), reads, writes, chan, extra)

    def mm(out, lhsT, rhs, start, stop, reads, writes):
        return P.add("pe", lambda e, o=out, l=lhsT, r=rhs, s=start, t=stop:
                     e.matmul(o, l, r, start=s, stop=t), reads, writes)

    def act(out, in_, func, reads, writes, bias=None, scale=None):
        def fn(e, o=out, i=in_, f=func, b=bias, s=scale):
            kw = {}
            if b is not None:
                kw["bias"] = b
            if s is not None:
                kw["scale"] = s
            return e.activation(out=o, in_=i, func=f, **kw)
        return P.add("act", fn, reads, writes)

    def tt(eng, out, in0, in1, op, reads, writes, nosync=False):
        return P.add(eng, lambda e, o=out, a=in0, b=in1, p=op: e.tensor_tensor(out=o, in0=a, in1=b, op=p),
                     reads, writes, nosync=nosync)

    def ts(eng, out, in0, s1, s2, op0, op1, reads, writes, nosync=False):
        def fn(e, o=out, a=in0, x=s1, y=s2, p0=op0, p1=op1):
            if p1 is None:
                return e.tensor_scalar(out=o, in0=a, scalar1=x, scalar2=None, op0=p0)
            return e.tensor_scalar(out=o, in0=a, scalar1=x, scalar2=y, op0=p0, op1=p1)
        return P.add(eng, fn, reads, writes)

    def stt(out, in0, scalar, in1, op0, op1, reads, writes):
        return P.add("dve", lambda e, o=out, a=in0, s=scalar, b=in1, p0=op0, p1=op1:
                     e.scalar_tensor_tensor(out=o, in0=a, scalar=s, in1=b, op0=p0, op1=p1), reads, writes)

    def cp(eng, out, in_, reads, writes):
        if eng == "act":
            return P.add("act", lambda e, o=out, i=in_: e.activation(out=o, in_=i, func=AF.Copy), reads, writes)
        return P.add(eng, lambda e, o=out, i=in_: e.tensor_copy(out=o, in_=i), reads, writes)

    def mset(eng, ap, val, writes):
        return P.add(eng, lambda e, a=ap, v=val: e.memset(a, v), (), writes)

    wstate = {"n": 0}
    wtile = [Tl("wb%d" % i) for i in range(NWB)]

    def wload(blk):
        i = wstate["n"] % NWB
        wstate["n"] += 1
        src = WALL[blk].rearrange("p (a b) -> p a b", b=512)
        dst = WB[i][:, :].rearrange("p (a b) -> p a b", b=512)
        dma("pool", dst, src, (), (wtile[i],), ch_w[i])
        return i

    cst = tl("const")
    for dst, src in ((ident, ident_d), (trim, trim_d), (tmask, tmask_d), (colm, colm_d),
                     (rowm, rowm_d), (gpre, gpre_d), (gpost, gpost_d), (lncol, lncol_d), (bglu, bglu_d)):
        dma("sp", dst[:], src, (), (cst,), par)
    dma("sp", bdm[64:96, :], bdm_d, (), (cst,), par)
    mset("pool", bdw[:], 0.0, (tl("tabA"),))
    mset("dve", ones_f[:], 1.0, (cst,))
    mset("dve", epsc[:], EPS, (cst,))
    mset("dve", hpic[:], float(np.pi / 2), (cst,))
    mset("dve", zero1[:], 0.0, (cst,))
    mset("dve", Hin0[:], 0.0, (cst,))
    mset("dve", stats[:], 0.0, (tl("stats"),))
    mset("dve", mv[:], 1.0, (tl("mv"),))
    mset("dve", barV[:], 0.0, (tl("barV"),))
    mset("pool", barG[:], 0.0, (tl("barG"),))
    cp("dve", ones_bf[:], ones_f[:], (cst,), (cst,))
    act(barA[:], zero1[:], AF.Copy, (cst,), (tl("barA"),))

    def prep_s5(j):
        base = A5[:, :, :].rearrange("p a b -> p (a b)")
        f = base.bitcast(F32)
        off = [0]

        def carve(n, shape=None):
            v = f[:, off[0]:off[0] + n]
            off[0] += n
            return v

        aL = carve(96).rearrange("p (a g) -> p a g", a=3)
        dt_ = carve(32)
        dar = carve(32)
        ang = carve(32)
        mag = carve(32)
        magi = carve(32)
        kf = carve(32)
        ki = carve(32).bitcast(I32)
        rr = carve(32)
        m1 = carve(32)
        sn = carve(32)
        cs = carve(32)
        ab = carve(32)
        t1 = carve(32)
        t2 = carve(32)
        t3 = carve(32)
        t4 = carve(32)
        nr = carve(32)
        den = carve(32)
        cfr = carve(32)
        cfi = carve(32)
        PW = carve(17 * 64).rearrange("p (n r g) -> p n r g", n=17, r=2)
        bL = carve(1024).rearrange("p (r g k) -> p r g k", r=2, g=32)
        cL = carve(1024).rearrange("p (r g k) -> p r g k", r=2, g=32)
        assert off[0] <= 4352
        f1 = A1[:, :, :].rearrange("p a b -> p (a b)").bitcast(F32)
        bbar = f1[:, 0:1024].rearrange("p (r g k) -> p r g k", r=2, g=32)
        big1 = f1[:, 1024:1536].rearrange("p (g k) -> p g k", g=32)
        dcl = f1[:, 2048:2112]
        tp = tl("prep_small")
        tpa, tpb, tpc, tpd = tl("prep_aL"), tl("prep_bL"), tl("prep_cL"), tl("prep_dcl")
        dma("sp", aL, aL2_d[j], (tp,), (tpa,), ch_pp[0])
        dma("sp", bL, bL2_d[j], (tp,), (tpb,), ch_pp[1])
        dma("sp", cL, cL2_d[j], (tp,), (tpc,), ch_pp[2])
        dma("sp", dcl, dcol_d[j], (tp,), (tpd,), ch_pp[3])
        R = (tp, cst, tpa, tpb, tpc, tpd)
        W_ = (tp,)
        ar, ai, ldt = aL[:, 0, :], aL[:, 1, :], aL[:, 2, :]
        act(dt_, ldt, AF.Exp, R, W_)
        tt("dve", dar, dt_, ar, ALU.mult, R, W_)
        tt("dve", ang, dt_, ai, ALU.mult, R, W_)
        act(mag, dar, AF.Exp, R, W_)
        act(magi, dar, AF.Exp, R, W_, scale=-1.0)
        ts("dve", kf, ang, float(1.0 / (2 * np.pi)), None, ALU.mult, None, R, W_)
        cp("dve", ki, kf, R, W_)
        cp("dve", kf, ki, R, W_)
        stt(rr, kf, float(-2 * np.pi), ang, ALU.mult, ALU.add, R, W_)
        ts("dve", m1, rr, float(np.pi), float(-2 * np.pi), ALU.is_gt, ALU.mult, R, W_)
        tt("dve", rr, rr, m1, ALU.add, R, W_)
        ts("dve", m1, rr, float(-np.pi), float(2 * np.pi), ALU.is_lt, ALU.mult, R, W_)
        tt("dve", rr, rr, m1, ALU.add, R, W_)
        act(sn, rr, AF.Sin, R, W_)
        act(ab, rr, AF.Abs, R, W_)
        act(cs, ab, AF.Sin, R, W_, bias=hpic[:], scale=-1.0)
        mset("dve", PW[:, 8, 0, :], 1.0, W_)
        mset("dve", PW[:, 8, 1, :], 0.0, W_)
        tt("dve", PW[:, 9, 0, :], mag, cs, ALU.mult, R, W_)
        tt("dve", PW[:, 9, 1, :], mag, sn, ALU.mult, R, W_)
        tt("dve", PW[:, 7, 0, :], magi, cs, ALU.mult, R, W_)
        stt(PW[:, 7, 1, :], magi, -1.0, sn, ALU.mult, ALU.mult, R, W_)

        wt = [carve(128), carve(128), carve(128), f1[:, 1600:1728]]

        def cmulw(o, a, b, w):
            T = [t_[:, 0:w * 32].rearrange("p (a g) -> p a g", a=w) for t_ in wt]
            a_r, a_i = a[:, :, 0, :], a[:, :, 1, :]
            b_r = b[:, :, 0, :].broadcast_to([128, w, 32])
            b_i = b[:, :, 1, :].broadcast_to([128, w, 32])
            tt("dve", T[0], a_r, b_r, ALU.mult, R, W_)
            tt("dve", T[1], a_i, b_i, ALU.mult, R, W_)
            tt("dve", T[2], a_r, b_i, ALU.mult, R, W_)
            tt("dve", T[3], a_i, b_r, ALU.mult, R, W_)
            tt("dve", o[:, :, 0, :], T[0], T[1], ALU.subtract, R, W_)
            tt("dve", o[:, :, 1, :], T[2], T[3], ALU.add, R, W_)

        cmulw(PW[:, 10:11], PW[:, 9:10], PW[:, 9:10], 1)
        cmulw(PW[:, 6:7], PW[:, 7:8], PW[:, 7:8], 1)
        cmulw(PW[:, 11:13], PW[:, 9:11], PW[:, 10:11], 2)
        cmulw(PW[:, 5:3:-1], PW[:, 7:5:-1], PW[:, 6:7], 2)
        cmulw(PW[:, 13:17], PW[:, 9:13], PW[:, 12:13], 4)
        cmulw(PW[:, 3::-1], PW[:, 7:3:-1], PW[:, 4:5], 4)
        for (src_n, tr, tp_, tn_) in ((16, AR2, AIp, AIn), (4, M4r, M4p, M4n)):
            trv = tr[:, j, :].rearrange("p (g r) -> p g r", r=2)
            cp("dve", trv[:, :, 0], PW[:, src_n, 0, :], R, (cst,))
            cp("dve", trv[:, :, 1], PW[:, src_n, 0, :], R, (cst,))
            cp("dve", tp_[:, j, :], PW[:, src_n, 1, :], R, (cst,))
            ts("dve", tn_[:, j, :], PW[:, src_n, 1, :], -1.0, None, ALU.mult, None, R, (cst,))
        ai2v = AI2[:, j, :].rearrange("p (g r) -> p g r", r=2)
        cp("dve", ai2v[:, :, 0], AIn[:, j, :], (cst,), (cst,))
        cp("dve", ai2v[:, :, 1], AIp[:, j, :], (cst,), (cst,))
        ts("dve", nr, PW[:, 9, 0, :], -1.0, None, ALU.add, None, R, W_)
        ni = PW[:, 9, 1, :]
        tt("dve", t1, ar, ar, ALU.mult, R, W_)
        tt("dve", t2, ai, ai, ALU.mult, R, W_)
        tt("dve", den, t1, t2, ALU.add, R, W_)
        P.add("dve", lambda e, o=den, i=den: e.reciprocal(out=o, in_=i), R, W_)
        tt("dve", t1, nr, ar, ALU.mult, R, W_)
        tt("dve", t2, ni, ai, ALU.mult, R, W_)
        tt("dve", t1, t1, t2, ALU.add, R, W_)
        tt("dve", cfr, t1, den, ALU.mult, R, W_)
        tt("dve", t1, ni, ar, ALU.mult, R, W_)
        tt("dve", t2, nr, ai, ALU.mult, R, W_)
        tt("dve", t1, t1, t2, ALU.subtract, R, W_)
        tt("dve", cfi, t1, den, ALU.mult, R, W_)

        def bc16(v):
            return v.unsqueeze(2).broadcast_to([128, 32, 16])

        tb = tl("prep_bbar")
        RB = (tp, cst, tb, tpa, tpb, tpc, tpd)
        WB_ = (tb,)
        tt("dve", bbar[:, 0], bc16(cfr), bL[:, 0], ALU.mult, RB, WB_)
        tt("dve", big1, bc16(cfi), bL[:, 1], ALU.mult, RB, WB_)
        tt("dve", bbar[:, 0], bbar[:, 0], big1, ALU.subtract, RB, WB_)
        tt("dve", bbar[:, 1], bc16(cfr), bL[:, 1], ALU.mult, RB, WB_)
        tt("dve", big1, bc16(cfi), bL[:, 0], ALU.mult, RB, WB_)
        tt("dve", bbar[:, 1], bbar[:, 1], big1, ALU.add, RB, WB_)

        PSM = f1[:, 1024:1568].rearrange("p (n g) -> p n g", n=17)
        tt("dve", PSM, PW[:, :, 0, :], PW[:, :, 1, :], ALU.add, (tp, tl("prep_bbar")), (tp, tl("prep_bbar")))
        for hf in range(2):
            g0 = 16 * hf
            a2f = A2[:, :].bitcast(F32)
            WcL = a2f[:, 0:4096].rearrange("p (g r s k) -> p g r s k", g=16, r=2, s=8)
            WnL = a2f[:, 4096:8192].rearrange("p (g r s k) -> p g r s k", g=16, r=2, s=8)
            a3f = A3[:, :].bitcast(F32)
            VL = a3f[:, 0:4096].rearrange("p (g r s k) -> p g r s k", g=16, r=2, s=8)
            tmps = {"dve": (a3f[:, 4096:4352].rearrange("p (g k) -> p g k", g=16),
                            a3f[:, 4352:4608].rearrange("p (g k) -> p g k", g=16)),
                    "pool": (a3f[:, 6656:6912].rearrange("p (g k) -> p g k", g=16),
                             a3f[:, 6912:7168].rearrange("p (g k) -> p g k", g=16))}
            VLm = a3f[:, 4608:6656].rearrange("p (g x r m) -> p g x r m", g=4, x=2, r=2)
            a0 = A0[:, :, :].rearrange("p a b -> p (a b)")
            Wst = a0[:, 0:4096].bitcast(BF16).rearrange("p (g x r m) -> p g x r m", g=16, x=2, r=2)
            Vst = a0[:, 4096:8192].bitcast(BF16).rearrange("p (g x r m) -> p g x r m", g=16, x=2, r=2)
            M0st = f1[:, 2176:4224].bitcast(BF16).rearrange("p (g m) -> p g m", g=32)
            tw = tl("prep_big")
            tstg = tl("prep_stg")

            def bcg(v):
                return v.unsqueeze(2).broadcast_to([128, 16, 16])

            sums = a3f[:, 7168:8192].rearrange("p (q g k) -> p q g k", q=4, g=16)
            tsm = tl("prep_sums")
            br = bbar[:, 0, g0:g0 + 16, :]
            bi = bbar[:, 1, g0:g0 + 16, :]
            crr = cL[:, 0, g0:g0 + 16, :]
            cii = cL[:, 1, g0:g0 + 16, :]
            tt("dve", sums[:, 0], br, bi, ALU.add, (tp, tb, tpb, tpc), (tsm,))
            tt("dve", sums[:, 1], bi, br, ALU.subtract, (tp, tb, tpb, tpc), (tsm,))
            tt("dve", sums[:, 2], crr, cii, ALU.add, (tp, tb, tpb, tpc), (tsm,))
            tt("dve", sums[:, 3], crr, cii, ALU.subtract, (tp, tb, tpb, tpc), (tsm,))

            def build(name, dst, n_of_s, src_r, xsum, xdif, kind):
                tiles = []
                for s_ in range(8):
                    n = n_of_s(s_) + 8
                    pr = bcg(PW[:, n, 0, g0:g0 + 16])
                    pi = bcg(PW[:, n, 1, g0:g0 + 16])
                    psm = bcg(PSM[:, n, g0:g0 + 16])
                    eng = "dve"
                    tA, tB = tmps[eng]
                    tmt = tl("prep_tmp_" + eng)
                    mt = tl("prep_%s_%d" % (name, s_))
                    tiles.append(mt)
                    RW = (tp, cst, tb, tsm, tmt, mt, tpb, tpc)
                    WW = (mt, tmt)
                    d0 = dst[:, :, 0, s_, :]
                    d1 = dst[:, :, 1, s_, :]
                    ns_ = BUILD_NOSYNC
                    tt(eng, d1, psm, src_r, ALU.mult, RW, WW, nosync=ns_)
                    tt(eng, tA, pi, xsum, ALU.mult, RW, WW, nosync=ns_)
                    tt(eng, d0, d1, tA, ALU.subtract, RW, WW, nosync=ns_)
                    tt(eng, tA, pr, xdif, ALU.mult, RW, WW, nosync=ns_)
                    if kind == "w":
                        tt(eng, d1, d1, tA, ALU.add, RW, WW, nosync=ns_)
                    else:
                        tt(eng, d1, tA, d1, ALU.subtract, RW, WW, nosync=ns_)
                return tuple(tiles)

            tWc = build("wc", WcL, lambda s_: 7 - s_, br, sums[:, 0], sums[:, 1], "w")
            tWn = build("wn", WnL, lambda s_: -(s_ + 1), br, sums[:, 0], sums[:, 1], "w")
            tV = build("v", VL, lambda s_: s_ + 1, crr, sums[:, 2], sums[:, 3], "v")
            tvst = tl("prep_vst")
            twst = tl("prep_wst")
            tm0 = tl("prep_m0st")
            twb = tl("prep_wnlb")
            for x in range(2):
                act(Vst[:, :, x].rearrange("p g r m -> p g (r m)"),
                    VL.rearrange("p g r s k -> p g (r s k)"), AF.Copy, tV + (cst,), (tvst,), scale=rowm[:, x:x + 1])
            WnLb = a3f[:, 4608:6656].bitcast(BF16).rearrange("p (g r m) -> p g r m", g=16, r=2)
            act(WnLb.rearrange("p g r m -> p (g r m)"), WnL.rearrange("p g r s k -> p (g r s k)"), AF.Copy,
                tWn, (twb,))
            for gl in range(16):
                for ri in range(2):
                    bk = (gl * 2 + ri) % 8
                    P.add("pe", lambda e, o=PS[:, bk, 0:128], i=WcL[:, gl, ri].rearrange("p s k -> p (s k)"):
                          e.transpose(o, i, ident[:]), tWc + (cst,), (bank[bk],))
                    tt("dve", Wst[:, gl, :, ri, :],
                       PS[:, bk, 0:128].unsqueeze(1).broadcast_to([128, 2, 128]), colm[:], ALU.mult,
                       (bank[bk], cst), (twst,))
            for gl in range(16):
                for x in range(2):
                    bk = (gl * 2 + x) % 8
                    gg = 2 * (g0 + gl) + x
                    for ri in range(2):
                        mm(PS[:, bk, 0:128], WnLb[:, gl, ri, :], Vst[:, gl, x, ri, :], ri == 0, ri == 1,
                           (twb, tvst), (bank[bk],))
                    tt("dve", tmpAB[hf][:], PS[:, bk, 0:128], tmask[:], ALU.mult, (bank[bk], cst, tl("tmpAB")),
                       (tl("tmpAB"),))
                    stt(M0st[:, gl * 2 + x, :], ident[:], dcl[:, gg:gg + 1], tmpAB[hf][:], ALU.mult, ALU.add,
                        (tp, tpd, cst, tl("tmpAB")), (tm0,))
            dma("sp", SW[j, g0:g0 + 16].rearrange("g p (x r m) -> p g x r m", x=2, r=2), Wst, (twst,),
                (tl("SW"),), ch_prep)
            dma("sp", SMV[j, 2 * g0:2 * g0 + 32, :, 0:128].rearrange("g p m -> p g m"), M0st, (tm0,),
                (tl("SMV"),), ch_prep)
            dma("sp", SMV[j, 2 * g0:2 * g0 + 32, :, 128:384].rearrange("(g x) p (r m) -> p g x r m", x=2, r=2),
                Vst, (tvst,), (tl("SMV"),), ch_prep)

    tmpAB = [sb("tmpAB%d" % i, [128, 128]) for i in range(2)]

    has_b = "B" in layers
    bar_ops = []
    if has_b:
        for j in range(2):
            prep_s5(j)
        tiny = {
            "act": lambda e: e.activation(out=barA[:], in_=zero1[:], func=AF.Copy),
            "dve": lambda e: e.memset(barV[:], 0.0),
            "pool": lambda e: e.memset(barG[:], 0.0),
        }
        bar_ops = P.barrier(tiny)

    mset("pool", A1[:, :, :], 0.0, tuple(tl("hn_%d_%d" % (kc_, t_)) for kc_ in range(8) for t_ in range(3)))

    ring = {"rst": 0, "sq": 0, "tmpb": 0, "mmb": 0, "xs": 0}
    rst_t = [Tl("rst%d" % i) for i in range(2)]
    sq_t = [Tl("sq%d" % i) for i in range(2)]
    tmpb_t = [Tl("tmpb%d" % i) for i in range(2)]
    xs_t = [Tl("xs%d" % i) for i in range(2)]
    STATB = [3, 4, 5]

    def nxt(name, n):
        i = ring[name] % n
        ring[name] += 1
        return i

    def rstd_from_bank(bk, n):
        i = nxt("rst", 2)
        act(rst[i][:, 0:n], PS[:, bk, 0:n], AF.Ln, (bank[bk], cst), (rst_t[i],), bias=epsc[:], scale=1.0 / D)
        act(rst[i][:, 0:n], rst[i][:, 0:n], AF.Exp, (rst_t[i],), (rst_t[i],), scale=-0.5)
        return i

    def xcols(name, kc, c0, n):
        return tl("%s_%d_%d" % (name, kc, c0))

    def prenorm(layer, kind):
        gcol = gpre[:, layer * 8:(layer + 1) * 8]
        if kind == "B":
            for kc in range(8):
                v = Hn[:, kc, :].rearrange("p (s c) -> p s c", c=CB)[:, 0:4, 128:136]
                mset("pool", v, 0.0, tuple(tl("hn_%d_%d" % (kc, t2)) for t2 in range(3)))
        for ti, (c0, n) in enumerate(TILES_A):
            bk = STATB[ti]
            for kc in range(8):
                i = nxt("sq", 2)
                act(sqr[i][:, 0:n], Xsb[:, kc, c0:c0 + n], AF.Square, (tl("x_%d_%d" % (kc, ti)),), (sq_t[i],))
                mm(PS[:, bk, 0:n], ones_bf[:], sqr[i][:, 0:n], kc == 0, kc == 7, (sq_t[i], cst), (bank[bk],))
            r = rstd_from_bank(bk, n)
            for kc in range(8):
                xin = Xsb[:, kc, c0:c0 + n]
                if kind == "A":
                    stt(Hn[:, kc, c0:c0 + n], xin, gcol[:, kc:kc + 1], rst[r][:, 0:n], ALU.mult, ALU.mult,
                        (tl("x_%d_%d" % (kc, ti)), rst_t[r], cst), (tl("hn_%d_%d" % (kc, ti)),))
                else:
                    hv = Hn[:, kc, :].rearrange("p (s c) -> p s c", c=CB)
                    if ti < 2:
                        ov = hv[:, :, 64 * ti:64 * ti + 64]
                        iv = xin.rearrange("p (c s) -> p s c", s=8)
                        rv = rst[r][:, 0:n].rearrange("p (c s) -> p s c", s=8)
                    else:
                        ov = hv[:, 4:8, 128:136]
                        iv = xin.rearrange("p (q i) -> p i q", i=4)
                        rv = rst[r][:, 0:n].rearrange("p (q i) -> p i q", i=4)
                    stt(ov, iv, gcol[:, kc:kc + 1], rv, ALU.mult, ALU.mult,
                        (tl("x_%d_%d" % (kc, ti)), rst_t[r], cst),
                        tuple(tl("hn_%d_%d" % (kc, t2)) for t2 in range(3)))

    def out_stage_evac(bk, dmc, ti, c0, n, Obuf):
        cp("act", Obuf[:, dmc, c0:c0 + n], PS[:, bk, 0:n], (bank[bk],), (tl("o_%d_%d" % (dmc, ti)),))
        i = nxt("sq", 2)
        act(sqr[i][:, 0:n], PS[:, bk, 0:n], AF.Square, (bank[bk],), (sq_t[i],))
        sb_ = STATB[ti]
        flush_stat()
        pend_stat.append((PS[:, sb_, 0:n], sqr[i][:, 0:n], dmc == 0, dmc == 7, (sq_t[i], cst), (bank[sb_],)))

    pend_stat = []

    def flush_stat():
        while pend_stat:
            o_, r_, st_, sp_, rd_, wr_ = pend_stat.pop(0)
            mm(o_, ones_bf[:], r_, st_, sp_, rd_, wr_)

    def postnorm(layer, kind, Obuf, tiles):
        gcol = gpost[:, layer * 8:(layer + 1) * 8]
        for ti, (c0, n) in enumerate(tiles):
            r = rstd_from_bank(STATB[ti], n)
            for dmc in range(8):
                stt(Obuf[:, dmc, c0:c0 + n], Obuf[:, dmc, c0:c0 + n], gcol[:, dmc:dmc + 1], rst[r][:, 0:n],
                    ALU.mult, ALU.mult, (tl("o_%d_%d" % (dmc, ti)), rst_t[r], cst), (tl("o_%d_%d" % (dmc, ti)),))
            if kind == "A":
                for dmc in range(8):
                    tt("dve", Xsb[:, dmc, c0:c0 + n], Xsb[:, dmc, c0:c0 + n], Obuf[:, dmc, c0:c0 + n], ALU.add,
                       (tl("o_%d_%d" % (dmc, ti)), tl("x_%d_%d" % (dmc, ti))), (tl("x_%d_%d" % (dmc, ti)),))
        if kind == "B":
            for hlf in range(3):
                for dmc in range(8):
                    ov = Obuf[:, dmc, :].rearrange("p (s c) -> p s c", c=CB)
                    allo = tuple(tl("o_%d_%d" % (dmc, t2)) for t2 in range(3))
                    if hlf < 2:
                        xv = Xsb[:, dmc, 512 * hlf:512 * hlf + 512].rearrange("p (c s) -> p s c", s=8)
                        tt("dve", xv, xv, ov[:, :, 64 * hlf:64 * hlf + 64], ALU.add,
                           allo + (tl("x_%d_%d" % (dmc, hlf)),), (tl("x_%d_%d" % (dmc, hlf)),))
                    else:
                        xv = Xsb[:, dmc, 1024:1056].rearrange("p (q i) -> p i q", i=4)
                        tt("dve", xv, xv, ov[:, 4:8, 128:136], ALU.add, allo + (tl("x_%d_2" % dmc),),
                           (tl("x_%d_2" % dmc),))

    MRING = (0, 1, 2, 6, 7)

    def mbank():
        return MRING[nxt("mmb", 5)]

    def layer_a(layer, ps):
        j = layer // 2
        wb0 = j * NBLK_PER_J
        tb = tl("tabA")
        lt2 = tl("ltmp2")
        ltmp2 = O_A[:, 0, 0:1024].rearrange("p (h t) -> p h t", h=8)
        ltmp3 = O_A[:, 1, 0:256].rearrange("p (h t) -> p h t", h=8)[64:96]
        dma("sp", ltmp, wsT_d[j], (), (tl("ltmp"),) + OT_ALL, par)
        dma("sp", ltmp2, bsB_d[j], (), (lt2,) + OT_ALL, par)
        dma("sp", ltmp3, wsrep_d[j], (), (lt2,) + OT_ALL, par)

        def tables_dve1():
            tt("dve", wsTm[:], ltmp, trim[:].unsqueeze(1).broadcast_to([128, 8, 128]), ALU.mult,
               (tl("ltmp"), cst), (tb,))
            tt("dve", ltmp, ltmp, trim[:].unsqueeze(1).broadcast_to([128, 8, 128]), ALU.mult,
               (tl("ltmp"), cst), (tl("ltmp"),))

        def tables_compute():
            for hh in range(2):
                mm(PS[:, 3 + hh, :], ones_f[:], ltmp[:, 4 * hh:4 * hh + 4, :].rearrange("p a b -> p (a b)"),
                   True, True, (tl("ltmp"), cst), (bank[3 + hh],))
            for dc in range(16):
                hh = dc // 2
                bk = 3 + hh // 4
                stt(ctab[:, dc, :], PS[:, bk, (hh % 4) * 128:(hh % 4) * 128 + 128], lncol[:, j, 1, dc:dc + 1],
                    ltmp2[:, hh, :], ALU.mult, ALU.add, (bank[bk], lt2, cst), (tb,))
            tt("dve", bdw[64:96], ltmp3, bdm[64:96, :].unsqueeze(1).broadcast_to([32, 8, 32]), ALU.mult,
               (lt2, cst), (tb,))

        prenorm(layer, "A")

        def hn_reads(ti):
            return tuple(tl("hn_%d_%d" % (kc, ti)) for kc in range(8))

        wl = [wload(wb0 + 4)]
        for blk in range(4):
            if blk < 3:
                wl.append(wload(wb0 + 4 + blk + 1))
            wi = wl[blk]
            wv = WB[wi][:, :].rearrange("p (a b) -> p a b", b=512)
            for n in range(9):
                if blk == 0 and n == 4:
                    tables_dve1()
                if blk == 1 and n == 0:
                    tables_compute()
                M = 128
                c0 = 128 * n if n < 8 else 960
                bk = mbank()
                for kc in range(8):
                    mm(PS[0:M, bk, :], Hn[:, kc, c0:c0 + M], wv[:, kc, :], kc == 0, kc == 7,
                       ((tl("hn_%d_%d" % (kc, n // 4)),) if n < 8 else (tl("hn_%d_1" % kc), tl("hn_%d_2" % kc)))
                       + (wtile[wi],), (bank[bk],))
                cp("act", Gv[0:M, n, blk * 512:(blk + 1) * 512], PS[0:M, bk, :], (bank[bk],),
                   tuple(tl("g_%d_%d" % (n, dc)) for dc in range(4 * blk, 4 * blk + 4)))
                P.add("dve", lambda e, o=stats[0:M, n, blk, :], i=Gv[0:M, n, blk * 512:(blk + 1) * 512]:
                      e.bn_stats(out=o, in_=i), tuple(tl("g_%d_%d" % (n, dc)) for dc in range(4 * blk, 4 * blk + 4)),
                      (tl("stats"),))
        for n in range(9):
            M = 128
            P.add("dve", lambda e, o=mv[0:M, n, :], i=stats[0:M, n, :, :].rearrange("p a b -> p (a b)"):
                  e.bn_aggr(out=o, in_=i), (tl("stats"),), (tl("mv"),))
        act(rstdv[:], mv[:, :, 1], AF.Sqrt, (tl("mv"), cst), (tl("mv"),), bias=epsc[:], scale=1.0)
        P.add("dve", lambda e: e.reciprocal(out=rstdv[:], in_=rstdv[:]), (tl("mv"),), (tl("mv"),))
        stt(nmr[:], mv[:, :, 0], -1.0, rstdv[:], ALU.mult, ALU.mult, (tl("mv"),), (tl("mv"),))
        for n in range(9):
            M = 128
            Nt = 128 if n < 8 else 32
            gts = tuple(tl("g_%d_%d" % (n, dc)) for dc in range(16))
            act(Gv[0:M, n, :], Gv[0:M, n, :], AF.Identity, gts + (tl("mv"),), gts,
                bias=nmr[0:M, n:n + 1], scale=rstdv[0:M, n:n + 1])
            if n == 8:
                cvt = tl("cvt")
                ofl = A3[:, :].bitcast(F32)
                cvs = ofl[64:96, 2112:4160]
                cvg = ofl[64:96, 4160:6208]
                cvb = ofl[64:96, 6208:8256]
                dma("sp", cvg, lnbc_d[j, 0], (), (cvt, tl("ltmp"), lt2) + OT_ALL, par)
                dma("sp", cvb, lnbc_d[j, 1], (), (cvt, tl("ltmp"), lt2) + OT_ALL, par)
                tt("dve", cvs, Gv[64:96, 8, :], cvg, ALU.mult, gts + (cvt,), (cvt,))
                tt("dve", cvs, cvs, cvb, ALU.add, (cvt,), (cvt,))
                dma("sp", cv[j, ps], cvs, (cvt,) + OT_ALL, (tl("cv_out"),), ch_out)
            for dc in range(16):
                bk = 4 + dc // 4
                rhs = wsTm[:, dc // 2, :] if n < 8 else bdw[:, dc // 2, :]
                mm(PS[:, bk, (dc % 4) * 128:(dc % 4) * 128 + Nt], G[0:M, n, dc, :], rhs, True, True,
                   (tl("g_%d_%d" % (n, dc)), tb), (bank[bk],))
            if n < 8:
                for b4 in range(4):
                    bk = 4 + b4
                    pv = PS[:, bk, :].rearrange("p (d t) -> p d t", d=4)
                    gB = lncol[:, j, 0, 4 * b4:4 * b4 + 4].unsqueeze(2).broadcast_to([128, 4, 128])
                    tt("dve", pv, pv, gB, ALU.mult, (bank[bk], cst), (bank[bk],))
                    tt("dve", G[:, n, 4 * b4:4 * b4 + 4, :], pv, ctab[:, 4 * b4:4 * b4 + 4, :], ALU.add,
                       (bank[bk], tb), tuple(tl("g_%d_%d" % (n, dc_)) for dc_ in range(4 * b4, 4 * b4 + 4)))
            for dc in range(16 if n == 8 else 0):
                bk = 4 + dc // 4
                pin = PS[:, bk, (dc % 4) * 128:(dc % 4) * 128 + Nt]
                if n < 8:
                    pass
                else:
                    stt(G[:, 8, dc, 0:32].rearrange("p (q i) -> p q i", i=4),
                        pin.rearrange("p (q i) -> p q i", i=4), lncol[:, j, 0, dc:dc + 1],
                        ctab[:, dc, 0:4].unsqueeze(1).broadcast_to([128, 8, 4]), ALU.mult, ALU.add,
                        (bank[bk], tb, cst), (tl("g_8_%d" % dc),))

        def gview(ti, dc):
            if ti < 2:
                return G[:, 4 * ti:4 * ti + 4, dc, :]
            return G[:, 8, dc, 0:32]

        def gtiles(ti, dc):
            if ti < 2:
                return tuple(tl("g_%d_%d" % (n, dc)) for n in range(4 * ti, 4 * ti + 4))
            return (tl("g_8_%d" % dc),)

        for stage, b0 in (("u", 0), ("z", 8)):
            wl = [wload(wb0 + b0)]
            for blk in range(4):
                if blk < 3:
                    wl.append(wload(wb0 + b0 + blk + 1))
                wi = wl[blk]
                wv = WB[wi][:, :].rearrange("p (a b) -> p a b", b=512)
                for dcl in range(4):
                    dc = 4 * blk + dcl
                    for ti, (c0, n) in enumerate(TILES_A):
                        bk = mbank()
                        for kc in range(8):
                            mm(PS[:, bk, 0:n], wv[:, kc, dcl * 128:(dcl + 1) * 128], Hn[:, kc, c0:c0 + n],
                               kc == 0, kc == 7, (tl("hn_%d_%d" % (kc, ti)), wtile[wi]), (bank[bk],))
                        pv = PS[:, bk, 0:n]
                        if ti < 2:
                            pv = pv.rearrange("p (a b) -> p a b", b=128)
                        gt = gtiles(ti, dc)
                        if stage == "u":
                            tt("dve", gview(ti, dc), pv, gview(ti, dc), ALU.mult, (bank[bk],) + gt, gt)
                        else:
                            i = nxt("tmpb", 2)
                            act(tmpb[i][:, 0:n], PS[:, bk, 0:n], AF.Silu, (bank[bk],), (tmpb_t[i],))
                            tv = tmpb[i][:, 0:n]
                            if ti < 2:
                                tv = tv.rearrange("p (a b) -> p a b", b=128)
                            tt("dve", gview(ti, dc), tv, gview(ti, dc), ALU.mult, (tmpb_t[i],) + gt, gt)
        wl = [wload(wb0 + 12)]
        for blk in range(4):
            if blk < 3:
                wl.append(wload(wb0 + 12 + blk + 1))
            wi = wl[blk]
            wv = WB[wi][:, :].rearrange("p (a b) -> p a b", b=256)
            for dml in range(2):
                dmc = 2 * blk + dml
                for ti, (c0, n) in enumerate(TILES_A):
                    bk = mbank()
                    for dc in range(16):
                        mm(PS[:, bk, 0:n], wv[:, dc, dml * 128:(dml + 1) * 128], gview(ti, dc), dc == 0, dc == 15,
                           gtiles(ti, dc) + (wtile[wi],), (bank[bk],))
                    out_stage_evac(bk, dmc, ti, c0, n, O_A)
        flush_stat()
        postnorm(layer, "A", O_A, TILES_A)

    def layer_b(layer, ps):
        j = layer // 2
        wb0 = j * NBLK_PER_J + 16
        prenorm(layer, "B")
        for nm_, t_ in list(T.items()):
            if nm_.startswith(("xs5_", "Yd_", "Xd_", "yf_", "y_")) and not nm_.startswith("y_out"):
                if t_.w is not None and t_.w.chan is not None:
                    t_.w = None
                for k_ in [k_ for k_ in t_.r if isinstance(k_, int)]:
                    del t_.r[k_]
        tsx = tl("s5small")
        dma("sp", h0t[:], h0_d[j, ps], (), (tsx,), par)

        def c3(v64):
            return v64.unsqueeze(1).broadcast_to([128, 8, 64])

        def c3h(v32):
            return v32.unsqueeze(1).broadcast_to([128, 8, 32])

        def cstep(dst, src, tr, tp_, tn_, addend, eng="dve"):
            sv = src.rearrange("p q (g r) -> p q g r", r=2)
            t2v = st2[:].rearrange("p q (g r) -> p q g r", r=2)
            tt(eng, st1[:], src, c3(tr), ALU.mult, (tsx, cst), (tsx,))
            tt(eng, t2v[:, :, :, 0], sv[:, :, :, 1], c3h(tn_), ALU.mult, (tsx, cst), (tsx,))
            tt(eng, t2v[:, :, :, 1], sv[:, :, :, 0], c3h(tp_), ALU.mult, (tsx, cst), (tsx,))
            tt(eng, st1[:], st1[:], st2[:], ALU.add, (tsx,), (tsx,))
            if addend is None:
                cp(eng, dst, st1[:], (tsx,), (tsx,))
            else:
                tt(eng, dst, st1[:], addend, ALU.add, (tsx, tl("ub_a"), tl("ub_d")), (tsx,))

        cstep(h0p[:], h0t[:], M4r[:, j, :], M4p[:, j, :], M4n[:, j, :], None)

        Xdv = Xd.rearrange("(g k) (s c) -> s k g c", k=16, c=CB)

        def xreadback(f_):
            for s8 in range(8):
                dma("sp", Xs5[16 * s8:16 * s8 + 16, 8 * f_:8 * f_ + 8, :],
                    Xdv[s8][:, 8 * f_:8 * f_ + 8, :],
                    (tl("Xd_%d" % f_),), (tl("xs5_%d_%d" % (s8, f_)),) + (OT_ALL if (s8 == 0 and f_ == 0) else ()),
                    ch_xr[f_][s8 % 2])

        wl = [wload(wb0 + 0), wload(wb0 + 1)]
        for fc in range(8):
            wi = wl[fc // 4]
            wv = WB[wi][:, :].rearrange("p (a b) -> p a b", b=512)
            dcl = fc % 4
            xi = nxt("xs", 2)
            for ti, (c0, n) in enumerate(TILES_B):
                bk = mbank()
                for kc in range(8):
                    mm(PS[:, bk, 0:n], wv[:, kc, dcl * 128:(dcl + 1) * 128], Hn[:, kc, c0:c0 + n],
                       kc == 0, kc == 7, (tl("hn_%d_%d" % (kc, ti)), wtile[wi]), (bank[bk],))
                cp("act", xs[xi][:, c0:c0 + n], PS[:, bk, 0:n], (bank[bk],), (xs_t[xi],))
            dma("sp", Xd[fc * 128:(fc + 1) * 128, :], xs[xi][:], (xs_t[xi],), (tl("Xd_%d" % fc),), ch_xs[xi])
            if fc >= 1:
                xreadback(fc - 1)
        xreadback(7)
        cp("act", Hb[:, :, :, 128:136].rearrange("p g r c -> p (g r) c"),
           h0p[:].rearrange("p q g -> p g q"), (tsx,), (tl("hb_5"),))
        ub = tl("ub_all")
        UBQ = [tl("ub_q%d" % q_) for q_ in range(4)]
        UBS = tl("ub_s")
        batches1 = [(g0_, min(3, 32 - g0_)) for g0_ in range(0, 32, 3)]

        def load1(b_):
            g0_, n_ = batches1[b_]
            return rbload(lambda t_, n_=n_: t_[:, 0:n_ * 512].rearrange("p (a m) -> p a m", a=n_),
                          SW[j, g0_:g0_ + n_].rearrange("a p m -> p a m"), tl("SW"))

        pend = [load1(0), load1(1)]
        for b_, (g0_, n_) in enumerate(batches1):
            if b_ + 2 < len(batches1):
                pend.append(load1(b_ + 2))
            si = pend[b_]
            for a_ in range(n_):
                gp = g0_ + a_
                wv = RB[si][:, a_ * 512:(a_ + 1) * 512].rearrange("p (x r m) -> p x r m", x=2, r=2)
                bk = mbank()
                pu = PS[:, bk, 0:272].rearrange("p (r c) -> p r c", r=2)
                for ri in range(2):
                    for x in range(2):
                        mm(pu[:, ri, :], wv[:, x, ri, :], Xs5[:, 2 * gp + x, :], x == 0, x == 1,
                           (RBT[si],) + XS5_F[gp // 4], (bank[bk],))
                cp("act" if gp % 2 == 0 else "dve", Ub[:, :, 2 * gp:2 * gp + 2].rearrange("p c r -> p r c"), pu,
                   (bank[bk],), (tl("ub_a" if gp % 2 == 0 else "ub_d"),))
        wl = [wload(wb0 + 2), wload(wb0 + 3)]
        for fc in range(8):
            wi = wl[fc // 4]
            wv = WB[wi][:, :].rearrange("p (a b) -> p a b", b=512)
            dcl = fc % 4
            for ti, (c0, n) in enumerate(TILES_B):
                bk = mbank()
                for kc in range(8):
                    mm(PS[:, bk, 0:n], wv[:, kc, dcl * 128:(dcl + 1) * 128], Hn[:, kc, c0:c0 + n],
                       kc == 0, kc == 7, (tl("hn_%d_%d" % (kc, ti)), wtile[wi]), (bank[bk],))
                act(Gb[:, fc, c0:c0 + n], PS[:, bk, 0:n], AF.Silu, (bank[bk],), (tl("gb_%d_%d" % (fc, ti)),))
        hin = Hin0[:] if ps == 0 else Hcar[:, j, :]
        tr = AR2[:, j, :]
        tpp = AIp[:, j, :]
        tnn = AIn[:, j, :]
        sc1 = st1[:, 0, :]
        sc2 = st2[:, 0, :].rearrange("p (g r) -> p g r", r=2)
        scn = tl("scan")
        hbv = Hb.rearrange("p g r c -> p c g r")
        cp("act", hbv[:, 0, :, :], hin.rearrange("p (g r) -> p g r", r=2), (cst, tsx, tl("hcar")), (tl("hb_0"),))
        PB = (0, 48, 96, 120, 127)
        UBX = UBQ + [tl("ub_q4")]

        def pq(c_):
            return 0 if c_ < 48 else 1 if c_ < 96 else 2 if c_ < 120 else 3 if c_ < 127 else 4

        for c in range(128):
            prev = hin if c == 0 else Ub[:, c - 1, :]
            pv = prev.rearrange("p (g r) -> p g r", r=2)
            uq = UBX[pq(c)]
            rd = (uq, UBX[pq(max(c - 1, 0))], scn, cst, tsx, tl("hcar"), tl("ub_a"), tl("ub_d"))
            ns = SCAN_NOSYNC and c > 0
            tt("dve", sc1, prev, tr, ALU.mult, rd, (scn,), nosync=ns)
            tt("dve", sc2, pv[:, :, ::-1], AI2[:, j, :].rearrange("p (g r) -> p g r", r=2), ALU.mult, rd, (scn,), nosync=ns)
            tt("dve", Ub[:, c, :], Ub[:, c, :], sc1, ALU.add, rd, (uq,), nosync=ns)
            tt("dve", Ub[:, c, :], Ub[:, c, :], st2[:, 0, :], ALU.add, rd, (uq,), nosync=ns)
            for q4 in range(4):
                if c == PB[q4 + 1] - 1:
                    lo_, hi_ = PB[q4], PB[q4 + 1]
                    cp("act", Hb[:, :, :, 1 + lo_:1 + hi_].rearrange("p g r c -> p (g r) c"),
                       Ub[:, lo_:hi_, :].rearrange("p c g -> p g c"), (uq,), (tl("hb_%d" % (q4 + 1)),))
        cp("dve", Hcar[:, j, :], Ub[:, 127, :], (UBX[4],), (tl("hcar"),))
        if ps == npass - 1:
            dma("sp", stp[j], Hcar[:, j, :], (tl("hcar"),), (tl("stp_out_%d" % j),), ch_out)
        wb1 = j * NBLK_PER_J + 20
        wl_glu1 = [wload(wb1 + 0), wload(wb1 + 1)]
        def load3(b_):
            return rbload(lambda t_: t_[:, :].rearrange("p (a m) -> p a m", a=4),
                          SMV[j, 4 * b_:4 * b_ + 4].rearrange("a p m -> p a m"), tl("SMV"))

        pend = [load3(0), load3(1)]
        for g in range(64):
            b_ = g // 4
            if g % 4 == 0 and b_ + 2 < 16:
                pend.append(load3(b_ + 2))
            si = pend[b_]
            mv_ = RB[si][:, (g % 4) * 384:(g % 4) * 384 + 384]
            bk = mbank()
            py = PS[:, bk, 0:CB]
            mm(py, mv_[:, 0:128], Xs5[:, g, :], True, False, (RBT[si], tl("y_%d" % g)) + XS5_F[g // 8],
               (bank[bk],))
            for ri in range(2):
                mm(py, mv_[:, 128 + 128 * ri:256 + 128 * ri], Hb[:, g // 2, ri, :], False, ri == 1,
                   (RBT[si],) + tuple(tl("hb_%d" % q) for q in range(6)), (bank[bk],))
            act(Xs5[:, g, :], py, AF.Gelu_apprx_tanh, (bank[bk],), (tl("y_%d" % g),))
            if g % 8 == 7:
                fc = g // 8
                Ydv = Yd.rearrange("(g k) (t c) -> t k g c", k=16, c=CB)
                ys = tuple(tl("y_%d" % g_) for g_ in range(8 * fc, 8 * fc + 8))
                for t8 in range(8):
                    dma("sp", Ydv[t8][:, 8 * fc:8 * fc + 8, :], Xs5[16 * t8:16 * t8 + 16, 8 * fc:8 * fc + 8, :],
                        ys + XS5_F[fc], (tl("Yd_%d_%d" % (fc, t8)),), ch_yw[fc % 2])
                for fr in ([fc - 1] if fc >= 1 else []) + ([7] if fc == 7 else []):
                    dma("sp", yF[:, fr, :], Yd[fr * 128:(fr + 1) * 128, :],
                        tuple(tl("Yd_%d_%d" % (fr, t_)) for t_ in range(8)),
                        (tl("yf_%d" % fr),) + tuple(tl("hn_%d_%d" % (fr, t)) for t in range(3)), ch_scr3)
        cstep(hsf[:], h0p[:], tr, tpp, tnn, Ub[:, 128:136, :])
        dma("sp", sts[j, ps], hsf[:], (tsx,), (tl("sts_out"),), ch_out)
        for which in range(2):
            wl = wl_glu1 if which == 0 else [wload(wb1 + 2), wload(wb1 + 3)]
            for fc in range(8):
                wi = wl[fc // 4]
                wv = WB[wi][:, :].rearrange("p (a b) -> p a b", b=512)
                dcl = fc % 4
                for ti, (c0, n) in enumerate(TILES_B):
                    bk = mbank()
                    for kc in range(8):
                        mm(PS[:, bk, 0:n], wv[:, kc, dcl * 128:(dcl + 1) * 128], yF[:, kc, c0:c0 + n],
                           kc == 0, kc == 7, (tl("yf_%d" % kc), wtile[wi]), (bank[bk],))
                    gt = tl("gb_%d_%d" % (fc, ti))
                    if which == 0:
                        stt(Gb[:, fc, c0:c0 + n], PS[:, bk, 0:n], bglu[:, j, 0, fc:fc + 1], Gb[:, fc, c0:c0 + n],
                            ALU.add, ALU.mult, (bank[bk], gt, cst), (gt,))
                    else:
                        i = nxt("tmpb", 2)
                        act(tmpb[i][:, 0:n], PS[:, bk, 0:n], AF.Sigmoid, (bank[bk], cst), (tmpb_t[i],),
                            bias=bglu[:, j, 1, fc:fc + 1], scale=1.0)
                        tt("dve", Gb[:, fc, c0:c0 + n], tmpb[i][:, 0:n], Gb[:, fc, c0:c0 + n], ALU.mult,
                           (tmpb_t[i], gt), (gt,))
        wb2 = j * NBLK_PER_J + 24
        wl = [wload(wb2), wload(wb2 + 1)]
        for dmc in range(8):
            wi = wl[dmc // 4]
            wv = WB[wi][:, :].rearrange("p (a b) -> p a b", b=512)
            dcl = dmc % 4
            for ti, (c0, n) in enumerate(TILES_B):
                bk = mbank()
                for kc in range(8):
                    mm(PS[:, bk, 0:n], wv[:, kc, dcl * 128:(dcl + 1) * 128], Gb[:, kc, c0:c0 + n],
                       kc == 0, kc == 7, (tl("gb_%d_%d" % (kc, ti)), wtile[wi]), (bank[bk],))
                out_stage_evac(bk, dmc, ti, c0, n, O_B)
        flush_stat()
        postnorm(layer, "B", O_B, TILES_B)

    for ps in range(npass):
        for kc in range(8):
            dma("sp", Xsb[:, kc, :], xT[ps, :, kc, :], (),
                tuple(tl("x_%d_%d" % (kc, ti)) for ti in range(3)), ch_x[ps % 2],
                extra=(bar_ops if ps == 0 else ()))
        for layer, kind in enumerate(layers):
            if kind == "A":
                layer_a(layer, ps)
            elif kind == "B":
                layer_b(layer, ps)
        for kc in range(8):
            dma("sp", yT[ps, :, kc, :], Xsb[:, kc, :], tuple(tl("x_%d_%d" % (kc, ti)) for ti in range(3)),
                (tl("y_out_%d" % kc),), ch_out)
    print('sbuf bytes remaining', nc.sbuf_bytes_remaining)
    if max_ops is not None:
        print('total ops', len(P.ops))
        P.ops = P.ops[:max_ops]
    P.emit()
    return nc


def _blk(w, nb, kc, n):
    return np.ascontiguousarray(w.reshape(kc, 128, nb, n).transpose(2, 1, 0, 3)).reshape(nb, 128, kc * n)


def _consts():
    ident = np.eye(128, dtype=np.float32)
    s = np.arange(128)
    trimask = (s[:, None] <= s[None, :]).astype(np.float32)
    q = np.arange(32)
    bdmask = ((q[:, None] // 4 == q[None, :] // 4) & (q[:, None] % 4 <= q[None, :] % 4)).astype(np.float32)
    tmask = ((s[:, None] // 16) <= (s[None, :] // 16)).astype(np.float32)
    colmask = np.zeros((128, 2, 128), np.float32)
    colmask[:, 0, :64] = 1
    colmask[:, 1, 64:] = 1
    rowmask = np.zeros((128, 2), np.float32)
    rowmask[:64, 0] = 1
    rowmask[64:, 1] = 1
    return dict(ident=ident, trimask=trimask, bdmask=bdmask, tmask=tmask, colmask=colmask, rowmask=rowmask)


def _shared_inputs(inp):
    f = lambda a: np.ascontiguousarray(np.asarray(a, dtype=np.float32))
    blocks = []
    for j in range(2):
        blocks.append(_blk(f(inp["w_in_a"][j]), 12, 8, 512))
        blocks.append(_blk(f(inp["w_out_a"][j]), 4, 16, 256))
        blocks.append(_blk(f(inp["w_in_b"][j]), 4, 8, 512))
        blocks.append(_blk(f(inp["w_glu1"][j]), 2, 8, 512))
        blocks.append(_blk(f(inp["w_glu2"][j]), 2, 8, 512))
        blocks.append(_blk(f(inp["w_out_b"][j]), 2, 8, 512))
    d = dict(wall=np.concatenate(blocks, axis=0))
    col = lambda v, n: np.ascontiguousarray(f(v).reshape(n, 128).T)
    d["gpre"] = np.concatenate([col(inp["norm_pre"][l], 8) for l in range(4)], axis=1)
    d["gpost"] = np.concatenate([col(inp["norm_post"][l], 8) for l in range(4)], axis=1)
    d["lncol"] = np.ascontiguousarray(np.stack(
        [np.stack([col(inp["ln_v_g"][j], 16), col(inp["ln_v_b"][j], 16)], axis=1) for j in range(2)], axis=1))
    d["lnbc"] = np.ascontiguousarray(np.stack(
        [np.stack([np.broadcast_to(f(inp["ln_v_g"][j])[None, :], (32, 2048)),
                   np.broadcast_to(f(inp["ln_v_b"][j])[None, :], (32, 2048))]) for j in range(2)]))
    ws = f(inp["w_s"])
    d["wsT"] = np.ascontiguousarray(ws.transpose(0, 3, 1, 2))
    corner = ws[:, :, :4, :4].transpose(0, 3, 1, 2)
    d["wsrep"] = np.ascontiguousarray(np.tile(corner, (1, 8, 1, 8)))
    d["bsB"] = np.ascontiguousarray(np.broadcast_to(f(inp["b_s"])[:, None, :, :], (2, 128, 8, 128)))
    d["bglu"] = np.ascontiguousarray(np.stack(
        [np.stack([col(inp["b_glu1"][j], 8), col(inp["b_glu2"][j], 8)], axis=1) for j in range(2)], axis=1))

    def l2(a):
        return a.reshape(32, 2, 64).transpose(1, 2, 0).reshape(128, 32)

    aL2 = np.zeros((2, 128, 3, 32), np.float32)
    bL2 = np.zeros((2, 128, 2, 32, 16), np.float32)
    cL2 = np.zeros((2, 128, 2, 32, 16), np.float32)
    dcol = np.zeros((2, 128, 64), np.float32)
    for j in range(2):
        aL2[j, :, 0] = l2(f(inp["a_re"][j]))
        aL2[j, :, 1] = l2(f(inp["a_im"][j]))
        aL2[j, :, 2] = l2(np.broadcast_to(f(inp["log_dt"][j])[:, None], (64, 64)))
        for r, nm in enumerate(("b_re", "b_im")):
            b = f(inp[nm][j])
            bL2[j, :, r] = b.reshape(32, 2, 64, 16).transpose(1, 2, 0, 3).reshape(128, 32, 16)
        for r, nm in enumerate(("c_re", "c_im")):
            c = f(inp[nm][j])
            cL2[j, :, r] = c.reshape(32, 2, 16, 64).transpose(1, 3, 0, 2).reshape(128, 32, 16)
        dk = f(inp["d_skip"][j]).reshape(64, 16)
        dcol[j] = np.tile(dk.T, (8, 1))
    d.update(aL2=aL2, bL2=bL2, cL2=cL2, dcol=dcol)
    d.update(_consts())
    return d


def _core_inputs(inp, cid):
    f = lambda a: np.asarray(a, dtype=np.float32)
    xp = f(inp["x_prompt"][cid])
    xsm = f(inp["x_sample"][16 * cid:16 * cid + 16])
    xT = np.zeros((2, 128, 8, NTA), np.float32)
    for ps in range(2):
        cols = np.concatenate([xp[1024 * ps:1024 * ps + 1024], xsm[8 * ps:8 * ps + 8].reshape(32, 1024)], axis=0)
        xT[ps] = cols.T.reshape(8, 128, NTA).transpose(1, 0, 2)
    h0 = np.zeros((2, 2, 128, 8, 64), np.float32)
    for j in range(2):
        for ps in range(2):
            sl = slice(16 * cid + 8 * ps, 16 * cid + 8 * ps + 8)
            re = f(inp["state_ssm_re"][j, sl])
            im = f(inp["state_ssm_im"][j, sl])
            st = np.stack([re, im], axis=-1)
            st = st.reshape(8, 32, 2, 64, 2).transpose(2, 3, 0, 1, 4)
            h0[j, ps] = st.reshape(128, 8, 64)
    return dict(xT=xT, h0L2=h0)


_NC_CACHE = {}


def kernel(**inputs):
    inp = {k: np.asarray(v) for k, v in inputs.items()}
    shared = _shared_inputs(inp)
    in_maps = []
    for cid in range(NCORES):
        m = dict(shared)
        m.update(_core_inputs(inp, cid))
        in_maps.append(m)
    if "nc" not in _NC_CACHE:
        _NC_CACHE["nc"] = build_program()
    nc = _NC_CACHE["nc"]
    res = run_bass_kernel_spmd(nc, in_maps, core_ids=list(range(NCORES)))
    y_prompt = np.zeros((8, 2048, 1024), np.float32)
    y_sample = np.zeros((128, 4, 1024), np.float32)
    cvs = np.zeros((2, 128, 4, 2048), np.float32)
    srp = np.zeros((2, 8, 64, 64), np.float32)
    sip = np.zeros((2, 8, 64, 64), np.float32)
    srs = np.zeros((2, 128, 64, 64), np.float32)
    sis = np.zeros((2, 128, 64, 64), np.float32)
    for cid in range(NCORES):
        r = res.results[cid]
        yT = np.asarray(r["yT"])
        for ps in range(2):
            cols = yT[ps].transpose(2, 1, 0).reshape(NTA, 1024)
            y_prompt[cid, 1024 * ps:1024 * ps + 1024] = cols[:1024]
            y_sample[16 * cid + 8 * ps:16 * cid + 8 * ps + 8] = cols[1024:].reshape(8, 4, 1024)
        cvv = np.asarray(r["cv"])
        stpv = np.asarray(r["stp"])
        stsv = np.asarray(r["sts"])
        for j in range(2):
            for ps in range(2):
                cvs[j, 16 * cid + 8 * ps:16 * cid + 8 * ps + 8] = cvv[j, ps].reshape(8, 4, 2048)
                s = stsv[j, ps].reshape(2, 64, 8, 32, 2).transpose(2, 3, 0, 1, 4).reshape(8, 64, 64, 2)
                srs[j, 16 * cid + 8 * ps:16 * cid + 8 * ps + 8] = s[..., 0]
                sis[j, 16 * cid + 8 * ps:16 * cid + 8 * ps + 8] = s[..., 1]
            s = stpv[j].reshape(2, 64, 32, 2).transpose(2, 0, 1, 3).reshape(64, 64, 2)
            srp[j, cid] = s[..., 0]
            sip[j, cid] = s[..., 1]
    return (y_prompt, y_sample, cvs, srp, sip, srs, sis)
```
